# Optimizing a Trainium2 kernel written in Bass

```python
import math
import numpy as np
import jax
import jax.numpy as jnp
from jax import lax

D_MODEL = 1024
BATCH = 16
SEQ = 2048
DEPTH = 1

HEAD_DIM = 64
ROPE_DIM = HEAD_DIM // 4
ROPE_THETA = 500000.0
DIFF_HEADS = 4
DIFF_V_DIM = 2 * HEAD_DIM
NSA_HEADS = 8
NSA_KV_HEADS = 2
NSA_GROUP = NSA_HEADS // NSA_KV_HEADS
N_BRANCH = 3
CMP_BLOCK = 32
CMP_STRIDE = 16
CMP_HIDDEN = 4 * HEAD_DIM
SEL_BLOCK = 64
SEL_TOP_N = 16
WINDOW = 512
D_FF = 2816
Q_BLOCK = 128
SEL_Q_CHUNK = 64
MIX_WIDTH = DIFF_HEADS * DIFF_V_DIM + NSA_HEADS * HEAD_DIM
IN_SIZES = (DIFF_HEADS * 2 * HEAD_DIM, DIFF_HEADS * 2 * HEAD_DIM, DIFF_HEADS * DIFF_V_DIM,
            NSA_HEADS * HEAD_DIM) + (NSA_KV_HEADS * HEAD_DIM,) * 6 + (NSA_HEADS * N_BRANCH,)
IN_WIDTH = sum(IN_SIZES)
EPS = 1e-6
NEG_INF = -1e30
FORCE_SCORE = 1e9

kernel_name = "hymba_diffattn_nsa_macaron_layer"


def rms_norm(x, g):
    xf = x.astype(jnp.float32)
    y = xf * lax.rsqrt(jnp.mean(xf * xf, axis=-1, keepdims=True) + EPS)
    return (y * g.astype(jnp.float32)).astype(x.dtype)


def rope_tables(S):
    pos = jnp.arange(S, dtype=jnp.float32)
    inv = ROPE_THETA ** (-jnp.arange(0, ROPE_DIM, 2, dtype=jnp.float32) / ROPE_DIM)
    ang = pos[:, None] * inv[None, :]
    return jnp.cos(ang), jnp.sin(ang)


def partial_rope(x, cos, sin):
    xr = x[..., :ROPE_DIM].astype(jnp.float32)
    x1, x2 = xr[..., :ROPE_DIM // 2], xr[..., ROPE_DIM // 2:]
    rot = jnp.concatenate([x1 * cos - x2 * sin, x2 * cos + x1 * sin], axis=-1)
    return jnp.concatenate([rot.astype(x.dtype), x[..., ROPE_DIM:]], axis=-1)


def swiglu(h, w_gate, w_up, w_down):
    return (jax.nn.silu(h @ w_gate) * (h @ w_up)) @ w_down


def diff_attention(q, k, v, lam, subln_g, lambda_init):
    B, Hd, _, S, dh = q.shape
    nb = S // Q_BLOCK
    scale = dh ** -0.5
    qb = q.reshape(B, Hd, 2, nb, Q_BLOCK, dh).transpose(3, 0, 1, 2, 4, 5)
    kpos = jnp.arange(S)

    def block(args):
        qi, i = args
        s = jnp.einsum('bhmqd,bhmkd->bhmqk', qi, k).astype(jnp.float32) * scale
        qpos = i * Q_BLOCK + jnp.arange(Q_BLOCK)
        s = jnp.where(kpos[None, :] <= qpos[:, None], s, NEG_INF)
        p = jax.nn.softmax(s, axis=-1)
        a = p[:, :, 0] - lam * p[:, :, 1]
        return jnp.einsum('bhqk,bhkd->bhqd', a.astype(v.dtype), v)

    o = lax.map(block, (qb, jnp.arange(nb)))
    o = o.transpose(1, 2, 0, 3, 4).reshape(B, Hd, S, 2 * dh)
    o = rms_norm(o, subln_g) * (1.0 - lambda_init)
    return o.transpose(0, 2, 1, 3).reshape(B, S, Hd * 2 * dh)


def compress(x, pe, w1, w2):
    B, G, S, dh = x.shape
    r = CMP_BLOCK // CMP_STRIDE
    nc = S // CMP_STRIDE - r + 1
    sub = x.reshape(B, G, S // CMP_STRIDE, CMP_STRIDE, dh)
    blocks = jnp.concatenate([sub[:, :, j:j + nc] for j in range(r)], axis=3)
    blocks = (blocks + pe).reshape(B, G, nc, CMP_BLOCK * dh)
    return jax.nn.gelu(blocks @ w1) @ w2


def sel_overlap(nc, n_sel):
    cs = np.arange(nc)[:, None] * CMP_STRIDE
    ss = np.arange(n_sel)[None, :] * SEL_BLOCK
    ov = np.minimum(cs + CMP_BLOCK, ss + SEL_BLOCK) - np.maximum(cs, ss)
    return jnp.asarray(np.clip(ov, 0, None) / CMP_BLOCK, dtype=jnp.float32)


def nsa_attention(q, k_cmp, v_cmp, k_sel, v_sel, k_win, v_win, gates,
                  pe_k, wk1, wk2, pe_v, wv1, wv2):
    B, H, S, dh = q.shape
    G, Hg = NSA_KV_HEADS, NSA_GROUP
    scale = dh ** -0.5
    qg = q.reshape(B, G, Hg, S, dh)
    t = jnp.arange(S)

    kc = compress(k_cmp, pe_k, wk1, wk2)
    vc = compress(v_cmp, pe_v, wv1, wv2)
    nc = kc.shape[2]
    cend = jnp.arange(nc) * CMP_STRIDE + CMP_BLOCK - 1
    cvalid = cend[None, :] <= t[:, None]
    s = jnp.einsum('bghsd,bgcd->bghsc', qg, kc).astype(jnp.float32) * scale
    p_cmp = jnp.where(cvalid, jax.nn.softmax(jnp.where(cvalid, s, NEG_INF), axis=-1), 0.0)
    o_cmp = jnp.einsum('bghsc,bgcd->bghsd', p_cmp.astype(vc.dtype), vc)

    n_sel = S // SEL_BLOCK
    top_n = min(SEL_TOP_N, n_sel)
    imp = jnp.einsum('bghsc,cj->bgsj', p_cmp, sel_overlap(nc, n_sel))
    blk = jnp.arange(n_sel)[None, :]
    cur = (t // SEL_BLOCK)[:, None]
    causal_blk = blk * SEL_BLOCK <= t[:, None]
    forced = (blk == 0) | ((blk <= cur) & (blk >= cur - 1))
    imp = jnp.where(forced, FORCE_SCORE, jnp.where(causal_blk, imp, -1.0))
    _, idx = lax.top_k(imp, top_n)

    kb = k_sel.reshape(B, G, n_sel, SEL_BLOCK, dh)
    vb = v_sel.reshape(B, G, n_sel, SEL_BLOCK, dh)
    C = SEL_Q_CHUNK
    nq = S // C
    q_c = qg.reshape(B, G, Hg, nq, C, dh).transpose(3, 0, 1, 2, 4, 5)
    idx_c = idx.reshape(B, G, nq, C, top_n).transpose(2, 0, 1, 3, 4)
    bi = jnp.arange(B)[:, None, None, None]
    gi = jnp.arange(G)[None, :, None, None]

    def sel_chunk(args):
        qi, ii, c = args
        ks = kb[bi, gi, ii]
        vs = vb[bi, gi, ii]
        s = jnp.einsum('bghqd,bgqnld->bghqnl', qi, ks).astype(jnp.float32) * scale
        kpos = ii[..., None] * SEL_BLOCK + jnp.arange(SEL_BLOCK)
        qpos = c * C + jnp.arange(C)
        mask = kpos <= qpos[None, None, :, None, None]
        s = jnp.where(mask[:, :, None], s, NEG_INF).reshape(B, G, Hg, C, top_n * SEL_BLOCK)
        p = jax.nn.softmax(s, axis=-1).reshape(B, G, Hg, C, top_n, SEL_BLOCK)
        return jnp.einsum('bghqnl,bgqnld->bghqd', p.astype(vs.dtype), vs)

    o_sel = lax.map(sel_chunk, (q_c, idx_c, jnp.arange(nq)))
    o_sel = o_sel.transpose(1, 2, 3, 0, 4, 5).reshape(B, G, Hg, S, dh)

    nb = S // Q_BLOCK
    span = WINDOW + Q_BLOCK
    kp = jnp.pad(k_win, ((0, 0), (0, 0), (WINDOW, 0), (0, 0)))
    vp = jnp.pad(v_win, ((0, 0), (0, 0), (WINDOW, 0), (0, 0)))
    q_w = qg.reshape(B, G, Hg, nb, Q_BLOCK, dh).transpose(3, 0, 1, 2, 4, 5)

    def win_block(args):
        qi, i = args
        start = i * Q_BLOCK
        ks = lax.dynamic_slice_in_dim(kp, start, span, axis=2)
        vs = lax.dynamic_slice_in_dim(vp, start, span, axis=2)
        s = jnp.einsum('bghqd,bgkd->bghqk', qi, ks).astype(jnp.float32) * scale
        qpos = start + jnp.arange(Q_BLOCK)
        kpos = start - WINDOW + jnp.arange(span)
        d = qpos[:, None] - kpos[None, :]
        mask = (d >= 0) & (d < WINDOW) & (kpos[None, :] >= 0)
        p = jax.nn.softmax(jnp.where(mask, s, NEG_INF), axis=-1)
        return jnp.einsum('bghqk,bgkd->bghqd', p.astype(vs.dtype), vs)

    o_win = lax.map(win_block, (q_w, jnp.arange(nb)))
    o_win = o_win.transpose(1, 2, 3, 0, 4, 5).reshape(B, G, Hg, S, dh)

    o = gates[..., 0:1] * o_cmp + gates[..., 1:2] * o_sel + gates[..., 2:3] * o_win
    return o.transpose(0, 3, 1, 2, 4).reshape(B, S, H * dh)


def hybrid_mixer(h, w_in, lambda_q1, lambda_k1, lambda_q2, lambda_k2, diff_subln,
                 cmp_pe_k, cmp_k_w1, cmp_k_w2, cmp_pe_v, cmp_v_w1, cmp_v_w2, w_out, lambda_init):
    B, S, _ = h.shape
    dh = HEAD_DIM
    cos, sin = rope_tables(S)
    pts = [int(v) for v in np.cumsum(IN_SIZES)[:-1]]
    dq, dk, dv, nq, kc, vc, ks, vs, kw, vw, gl = jnp.split(h @ w_in, pts, axis=-1)

    dq = partial_rope(dq.reshape(B, S, DIFF_HEADS, 2, dh).transpose(0, 2, 3, 1, 4), cos, sin)
    dk = partial_rope(dk.reshape(B, S, DIFF_HEADS, 2, dh).transpose(0, 2, 3, 1, 4), cos, sin)
    dv = dv.reshape(B, S, DIFF_HEADS, DIFF_V_DIM).transpose(0, 2, 1, 3)
    lq1, lk1 = lambda_q1.astype(jnp.float32), lambda_k1.astype(jnp.float32)
    lq2, lk2 = lambda_q2.astype(jnp.float32), lambda_k2.astype(jnp.float32)
    lam = jnp.exp(jnp.sum(lq1 * lk1)) - jnp.exp(jnp.sum(lq2 * lk2)) + lambda_init
    o_diff = diff_attention(dq, dk, dv, lam, diff_subln, lambda_init)

    def kv_heads(a):
        return a.reshape(B, S, NSA_KV_HEADS, dh).transpose(0, 2, 1, 3)
    nq = partial_rope(nq.reshape(B, S, NSA_HEADS, dh).transpose(0, 2, 1, 3), cos, sin)
    gates = jax.nn.sigmoid(gl.reshape(B, S, NSA_HEADS, N_BRANCH)).transpose(0, 2, 1, 3)
    gates = gates.reshape(B, NSA_KV_HEADS, NSA_GROUP, S, N_BRANCH)
    o_nsa = nsa_attention(nq, kv_heads(kc), kv_heads(vc),
                          partial_rope(kv_heads(ks), cos, sin), kv_heads(vs),
                          partial_rope(kv_heads(kw), cos, sin), kv_heads(vw), gates,
                          cmp_pe_k, cmp_k_w1, cmp_k_w2, cmp_pe_v, cmp_v_w1, cmp_v_w2)

    return jnp.concatenate([o_diff, o_nsa], axis=-1) @ w_out


def setup_inputs(seed: int = 0) -> dict:
    key = jax.random.key(seed)
    ks = jax.random.split(key, 26)
    f32 = jnp.float32
    L = DEPTH

    def w(k, shape, fan_in):
        return jax.random.normal(k, shape, f32) * fan_in ** -0.5

    def gain(k, n):
        return 1.0 + 0.1 * jax.random.normal(k, (L, n), f32)

    return {
        'x': jax.random.normal(ks[0], (BATCH, SEQ, D_MODEL), f32),
        'ff1_norm_pre': gain(ks[1], D_MODEL),
        'ff1_w_gate': w(ks[2], (L, D_MODEL, D_FF), D_MODEL),
        'ff1_w_up': w(ks[3], (L, D_MODEL, D_FF), D_MODEL),
        'ff1_w_down': w(ks[4], (L, D_FF, D_MODEL), D_FF),
        'ff1_norm_post': gain(ks[5], D_MODEL),
        'mix_norm_pre': gain(ks[6], D_MODEL),
        'w_in': w(ks[7], (L, D_MODEL, IN_WIDTH), D_MODEL),
        'lambda_q1': 0.1 * jax.random.normal(ks[8], (L, HEAD_DIM), f32),
        'lambda_k1': 0.1 * jax.random.normal(ks[9], (L, HEAD_DIM), f32),
        'lambda_q2': 0.1 * jax.random.normal(ks[10], (L, HEAD_DIM), f32),
        'lambda_k2': 0.1 * jax.random.normal(ks[11], (L, HEAD_DIM), f32),
        'diff_subln': gain(ks[12], DIFF_V_DIM),
        'cmp_pe_k': 0.1 * jax.random.normal(ks[13], (L, CMP_BLOCK, HEAD_DIM), f32),
        'cmp_k_w1': w(ks[14], (L, CMP_BLOCK * HEAD_DIM, CMP_HIDDEN), CMP_BLOCK * HEAD_DIM),
        'cmp_k_w2': w(ks[15], (L, CMP_HIDDEN, HEAD_DIM), CMP_HIDDEN),
        'cmp_pe_v': 0.1 * jax.random.normal(ks[16], (L, CMP_BLOCK, HEAD_DIM), f32),
        'cmp_v_w1': w(ks[17], (L, CMP_BLOCK * HEAD_DIM, CMP_HIDDEN), CMP_BLOCK * HEAD_DIM),
        'cmp_v_w2': w(ks[18], (L, CMP_HIDDEN, HEAD_DIM), CMP_HIDDEN),
        'w_out': w(ks[19], (L, MIX_WIDTH, D_MODEL), MIX_WIDTH),
        'mix_norm_post': gain(ks[20], D_MODEL),
        'ff2_norm_pre': gain(ks[21], D_MODEL),
        'ff2_w_gate': w(ks[22], (L, D_MODEL, D_FF), D_MODEL),
        'ff2_w_up': w(ks[23], (L, D_MODEL, D_FF), D_MODEL),
        'ff2_w_down': w(ks[24], (L, D_FF, D_MODEL), D_FF),
        'ff2_norm_post': gain(ks[25], D_MODEL),
    }


def reference(x, ff1_norm_pre, ff1_w_gate, ff1_w_up, ff1_w_down, ff1_norm_post,
              mix_norm_pre, w_in, lambda_q1, lambda_k1, lambda_q2, lambda_k2, diff_subln,
              cmp_pe_k, cmp_k_w1, cmp_k_w2, cmp_pe_v, cmp_v_w1, cmp_v_w2, w_out, mix_norm_post,
              ff2_norm_pre, ff2_w_gate, ff2_w_up, ff2_w_down, ff2_norm_post):
    for l in range(DEPTH):
        lambda_init = 0.8 - 0.6 * math.exp(-0.3 * l)
        h = swiglu(rms_norm(x, ff1_norm_pre[l]), ff1_w_gate[l], ff1_w_up[l], ff1_w_down[l])
        x = x + 0.5 * rms_norm(h, ff1_norm_post[l])
        h = hybrid_mixer(rms_norm(x, mix_norm_pre[l]), w_in[l], lambda_q1[l], lambda_k1[l],
                         lambda_q2[l], lambda_k2[l], diff_subln[l], cmp_pe_k[l], cmp_k_w1[l],
                         cmp_k_w2[l], cmp_pe_v[l], cmp_v_w1[l], cmp_v_w2[l], w_out[l], lambda_init)
        x = x + rms_norm(h, mix_norm_post[l])
        h = swiglu(rms_norm(x, ff2_norm_pre[l]), ff2_w_gate[l], ff2_w_up[l], ff2_w_down[l])
        x = x + 0.5 * rms_norm(h, ff2_norm_post[l])
    return x
```

```python
import numpy as np
import ml_dtypes
from contextlib import ExitStack
import concourse.bass as bass
import concourse.mybir as mybir
from concourse.bass_utils import run_bass_kernel_spmd

F32 = mybir.dt.float32
BF16 = mybir.dt.bfloat16
AF = mybir.ActivationFunctionType
ALU = mybir.AluOpType
AX = mybir.AxisListType

S = 2048
D = 1024
DFF = 2816
NF = DFF // 128
NB = 2
EPS = 1e-6
ENGS = ("pe", "act", "dve", "pool", "sp")


import re
PSUM_RE = re.compile(r"^(tr\d|gu\d|dn\d|pj\d|tq\d|pbps|hps\d|cps|sps\d|ab\d|tps)")


class Res:
    __slots__ = ("name", "w", "r")

    def __init__(self, name):
        self.name = name
        self.w = None
        self.r = {}


class Op:
    __slots__ = ("eng", "fn", "deps", "kind", "key", "val", "idx", "sig", "waits", "ordinal")


class Prog:
    def __init__(self, nc, stack):
        self.nc = nc
        self.stack = stack
        self.res = {}
        self.ops = []
        self.eng_n = {e: 0 for e in ENGS}
        self.sigbase = {e: 0 for e in ENGS}
        self.esem = {e: stack.enter_context(nc.semaphore("sem_" + e)) for e in ENGS if e != "sp"}
        self.dsem = {}
        self.dcount = {}
        self.free_sems = {"sw": [], "hw": []}
        self.dcls = {}
        self.live = []
        self.seen = {e: {} for e in ENGS}
        self.sigord = {e: {} for e in ENGS}

    def R(self, name):
        r = self.res.get(name)
        if r is None:
            r = self.res[name] = Res(name)
        return r

    def _deps(self, reads, writes):
        deps = []
        for n in reads:
            r = self.R(n)
            if r.w is not None:
                deps.append(r.w)
        for n in writes:
            r = self.R(n)
            if r.w is not None:
                deps.append(r.w)
            deps.extend(r.r.values())
        return deps

    def _mark(self, reads, writes, ev, rkey):
        for n in reads:
            self.R(n).r[rkey] = ev
        for n in writes:
            r = self.R(n)
            r.w = ev
            r.r = {}

    def op(self, eng, fn, r=(), w=()):
        w = list(w) + [n + "#rd" for n in r if PSUM_RE.match(n)]
        o = Op()
        o.eng = eng
        o.fn = fn
        o.kind = "c"
        o.deps = self._deps(r, w)
        o.idx = self.eng_n[eng]
        self.eng_n[eng] += 1
        o.sig = False
        self._mark(r, w, ("c", eng, o.idx), eng)
        self.ops.append(o)
        return o

    def dma(self, q, fn, key, r=(), w=()):
        o = Op()
        o.eng = q
        o.fn = fn
        o.kind = "d"
        o.key = key
        if key not in self.dsem:
            cls = "sw" if q == "pool" else "hw"
            self.dcls[key] = cls
            if self.free_sems[cls]:
                self.dsem[key], self.dcount[key] = self.free_sems[cls].pop()
            else:
                self.dsem[key] = self.stack.enter_context(self.nc.semaphore("dsem%d" % len(self.dsem)))
                self.dcount[key] = 0
            self.live.append(key)
        o.deps = self._deps(r, w)
        self.dcount[key] += 16
        o.val = self.dcount[key]
        o.idx = self.eng_n[q]
        self.eng_n[q] += 1
        o.sig = False
        self._mark(r, w, ("d", key, o.val), "d_" + key)
        self.ops.append(o)
        return o

    def barrier_all(self):
        allres = list(self.res.keys())
        for e in ENGS:
            self.op(e, None, r=allres)
        self.res = {}
        for k in self.live:
            self.free_sems[self.dcls[k]].append((self.dsem[k], self.dcount[k]))
        self.live = []

    def flush(self):
        nc = self.nc
        ops = self.ops
        self.ops = []
        self.nflush = getattr(self, "nflush", 0) + 1
        if LIMIT is not None and self.nflush == LIMIT[0]:
            ops = ops[:LIMIT[1]]
        byidx = {}
        for o in ops:
            if o.kind == "c":
                byidx[(o.eng, o.idx)] = o
        for o in ops:
            o.waits = []
            seen = self.seen[o.eng]
            best = {}
            for d in o.deps:
                if d[0] == "c":
                    if d[1] == "pe" and o.eng == "pe":
                        continue
                    k = ("c", d[1])
                else:
                    k = ("d", d[1])
                if k not in best or d[2] > best[k][2]:
                    best[k] = d
            for d in best.values():
                if d[0] == "c":
                    _, pe, pi = d
                    if seen.get(pe, -1) >= pi:
                        continue
                    prod = byidx.get((pe, pi))
                    if prod is None:
                        assert pi in self.sigord[pe], (pe, pi)
                    else:
                        prod.sig = True
                    seen[pe] = pi
                    o.waits.append(d)
                else:
                    _, key, val = d
                    k = "d_" + key
                    if seen.get(k, 0) >= val:
                        continue
                    seen[k] = val
                    o.waits.append(d)
        last = {}
        for o in ops:
            if o.kind == "c":
                last[o.eng] = o
        for o in last.values():
            o.sig = True
        for o in ops:
            if o.kind == "c" and o.sig:
                self.sigbase[o.eng] += 1
                self.sigord[o.eng][o.idx] = self.sigbase[o.eng]
        per = {e: [o for o in ops if o.eng == e] for e in ENGS}

        def emit(eng_name, eng):
            for o in per[eng_name]:
                for d in o.waits:
                    if d[0] == "c":
                        eng.wait_ge(self.esem[d[1]], self.sigord[d[1]][d[2]])
                    else:
                        eng.wait_ge(self.dsem[d[1]], d[2])
                if o.fn is None:
                    if o.kind == "c" and o.sig:
                        eng.nop().then_inc(self.esem[eng_name], 1) if eng_name != "sp" else None
                    continue
                ins = o.fn(eng)
                if o.kind == "d":
                    ins.then_inc(self.dsem[o.key], 16)
                elif o.sig:
                    ins.then_inc(self.esem[eng_name], 1)

        with nc.Block() as block:
            @block.tensor
            def _(e):
                emit("pe", e)

            @block.scalar
            def _(e):
                emit("act", e)

            @block.vector
            def _(e):
                emit("dve", e)

            @block.gpsimd
            def _(e):
                emit("pool", e)

            @block.sync
            def _(e):
                emit("sp", e)


def ffn_phase(nc, P, tag, src, dst, wg_d, wu_d, wd_d, gpre_d, gpost_d, ident_d):
    with ExitStack() as st:
        def sb(name, shape, dt):
            return st.enter_context(nc.sbuf_tensor(tag + name, shape, dt))

        def ps(name, shape, dt):
            return st.enter_context(nc.psum_tensor(tag + name, shape, dt))

        Wg = sb("Wg", [128, 8, DFF], BF16)
        Wu = sb("Wu", [128, 8, DFF], BF16)
        Wd = sb("Wd", [128, NF, D], BF16)
        xin = [sb("xin%d" % i, [128, D], F32) for i in range(2)]
        hn = sb("hn", [128, 4, D], BF16)
        hT = sb("hT", [128, 8, 512], BF16)
        actT = sb("actT", [128, NF, 512], BF16)
        G = sb("G", [128, D], F32)
        gpre = sb("gpre", [128, 8], F32)
        ident = sb("ident", [128, 128], BF16)
        xres = [sb("xres%d" % i, [128, D], F32) for i in range(2)]
        ytmp = [sb("ytmp%d" % i, [128, D], F32) for i in range(2)]
        sg = [sb("sg%d" % i, [128, 512], F32) for i in range(2)]
        ss = sb("ss", [128, 8], F32)
        rstd = sb("rstd", [128, 8], F32)
        ss2 = sb("ss2", [128, 2], F32)
        rstd2 = sb("rstd2", [128, 2], F32)
        epsb = sb("epsb", [128, 1], F32)
        P.op("pool", lambda e: e.memset(epsb[:], EPS), w=["epsb"])
        tr = [ps("tr%d" % i, [128, 2, 512], BF16) for i in range(2)]
        gu = [ps("gu%d" % i, [128, 512], F32) for i in range(4)]
        dn = ps("dn", [128, D], F32)

        P.dma("sp", lambda e: e.dma_start(out=gpre[:], in_=gpre_d), tag + "c0", w=["gpre"])
        P.dma("sp", lambda e: e.dma_start(out=G[:], in_=gpost_d), tag + "c1", w=["G"])
        P.dma("sp", lambda e: e.dma_start(out=ident[:], in_=ident_d), tag + "c2", w=["ident"])
        P.op("dve", lambda e: e.tensor_scalar(out=G[:], in0=G[:], scalar1=0.5, scalar2=None, op0=ALU.mult),
             r=["G"], w=["G"])
        wg_v = wg_d.rearrange("(k p) f -> p k f", p=128)
        wu_v = wu_d.rearrange("(k p) f -> p k f", p=128)
        wd_v = wd_d.rearrange("(f p) d -> p f d", p=128)
        FG = [(0, 2), (2, 6), (6, 10), (10, 14), (14, 18), (18, 22)]
        fgrp = {}
        for gi, (a, b) in enumerate(FG):
            for f in range(a, b):
                fgrp[f] = gi
            for nm, W, v in (("Wg", Wg, wg_v), ("Wu", Wu, wu_v)):
                for kh in range(2):
                    P.dma("pool",
                          lambda e, W=W, v=v, a=a, b=b, kh=kh: e.dma_start(
                              out=W[:, kh * 4:(kh + 1) * 4, a * 128:b * 128],
                              in_=v[:, kh * 4:(kh + 1) * 4, a * 128:b * 128]),
                          "%s%s%d" % (tag, nm, gi), w=["%s%d" % (nm, gi)])
        for gi, (a, b) in enumerate(FG):
            P.dma("pool", lambda e, a=a, b=b: e.dma_start(out=Wd[:, a:b, :], in_=wd_v[:, a:b, :]),
                  "%sWd%d" % (tag, gi), w=["Wd%d" % gi])

        blocks = [(b, i) for b in range(NB) for i in range(4)]
        nxi = [0]

        def emit_N(bi, j):
            b, i = blocks[bi]
            t0 = i * 512 + j * 128
            xb = xin[nxi[0] % 2]
            xn = "xin%d" % (nxi[0] % 2)
            nxi[0] += 1
            P.dma("sp", lambda e: e.dma_start(out=xb[:], in_=src[b, t0:t0 + 128, :]), tag + xn, w=[xn])
            P.op("act", lambda e: e.activation(out=hn[:, j, :], in_=xb[:], func=AF.Square,
                                               accum_out=ss[:, j:j + 1]),
                 r=[xn], w=["hn%d" % j, "ss%d" % j])
            P.op("act", lambda e: e.activation(out=ss[:, j:j + 1], in_=ss[:, j:j + 1], func=AF.Sqrt,
                                               scale=1.0 / D, bias=epsb[:, 0:1]),
                 r=["ss%d" % j, "epsb"], w=["ss%d" % j])
            P.op("dve", lambda e: e.reciprocal(out=rstd[:, j:j + 1], in_=ss[:, j:j + 1]),
                 r=["ss%d" % j], w=["rstd%d" % j])
            P.op("act", lambda e: e.activation(out=hn[:, j, :], in_=xb[:], func=AF.Copy,
                                               scale=rstd[:, j:j + 1]),
                 r=[xn, "rstd%d" % j], w=["hn%d" % j])

        def emit_T(bi):
            for kp in range(4):
                bank = kp % 2
                for kk in range(2):
                    k = kp * 2 + kk
                    for j in range(4):
                        P.op("pe", lambda e, k=k, kk=kk, j=j, bank=bank: e.transpose(
                            tr[bank][:, kk, j * 128:(j + 1) * 128], hn[:, j, k * 128:(k + 1) * 128], ident[:]),
                             r=["hn%d" % j, "ident"], w=["tr%d" % bank])
                for kk in range(2):
                    k = kp * 2 + kk
                    P.op("dve", lambda e, k=k, kk=kk, bank=bank: e.tensor_scalar(
                        out=hT[:, k, :], in0=tr[bank][:, kk, :], scalar1=gpre[:, k:k + 1], scalar2=None,
                        op0=ALU.mult),
                         r=["tr%d" % bank, "gpre"], w=["hT%d" % k])

        gui = [0]

        def emit_GU_f(bi, f):
            pr = gui[0] % 2
            gui[0] += 1
            pg, pu = gu[2 * pr], gu[2 * pr + 1]
            gi = fgrp[f]
            for nm, W, pt in (("Wg", Wg, pg), ("Wu", Wu, pu)):
                pn = "gu%d%s" % (pr, nm)
                for k in range(8):
                    P.op("pe", lambda e, W=W, pt=pt, k=k: e.matmul(
                        pt[:], W[:, k, f * 128:(f + 1) * 128], hT[:, k, :], start=(k == 0), stop=(k == 7)),
                         r=["%s%d" % (nm, gi), "hT%d" % k], w=[pn])
            s = sg[pr]
            P.op("act", lambda e: e.activation(out=s[:], in_=pg[:], func=AF.Silu),
                 r=["gu%dWg" % pr], w=["sg%d" % pr])
            P.op("dve", lambda e: e.tensor_tensor(out=actT[:, f, :], in0=s[:], in1=pu[:], op=ALU.mult),
                 r=["sg%d" % pr, "gu%dWu" % pr], w=["actT%d" % f])

        dni = [0]

        def emit_D(bi):
            b, i = blocks[bi]
            for j in range(4):
                t0 = i * 512 + j * 128
                q = dni[0] % 2
                dni[0] += 1
                xr, yt = xres[q], ytmp[q]
                xrn, ytn = "xres%d" % q, "ytmp%d" % q
                P.dma("act", lambda e, xr=xr, t0=t0: e.dma_start(out=xr[:], in_=src[b, t0:t0 + 128, :]),
                      tag + xrn + "i", w=[xrn])
                for n in range(2):
                    for f in range(NF):
                        P.op("pe", lambda e, n=n, f=f, j=j: e.matmul(
                            dn[:, n * 512:(n + 1) * 512], actT[:, f, j * 128:(j + 1) * 128],
                            Wd[:, f, n * 512:(n + 1) * 512], start=(f == 0), stop=(f == NF - 1)),
                             r=["actT%d" % f, "Wd%d" % fgrp[f]], w=["dn%d" % n])
                P.op("act", lambda e, yt=yt, q=q: e.activation(out=yt[:], in_=dn[:], func=AF.Square,
                                                             accum_out=ss2[:, q:q + 1]),
                     r=["dn0", "dn1"], w=[ytn, "ss2%d" % q])
                P.op("act", lambda e, q=q: e.activation(out=ss2[:, q:q + 1], in_=ss2[:, q:q + 1], func=AF.Sqrt,
                                                        scale=1.0 / D, bias=epsb[:, 0:1]),
                     r=["ss2%d" % q, "epsb"], w=["ss2%d" % q])
                P.op("dve", lambda e, q=q: e.reciprocal(out=rstd2[:, q:q + 1], in_=ss2[:, q:q + 1]),
                     r=["ss2%d" % q], w=["rstd2%d" % q])
                P.op("dve", lambda e, yt=yt, q=q: e.scalar_tensor_tensor(
                    out=yt[:], in0=dn[:], scalar=rstd2[:, q:q + 1], in1=G[:], op0=ALU.mult, op1=ALU.mult),
                     r=["dn0", "dn1", "rstd2%d" % q, "G"], w=[ytn])
                P.op("pool", lambda e, xr=xr, yt=yt: e.tensor_tensor(out=xr[:], in0=xr[:], in1=yt[:],
                                                                     op=ALU.add),
                     r=[ytn, xrn], w=[xrn])
                P.dma("sp", lambda e, xr=xr, t0=t0: e.dma_start(out=dst[b, t0:t0 + 128, :], in_=xr[:]),
                      tag + xrn + "o", r=[xrn], w=["dst_%s_%d_%d" % (tag, b, t0)])

        nblk = len(blocks)
        for j in range(4):
            emit_N(0, j)
        emit_T(0)
        for bi in range(nblk):
            for f in range(NF):
                emit_GU_f(bi, f)
                if bi + 1 < nblk and f in (3, 8, 13, 18):
                    emit_N(bi + 1, (f - 3) // 5)
            if bi + 1 < nblk:
                emit_T(bi + 1)
            emit_D(bi)
        P.barrier_all()
        P.flush()


NT = S // 128
LIMIT = None
NSEQ = NB
SUB = 9
CUT = 9
BARQT = False
EVAC_ACT = False
BIG = 30000.0
C_DQ, C_DK, C_DV, C_NQ, C_KC, C_VC, C_KS, C_VS, C_KW, C_VW, C_GL, C_END = (
    0, 512, 1024, 1536, 2048, 2176, 2304, 2432, 2560, 2688, 2816, 2840)
S_DQ, S_DK, S_NQ, S_KS, S_KW, S_KC, S_VC, S_END = 0, 512, 1024, 1536, 1664, 1792, 1920, 2048


class Rot:
    def __init__(self, items, names):
        self.items, self.names, self.i = items, names, 0

    def next(self):
        k = self.i % len(self.items)
        self.i += 1
        return self.items[k], self.names[k]


def mixer_phase(nc, P, src, dst, dd):
    TB = 2
    with ExitStack() as st0:
        def sb0(name, shape, dt):
            return st0.enter_context(nc.sbuf_tensor("B" + name, shape, dt))

        dqT = sb0("dqT", [128, 4, S], BF16)
        dkT = sb0("dkT", [128, 4, S], BF16)
        Qg = [sb0("Qg%d" % g, [128, 4, S], BF16) for g in range(2)]
        KSg = [sb0("KSg%d" % g, [128, S], BF16) for g in range(2)]
        KWg = [sb0("KWg%d" % g, [128, S], BF16) for g in range(2)]
        dva = sb0("dva", [128, NT, 4, 130], BF16)
        vsa = sb0("vsa", [128, NT, 2, 66], BF16)
        vwa = sb0("vwa", [128, NT, 2, 66], BF16)
        glr = sb0("glr", [128, NT, 24], F32)
        kcc = [sb0("kcc%d" % g, [128, 128], BF16) for g in range(2)]
        vca = sb0("vca", [128, 2, 98], BF16)
        ident = sb0("ident", [128, 128], BF16)
        epsb = sb0("epsb", [128, 1], F32)

        P.dma("sp", lambda e: e.dma_start(out=ident[:], in_=dd["ident"]), "Bc_ident", w=["ident"])
        P.op("pool", lambda e: e.memset(epsb[:], EPS), w=["epsb"])
        P.op("pool", lambda e: e.memset(dva[:, :, :, 128:130], 1.0), w=["dva"])
        P.op("pool", lambda e: e.memset(vsa[:, :, :, 64:66], 1.0), w=["vsa"])
        P.op("pool", lambda e: e.memset(vwa[:, :, :, 64:66], 1.0), w=["vwa"])
        P.op("pool", lambda e: e.memset(vca[:], 0.0), w=["vca"])
        P.op("pool", lambda e: e.memset(vca[:, :, 64:65], 1.0), r=["vca"], w=["vca"])
        for g in range(2):
            P.dma("sp", lambda e, g=g: e.dma_start(out=vca[0:127, g, 65:97], in_=dd["ov"]), "Bc_ov%d" % g,
                  r=["vca"], w=["vca_ov%d" % g])
            P.dma("sp", lambda e, g=g: e.dma_start(out=KSg[g][64:96, :], in_=dd["ET"]), "Bc_ET%d" % g,
                  w=["KSgE%d" % g])

        for b in range(NSEQ):
            with ExitStack() as st1:
                kcT = st1.enter_context(nc.sbuf_tensor("BkcT%d" % b, [128, S], BF16))
                vcT = st1.enter_context(nc.sbuf_tensor("BvcT%d" % b, [128, S], BF16))
                mixer_proj(nc, P, b, TB, src, dd, dict(dqT=dqT, dkT=dkT, Qg=Qg, KSg=KSg, KWg=KWg, dva=dva,
                                                      vsa=vsa, vwa=vwa, glr=glr, kcT=kcT, vcT=vcT,
                                                      ident=ident, epsb=epsb))
                if SUB >= 2:
                    mixer_compress(nc, P, b, dd, dict(kcT=kcT, vcT=vcT, kcc=kcc, vca=vca))
            if SUB >= 3:
                mixer_attn(nc, P, b, src, dst, dd, dict(dqT=dqT, dkT=dkT, Qg=Qg, KSg=KSg, KWg=KWg, dva=dva,
                                                   vsa=vsa, vwa=vwa, glr=glr, kcc=kcc, vca=vca,
                                                   ident=ident, epsb=epsb))


def mixer_proj(nc, P, b, TB, src, dd, t):
    tag = "P%d" % b
    dqT, dkT, Qg, KSg, KWg = t["dqT"], t["dkT"], t["Qg"], t["KSg"], t["KWg"]
    dva, vsa, vwa, glr, kcT, vcT, ident, epsb = (t["dva"], t["vsa"], t["vwa"], t["glr"], t["kcT"], t["vcT"],
                                                 t["ident"], t["epsb"])
    with ExitStack() as st:
        def sb(name, shape, dt):
            return st.enter_context(nc.sbuf_tensor(tag + name, shape, dt))

        def ps(name, shape, dt):
            return st.enter_context(nc.psum_tensor(tag + name, shape, dt))

        rt = [[sb("rt%d_%d" % (i, q), [128, 8, 8], F32) for q in range(4)] for i in range(2)]
        xs = [sb("xs%d" % i, [128, 512], F32) for i in range(2)]
        xin = [sb("xin%d" % i, [128, D], F32) for i in range(2)]
        hn = sb("hn", [128, TB, D], BF16)
        hT = sb("hT", [128, 8, TB * 128], BF16)
        stg = sb("stg", [128, TB, S_END], BF16)
        cosR = sb("cosR", [128, NT, 8, 8], F32)
        sinR = sb("sinR", [128, NT, 8, 8], F32)
        gpre = sb("gpre", [128, 8], F32)
        ss = sb("ss", [128, TB], F32)
        rstd = sb("rstd", [128, TB], F32)
        Win = sb("Win", [128, 8, 2048], BF16)
        tr = [ps("tr%d" % i, [128, 1024], BF16) for i in range(2)]
        pj = [ps("pj%d" % i, [128, 512], F32) for i in range(3)]
        tq = [ps("tq%d" % i, [128, 1024], BF16) for i in range(2)]

        P.dma("sp", lambda e: e.dma_start(out=gpre[:], in_=dd["m_gpre"]), tag + "gpre", w=["gpre"])
        P.dma("sp", lambda e: e.dma_start(out=cosR[:], in_=dd["cosR"]), tag + "cos", w=["cosR"])
        P.dma("sp", lambda e: e.dma_start(out=sinR[:], in_=dd["sinR"]), tag + "sin", w=["sinR"])
        win_v = dd["w_in"].rearrange("(k p) f -> p k f", p=128)
        CB = [(0, 512), (512, 1024), (1024, 1536), (1536, 2048), (2048, 2560), (2560, C_END)]
        SEC = [(C_KS, 128), (C_KW, 128), (C_KC, 128), (C_VC, 128), (C_VS, 128), (C_VW, 128), (C_GL, 24)]
        def load_pass(ph):
            if ph == 0:
                for ci, (c0, c1) in enumerate(CB[:4]):
                    for kh in range(2):
                        P.dma("pool", lambda e, c0=c0, c1=c1, kh=kh: e.dma_start(
                            out=Win[:, kh * 4:(kh + 1) * 4, c0:c1], in_=win_v[:, kh * 4:(kh + 1) * 4, c0:c1]),
                              "%sWin%d" % (tag, ci), w=["Win%d" % ci])
            else:
                dcol = 0
                for si, (sc, sw) in enumerate(SEC):
                    ci = 4 if si < 4 else 5
                    P.dma("pool", lambda e, sc=sc, sw=sw, dcol=dcol: e.dma_start(
                        out=Win[:, :, dcol:dcol + sw], in_=win_v[:, :, sc:sc + sw]),
                          "%sWin%d" % (tag, ci), w=["Win%d" % ci] + (["Win0", "Win1"] if si == 0 else []))
                    dcol += sw
                P.dma("sp", lambda e: e.dma_start(out=cosR[:], in_=dd["cosR2"]), tag + "cos", w=["cosR"])
                P.dma("sp", lambda e: e.dma_start(out=sinR[:], in_=dd["sinR2"]), tag + "sin", w=["sinR"])

        PH = [0]
        nxi = [0]
        pji = [0]
        rti = [0]
        tqi = [0]
        evi = [0]

        def emit_N(i, j):
            t0 = (i * TB + j) * 128
            q = nxi[0] % 2
            nxi[0] += 1
            xb, xn = xin[q], "xin%d" % q
            P.dma("sp", lambda e: e.dma_start(out=xb[:], in_=src[b, t0:t0 + 128, :]), tag + xn, w=[xn])
            P.op("act", lambda e: e.activation(out=hn[:, j, :], in_=xb[:], func=AF.Square,
                                               accum_out=ss[:, j:j + 1]),
                 r=[xn], w=["hn%d" % j, "ss%d" % j])
            P.op("act", lambda e: e.activation(out=ss[:, j:j + 1], in_=ss[:, j:j + 1], func=AF.Sqrt,
                                               scale=1.0 / D, bias=epsb[:, 0:1]),
                 r=["ss%d" % j, "epsb"], w=["ss%d" % j])
            P.op("dve", lambda e: e.reciprocal(out=rstd[:, j:j + 1], in_=ss[:, j:j + 1]),
                 r=["ss%d" % j], w=["rstd%d" % j])
            P.op("act", lambda e: e.activation(out=hn[:, j, :], in_=xb[:], func=AF.Copy,
                                               scale=rstd[:, j:j + 1]),
                 r=[xn, "rstd%d" % j], w=["hn%d" % j])

        def emit_T(i):
            W = TB * 128
            for kq in range(2):
                bank = kq
                for kk in range(4):
                    k = kq * 4 + kk
                    for j in range(TB):
                        P.op("pe", lambda e, k=k, kk=kk, j=j, bank=bank: e.transpose(
                            tr[bank][:, kk * W + j * 128:kk * W + (j + 1) * 128],
                            hn[:, j, k * 128:(k + 1) * 128], ident[:]),
                             r=["hn%d" % j, "ident"], w=["tr%d" % bank])
                for kk in range(4):
                    k = kq * 4 + kk
                    P.op("dve", lambda e, k=k, kk=kk, bank=bank: e.tensor_scalar(
                        out=hT[:, k, :], in0=tr[bank][:, kk * W:(kk + 1) * W], scalar1=gpre[:, k:k + 1],
                        scalar2=None, op0=ALU.mult),
                         r=["tr%d" % bank, "gpre"], w=["hT%d" % k])

        def rope(pjt, pjn, o, nh, j, so, T, tabs=None):
            q = rti[0] % 2
            rti[0] += 1
            ta, tb_, tc, td = rt[q]
            rn = ["rt%d_%d" % (q, x) for x in range(4)]
            xst, xsn = xs[q], "xs%d" % q
            if (CUT == 4.26 and nh == 2) or (CUT == 4.28 and tabs is not None):
                sres = "stg%d_%d" % (j, so)
                P.op("act", lambda e: e.activation(out=stg[:, j, so:so + nh * 64], in_=pjt[:, o:o + nh * 64],
                                                   func=AF.Copy), r=[pjn], w=[sres + "a"])
                return [sres + "a"]
            P.op("act", lambda e: e.activation(out=xst[:, 0:nh * 64], in_=pjt[:, o:o + nh * 64], func=AF.Copy),
                 r=[pjn], w=[xsn])
            pv = xst[:, 0:nh * 64].rearrange("p (h d) -> p h d", d=64)
            sv = stg[:, j, so:so + nh * 64].rearrange("p (h d) -> p h d", d=64)
            x1, x2 = pv[:, :, 0:8], pv[:, :, 8:16]
            ct_, st_, ctn, stn = (cosR, sinR, "cosR", "sinR") if tabs is None else tabs
            cs, sn = ct_[:, T, 0:nh, :], st_[:, T, 0:nh, :]
            sres = "stg%d_%d" % (j, so)
            if CUT < 3.06:
                return [sres + "a", sres + "b", sres + "c"]
            P.op("dve", lambda e: e.tensor_tensor(out=ta[:, 0:nh, :], in0=x1, in1=cs, op=ALU.mult),
                 r=[xsn, ctn], w=[rn[0]])
            if CUT < 3.07:
                return [sres + "a", sres + "b", sres + "c"]
            P.op("dve", lambda e: e.tensor_tensor(out=tb_[:, 0:nh, :], in0=x2, in1=sn, op=ALU.mult),
                 r=[xsn, stn], w=[rn[1]])
            if CUT < 3.08:
                return [sres + "a", sres + "b", sres + "c"]
            P.op("dve", lambda e: e.tensor_tensor(out=tc[:, 0:nh, :], in0=x2, in1=cs, op=ALU.mult),
                 r=[xsn, ctn], w=[rn[2]])
            if CUT == 3.095:
                P.op("dve", lambda e: e.tensor_tensor(out=tc[:, 0:nh, :], in0=x1, in1=sn, op=ALU.mult),
                     r=[xsn, "sinR"], w=[rn[2]])
                return [sres + "a", sres + "b", sres + "c"]
            P.op("dve", lambda e: e.tensor_tensor(out=td[:, 0:nh, :], in0=x1, in1=sn, op=ALU.mult),
                 r=[xsn, stn], w=[rn[3]])
            P.op("dve", lambda e: e.tensor_tensor(out=pv[:, :, 0:8], in0=ta[:, 0:nh, :], in1=tb_[:, 0:nh, :],
                                                  op=ALU.subtract),
                 r=[rn[0], rn[1], rn[2], rn[3], xsn], w=[xsn])
            P.op("dve", lambda e: e.tensor_tensor(out=pv[:, :, 8:16], in0=tc[:, 0:nh, :], in1=td[:, 0:nh, :],
                                                  op=ALU.add),
                 r=[rn[2], rn[3], xsn], w=[xsn])
            P.op("act", lambda e: e.activation(out=stg[:, j, so:so + nh * 64], in_=xst[:, 0:nh * 64], func=AF.Copy),
                 r=[xsn], w=[sres + "a"])
            return [sres + "a"]

        def emit_proj(i, j, stres):
            T = i * TB + j
            for ci, (c0, c1) in enumerate(CB):
                if (ci < 4) != (PH[0] == 0):
                    continue
                if ci >= 4:
                    c0, c1 = c0 - 2048, c1 - 2048
                q = pji[0] % 3
                pji[0] += 1
                pjt, pjn = pj[q], "pj%d" % q
                ncol = c1 - c0
                for k in range(8):
                    P.op("pe", lambda e, k=k, pjt=pjt, c0=c0, c1=c1, ncol=ncol: e.matmul(
                        pjt[:, 0:ncol], hT[:, k, j * 128:(j + 1) * 128], Win[:, k, c0:c1],
                        start=(k == 0), stop=(k == 7)),
                         r=["hT%d" % k, "Win%d" % ci], w=[pjn])
                if ci == 0:
                    stres["dq"][j] = rope(pjt, pjn, 0, 8, j, S_DQ, T)
                elif ci == 1:
                    stres["dk"][j] = rope(pjt, pjn, 0, 8, j, S_DK, T)
                elif ci == 2:
                    P.op("act", lambda e, pjt=pjt, T=T: e.activation(
                        out=dva[:, T, :, 0:128], in_=pjt[:, 0:512].rearrange("p (h d) -> p h d", d=128),
                        func=AF.Copy), r=[pjn], w=["dva%d" % T])
                elif ci == 3:
                    stres["nq"][j] = rope(pjt, pjn, 0, 8, j, S_NQ, T)
                elif ci == 4:
                    rr = rope(pjt, pjn, 0, 8, j, S_KS, T)
                    stres["ks"][j] = rr
                    stres["kw"][j] = rr
                    stres["kcvc"][j] = rr
                else:
                    P.op("act", lambda e, pjt=pjt, T=T: e.activation(
                        out=vsa[:, T, :, 0:64], in_=pjt[:, 0:128].rearrange("p (h d) -> p h d", d=64),
                        func=AF.Copy), r=[pjn], w=["vsa%d" % T])
                    P.op("act", lambda e, pjt=pjt, T=T: e.activation(
                        out=vwa[:, T, :, 0:64], in_=pjt[:, 128:256].rearrange("p (h d) -> p h d", d=64),
                        func=AF.Copy), r=[pjn], w=["vwa%d" % T])
                    P.op("dve", lambda e, pjt=pjt, T=T: e.tensor_copy(out=glr[:, T, :], in_=pjt[:, 256:280]),
                         r=[pjn], w=["glr%d" % T])

        def emit_QT(i, stres):
            tk0 = i * TB * 128
            W = TB * 128
            units = []
            for h in range(4):
                units.append((S_DQ + h * 128, 128, dqT[:, h, tk0:tk0 + W], "dqT", "dq"))
            for h in range(4):
                units.append((S_DK + h * 128, 128, dkT[:, h, tk0:tk0 + W], "dkT", "dk"))
            for n in range(8):
                units.append((S_NQ + n * 64, 64, Qg[n // 4][0:64, n % 4, tk0:tk0 + W], "Qq%d" % (n // 4), "nq"))
            for g in range(2):
                units.append((S_KS + g * 64, 64, KSg[g][0:64, tk0:tk0 + W], "KSq%d" % g, "ks"))
            for g in range(2):
                units.append((S_KW + g * 64, 64, KWg[g][0:64, tk0:tk0 + W], "KWq%d" % g, "kw"))
            units.append((S_KC, 128, kcT[:, tk0:tk0 + W], "kcT", "kcvc"))
            units.append((S_VC, 128, vcT[:, tk0:tk0 + W], "vcT", "kcvc"))
            units = units[0:16] if PH[0] == 0 else units[16:22]
            if CUT == 9:
                pass
            elif CUT == 4.23:
                units = [(S_NQ + g * 64, 64, KSg[g][0:64, tk0:tk0 + W], "KSq%d" % g, "nq") for g in range(2)]
            elif CUT == 4.24:
                units = [(S_KS + g * 64, 64, Qg[g][0:64, 0, tk0:tk0 + W], "Qq%d" % g, "ks") for g in range(2)]
            elif CUT in (4.21, 4.26, 4.28):
                units = units[16:18]
            elif CUT == 4.22:
                units = units[18:20]
            elif CUT < 4.1:
                units = units[0:8]
            elif CUT < 4.2:
                units = units[0:16]
            elif CUT < 4.3:
                units = units[0:20]
            for u0 in range(0, len(units), 4):
                bank = tqi[0] % 2
                tqi[0] += 1
                grp = units[u0:u0 + 4]
                for ui, (so, ncol, dstap, dres, skey) in enumerate(grp):
                    for j in range(TB):
                        P.op("pe", lambda e, ui=ui, j=j, so=so, ncol=ncol, bank=bank: e.transpose(
                            tq[bank][0:ncol, ui * W + j * 128:ui * W + (j + 1) * 128],
                            stg[:, j, so:so + ncol], ident[:]),
                             r=stres[skey][j] + ["ident"], w=["tq%d" % bank])
                for ui, (so, ncol, dstap, dres, skey) in enumerate(grp):
                    eng = "act" if (evi[0] % 2 == 0 or EVAC_ACT) else "dve"
                    evi[0] += 1
                    if eng == "act":
                        P.op("act", lambda e, ui=ui, ncol=ncol, dstap=dstap, bank=bank: e.activation(
                            out=dstap, in_=tq[bank][0:ncol, ui * W:(ui + 1) * W], func=AF.Copy),
                             r=["tq%d" % bank], w=["%s_%d" % (dres, i)])
                    else:
                        P.op("dve", lambda e, ui=ui, ncol=ncol, dstap=dstap, bank=bank: e.tensor_copy(
                            out=dstap, in_=tq[bank][0:ncol, ui * W:(ui + 1) * W]),
                             r=["tq%d" % bank], w=["%s_%d" % (dres, i)])

        nblk = NT // TB
        for ph in range(2):
            PH[0] = ph
            load_pass(ph)
            for j in range(TB):
                emit_N(0, j)
            emit_T(0)
            for i in range(nblk):
                stres = {k: [None] * TB for k in ("dq", "dk", "nq", "kcvc", "ks", "kw")}
                for j in range(TB):
                    emit_proj(i, j, stres)
                    if i + 1 < nblk:
                        emit_N(i + 1, j)
                emit_QT(i, stres)
                if i + 1 < nblk:
                    emit_T(i + 1)
        P.barrier_all()
        P.flush()


def mixer_compress(nc, P, b, dd, t):
    tag = "Z%d" % b
    kcT, vcT, kcc, vca = t["kcT"], t["vcT"], t["kcc"], t["vca"]
    NCB = 127
    with ExitStack() as st:
        def sb(name, shape, dt):
            return st.enter_context(nc.sbuf_tensor(tag + name, shape, dt))

        def ps(name, shape, dt):
            return st.enter_context(nc.psum_tensor(tag + name, shape, dt))

        W1 = [sb("W1_%d" % kv, [128, 32, 256], BF16) for kv in range(2)]
        W2 = [sb("W2_%d" % kv, [128, 2, 64], BF16) for kv in range(2)]
        peT = [sb("peT%d" % kv, [128, 32], BF16) for kv in range(2)]
        pb = sb("pb", [128, 4], F32)
        xh = sb("xh", [128, 2, 128], F32)
        u = sb("u", [128, 2, 128], F32)
        sg = sb("sg", [128, 2, 128], F32)
        hact = sb("hact", [128, 2, 128], BF16)
        pbps = ps("pbps", [128, 4], F32)
        hps = [ps("hps%d" % i, [128, 2, 128], F32) for i in range(2)]
        cps = ps("cps", [128, 128], F32)

        for kv, (w1n, w2n, pen) in enumerate((("ck_w1", "ck_w2", "ck_peT"), ("cv_w1", "cv_w2", "cv_peT"))):
            w1v = dd[w1n].rearrange("(l d) h -> d l h", d=64)
            for half in range(2):
                P.dma("pool", lambda e, kv=kv, half=half, w1v=w1v: e.dma_start(
                    out=W1[kv][half * 64:(half + 1) * 64, :, :], in_=w1v),
                      "%sW1_%d" % (tag, kv), w=["W1_%d" % kv])
            P.dma("pool", lambda e, kv=kv, w2n=w2n: e.dma_start(
                out=W2[kv][:], in_=dd[w2n].rearrange("(c p) d -> p c d", p=128)),
                  "%sW2_%d" % (tag, kv), w=["W2_%d" % kv])
            P.dma("pool", lambda e, kv=kv, pen=pen: e.dma_start(out=peT[kv][:], in_=dd[pen]),
                  "%spe_%d" % (tag, kv), w=["peT%d" % kv])
        for kv in range(2):
            for ch in range(2):
                col = kv * 2 + ch
                for l in range(32):
                    P.op("pe", lambda e, kv=kv, ch=ch, l=l, col=col: e.matmul(
                        pbps[:, col:col + 1], W1[kv][0:64, l, ch * 128:(ch + 1) * 128], peT[kv][0:64, l:l + 1],
                        start=(l == 0), stop=(l == 31)),
                         r=["W1_%d" % kv, "peT%d" % kv], w=["pbps"])
        P.op("dve", lambda e: e.tensor_copy(out=pb[:], in_=pbps[:]), r=["pbps"], w=["pb"])
        hi = [0]
        for kv in range(2):
            xT = kcT if kv == 0 else vcT
            for g in range(2):
                hp = hps[hi[0] % 2]
                hpn = "hps%d" % (hi[0] % 2)
                hi[0] += 1
                for ch in range(2):
                    for l in range(32):
                        P.op("pe", lambda e, kv=kv, g=g, ch=ch, l=l, hp=hp, xT=xT: e.matmul(
                            hp[:, ch, 0:NCB], W1[kv][g * 64:(g + 1) * 64, l, ch * 128:(ch + 1) * 128],
                            xT[g * 64:(g + 1) * 64, l:l + 16 * (NCB - 1) + 1:16],
                            start=(l == 0), stop=(l == 31)),
                             r=["W1_%d" % kv], w=[hpn])
                for ch in range(2):
                    col = kv * 2 + ch
                    P.op("act", lambda e, ch=ch, col=col, hp=hp: e.activation(
                        out=xh[:, ch, 0:NCB], in_=hp[:, ch, 0:NCB], func=AF.Identity, bias=pb[:, col:col + 1]),
                         r=[hpn, "pb"], w=["xh%d" % ch])
                X, U, SG, HA = xh[:, :, 0:NCB], u[:, :, 0:NCB], sg[:, :, 0:NCB], hact[:, :, 0:NCB]
                P.op("dve", lambda e, X=X, U=U: e.tensor_tensor(out=U, in0=X, in1=X, op=ALU.mult),
                     r=["xh0", "xh1"], w=["u"])
                P.op("dve", lambda e, U=U: e.tensor_scalar(out=U, in0=U, scalar1=0.044715, scalar2=1.0,
                                                        op0=ALU.mult, op1=ALU.add), r=["u"], w=["u"])
                P.op("dve", lambda e, X=X, U=U: e.tensor_tensor(out=U, in0=U, in1=X, op=ALU.mult),
                     r=["u", "xh0", "xh1"], w=["u"])
                P.op("act", lambda e, U=U, SG=SG: e.activation(out=SG, in_=U, func=AF.Sigmoid,
                                                              scale=1.5957691216057308),
                     r=["u"], w=["sg"])
                P.op("dve", lambda e, X=X, SG=SG, HA=HA: e.tensor_tensor(out=HA, in0=X, in1=SG, op=ALU.mult),
                     r=["sg", "xh0", "xh1"], w=["hact"])
                if kv == 0:
                    for ch in range(2):
                        P.op("pe", lambda e, ch=ch: e.matmul(cps[0:64, 0:NCB], W2[0][:, ch, :], hact[:, ch, 0:NCB],
                                                            start=(ch == 0), stop=(ch == 1)),
                             r=["hact", "W2_0"], w=["cps"])
                    P.op("act", lambda e, g=g: e.activation(out=kcc[g][0:64, 0:NCB], in_=cps[0:64, 0:NCB],
                                                           func=AF.Copy), r=["cps"], w=["kcc%d" % g])
                else:
                    for ch in range(2):
                        P.op("pe", lambda e, ch=ch: e.matmul(cps[0:NCB, 0:64], hact[:, ch, 0:NCB], W2[1][:, ch, :],
                                                            start=(ch == 0), stop=(ch == 1)),
                             r=["hact", "W2_1"], w=["cps"])
                    P.op("act", lambda e, g=g: e.activation(out=vca[0:NCB, g, 0:64], in_=cps[0:NCB, 0:64],
                                                           func=AF.Copy), r=["cps"], w=["vca_v%d" % g])
        P.barrier_all()
        P.flush()


class Pipe:
    def __init__(self, lag=2):
        self.q, self.lag = [], lag

    def push(self, first, rest):
        if first is not None:
            first()
        self.q.append(rest)
        while len(self.q) > self.lag:
            self.q.pop(0)()

    def drain(self):
        while self.q:
            self.q.pop(0)()


def mixer_attn(nc, P, b, src, dst, dd, t):
    tag = "T%d" % b
    dqT, dkT, Qg, KSg, KWg = t["dqT"], t["dkT"], t["Qg"], t["KSg"], t["KWg"]
    dva, vsa, vwa, glr, kcc, vca, ident, epsb = (t["dva"], t["vsa"], t["vwa"], t["glr"], t["kcc"], t["vca"],
                                                 t["ident"], t["epsb"])
    with ExitStack() as st:
        def sb(name, shape, dt):
            return st.enter_context(nc.sbuf_tensor(tag + name, shape, dt))

        def ps(name, shape, dt):
            return st.enter_context(nc.psum_tensor(tag + name, shape, dt))

        om = sb("om", [128, NT, 1024], BF16)
        tmpf = [sb("tmpf%d" % i, [128, 64], F32) for i in range(2)]
        Wo = sb("Wo", [128, 8, 1024], BF16)
        pts = [sb("p%d" % i, [128, 512], BF16) for i in range(4)]
        gates = sb("gates", [128, NT, 24], F32)
        Am = sb("Am", [128, NT, 32], F32)
        Bm = sb("Bm", [128, NT, 32], F32)
        Gs = sb("Gs", [128, 128], F32)
        Gm = sb("Gm", [128, D], F32)
        lv = [sb("lv%d" % i, [128, 64], F32) for i in range(4)]
        lt = sb("lt", [128, 64], F32)
        le = sb("le", [128, 2], F32)
        neglam = sb("neglam", [128, 1], F32)
        o1 = [sb("o1_%d" % i, [128, 4, 128], F32) for i in range(2)]
        junk = sb("junk", [128, 128], F32)
        sm = [dict((n, sb("%s_%d" % (n, i), [128, w], F32)) for n, w in
                   (("rd", 4), ("nl", 4), ("ss4", 4), ("ln4", 4), ("r4", 4), ("den", 4), ("gr", 4), ("rs", 4),
                    ("rw", 4), ("imp", 32), ("impm", 32), ("wk", 32), ("m1", 8), ("m2", 8))) for i in range(2)]
        nsp = [sb("nsp%d" % i, [128, 96], BF16) for i in range(2)]
        omT = [sb("omT%d" % i, [128, 8, 128], BF16) for i in range(2)]
        xres = [sb("xres%d" % i, [128, D], F32) for i in range(2)]
        ytmp = [sb("ytmp%d" % i, [128, D], F32) for i in range(2)]
        ss2 = sb("ss2", [128, 2], F32)
        rstd2 = sb("rstd2", [128, 2], F32)
        sps = [ps("sps%d" % i, [128, 512], F32) for i in range(3)]
        ab = ps("ab", [128, 4, 512], F32)
        tps = ps("tps", [128, 1024], BF16)
        SPS = Rot(sps, ["sps%d" % i for i in range(3)])
        PT = Rot(pts, ["p%d" % i for i in range(4)])

        for nm, tl in (("Am", Am), ("Bm", Bm), ("Gs", Gs), ("Gm", Gm)):
            P.dma("sp", lambda e, nm=nm, tl=tl: e.dma_start(out=tl[:], in_=dd[nm]), tag + nm, w=[nm])
        for i, nm in enumerate(("lq1", "lk1", "lq2", "lk2")):
            P.dma("sp", lambda e, i=i, nm=nm: e.dma_start(out=lv[i][:], in_=dd[nm]), tag + nm, w=["lv%d" % i])
        wo_v = dd["w_out"].rearrange("(k p) f -> p k f", p=128)
        for kh in range(2):
            P.dma("pool", lambda e, kh=kh: e.dma_start(out=Wo[:, kh * 4:(kh + 1) * 4, :],
                                                       in_=wo_v[:, kh * 4:(kh + 1) * 4, :]),
                  tag + "Wo", w=["Wo"])
        for nm in ("nsp0", "nsp1"):
            pass
        P.op("pool", lambda e: e.memset(nsp[0][:], 0.0), w=["nsp0"])
        P.op("pool", lambda e: e.memset(nsp[1][:], 0.0), w=["nsp1"])
        P.op("dve", lambda e: e.tensor_scalar(out=Gs[:], in0=Gs[:], scalar1=0.8, scalar2=None, op0=ALU.mult),
             r=["Gs"], w=["Gs"])
        P.op("act", lambda e: e.activation(out=gates[:], in_=glr[:], func=AF.Sigmoid), w=["gates"])
        for i in range(2):
            P.op("dve", lambda e, i=i: e.tensor_tensor(out=lt[:], in0=lv[2 * i][:], in1=lv[2 * i + 1][:],
                                                       op=ALU.mult),
                 r=["lv%d" % (2 * i), "lv%d" % (2 * i + 1)], w=["lt"])
            P.op("dve", lambda e, i=i: e.reduce_sum(out=le[:, i:i + 1], in_=lt[:], axis=AX.X),
                 r=["lt"], w=["le%d" % i])
        P.op("act", lambda e: e.activation(out=le[:], in_=le[:], func=AF.Exp), r=["le0", "le1"], w=["le0", "le1"])
        P.op("dve", lambda e: e.tensor_tensor(out=neglam[:], in0=le[:, 1:2], in1=le[:, 0:1], op=ALU.subtract),
             r=["le0", "le1"], w=["neglam"])
        P.op("dve", lambda e: e.tensor_scalar(out=neglam[:], in0=neglam[:], scalar1=-0.2, scalar2=None,
                                              op0=ALU.add), r=["neglam"], w=["neglam"])

        pipe = Pipe(2)
        first_in_bank = {}

        def acc_mm(bank_i, col0, ncol, lhsT, rhs, last, reads):
            bn = "ab%d" % bank_i
            first = first_in_bank.get(bn, True)
            first_in_bank[bn] = False
            P.op("pe", lambda e: e.matmul(ab[:, bank_i, col0:col0 + ncol], lhsT, rhs, start=first, stop=last,
                                          skip_group_check=True),
                 r=reads, w=[bn])

        def exp_tile(spt, spn, rows, masks):
            p, pn = PT.next()
            P.op("act", lambda e: e.activation(out=p[0:rows, :], in_=spt[0:rows, :], func=AF.Exp, scale=0.125),
                 r=[spn], w=[pn])
            for (pattern, base, cm) in masks:
                P.op("pool", lambda e, pattern=pattern, base=base, cm=cm: e.affine_select(
                    out=p[0:rows, :], in_=p[0:rows, :], pattern=pattern, compare_op=ALU.is_ge, fill=0.0,
                    base=base, channel_multiplier=cm), r=[pn], w=[pn])
            return p, pn

        ci = 0
        for h in range(4):
            for qb in range(4):
                ob, obn = o1[(h * 4 + qb) % 2], "o1_%d" % ((h * 4 + qb) % 2)
                smx = sm[(h * 4 + qb) % 2]
                sfx = "_%d" % ((h * 4 + qb) % 2)
                for m in range(2):
                    bA, bB = 2 * (ci % 2), 2 * (ci % 2) + 1
                    ci += 1
                    first_in_bank["ab%d" % bA] = True
                    first_in_bank["ab%d" % bB] = True
                    nkt = 4 * qb + 4
                    for kt in range(nkt):
                        spt, spn = SPS.next()

                        def qk(spt=spt, spn=spn, kt=kt, m=m, h=h, qb=qb):
                            P.op("pe", lambda e: e.matmul(
                                spt[:], dkT[m * 64:(m + 1) * 64, h, kt * 128:(kt + 1) * 128],
                                dqT[m * 64:(m + 1) * 64, h, qb * 512:(qb + 1) * 512], start=True, stop=True),
                                 w=[spn])

                        def rest(spt=spt, spn=spn, kt=kt, h=h, qb=qb, bA=bA, bB=bB):
                            masks = []
                            if kt >= 4 * qb:
                                masks.append(([[1, 512]], qb * 512 - kt * 128, -1))
                            p, pn = exp_tile(spt, spn, 128, masks)
                            for jq in range(4):
                                if kt > 4 * qb + jq:
                                    continue
                                bank_i, col0 = (bA, jq * 130) if jq < 3 else (bB, 0)
                                acc_mm(bank_i, col0, 129, p[:, jq * 128:(jq + 1) * 128], dva[:, kt, h, 0:129],
                                       kt == 4 * qb + jq, [pn])

                        pipe.push(qk, rest)

                    def post(m=m, h=h, qb=qb, bA=bA, bB=bB, ob=ob, obn=obn, smx=smx, sfx=sfx):
                        bnA, bnB = "ab%d" % bA, "ab%d" % bB
                        rd = smx["rd"]
                        denA = ab[:, bA, 0:390].rearrange("p (j c) -> p j c", c=130)[:, :, 128]
                        P.op("dve", lambda e: e.reciprocal(out=rd[:, 0:3], in_=denA), r=[bnA], w=["rdA" + sfx])
                        P.op("dve", lambda e: e.reciprocal(out=rd[:, 3:4], in_=ab[:, bB, 128:129]), r=[bnB],
                             w=["rdB" + sfx])

                        def region(jq):
                            return (ab[:, bA, jq * 130:jq * 130 + 128], bnA) if jq < 3 else (ab[:, bB, 0:128], bnB)

                        if m == 0:
                            for jq in range(4):
                                reg, bn = region(jq)
                                P.op("act", lambda e, reg=reg, jq=jq: e.activation(
                                    out=ob[:, jq, :], in_=reg, func=AF.Copy, scale=rd[:, jq:jq + 1]),
                                     r=[bn, "rdA" + sfx, "rdB" + sfx], w=[obn + "_%d" % jq])
                        else:
                            nl, ss4, ln4, r4 = smx["nl"], smx["ss4"], smx["ln4"], smx["r4"]
                            P.op("dve", lambda e: e.tensor_scalar(out=nl[:], in0=rd[:], scalar1=neglam[:, 0:1],
                                                                  scalar2=None, op0=ALU.mult),
                                 r=["rdA" + sfx, "rdB" + sfx, "neglam"], w=["nl" + sfx])
                            for jq in range(4):
                                reg, bn = region(jq)
                                P.op("dve", lambda e, reg=reg, jq=jq: e.scalar_tensor_tensor(
                                    out=ob[:, jq, :], in0=reg, scalar=nl[:, jq:jq + 1], in1=ob[:, jq, :],
                                    op0=ALU.mult, op1=ALU.add),
                                     r=[bn, "nl" + sfx, obn + "_%d" % jq], w=[obn + "_%d" % jq])
                                P.op("act", lambda e, jq=jq: e.activation(
                                    out=junk[:], in_=ob[:, jq, :], func=AF.Square, accum_out=ss4[:, jq:jq + 1]),
                                     r=[obn + "_%d" % jq], w=["junk", "ss4%s_%d" % (sfx, jq)])
                            ssr = ["ss4%s_%d" % (sfx, jq) for jq in range(4)]
                            P.op("act", lambda e: e.activation(out=ln4[:], in_=ss4[:], func=AF.Ln, scale=1.0 / 128,
                                                               bias=epsb[:, 0:1]), r=ssr + ["epsb"], w=["ln4" + sfx])
                            P.op("act", lambda e: e.activation(out=r4[:], in_=ln4[:], func=AF.Exp, scale=-0.5),
                                 r=["ln4" + sfx], w=["r4" + sfx])
                            for jq in range(4):
                                T = 4 * qb + jq
                                P.op("dve", lambda e, jq=jq, T=T: e.scalar_tensor_tensor(
                                    out=om[:, T, h * 128:(h + 1) * 128], in0=ob[:, jq, :], scalar=r4[:, jq:jq + 1],
                                    in1=Gs[:], op0=ALU.mult, op1=ALU.mult),
                                     r=[obn + "_%d" % jq, "r4" + sfx, "Gs"], w=["om%d_d%d" % (T, h)])

                    pipe.push(None, post)
        pipe.drain()

        ci = 0
        for g in range(2):
            for T in range(NT):
                bk = ci % 4
                smx = sm[ci % 2]
                sfx = "_%d" % (ci % 2)
                nspt, nspn = nsp[ci % 2], "nsp%d" % (ci % 2)
                ci += 1
                spt, spn = SPS.next()

                def qk(spt=spt, spn=spn, g=g, T=T):
                    P.op("pe", lambda e: e.matmul(spt[0:127, :], kcc[g][0:64, 0:127],
                                                  Qg[g][0:64, :, T * 128:(T + 1) * 128], start=True, stop=True),
                         w=[spn])

                def rest(spt=spt, spn=spn, g=g, T=T, bk=bk):
                    p, pn = exp_tile(spt, spn, 127, [([[0, 4], [1, 128]], T * 128 - 31, -16)])
                    for hg in range(4):
                        P.op("pe", lambda e, hg=hg: e.matmul(ab[:, bk, hg * 128:hg * 128 + 97],
                                                             p[0:127, hg * 128:(hg + 1) * 128], vca[0:127, g, 0:97],
                                                             start=True, stop=True, skip_group_check=True),
                             r=[pn], w=["ab%d" % bk])

                def post(g=g, T=T, bk=bk, smx=smx, sfx=sfx, nspt=nspt, nspn=nspn):
                    bn = "ab%d" % bk
                    den, rd, gr, imp, impm, wk, m1, m2 = (smx["den"], smx["rd"], smx["gr"], smx["imp"],
                                                          smx["impm"], smx["wk"], smx["m1"], smx["m2"])
                    ov4 = ab[:, bk, :].rearrange("p (h c) -> p h c", c=128)
                    gv = gates[:, T, :].rearrange("p (h c) -> p h c", c=3)
                    P.op("dve", lambda e: e.tensor_scalar(out=den[:], in0=ov4[:, :, 64], scalar1=1e-30, scalar2=None,
                                                          op0=ALU.max), r=[bn], w=["den" + sfx])
                    P.op("dve", lambda e: e.reciprocal(out=rd[:], in_=den[:]), r=["den" + sfx], w=["rd" + sfx])
                    P.op("dve", lambda e: e.tensor_tensor(out=gr[:], in0=rd[:], in1=gv[:, g * 4:(g + 1) * 4, 0],
                                                          op=ALU.mult), r=["rd" + sfx, "gates"], w=["gr" + sfx])
                    for hg in range(4):
                        hd = g * 4 + hg
                        P.op("act", lambda e, hg=hg, hd=hd: e.activation(
                            out=om[:, T, 512 + hd * 64:512 + (hd + 1) * 64], in_=ab[:, bk, hg * 128:hg * 128 + 64],
                            func=AF.Copy, scale=gr[:, hg:hg + 1]), r=[bn, "gr" + sfx], w=["om%d_n%d" % (T, hd)])
                    P.op("dve", lambda e: e.tensor_scalar(out=imp[:], in0=ab[:, bk, 65:97], scalar1=rd[:, 0:1],
                                                          scalar2=None, op0=ALU.mult),
                         r=[bn, "rd" + sfx], w=["imp" + sfx])
                    for hg in range(1, 4):
                        P.op("dve", lambda e, hg=hg: e.scalar_tensor_tensor(
                            out=imp[:], in0=ab[:, bk, hg * 128 + 65:hg * 128 + 97], scalar=rd[:, hg:hg + 1],
                            in1=imp[:], op0=ALU.mult, op1=ALU.add), r=[bn, "rd" + sfx, "imp" + sfx],
                             w=["imp" + sfx])
                    P.op("dve", lambda e: e.tensor_tensor(out=impm[:], in0=imp[:], in1=Am[:, T, :], op=ALU.mult),
                         r=["imp" + sfx, "Am"], w=["impm" + sfx])
                    P.op("dve", lambda e: e.tensor_tensor(out=impm[:], in0=impm[:], in1=Bm[:, T, :], op=ALU.add),
                         r=["impm" + sfx, "Bm"], w=["impm" + sfx])
                    P.op("dve", lambda e: e.max(out=m1[:], in_=impm[:]), r=["impm" + sfx], w=["m1" + sfx])
                    P.op("dve", lambda e: e.match_replace(out=wk[:], in_to_replace=m1[:], in_values=impm[:],
                                                          imm_value=-2.0),
                         r=["impm" + sfx, "m1" + sfx], w=["wk" + sfx])
                    P.op("dve", lambda e: e.max(out=m2[:], in_=wk[:]), r=["wk" + sfx], w=["m2" + sfx])
                    P.op("dve", lambda e: e.tensor_scalar(out=nspt[:, 64:96], in0=impm[:], scalar1=m2[:, 7:8],
                                                          scalar2=-BIG, op0=ALU.is_lt, op1=ALU.mult),
                         r=["impm" + sfx, "m2" + sfx], w=[nspn])
                    P.op("pe", lambda e: e.transpose(tps[0:96, 0:128], nspt[:, 0:96], ident[:]),
                         r=[nspn, "ident"], w=["tps"])
                    tsl = slice(T * 128, (T + 1) * 128)
                    P.op("act", lambda e: e.activation(out=Qg[g][64:96, 0, tsl], in_=tps[64:96, 0:128],
                                                       func=AF.Copy), r=["tps"], w=["Qs%d_%d" % (g, T)])
                    for hg in range(1, 4):
                        P.op("pool", lambda e, hg=hg: e.tensor_copy(out=Qg[g][64:96, hg, tsl],
                                                                    in_=Qg[g][64:96, 0, tsl]),
                             r=["Qs%d_%d" % (g, T)], w=["Qs%d_%d_%d" % (g, T, hg)])

                pipe.push(qk, rest)
                pipe.push(None, post)
        pipe.drain()

        ci = 0
        for g in range(2):
            for T in range(NT):
                bS, bW = 2 * (ci % 2), 2 * (ci % 2) + 1
                smx = sm[ci % 2]
                sfx = "_%d" % (ci % 2)
                ci += 1
                first_in_bank["ab%d" % bS] = True
                first_in_bank["ab%d" % bW] = True
                selr = ["Qs%d_%d" % (g, T)] + ["Qs%d_%d_%d" % (g, T, hg) for hg in range(1, 4)]
                jobs = [("s", kt) for kt in range(T + 1)] + [("w", kt) for kt in range(max(0, T - 4), T + 1)]
                for kind, kt in jobs:
                    spt, spn = SPS.next()

                    def qk(spt=spt, spn=spn, kind=kind, kt=kt, g=g, T=T, selr=selr):
                        ksl = slice(kt * 128, (kt + 1) * 128)
                        tsl = slice(T * 128, (T + 1) * 128)
                        if kind == "s":
                            P.op("pe", lambda e: e.matmul(spt[:], KSg[g][0:96, ksl], Qg[g][0:96, :, tsl],
                                                          start=True, stop=True), r=selr, w=[spn])
                        else:
                            P.op("pe", lambda e: e.matmul(spt[:], KWg[g][0:64, ksl], Qg[g][0:64, :, tsl],
                                                          start=True, stop=True), w=[spn])

                    def rest(spt=spt, spn=spn, kind=kind, kt=kt, g=g, T=T, bS=bS, bW=bW):
                        masks = []
                        if kt == T:
                            masks.append(([[0, 4], [1, 128]], 0, -1))
                        if kind == "w" and kt == T - 4:
                            masks.append(([[0, 4], [-1, 128]], -1, 1))
                        p, pn = exp_tile(spt, spn, 128, masks)
                        va = vsa if kind == "s" else vwa
                        bank_i = bS if kind == "s" else bW
                        for hg in range(4):
                            acc_mm(bank_i, hg * 128, 65, p[:, hg * 128:(hg + 1) * 128], va[:, kt, g, 0:65],
                                   kt == T, [pn])

                    pipe.push(qk, rest)

                def post(g=g, T=T, bS=bS, bW=bW, smx=smx, sfx=sfx):
                    bnS, bnW = "ab%d" % bS, "ab%d" % bW
                    rs, rw = smx["rs"], smx["rw"]
                    gv = gates[:, T, :].rearrange("p (h c) -> p h c", c=3)
                    oS = ab[:, bS, :].rearrange("p (h c) -> p h c", c=128)
                    oW = ab[:, bW, :].rearrange("p (h c) -> p h c", c=128)
                    P.op("dve", lambda e: e.reciprocal(out=rs[:], in_=oS[:, :, 64]), r=[bnS], w=["rs" + sfx])
                    P.op("dve", lambda e: e.tensor_tensor(out=rs[:], in0=rs[:], in1=gv[:, g * 4:(g + 1) * 4, 1],
                                                          op=ALU.mult), r=["rs" + sfx, "gates"], w=["rs" + sfx])
                    P.op("dve", lambda e: e.reciprocal(out=rw[:], in_=oW[:, :, 64]), r=[bnW], w=["rw" + sfx])
                    P.op("dve", lambda e: e.tensor_tensor(out=rw[:], in0=rw[:], in1=gv[:, g * 4:(g + 1) * 4, 2],
                                                          op=ALU.mult), r=["rw" + sfx, "gates"], w=["rw" + sfx])
                    for hg in range(4):
                        hd = g * 4 + hg
                        on = "om%d_n%d" % (T, hd)
                        osl = om[:, T, 512 + hd * 64:512 + (hd + 1) * 64]
                        tf, tfn = tmpf[hg % 2], "tmpf%d" % (hg % 2)
                        P.op("dve", lambda e, hg=hg, osl=osl, tf=tf: e.scalar_tensor_tensor(
                            out=tf[:], in0=ab[:, bS, hg * 128:hg * 128 + 64], scalar=rs[:, hg:hg + 1], in1=osl,
                            op0=ALU.mult, op1=ALU.add), r=[bnS, "rs" + sfx, on], w=[tfn])
                        P.op("dve", lambda e, hg=hg, osl=osl, tf=tf: e.scalar_tensor_tensor(
                            out=osl, in0=ab[:, bW, hg * 128:hg * 128 + 64], scalar=rw[:, hg:hg + 1], in1=tf[:],
                            op0=ALU.mult, op1=ALU.add), r=[bnW, "rw" + sfx, tfn], w=[on])

                pipe.push(None, post)
        pipe.drain()

        for T in range(NT):
            q = T % 2
            t0 = T * 128
            omr = ["om%d_d%d" % (T, h) for h in range(4)] + ["om%d_n%d" % (T, hd) for hd in range(8)]
            for k in range(8):
                P.op("pe", lambda e, k=k, T=T: e.transpose(tps[:, k * 128:(k + 1) * 128],
                                                          om[:, T, k * 128:(k + 1) * 128], ident[:]),
                     r=omr + ["ident"], w=["tps"])
            oT, oTn = omT[q], "omT%d" % q
            P.op("act", lambda e, oT=oT: e.activation(out=oT[:].rearrange("p k t -> p (k t)"), in_=tps[:],
                                                      func=AF.Copy), r=["tps"], w=[oTn])
            b0 = 2 * q
            for n in range(2):
                for k in range(8):
                    P.op("pe", lambda e, n=n, k=k, oT=oT, b0=b0: e.matmul(
                        ab[:, b0 + n, :], oT[:, k, :], Wo[:, k, n * 512:(n + 1) * 512], start=(k == 0),
                        stop=(k == 7)), r=[oTn, "Wo"], w=["ab%d" % (b0 + n)])
            xr, yt = xres[q], ytmp[q]
            xrn, ytn = "xres%d" % q, "ytmp%d" % q
            wo2 = ab[:, b0:b0 + 2, :]
            br = ["ab%d" % b0, "ab%d" % (b0 + 1)]
            P.dma("act", lambda e, xr=xr, t0=t0: e.dma_start(out=xr[:], in_=src[b, t0:t0 + 128, :]),
                  tag + xrn + "i", w=[xrn])
            P.op("act", lambda e, yt=yt, q=q, wo2=wo2: e.activation(
                out=yt[:].rearrange("p (n f) -> p n f", n=2), in_=wo2, func=AF.Square, accum_out=ss2[:, q:q + 1]),
                 r=br, w=[ytn, "ss2%d" % q])
            P.op("act", lambda e, q=q: e.activation(out=ss2[:, q:q + 1], in_=ss2[:, q:q + 1], func=AF.Sqrt,
                                                    scale=1.0 / D, bias=epsb[:, 0:1]),
                 r=["ss2%d" % q, "epsb"], w=["ss2%d" % q])
            P.op("dve", lambda e, q=q: e.reciprocal(out=rstd2[:, q:q + 1], in_=ss2[:, q:q + 1]),
                 r=["ss2%d" % q], w=["rstd2%d" % q])
            P.op("dve", lambda e, yt=yt, q=q, wo2=wo2: e.scalar_tensor_tensor(
                out=yt[:].rearrange("p (n f) -> p n f", n=2), in0=wo2, scalar=rstd2[:, q:q + 1],
                in1=Gm[:].rearrange("p (n f) -> p n f", n=2), op0=ALU.mult, op1=ALU.mult),
                 r=br + ["rstd2%d" % q, "Gm"], w=[ytn])
            P.op("pool", lambda e, xr=xr, yt=yt: e.tensor_tensor(out=xr[:], in0=xr[:], in1=yt[:], op=ALU.add),
                 r=[ytn, xrn], w=[xrn])
            P.dma("sp", lambda e, xr=xr, t0=t0: e.dma_start(out=dst[b, t0:t0 + 128, :], in_=xr[:]),
                  tag + xrn + "o", r=[xrn], w=["dst_B_%d_%d" % (b, t0)])
        P.barrier_all()
        P.flush()


def build(stage=3):
    nc = bass.Bass("TRN2", target_bir_lowering=False)

    def dt(n, s, d=F32, k="ExternalInput"):
        return nc.dram_tensor(n, s, d, kind=k).ap()

    x = dt("x", [NB, S, D])
    out = dt("out", [NB, S, D], F32, "ExternalOutput")
    ident_d = dt("ident", [128, 128], BF16)
    f = {}
    for t in ("f1", "f2"):
        f[t] = dict(wg=dt(t + "_wg", [D, DFF]), wu=dt(t + "_wu", [D, DFF]), wd=dt(t + "_wd", [DFF, D]),
                    gpre=dt(t + "_gpre", [128, 8]), gpost=dt(t + "_gpost", [128, D]))
    dd = dict(ident=ident_d,
              w_in=dt("w_in", [D, C_END]), w_out=dt("w_out", [D, D]), m_gpre=dt("m_gpre", [128, 8]),
              Gm=dt("Gm", [128, D]), Gs=dt("Gs", [128, 128]),
              cosR=dt("cosR", [128, NT, 8, 8]), sinR=dt("sinR", [128, NT, 8, 8]),
              cosR2=dt("cosR2", [128, NT, 8, 8]), sinR2=dt("sinR2", [128, NT, 8, 8]),
              ov=dt("ov", [127, 32], BF16), ET=dt("ET", [32, S], BF16),
              Am=dt("Am", [128, NT, 32]), Bm=dt("Bm", [128, NT, 32]),
              ck_w1=dt("ck_w1", [2048, 256]), ck_w2=dt("ck_w2", [256, 64]), ck_peT=dt("ck_peT", [128, 32]),
              cv_w1=dt("cv_w1", [2048, 256]), cv_w2=dt("cv_w2", [256, 64]), cv_peT=dt("cv_peT", [128, 32]),
              lq1=dt("lq1", [128, 64]), lk1=dt("lk1", [128, 64]), lq2=dt("lq2", [128, 64]),
              lk2=dt("lk2", [128, 64]))
    x1 = nc.dram_tensor("x1s", [NB, S, D], F32).ap()
    x2 = nc.dram_tensor("x2s", [NB, S, D], F32).ap()
    with ExitStack() as stack:
        P = Prog(nc, stack)
        fa = f["f1"]
        ffn_phase(nc, P, "A", x, out if stage == 1 else x1, fa["wg"], fa["wu"], fa["wd"], fa["gpre"],
                  fa["gpost"], ident_d)
        if stage >= 2:
            mixer_phase(nc, P, x1, out if stage == 2 else x2, dd)
        if stage >= 3:
            fc = f["f2"]
            ffn_phase(nc, P, "C", x2, out, fc["wg"], fc["wu"], fc["wd"], fc["gpre"], fc["gpost"], ident_d)
    return nc


def host_inputs(inp):
    def g(k):
        return np.ascontiguousarray(np.asarray(inp[k], dtype=np.float32))

    bf = ml_dtypes.bfloat16

    def bc(v, n=128):
        return np.ascontiguousarray(np.broadcast_to(v[None, :], (n, v.shape[0])))

    common = {"ident": np.eye(128, dtype=np.float32).astype(bf)}
    for t, pfx in (("f1", "ff1"), ("f2", "ff2")):
        common[t + "_wg"] = g(pfx + "_w_gate")[0]
        common[t + "_wu"] = g(pfx + "_w_up")[0]
        common[t + "_wd"] = g(pfx + "_w_down")[0]
        common[t + "_gpre"] = np.ascontiguousarray(g(pfx + "_norm_pre")[0].reshape(8, 128).T)
        common[t + "_gpost"] = bc(g(pfx + "_norm_post")[0])
    common["w_in"] = g("w_in")[0]
    common["w_out"] = g("w_out")[0]
    common["m_gpre"] = np.ascontiguousarray(g("mix_norm_pre")[0].reshape(8, 128).T)
    common["Gm"] = bc(g("mix_norm_post")[0])
    common["Gs"] = bc(g("diff_subln")[0])
    for k, n in (("lq1", "lambda_q1"), ("lk1", "lambda_k1"), ("lq2", "lambda_q2"), ("lk2", "lambda_k2")):
        common[k] = bc(g(n)[0])
    for kv, pfx in (("ck", "cmp_k"), ("cv", "cmp_v")):
        common[kv + "_w1"] = g("cmp_%s_w1" % kv[1])[0]
        common[kv + "_w2"] = g("cmp_%s_w2" % kv[1])[0]
        peT = g("cmp_pe_%s" % kv[1])[0].T
        common[kv + "_peT"] = np.ascontiguousarray(np.concatenate([peT, peT], axis=0))
    pos = np.arange(S, dtype=np.float32)
    inv = (np.float32(500000.0) ** (-np.arange(0, 16, 2, dtype=np.float32) / np.float32(16))).astype(np.float32)
    ang = (pos[:, None] * inv[None, :]).astype(np.float32)
    cs, sn = np.cos(ang).astype(np.float32), np.sin(ang).astype(np.float32)

    def tab(a):
        a = a.reshape(NT, 128, 8).transpose(1, 0, 2)
        return np.ascontiguousarray(np.broadcast_to(a[:, :, None, :], (128, NT, 8, 8)))

    common["cosR"], common["sinR"] = tab(cs), tab(sn)
    c2, s2 = tab(cs).copy(), tab(sn).copy()
    c2[:, :, 4:8, :] = 1.0
    s2[:, :, 4:8, :] = 0.0
    common["cosR2"], common["sinR2"] = c2, s2
    c = np.arange(127)[:, None] * 16
    j = np.arange(32)[None, :] * 64
    ov = np.clip(np.minimum(c + 32, j + 64) - np.maximum(c, j), 0, None) / 32.0
    common["ov"] = ov.astype(np.float32).astype(bf)
    common["ET"] = (np.arange(S)[None, :] // 64 == np.arange(32)[:, None]).astype(np.float32).astype(bf)
    tt = np.arange(S)
    cur = (tt // 64)[:, None]
    blk = np.arange(32)[None, :]
    forced = (blk == 0) | ((blk <= cur) & (blk >= cur - 1))
    causal = blk <= cur
    A = (~forced & causal).astype(np.float32)
    Bc = np.where(forced, np.float32(1e9), np.where(causal, np.float32(0.0), np.float32(-1.0))).astype(np.float32)
    common["Am"] = np.ascontiguousarray(A.reshape(NT, 128, 32).transpose(1, 0, 2))
    common["Bm"] = np.ascontiguousarray(Bc.reshape(NT, 128, 32).transpose(1, 0, 2))
    x = g("x")
    maps = []
    for c_ in range(8):
        m = dict(common)
        m["x"] = x[c_ * NB:(c_ + 1) * NB]
        maps.append(m)
    return maps


def kernel(**inputs):
    nc = build(3)
    maps = host_inputs(inputs)
    res = run_bass_kernel_spmd(nc, maps, core_ids=list(range(8)))
    return np.concatenate([np.asarray(r["out"]) for r in res.results], axis=0).astype(np.float32)
```

```python
import numpy as np
import ml_dtypes
from contextlib import ExitStack
import concourse.bass as bass
import concourse.mybir as mybir
from concourse.bass_utils import run_bass_kernel_spmd

F32 = mybir.dt.float32
BF16 = mybir.dt.bfloat16
AF = mybir.ActivationFunctionType
ALU = mybir.AluOpType
AX = mybir.AxisListType

S = 2048
D = 1024
DFF = 2816
NF = DFF // 128
NB = 2
EPS = 1e-6
ENGS = ("pe", "act", "dve", "pool", "sp")


import re
PSUM_RE = re.compile(r"^(tr\d|gu\d|dn\d|pj\d|tq\d|pbps|hps\d|cps|sps\d|ab\d|tps)")


class Res:
    __slots__ = ("name", "w", "r")

    def __init__(self, name):
        self.name = name
        self.w = None
        self.r = {}


class Op:
    __slots__ = ("eng", "fn", "deps", "kind", "key", "val", "idx", "sig", "waits", "ordinal")


class Prog:
    def __init__(self, nc, stack):
        self.nc = nc
        self.stack = stack
        self.res = {}
        self.ops = []
        self.eng_n = {e: 0 for e in ENGS}
        self.sigbase = {e: 0 for e in ENGS}
        self.esem = {e: stack.enter_context(nc.semaphore("sem_" + e)) for e in ENGS if e != "sp"}
        self.dsem = {}
        self.dcount = {}
        self.free_sems = {"sw": [], "hw": []}
        self.dcls = {}
        self.live = []
        self.seen = {e: {} for e in ENGS}
        self.sigord = {e: {} for e in ENGS}

    def R(self, name):
        r = self.res.get(name)
        if r is None:
            r = self.res[name] = Res(name)
        return r

    def _deps(self, reads, writes):
        deps = []
        for n in reads:
            r = self.R(n)
            if r.w is not None:
                deps.append(r.w)
        for n in writes:
            r = self.R(n)
            if r.w is not None:
                deps.append(r.w)
            deps.extend(r.r.values())
        return deps

    def _mark(self, reads, writes, ev, rkey):
        for n in reads:
            self.R(n).r[rkey] = ev
        for n in writes:
            r = self.R(n)
            r.w = ev
            r.r = {}

    def op(self, eng, fn, r=(), w=()):
        w = list(w) + [n + "#rd" for n in r if PSUM_RE.match(n)]
        o = Op()
        o.eng = eng
        o.fn = fn
        o.kind = "c"
        o.deps = self._deps(r, w)
        o.idx = self.eng_n[eng]
        self.eng_n[eng] += 1
        o.sig = False
        self._mark(r, w, ("c", eng, o.idx), eng)
        self.ops.append(o)
        return o

    def dma(self, q, fn, key, r=(), w=()):
        o = Op()
        o.eng = q
        o.fn = fn
        o.kind = "d"
        o.key = key
        if key not in self.dsem:
            cls = "sw" if q == "pool" else "hw"
            self.dcls[key] = cls
            if self.free_sems[cls]:
                self.dsem[key], self.dcount[key] = self.free_sems[cls].pop()
            else:
                self.dsem[key] = self.stack.enter_context(self.nc.semaphore("dsem%d" % len(self.dsem)))
                self.dcount[key] = 0
            self.live.append(key)
        o.deps = self._deps(r, w)
        self.dcount[key] += 16
        o.val = self.dcount[key]
        o.idx = self.eng_n[q]
        self.eng_n[q] += 1
        o.sig = False
        self._mark(r, w, ("d", key, o.val), "d_" + key)
        self.ops.append(o)
        return o

    def barrier_all(self):
        allres = list(self.res.keys())
        for e in ENGS:
            self.op(e, None, r=allres)
        self.res = {}
        for k in self.live:
            self.free_sems[self.dcls[k]].append((self.dsem[k], self.dcount[k]))
        self.live = []

    def flush(self):
        nc = self.nc
        ops = self.ops
        self.ops = []
        self.nflush = getattr(self, "nflush", 0) + 1
        if LIMIT is not None and self.nflush == LIMIT[0]:
            ops = ops[:LIMIT[1]]
        byidx = {}
        for o in ops:
            if o.kind == "c":
                byidx[(o.eng, o.idx)] = o
        for o in ops:
            o.waits = []
            seen = self.seen[o.eng]
            best = {}
            for d in o.deps:
                if d[0] == "c":
                    if d[1] == "pe" and o.eng == "pe":
                        continue
                    k = ("c", d[1])
                else:
                    k = ("d", d[1])
                if k not in best or d[2] > best[k][2]:
                    best[k] = d
            for d in best.values():
                if d[0] == "c":
                    _, pe, pi = d
                    if seen.get(pe, -1) >= pi:
                        continue
                    prod = byidx.get((pe, pi))
                    if prod is None:
                        assert pi in self.sigord[pe], (pe, pi)
                    else:
                        prod.sig = True
                    seen[pe] = pi
                    o.waits.append(d)
                else:
                    _, key, val = d
                    k = "d_" + key
                    if seen.get(k, 0) >= val:
                        continue
                    seen[k] = val
                    o.waits.append(d)
        last = {}
        for o in ops:
            if o.kind == "c":
                last[o.eng] = o
        for o in last.values():
            o.sig = True
        for o in ops:
            if o.kind == "c" and o.sig:
                self.sigbase[o.eng] += 1
                self.sigord[o.eng][o.idx] = self.sigbase[o.eng]
        per = {e: [o for o in ops if o.eng == e] for e in ENGS}

        def emit(eng_name, eng):
            for o in per[eng_name]:
                for d in o.waits:
                    if d[0] == "c":
                        eng.wait_ge(self.esem[d[1]], self.sigord[d[1]][d[2]])
                    else:
                        eng.wait_ge(self.dsem[d[1]], d[2])
                if o.fn is None:
                    if o.kind == "c" and o.sig:
                        eng.nop().then_inc(self.esem[eng_name], 1) if eng_name != "sp" else None
                    continue
                ins = o.fn(eng)
                if o.kind == "d":
                    ins.then_inc(self.dsem[o.key], 16)
                elif o.sig:
                    ins.then_inc(self.esem[eng_name], 1)

        with nc.Block() as block:
            @block.tensor
            def _(e):
                emit("pe", e)

            @block.scalar
            def _(e):
                emit("act", e)

            @block.vector
            def _(e):
                emit("dve", e)

            @block.gpsimd
            def _(e):
                emit("pool", e)

            @block.sync
            def _(e):
                emit("sp", e)


def ffn_phase(nc, P, tag, src, dst, wg_d, wu_d, wd_d, gpre_d, gpost_d, ident_d):
    with ExitStack() as st:
        def sb(name, shape, dt):
            return st.enter_context(nc.sbuf_tensor(tag + name, shape, dt))

        def ps(name, shape, dt):
            return st.enter_context(nc.psum_tensor(tag + name, shape, dt))

        Wg = sb("Wg", [128, 8, DFF], BF16)
        Wu = sb("Wu", [128, 8, DFF], BF16)
        Wd = sb("Wd", [128, NF, D], BF16)
        xin = [sb("xin%d" % i, [128, D], F32) for i in range(2)]
        hn = sb("hn", [128, 4, D], BF16)
        hT = sb("hT", [128, 8, 512], BF16)
        actT = sb("actT", [128, NF, 512], BF16)
        G = sb("G", [128, D], F32)
        gpre = sb("gpre", [128, 8], F32)
        ident = sb("ident", [128, 128], BF16)
        xres = [sb("xres%d" % i, [128, D], F32) for i in range(2)]
        ytmp = [sb("ytmp%d" % i, [128, D], F32) for i in range(2)]
        sg = [sb("sg%d" % i, [128, 512], F32) for i in range(2)]
        ss = sb("ss", [128, 8], F32)
        rstd = sb("rstd", [128, 8], F32)
        ss2 = sb("ss2", [128, 2], F32)
        rstd2 = sb("rstd2", [128, 2], F32)
        epsb = sb("epsb", [128, 1], F32)
        P.op("pool", lambda e: e.memset(epsb[:], EPS), w=["epsb"])
        tr = [ps("tr%d" % i, [128, 2, 512], BF16) for i in range(2)]
        gu = [ps("gu%d" % i, [128, 512], F32) for i in range(4)]
        dn = ps("dn", [128, D], F32)

        P.dma("sp", lambda e: e.dma_start(out=gpre[:], in_=gpre_d), tag + "c0", w=["gpre"])
        P.dma("sp", lambda e: e.dma_start(out=G[:], in_=gpost_d), tag + "c1", w=["G"])
        P.dma("sp", lambda e: e.dma_start(out=ident[:], in_=ident_d), tag + "c2", w=["ident"])
        P.op("dve", lambda e: e.tensor_scalar(out=G[:], in0=G[:], scalar1=0.5, scalar2=None, op0=ALU.mult),
             r=["G"], w=["G"])
        wg_v = wg_d.rearrange("(k p) f -> p k f", p=128)
        wu_v = wu_d.rearrange("(k p) f -> p k f", p=128)
        wd_v = wd_d.rearrange("(f p) d -> p f d", p=128)
        FG = [(0, 2), (2, 6), (6, 10), (10, 14), (14, 18), (18, 22)]
        fgrp = {}
        for gi, (a, b) in enumerate(FG):
            for f in range(a, b):
                fgrp[f] = gi
            for nm, W, v in (("Wg", Wg, wg_v), ("Wu", Wu, wu_v)):
                for kh in range(2):
                    P.dma("pool",
                          lambda e, W=W, v=v, a=a, b=b, kh=kh: e.dma_start(
                              out=W[:, kh * 4:(kh + 1) * 4, a * 128:b * 128],
                              in_=v[:, kh * 4:(kh + 1) * 4, a * 128:b * 128]),
                          "%s%s%d" % (tag, nm, gi), w=["%s%d" % (nm, gi)])
        for gi, (a, b) in enumerate(FG):
            P.dma("pool", lambda e, a=a, b=b: e.dma_start(out=Wd[:, a:b, :], in_=wd_v[:, a:b, :]),
                  "%sWd%d" % (tag, gi), w=["Wd%d" % gi])

        blocks = [(b, i) for b in range(NB) for i in range(4)]
        nxi = [0]

        def emit_N(bi, j):
            b, i = blocks[bi]
            t0 = i * 512 + j * 128
            xb = xin[nxi[0] % 2]
            xn = "xin%d" % (nxi[0] % 2)
            nxi[0] += 1
            P.dma("sp", lambda e: e.dma_start(out=xb[:], in_=src[b, t0:t0 + 128, :]), tag + xn, w=[xn])
            P.op("act", lambda e: e.activation(out=hn[:, j, :], in_=xb[:], func=AF.Square,
                                               accum_out=ss[:, j:j + 1]),
                 r=[xn], w=["hn%d" % j, "ss%d" % j])
            P.op("act", lambda e: e.activation(out=ss[:, j:j + 1], in_=ss[:, j:j + 1], func=AF.Sqrt,
                                               scale=1.0 / D, bias=epsb[:, 0:1]),
                 r=["ss%d" % j, "epsb"], w=["ss%d" % j])
            P.op("dve", lambda e: e.reciprocal(out=rstd[:, j:j + 1], in_=ss[:, j:j + 1]),
                 r=["ss%d" % j], w=["rstd%d" % j])
            P.op("act", lambda e: e.activation(out=hn[:, j, :], in_=xb[:], func=AF.Copy,
                                               scale=rstd[:, j:j + 1]),
                 r=[xn, "rstd%d" % j], w=["hn%d" % j])

        def emit_T(bi):
            for kp in range(4):
                bank = kp % 2
                for kk in range(2):
                    k = kp * 2 + kk
                    for j in range(4):
                        P.op("pe", lambda e, k=k, kk=kk, j=j, bank=bank: e.transpose(
                            tr[bank][:, kk, j * 128:(j + 1) * 128], hn[:, j, k * 128:(k + 1) * 128], ident[:]),
                             r=["hn%d" % j, "ident"], w=["tr%d" % bank])
                for kk in range(2):
                    k = kp * 2 + kk
                    P.op("dve", lambda e, k=k, kk=kk, bank=bank: e.tensor_scalar(
                        out=hT[:, k, :], in0=tr[bank][:, kk, :], scalar1=gpre[:, k:k + 1], scalar2=None,
                        op0=ALU.mult),
                         r=["tr%d" % bank, "gpre"], w=["hT%d" % k])

        gui = [0]

        def emit_GU_f(bi, f):
            pr = gui[0] % 2
            gui[0] += 1
            pg, pu = gu[2 * pr], gu[2 * pr + 1]
            gi = fgrp[f]
            for nm, W, pt in (("Wg", Wg, pg), ("Wu", Wu, pu)):
                pn = "gu%d%s" % (pr, nm)
                for k in range(8):
                    P.op("pe", lambda e, W=W, pt=pt, k=k: e.matmul(
                        pt[:], W[:, k, f * 128:(f + 1) * 128], hT[:, k, :], start=(k == 0), stop=(k == 7)),
                         r=["%s%d" % (nm, gi), "hT%d" % k], w=[pn])
            s = sg[pr]
            P.op("act", lambda e: e.activation(out=s[:], in_=pg[:], func=AF.Silu),
                 r=["gu%dWg" % pr], w=["sg%d" % pr])
            P.op("dve", lambda e: e.tensor_tensor(out=actT[:, f, :], in0=s[:], in1=pu[:], op=ALU.mult),
                 r=["sg%d" % pr, "gu%dWu" % pr], w=["actT%d" % f])

        dni = [0]

        def emit_D(bi):
            b, i = blocks[bi]
            for j in range(4):
                t0 = i * 512 + j * 128
                q = dni[0] % 2
                dni[0] += 1
                xr, yt = xres[q], ytmp[q]
                xrn, ytn = "xres%d" % q, "ytmp%d" % q
                P.dma("act", lambda e, xr=xr, t0=t0: e.dma_start(out=xr[:], in_=src[b, t0:t0 + 128, :]),
                      tag + xrn + "i", w=[xrn])
                for n in range(2):
                    for f in range(NF):
                        P.op("pe", lambda e, n=n, f=f, j=j: e.matmul(
                            dn[:, n * 512:(n + 1) * 512], actT[:, f, j * 128:(j + 1) * 128],
                            Wd[:, f, n * 512:(n + 1) * 512], start=(f == 0), stop=(f == NF - 1)),
                             r=["actT%d" % f, "Wd%d" % fgrp[f]], w=["dn%d" % n])
                P.op("act", lambda e, yt=yt, q=q: e.activation(out=yt[:], in_=dn[:], func=AF.Square,
                                                             accum_out=ss2[:, q:q + 1]),
                     r=["dn0", "dn1"], w=[ytn, "ss2%d" % q])
                P.op("act", lambda e, q=q: e.activation(out=ss2[:, q:q + 1], in_=ss2[:, q:q + 1], func=AF.Sqrt,
                                                        scale=1.0 / D, bias=epsb[:, 0:1]),
                     r=["ss2%d" % q, "epsb"], w=["ss2%d" % q])
                P.op("dve", lambda e, q=q: e.reciprocal(out=rstd2[:, q:q + 1], in_=ss2[:, q:q + 1]),
                     r=["ss2%d" % q], w=["rstd2%d" % q])
                P.op("dve", lambda e, yt=yt, q=q: e.scalar_tensor_tensor(
                    out=yt[:], in0=dn[:], scalar=rstd2[:, q:q + 1], in1=G[:], op0=ALU.mult, op1=ALU.mult),
                     r=["dn0", "dn1", "rstd2%d" % q, "G"], w=[ytn])
                P.op("pool", lambda e, xr=xr, yt=yt: e.tensor_tensor(out=xr[:], in0=xr[:], in1=yt[:],
                                                                     op=ALU.add),
                     r=[ytn, xrn], w=[xrn])
                P.dma("sp", lambda e, xr=xr, t0=t0: e.dma_start(out=dst[b, t0:t0 + 128, :], in_=xr[:]),
                      tag + xrn + "o", r=[xrn], w=["dst_%s_%d_%d" % (tag, b, t0)])

        nblk = len(blocks)
        for j in range(4):
            emit_N(0, j)
        emit_T(0)
        for bi in range(nblk):
            for f in range(NF):
                emit_GU_f(bi, f)
                if bi + 1 < nblk and f in (3, 8, 13, 18):
                    emit_N(bi + 1, (f - 3) // 5)
            if bi + 1 < nblk:
                emit_T(bi + 1)
            emit_D(bi)
        P.barrier_all()
        P.flush()


NT = S // 128
LIMIT = None
NSEQ = NB
SUB = 9
CUT = 9
BARQT = False
EVAC_ACT = False
BIG = 30000.0
C_DQ, C_DK, C_DV, C_NQ, C_KC, C_VC, C_KS, C_VS, C_KW, C_VW, C_GL, C_END = (
    0, 512, 1024, 1536, 2048, 2176, 2304, 2432, 2560, 2688, 2816, 2840)
S_DQ, S_DK, S_NQ, S_KS, S_KW, S_KC, S_VC, S_END = 0, 512, 1024, 1536, 1664, 1792, 1920, 2048


class Rot:
    def __init__(self, items, names):
        self.items, self.names, self.i = items, names, 0

    def next(self):
        k = self.i % len(self.items)
        self.i += 1
        return self.items[k], self.names[k]


def mixer_phase(nc, P, src, dst, dd):
    TB = 2
    with ExitStack() as st0:
        def sb0(name, shape, dt):
            return st0.enter_context(nc.sbuf_tensor("B" + name, shape, dt))

        dqT = sb0("dqT", [128, 4, S], BF16)
        dkT = sb0("dkT", [128, 4, S], BF16)
        Qg = [sb0("Qg%d" % g, [128, 4, S], BF16) for g in range(2)]
        KSg = [sb0("KSg%d" % g, [128, S], BF16) for g in range(2)]
        KWg = [sb0("KWg%d" % g, [128, S], BF16) for g in range(2)]
        dva = sb0("dva", [128, NT, 4, 130], BF16)
        vsa = sb0("vsa", [128, NT, 2, 66], BF16)
        vwa = sb0("vwa", [128, NT, 2, 66], BF16)
        glr = sb0("glr", [128, NT, 24], F32)
        kcc = [sb0("kcc%d" % g, [128, 128], BF16) for g in range(2)]
        vca = sb0("vca", [128, 2, 98], BF16)
        ident = sb0("ident", [128, 128], BF16)
        epsb = sb0("epsb", [128, 1], F32)

        P.dma("sp", lambda e: e.dma_start(out=ident[:], in_=dd["ident"]), "Bc_ident", w=["ident"])
        P.op("pool", lambda e: e.memset(epsb[:], EPS), w=["epsb"])
        P.op("pool", lambda e: e.memset(dva[:, :, :, 128:130], 1.0), w=["dva"])
        P.op("pool", lambda e: e.memset(vsa[:, :, :, 64:66], 1.0), w=["vsa"])
        P.op("pool", lambda e: e.memset(vwa[:, :, :, 64:66], 1.0), w=["vwa"])
        P.op("pool", lambda e: e.memset(vca[:], 0.0), w=["vca"])
        P.op("pool", lambda e: e.memset(vca[:, :, 64:65], 1.0), r=["vca"], w=["vca"])
        for g in range(2):
            P.dma("sp", lambda e, g=g: e.dma_start(out=vca[0:127, g, 65:97], in_=dd["ov"]), "Bc_ov%d" % g,
                  r=["vca"], w=["vca_ov%d" % g])
            P.dma("sp", lambda e, g=g: e.dma_start(out=KSg[g][64:96, :], in_=dd["ET"]), "Bc_ET%d" % g,
                  w=["KSgE%d" % g])

        for b in range(NSEQ):
            with ExitStack() as st1:
                kcT = st1.enter_context(nc.sbuf_tensor("BkcT%d" % b, [128, S], BF16))
                vcT = st1.enter_context(nc.sbuf_tensor("BvcT%d" % b, [128, S], BF16))
                mixer_proj(nc, P, b, TB, src, dd, dict(dqT=dqT, dkT=dkT, Qg=Qg, KSg=KSg, KWg=KWg, dva=dva,
                                                      vsa=vsa, vwa=vwa, glr=glr, kcT=kcT, vcT=vcT,
                                                      ident=ident, epsb=epsb))
                if SUB >= 2:
                    mixer_compress(nc, P, b, dd, dict(kcT=kcT, vcT=vcT, kcc=kcc, vca=vca))
            if SUB >= 3:
                mixer_attn(nc, P, b, src, dst, dd, dict(dqT=dqT, dkT=dkT, Qg=Qg, KSg=KSg, KWg=KWg, dva=dva,
                                                   vsa=vsa, vwa=vwa, glr=glr, kcc=kcc, vca=vca,
                                                   ident=ident, epsb=epsb))


def mixer_proj(nc, P, b, TB, src, dd, t):
    tag = "P%d" % b
    dqT, dkT, Qg, KSg, KWg = t["dqT"], t["dkT"], t["Qg"], t["KSg"], t["KWg"]
    dva, vsa, vwa, glr, kcT, vcT, ident, epsb = (t["dva"], t["vsa"], t["vwa"], t["glr"], t["kcT"], t["vcT"],
                                                 t["ident"], t["epsb"])
    with ExitStack() as st:
        def sb(name, shape, dt):
            return st.enter_context(nc.sbuf_tensor(tag + name, shape, dt))

        def ps(name, shape, dt):
            return st.enter_context(nc.psum_tensor(tag + name, shape, dt))

        rt = [[sb("rt%d_%d" % (i, q), [128, 8, 8], F32) for q in range(4)] for i in range(2)]
        xs = [sb("xs%d" % i, [128, 512], F32) for i in range(2)]
        xin = [sb("xin%d" % i, [128, D], F32) for i in range(2)]
        hn = sb("hn", [128, TB, D], BF16)
        hT = sb("hT", [128, 8, TB * 128], BF16)
        stg = sb("stg", [128, TB, S_END], BF16)
        cosR = sb("cosR", [128, NT, 8, 8], F32)
        sinR = sb("sinR", [128, NT, 8, 8], F32)
        gpre = sb("gpre", [128, 8], F32)
        ss = sb("ss", [128, TB], F32)
        rstd = sb("rstd", [128, TB], F32)
        Win = sb("Win", [128, 8, 2048], BF16)
        tr = [ps("tr%d" % i, [128, 1024], BF16) for i in range(2)]
        pj = [ps("pj%d" % i, [128, 512], F32) for i in range(3)]
        tq = [ps("tq%d" % i, [128, 1024], BF16) for i in range(2)]

        P.dma("sp", lambda e: e.dma_start(out=gpre[:], in_=dd["m_gpre"]), tag + "gpre", w=["gpre"])
        P.dma("sp", lambda e: e.dma_start(out=cosR[:], in_=dd["cosR"]), tag + "cos", w=["cosR"])
        P.dma("sp", lambda e: e.dma_start(out=sinR[:], in_=dd["sinR"]), tag + "sin", w=["sinR"])
        win_v = dd["w_in"].rearrange("(k p) f -> p k f", p=128)
        CB = [(0, 512), (512, 1024), (1024, 1536), (1536, 2048), (2048, 2560), (2560, C_END)]
        SEC = [(C_KS, 128), (C_KW, 128), (C_KC, 128), (C_VC, 128), (C_VS, 128), (C_VW, 128), (C_GL, 24)]
        def load_pass(ph):
            if ph == 0:
                for ci, (c0, c1) in enumerate(CB[:4]):
                    for kh in range(2):
                        P.dma("pool", lambda e, c0=c0, c1=c1, kh=kh: e.dma_start(
                            out=Win[:, kh * 4:(kh + 1) * 4, c0:c1], in_=win_v[:, kh * 4:(kh + 1) * 4, c0:c1]),
                              "%sWin%d" % (tag, ci), w=["Win%d" % ci])
            else:
                dcol = 0
                for si, (sc, sw) in enumerate(SEC):
                    ci = 4 if si < 4 else 5
                    P.dma("pool", lambda e, sc=sc, sw=sw, dcol=dcol: e.dma_start(
                        out=Win[:, :, dcol:dcol + sw], in_=win_v[:, :, sc:sc + sw]),
                          "%sWin%d" % (tag, ci), w=["Win%d" % ci] + (["Win0", "Win1"] if si == 0 else []))
                    dcol += sw
                P.dma("sp", lambda e: e.dma_start(out=cosR[:], in_=dd["cosR2"]), tag + "cos", w=["cosR"])
                P.dma("sp", lambda e: e.dma_start(out=sinR[:], in_=dd["sinR2"]), tag + "sin", w=["sinR"])

        PH = [0]
        nxi = [0]
        pji = [0]
        rti = [0]
        tqi = [0]
        evi = [0]

        def emit_N(i, j):
            t0 = (i * TB + j) * 128
            q = nxi[0] % 2
            nxi[0] += 1
            xb, xn = xin[q], "xin%d" % q
            P.dma("sp", lambda e: e.dma_start(out=xb[:], in_=src[b, t0:t0 + 128, :]), tag + xn, w=[xn])
            P.op("act", lambda e: e.activation(out=hn[:, j, :], in_=xb[:], func=AF.Square,
                                               accum_out=ss[:, j:j + 1]),
                 r=[xn], w=["hn%d" % j, "ss%d" % j])
            P.op("act", lambda e: e.activation(out=ss[:, j:j + 1], in_=ss[:, j:j + 1], func=AF.Sqrt,
                                               scale=1.0 / D, bias=epsb[:, 0:1]),
                 r=["ss%d" % j, "epsb"], w=["ss%d" % j])
            P.op("dve", lambda e: e.reciprocal(out=rstd[:, j:j + 1], in_=ss[:, j:j + 1]),
                 r=["ss%d" % j], w=["rstd%d" % j])
            P.op("act", lambda e: e.activation(out=hn[:, j, :], in_=xb[:], func=AF.Copy,
                                               scale=rstd[:, j:j + 1]),
                 r=[xn, "rstd%d" % j], w=["hn%d" % j])

        def emit_T(i):
            W = TB * 128
            for kq in range(2):
                bank = kq
                for kk in range(4):
                    k = kq * 4 + kk
                    for j in range(TB):
                        P.op("pe", lambda e, k=k, kk=kk, j=j, bank=bank: e.transpose(
                            tr[bank][:, kk * W + j * 128:kk * W + (j + 1) * 128],
                            hn[:, j, k * 128:(k + 1) * 128], ident[:]),
                             r=["hn%d" % j, "ident"], w=["tr%d" % bank])
                for kk in range(4):
                    k = kq * 4 + kk
                    P.op("dve", lambda e, k=k, kk=kk, bank=bank: e.tensor_scalar(
                        out=hT[:, k, :], in0=tr[bank][:, kk * W:(kk + 1) * W], scalar1=gpre[:, k:k + 1],
                        scalar2=None, op0=ALU.mult),
                         r=["tr%d" % bank, "gpre"], w=["hT%d" % k])

        def rope(pjt, pjn, o, nh, j, so, T, tabs=None):
            q = rti[0] % 2
            rti[0] += 1
            ta, tb_, tc, td = rt[q]
            rn = ["rt%d_%d" % (q, x) for x in range(4)]
            xst, xsn = xs[q], "xs%d" % q
            if (CUT == 4.26 and nh == 2) or (CUT == 4.28 and tabs is not None):
                sres = "stg%d_%d" % (j, so)
                P.op("act", lambda e: e.activation(out=stg[:, j, so:so + nh * 64], in_=pjt[:, o:o + nh * 64],
                                                   func=AF.Copy), r=[pjn], w=[sres + "a"])
                return [sres + "a"]
            P.op("act", lambda e: e.activation(out=xst[:, 0:nh * 64], in_=pjt[:, o:o + nh * 64], func=AF.Copy),
                 r=[pjn], w=[xsn])
            pv = xst[:, 0:nh * 64].rearrange("p (h d) -> p h d", d=64)
            sv = stg[:, j, so:so + nh * 64].rearrange("p (h d) -> p h d", d=64)
            x1, x2 = pv[:, :, 0:8], pv[:, :, 8:16]
            ct_, st_, ctn, stn = (cosR, sinR, "cosR", "sinR") if tabs is None else tabs
            cs, sn = ct_[:, T, 0:nh, :], st_[:, T, 0:nh, :]
            sres = "stg%d_%d" % (j, so)
            if CUT < 3.06:
                return [sres + "a", sres + "b", sres + "c"]
            P.op("dve", lambda e: e.tensor_tensor(out=ta[:, 0:nh, :], in0=x1, in1=cs, op=ALU.mult),
                 r=[xsn, ctn], w=[rn[0]])
            if CUT < 3.07:
                return [sres + "a", sres + "b", sres + "c"]
            P.op("dve", lambda e: e.tensor_tensor(out=tb_[:, 0:nh, :], in0=x2, in1=sn, op=ALU.mult),
                 r=[xsn, stn], w=[rn[1]])
            if CUT < 3.08:
                return [sres + "a", sres + "b", sres + "c"]
            P.op("dve", lambda e: e.tensor_tensor(out=tc[:, 0:nh, :], in0=x2, in1=cs, op=ALU.mult),
                 r=[xsn, ctn], w=[rn[2]])
            if CUT == 3.095:
                P.op("dve", lambda e: e.tensor_tensor(out=tc[:, 0:nh, :], in0=x1, in1=sn, op=ALU.mult),
                     r=[xsn, "sinR"], w=[rn[2]])
                return [sres + "a", sres + "b", sres + "c"]
            P.op("dve", lambda e: e.tensor_tensor(out=td[:, 0:nh, :], in0=x1, in1=sn, op=ALU.mult),
                 r=[xsn, stn], w=[rn[3]])
            P.op("dve", lambda e: e.tensor_tensor(out=pv[:, :, 0:8], in0=ta[:, 0:nh, :], in1=tb_[:, 0:nh, :],
                                                  op=ALU.subtract),
                 r=[rn[0], rn[1], rn[2], rn[3], xsn], w=[xsn])
            P.op("dve", lambda e: e.tensor_tensor(out=pv[:, :, 8:16], in0=tc[:, 0:nh, :], in1=td[:, 0:nh, :],
                                                  op=ALU.add),
                 r=[rn[2], rn[3], xsn], w=[xsn])
            P.op("act", lambda e: e.activation(out=stg[:, j, so:so + nh * 64], in_=xst[:, 0:nh * 64], func=AF.Copy),
                 r=[xsn], w=[sres + "a"])
            return [sres + "a"]

        def emit_proj(i, j, stres):
            T = i * TB + j
            for ci, (c0, c1) in enumerate(CB):
                if (ci < 4) != (PH[0] == 0):
                    continue
                if ci >= 4:
                    c0, c1 = c0 - 2048, c1 - 2048
                q = pji[0] % 3
                pji[0] += 1
                pjt, pjn = pj[q], "pj%d" % q
                ncol = c1 - c0
                for k in range(8):
                    P.op("pe", lambda e, k=k, pjt=pjt, c0=c0, c1=c1, ncol=ncol: e.matmul(
                        pjt[:, 0:ncol], hT[:, k, j * 128:(j + 1) * 128], Win[:, k, c0:c1],
                        start=(k == 0), stop=(k == 7)),
                         r=["hT%d" % k, "Win%d" % ci], w=[pjn])
                if ci == 0:
                    stres["dq"][j] = rope(pjt, pjn, 0, 8, j, S_DQ, T)
                elif ci == 1:
                    stres["dk"][j] = rope(pjt, pjn, 0, 8, j, S_DK, T)
                elif ci == 2:
                    P.op("act", lambda e, pjt=pjt, T=T: e.activation(
                        out=dva[:, T, :, 0:128], in_=pjt[:, 0:512].rearrange("p (h d) -> p h d", d=128),
                        func=AF.Copy), r=[pjn], w=["dva%d" % T])
                elif ci == 3:
                    stres["nq"][j] = rope(pjt, pjn, 0, 8, j, S_NQ, T)
                elif ci == 4:
                    rr = rope(pjt, pjn, 0, 8, j, S_KS, T)
                    stres["ks"][j] = rr
                    stres["kw"][j] = rr
                    stres["kcvc"][j] = rr
                else:
                    P.op("act", lambda e, pjt=pjt, T=T: e.activation(
                        out=vsa[:, T, :, 0:64], in_=pjt[:, 0:128].rearrange("p (h d) -> p h d", d=64),
                        func=AF.Copy), r=[pjn], w=["vsa%d" % T])
                    P.op("act", lambda e, pjt=pjt, T=T: e.activation(
                        out=vwa[:, T, :, 0:64], in_=pjt[:, 128:256].rearrange("p (h d) -> p h d", d=64),
                        func=AF.Copy), r=[pjn], w=["vwa%d" % T])
                    P.op("dve", lambda e, pjt=pjt, T=T: e.tensor_copy(out=glr[:, T, :], in_=pjt[:, 256:280]),
                         r=[pjn], w=["glr%d" % T])

        def emit_QT(i, stres):
            tk0 = i * TB * 128
            W = TB * 128
            units = []
            for h in range(4):
                units.append((S_DQ + h * 128, 128, dqT[:, h, tk0:tk0 + W], "dqT", "dq"))
            for h in range(4):
                units.append((S_DK + h * 128, 128, dkT[:, h, tk0:tk0 + W], "dkT", "dk"))
            for n in range(8):
                units.append((S_NQ + n * 64, 64, Qg[n // 4][0:64, n % 4, tk0:tk0 + W], "Qq%d" % (n // 4), "nq"))
            for g in range(2):
                units.append((S_KS + g * 64, 64, KSg[g][0:64, tk0:tk0 + W], "KSq%d" % g, "ks"))
            for g in range(2):
                units.append((S_KW + g * 64, 64, KWg[g][0:64, tk0:tk0 + W], "KWq%d" % g, "kw"))
            units.append((S_KC, 128, kcT[:, tk0:tk0 + W], "kcT", "kcvc"))
            units.append((S_VC, 128, vcT[:, tk0:tk0 + W], "vcT", "kcvc"))
            units = units[0:16] if PH[0] == 0 else units[16:22]
            if CUT == 9:
                pass
            elif CUT == 4.23:
                units = [(S_NQ + g * 64, 64, KSg[g][0:64, tk0:tk0 + W], "KSq%d" % g, "nq") for g in range(2)]
            elif CUT == 4.24:
                units = [(S_KS + g * 64, 64, Qg[g][0:64, 0, tk0:tk0 + W], "Qq%d" % g, "ks") for g in range(2)]
            elif CUT in (4.21, 4.26, 4.28):
                units = units[16:18]
            elif CUT == 4.22:
                units = units[18:20]
            elif CUT < 4.1:
                units = units[0:8]
            elif CUT < 4.2:
                units = units[0:16]
            elif CUT < 4.3:
                units = units[0:20]
            for u0 in range(0, len(units), 4):
                bank = tqi[0] % 2
                tqi[0] += 1
                grp = units[u0:u0 + 4]
                for ui, (so, ncol, dstap, dres, skey) in enumerate(grp):
                    for j in range(TB):
                        P.op("pe", lambda e, ui=ui, j=j, so=so, ncol=ncol, bank=bank: e.transpose(
                            tq[bank][0:ncol, ui * W + j * 128:ui * W + (j + 1) * 128],
                            stg[:, j, so:so + ncol], ident[:]),
                             r=stres[skey][j] + ["ident"], w=["tq%d" % bank])
                for ui, (so, ncol, dstap, dres, skey) in enumerate(grp):
                    eng = "act" if (evi[0] % 2 == 0 or EVAC_ACT) else "dve"
                    evi[0] += 1
                    if eng == "act":
                        P.op("act", lambda e, ui=ui, ncol=ncol, dstap=dstap, bank=bank: e.activation(
                            out=dstap, in_=tq[bank][0:ncol, ui * W:(ui + 1) * W], func=AF.Copy),
                             r=["tq%d" % bank], w=["%s_%d" % (dres, i)])
                    else:
                        P.op("dve", lambda e, ui=ui, ncol=ncol, dstap=dstap, bank=bank: e.tensor_copy(
                            out=dstap, in_=tq[bank][0:ncol, ui * W:(ui + 1) * W]),
                             r=["tq%d" % bank], w=["%s_%d" % (dres, i)])

        nblk = NT // TB
        for ph in range(2):
            PH[0] = ph
            load_pass(ph)
            for j in range(TB):
                emit_N(0, j)
            emit_T(0)
            for i in range(nblk):
                stres = {k: [None] * TB for k in ("dq", "dk", "nq", "kcvc", "ks", "kw")}
                for j in range(TB):
                    emit_proj(i, j, stres)
                    if i + 1 < nblk:
                        emit_N(i + 1, j)
                emit_QT(i, stres)
                if i + 1 < nblk:
                    emit_T(i + 1)
        P.barrier_all()
        P.flush()


def mixer_compress(nc, P, b, dd, t):
    tag = "Z%d" % b
    kcT, vcT, kcc, vca = t["kcT"], t["vcT"], t["kcc"], t["vca"]
    NCB = 127
    with ExitStack() as st:
        def sb(name, shape, dt):
            return st.enter_context(nc.sbuf_tensor(tag + name, shape, dt))

        def ps(name, shape, dt):
            return st.enter_context(nc.psum_tensor(tag + name, shape, dt))

        W1 = [sb("W1_%d" % kv, [128, 32, 256], BF16) for kv in range(2)]
        W2 = [sb("W2_%d" % kv, [128, 2, 64], BF16) for kv in range(2)]
        peT = [sb("peT%d" % kv, [128, 32], BF16) for kv in range(2)]
        pb = sb("pb", [128, 4], F32)
        xh = sb("xh", [128, 2, 128], F32)
        u = sb("u", [128, 2, 128], F32)
        sg = sb("sg", [128, 2, 128], F32)
        hact = sb("hact", [128, 2, 128], BF16)
        pbps = ps("pbps", [128, 4], F32)
        hps = [ps("hps%d" % i, [128, 2, 128], F32) for i in range(2)]
        cps = ps("cps", [128, 128], F32)

        for kv, (w1n, w2n, pen) in enumerate((("ck_w1", "ck_w2", "ck_peT"), ("cv_w1", "cv_w2", "cv_peT"))):
            w1v = dd[w1n].rearrange("(l d) h -> d l h", d=64)
            for half in range(2):
                P.dma("pool", lambda e, kv=kv, half=half, w1v=w1v: e.dma_start(
                    out=W1[kv][half * 64:(half + 1) * 64, :, :], in_=w1v),
                      "%sW1_%d" % (tag, kv), w=["W1_%d" % kv])
            P.dma("pool", lambda e, kv=kv, w2n=w2n: e.dma_start(
                out=W2[kv][:], in_=dd[w2n].rearrange("(c p) d -> p c d", p=128)),
                  "%sW2_%d" % (tag, kv), w=["W2_%d" % kv])
            P.dma("pool", lambda e, kv=kv, pen=pen: e.dma_start(out=peT[kv][:], in_=dd[pen]),
                  "%spe_%d" % (tag, kv), w=["peT%d" % kv])
        for kv in range(2):
            for ch in range(2):
                col = kv * 2 + ch
                for l in range(32):
                    P.op("pe", lambda e, kv=kv, ch=ch, l=l, col=col: e.matmul(
                        pbps[:, col:col + 1], W1[kv][0:64, l, ch * 128:(ch + 1) * 128], peT[kv][0:64, l:l + 1],
                        start=(l == 0), stop=(l == 31)),
                         r=["W1_%d" % kv, "peT%d" % kv], w=["pbps"])
        P.op("dve", lambda e: e.tensor_copy(out=pb[:], in_=pbps[:]), r=["pbps"], w=["pb"])
        hi = [0]
        for kv in range(2):
            xT = kcT if kv == 0 else vcT
            for g in range(2):
                hp = hps[hi[0] % 2]
                hpn = "hps%d" % (hi[0] % 2)
                hi[0] += 1
                for ch in range(2):
                    for l in range(32):
                        P.op("pe", lambda e, kv=kv, g=g, ch=ch, l=l, hp=hp, xT=xT: e.matmul(
                            hp[:, ch, 0:NCB], W1[kv][g * 64:(g + 1) * 64, l, ch * 128:(ch + 1) * 128],
                            xT[g * 64:(g + 1) * 64, l:l + 16 * (NCB - 1) + 1:16],
                            start=(l == 0), stop=(l == 31)),
                             r=["W1_%d" % kv], w=[hpn])
                for ch in range(2):
                    col = kv * 2 + ch
                    P.op("act", lambda e, ch=ch, col=col, hp=hp: e.activation(
                        out=xh[:, ch, 0:NCB], in_=hp[:, ch, 0:NCB], func=AF.Identity, bias=pb[:, col:col + 1]),
                         r=[hpn, "pb"], w=["xh%d" % ch])
                X, U, SG, HA = xh[:, :, 0:NCB], u[:, :, 0:NCB], sg[:, :, 0:NCB], hact[:, :, 0:NCB]
                P.op("dve", lambda e, X=X, U=U: e.tensor_tensor(out=U, in0=X, in1=X, op=ALU.mult),
                     r=["xh0", "xh1"], w=["u"])
                P.op("dve", lambda e, U=U: e.tensor_scalar(out=U, in0=U, scalar1=0.044715, scalar2=1.0,
                                                        op0=ALU.mult, op1=ALU.add), r=["u"], w=["u"])
                P.op("dve", lambda e, X=X, U=U: e.tensor_tensor(out=U, in0=U, in1=X, op=ALU.mult),
                     r=["u", "xh0", "xh1"], w=["u"])
                P.op("act", lambda e, U=U, SG=SG: e.activation(out=SG, in_=U, func=AF.Sigmoid,
                                                              scale=1.5957691216057308),
                     r=["u"], w=["sg"])
                P.op("dve", lambda e, X=X, SG=SG, HA=HA: e.tensor_tensor(out=HA, in0=X, in1=SG, op=ALU.mult),
                     r=["sg", "xh0", "xh1"], w=["hact"])
                if kv == 0:
                    for ch in range(2):
                        P.op("pe", lambda e, ch=ch: e.matmul(cps[0:64, 0:NCB], W2[0][:, ch, :], hact[:, ch, 0:NCB],
                                                            start=(ch == 0), stop=(ch == 1)),
                             r=["hact", "W2_0"], w=["cps"])
                    P.op("act", lambda e, g=g: e.activation(out=kcc[g][0:64, 0:NCB], in_=cps[0:64, 0:NCB],
                                                           func=AF.Copy), r=["cps"], w=["kcc%d" % g])
                else:
                    for ch in range(2):
                        P.op("pe", lambda e, ch=ch: e.matmul(cps[0:NCB, 0:64], hact[:, ch, 0:NCB], W2[1][:, ch, :],
                                                            start=(ch == 0), stop=(ch == 1)),
                             r=["hact", "W2_1"], w=["cps"])
                    P.op("act", lambda e, g=g: e.activation(out=vca[0:NCB, g, 0:64], in_=cps[0:NCB, 0:64],
                                                           func=AF.Copy), r=["cps"], w=["vca_v%d" % g])
        P.barrier_all()
        P.flush()


class Pipe:
    def __init__(self, lag=2):
        self.q, self.lag = [], lag

    def push(self, first, rest):
        if first is not None:
            first()
        self.q.append(rest)
        while len(self.q) > self.lag:
            self.q.pop(0)()

    def drain(self):
        while self.q:
            self.q.pop(0)()


def mixer_attn(nc, P, b, src, dst, dd, t):
    tag = "T%d" % b
    dqT, dkT, Qg, KSg, KWg = t["dqT"], t["dkT"], t["Qg"], t["KSg"], t["KWg"]
    dva, vsa, vwa, glr, kcc, vca, ident, epsb = (t["dva"], t["vsa"], t["vwa"], t["glr"], t["kcc"], t["vca"],
                                                 t["ident"], t["epsb"])
    with ExitStack() as st:
        def sb(name, shape, dt):
            return st.enter_context(nc.sbuf_tensor(tag + name, shape, dt))

        def ps(name, shape, dt):
            return st.enter_context(nc.psum_tensor(tag + name, shape, dt))

        om = sb("om", [128, NT, 1024], BF16)
        tmpf = [sb("tmpf%d" % i, [128, 64], F32) for i in range(2)]
        Wo = sb("Wo", [128, 8, 1024], BF16)
        pts = [sb("p%d" % i, [128, 512], BF16) for i in range(4)]
        gates = sb("gates", [128, NT, 24], F32)
        Am = sb("Am", [128, NT, 32], F32)
        Bm = sb("Bm", [128, NT, 32], F32)
        Gs = sb("Gs", [128, 128], F32)
        Gm = sb("Gm", [128, D], F32)
        lv = [sb("lv%d" % i, [128, 64], F32) for i in range(4)]
        lt = sb("lt", [128, 64], F32)
        le = sb("le", [128, 2], F32)
        neglam = sb("neglam", [128, 1], F32)
        o1 = [sb("o1_%d" % i, [128, 4, 128], F32) for i in range(2)]
        junk = sb("junk", [128, 128], F32)
        sm = [dict((n, sb("%s_%d" % (n, i), [128, w], F32)) for n, w in
                   (("rd", 4), ("nl", 4), ("ss4", 4), ("ln4", 4), ("r4", 4), ("den", 4), ("gr", 4), ("rs", 4),
                    ("rw", 4), ("imp", 32), ("impm", 32), ("wk", 32), ("m1", 8), ("m2", 8))) for i in range(2)]
        nsp = [sb("nsp%d" % i, [128, 96], BF16) for i in range(2)]
        omT = [sb("omT%d" % i, [128, 8, 128], BF16) for i in range(2)]
        xres = [sb("xres%d" % i, [128, D], F32) for i in range(2)]
        ytmp = [sb("ytmp%d" % i, [128, D], F32) for i in range(2)]
        ss2 = sb("ss2", [128, 2], F32)
        rstd2 = sb("rstd2", [128, 2], F32)
        sps = [ps("sps%d" % i, [128, 512], F32) for i in range(3)]
        ab = ps("ab", [128, 4, 512], F32)
        tps = ps("tps", [128, 1024], BF16)
        SPS = Rot(sps, ["sps%d" % i for i in range(3)])
        PT = Rot(pts, ["p%d" % i for i in range(4)])

        for nm, tl in (("Am", Am), ("Bm", Bm), ("Gs", Gs), ("Gm", Gm)):
            P.dma("sp", lambda e, nm=nm, tl=tl: e.dma_start(out=tl[:], in_=dd[nm]), tag + nm, w=[nm])
        for i, nm in enumerate(("lq1", "lk1", "lq2", "lk2")):
            P.dma("sp", lambda e, i=i, nm=nm: e.dma_start(out=lv[i][:], in_=dd[nm]), tag + nm, w=["lv%d" % i])
        wo_v = dd["w_out"].rearrange("(k p) f -> p k f", p=128)
        for kh in range(2):
            P.dma("pool", lambda e, kh=kh: e.dma_start(out=Wo[:, kh * 4:(kh + 1) * 4, :],
                                                       in_=wo_v[:, kh * 4:(kh + 1) * 4, :]),
                  tag + "Wo", w=["Wo"])
        for nm in ("nsp0", "nsp1"):
            pass
        P.op("pool", lambda e: e.memset(nsp[0][:], 0.0), w=["nsp0"])
        P.op("pool", lambda e: e.memset(nsp[1][:], 0.0), w=["nsp1"])
        P.op("dve", lambda e: e.tensor_scalar(out=Gs[:], in0=Gs[:], scalar1=0.8, scalar2=None, op0=ALU.mult),
             r=["Gs"], w=["Gs"])
        P.op("act", lambda e: e.activation(out=gates[:], in_=glr[:], func=AF.Sigmoid), w=["gates"])
        for i in range(2):
            P.op("dve", lambda e, i=i: e.tensor_tensor(out=lt[:], in0=lv[2 * i][:], in1=lv[2 * i + 1][:],
                                                       op=ALU.mult),
                 r=["lv%d" % (2 * i), "lv%d" % (2 * i + 1)], w=["lt"])
            P.op("dve", lambda e, i=i: e.reduce_sum(out=le[:, i:i + 1], in_=lt[:], axis=AX.X),
                 r=["lt"], w=["le%d" % i])
        P.op("act", lambda e: e.activation(out=le[:], in_=le[:], func=AF.Exp), r=["le0", "le1"], w=["le0", "le1"])
        P.op("dve", lambda e: e.tensor_tensor(out=neglam[:], in0=le[:, 1:2], in1=le[:, 0:1], op=ALU.subtract),
             r=["le0", "le1"], w=["neglam"])
        P.op("dve", lambda e: e.tensor_scalar(out=neglam[:], in0=neglam[:], scalar1=-0.2, scalar2=None,
                                              op0=ALU.add), r=["neglam"], w=["neglam"])

        pipe = Pipe(2)
        first_in_bank = {}

        def acc_mm(bank_i, col0, ncol, lhsT, rhs, last, reads):
            bn = "ab%d" % bank_i
            first = first_in_bank.get(bn, True)
            first_in_bank[bn] = False
            P.op("pe", lambda e: e.matmul(ab[:, bank_i, col0:col0 + ncol], lhsT, rhs, start=first, stop=last,
                                          skip_group_check=True),
                 r=reads, w=[bn])

        def exp_tile(spt, spn, rows, masks):
            p, pn = PT.next()
            P.op("act", lambda e: e.activation(out=p[0:rows, :], in_=spt[0:rows, :], func=AF.Exp, scale=0.125),
                 r=[spn], w=[pn])
            for (pattern, base, cm) in masks:
                P.op("pool", lambda e, pattern=pattern, base=base, cm=cm: e.affine_select(
                    out=p[0:rows, :], in_=p[0:rows, :], pattern=pattern, compare_op=ALU.is_ge, fill=0.0,
                    base=base, channel_multiplier=cm), r=[pn], w=[pn])
            return p, pn

        ci = 0
        for h in range(4):
            for qb in range(4):
                ob, obn = o1[(h * 4 + qb) % 2], "o1_%d" % ((h * 4 + qb) % 2)
                smx = sm[(h * 4 + qb) % 2]
                sfx = "_%d" % ((h * 4 + qb) % 2)
                for m in range(2):
                    bA, bB = 2 * (ci % 2), 2 * (ci % 2) + 1
                    ci += 1
                    first_in_bank["ab%d" % bA] = True
                    first_in_bank["ab%d" % bB] = True
                    nkt = 4 * qb + 4
                    for kt in range(nkt):
                        spt, spn = SPS.next()

                        def qk(spt=spt, spn=spn, kt=kt, m=m, h=h, qb=qb):
                            P.op("pe", lambda e: e.matmul(
                                spt[:], dkT[m * 64:(m + 1) * 64, h, kt * 128:(kt + 1) * 128],
                                dqT[m * 64:(m + 1) * 64, h, qb * 512:(qb + 1) * 512], start=True, stop=True),
                                 w=[spn])

                        def rest(spt=spt, spn=spn, kt=kt, h=h, qb=qb, bA=bA, bB=bB):
                            masks = []
                            if kt >= 4 * qb:
                                masks.append(([[1, 512]], qb * 512 - kt * 128, -1))
                            p, pn = exp_tile(spt, spn, 128, masks)
                            for jq in range(4):
                                if kt > 4 * qb + jq:
                                    continue
                                bank_i, col0 = (bA, jq * 130) if jq < 3 else (bB, 0)
                                acc_mm(bank_i, col0, 129, p[:, jq * 128:(jq + 1) * 128], dva[:, kt, h, 0:129],
                                       kt == 4 * qb + jq, [pn])

                        pipe.push(qk, rest)

                    def post(m=m, h=h, qb=qb, bA=bA, bB=bB, ob=ob, obn=obn, smx=smx, sfx=sfx):
                        bnA, bnB = "ab%d" % bA, "ab%d" % bB
                        rd = smx["rd"]
                        denA = ab[:, bA, 0:390].rearrange("p (j c) -> p j c", c=130)[:, :, 128]
                        P.op("dve", lambda e: e.reciprocal(out=rd[:, 0:3], in_=denA), r=[bnA], w=["rdA" + sfx])
                        P.op("dve", lambda e: e.reciprocal(out=rd[:, 3:4], in_=ab[:, bB, 128:129]), r=[bnB],
                             w=["rdB" + sfx])

                        def region(jq):
                            return (ab[:, bA, jq * 130:jq * 130 + 128], bnA) if jq < 3 else (ab[:, bB, 0:128], bnB)

                        if m == 0:
                            for jq in range(4):
                                reg, bn = region(jq)
                                P.op("act", lambda e, reg=reg, jq=jq: e.activation(
                                    out=ob[:, jq, :], in_=reg, func=AF.Copy, scale=rd[:, jq:jq + 1]),
                                     r=[bn, "rdA" + sfx, "rdB" + sfx], w=[obn + "_%d" % jq])
                        else:
                            nl, ss4, ln4, r4 = smx["nl"], smx["ss4"], smx["ln4"], smx["r4"]
                            P.op("dve", lambda e: e.tensor_scalar(out=nl[:], in0=rd[:], scalar1=neglam[:, 0:1],
                                                                  scalar2=None, op0=ALU.mult),
                                 r=["rdA" + sfx, "rdB" + sfx, "neglam"], w=["nl" + sfx])
                            for jq in range(4):
                                reg, bn = region(jq)
                                P.op("dve", lambda e, reg=reg, jq=jq: e.scalar_tensor_tensor(
                                    out=ob[:, jq, :], in0=reg, scalar=nl[:, jq:jq + 1], in1=ob[:, jq, :],
                                    op0=ALU.mult, op1=ALU.add),
                                     r=[bn, "nl" + sfx, obn + "_%d" % jq], w=[obn + "_%d" % jq])
                                P.op("act", lambda e, jq=jq: e.activation(
                                    out=junk[:], in_=ob[:, jq, :], func=AF.Square, accum_out=ss4[:, jq:jq + 1]),
                                     r=[obn + "_%d" % jq], w=["junk", "ss4%s_%d" % (sfx, jq)])
                            ssr = ["ss4%s_%d" % (sfx, jq) for jq in range(4)]
                            P.op("act", lambda e: e.activation(out=ln4[:], in_=ss4[:], func=AF.Ln, scale=1.0 / 128,
                                                               bias=epsb[:, 0:1]), r=ssr + ["epsb"], w=["ln4" + sfx])
                            P.op("act", lambda e: e.activation(out=r4[:], in_=ln4[:], func=AF.Exp, scale=-0.5),
                                 r=["ln4" + sfx], w=["r4" + sfx])
                            for jq in range(4):
                                T = 4 * qb + jq
                                P.op("dve", lambda e, jq=jq, T=T: e.scalar_tensor_tensor(
                                    out=om[:, T, h * 128:(h + 1) * 128], in0=ob[:, jq, :], scalar=r4[:, jq:jq + 1],
                                    in1=Gs[:], op0=ALU.mult, op1=ALU.mult),
                                     r=[obn + "_%d" % jq, "r4" + sfx, "Gs"], w=["om%d_d%d" % (T, h)])

                    pipe.push(None, post)
        pipe.drain()

        ci = 0
        for g in range(2):
            for T in range(NT):
                bk = ci % 4
                smx = sm[ci % 2]
                sfx = "_%d" % (ci % 2)
                nspt, nspn = nsp[ci % 2], "nsp%d" % (ci % 2)
                ci += 1
                spt, spn = SPS.next()

                def qk(spt=spt, spn=spn, g=g, T=T):
                    P.op("pe", lambda e: e.matmul(spt[0:127, :], kcc[g][0:64, 0:127],
                                                  Qg[g][0:64, :, T * 128:(T + 1) * 128], start=True, stop=True),
                         w=[spn])

                def rest(spt=spt, spn=spn, g=g, T=T, bk=bk):
                    p, pn = exp_tile(spt, spn, 127, [([[0, 4], [1, 128]], T * 128 - 31, -16)])
                    for hg in range(4):
                        P.op("pe", lambda e, hg=hg: e.matmul(ab[:, bk, hg * 128:hg * 128 + 97],
                                                             p[0:127, hg * 128:(hg + 1) * 128], vca[0:127, g, 0:97],
                                                             start=True, stop=True, skip_group_check=True),
                             r=[pn], w=["ab%d" % bk])

                def post(g=g, T=T, bk=bk, smx=smx, sfx=sfx, nspt=nspt, nspn=nspn):
                    bn = "ab%d" % bk
                    den, rd, gr, imp, impm, wk, m1, m2 = (smx["den"], smx["rd"], smx["gr"], smx["imp"],
                                                          smx["impm"], smx["wk"], smx["m1"], smx["m2"])
                    ov4 = ab[:, bk, :].rearrange("p (h c) -> p h c", c=128)
                    gv = gates[:, T, :].rearrange("p (h c) -> p h c", c=3)
                    P.op("dve", lambda e: e.tensor_scalar(out=den[:], in0=ov4[:, :, 64], scalar1=1e-30, scalar2=None,
                                                          op0=ALU.max), r=[bn], w=["den" + sfx])
                    P.op("dve", lambda e: e.reciprocal(out=rd[:], in_=den[:]), r=["den" + sfx], w=["rd" + sfx])
                    P.op("dve", lambda e: e.tensor_tensor(out=gr[:], in0=rd[:], in1=gv[:, g * 4:(g + 1) * 4, 0],
                                                          op=ALU.mult), r=["rd" + sfx, "gates"], w=["gr" + sfx])
                    for hg in range(4):
                        hd = g * 4 + hg
                        P.op("act", lambda e, hg=hg, hd=hd: e.activation(
                            out=om[:, T, 512 + hd * 64:512 + (hd + 1) * 64], in_=ab[:, bk, hg * 128:hg * 128 + 64],
                            func=AF.Copy, scale=gr[:, hg:hg + 1]), r=[bn, "gr" + sfx], w=["om%d_n%d" % (T, hd)])
                    P.op("dve", lambda e: e.tensor_scalar(out=imp[:], in0=ab[:, bk, 65:97], scalar1=rd[:, 0:1],
                                                          scalar2=None, op0=ALU.mult),
                         r=[bn, "rd" + sfx], w=["imp" + sfx])
                    for hg in range(1, 4):
                        P.op("dve", lambda e, hg=hg: e.scalar_tensor_tensor(
                            out=imp[:], in0=ab[:, bk, hg * 128 + 65:hg * 128 + 97], scalar=rd[:, hg:hg + 1],
                            in1=imp[:], op0=ALU.mult, op1=ALU.add), r=[bn, "rd" + sfx, "imp" + sfx],
                             w=["imp" + sfx])
                    P.op("dve", lambda e: e.tensor_tensor(out=impm[:], in0=imp[:], in1=Am[:, T, :], op=ALU.mult),
                         r=["imp" + sfx, "Am"], w=["impm" + sfx])
                    P.op("dve", lambda e: e.tensor_tensor(out=impm[:], in0=impm[:], in1=Bm[:, T, :], op=ALU.add),
                         r=["impm" + sfx, "Bm"], w=["impm" + sfx])
                    P.op("dve", lambda e: e.max(out=m1[:], in_=impm[:]), r=["impm" + sfx], w=["m1" + sfx])
                    P.op("dve", lambda e: e.match_replace(out=wk[:], in_to_replace=m1[:], in_values=impm[:],
                                                          imm_value=-2.0),
                         r=["impm" + sfx, "m1" + sfx], w=["wk" + sfx])
                    P.op("dve", lambda e: e.max(out=m2[:], in_=wk[:]), r=["wk" + sfx], w=["m2" + sfx])
                    P.op("dve", lambda e: e.tensor_scalar(out=nspt[:, 64:96], in0=impm[:], scalar1=m2[:, 7:8],
                                                          scalar2=-BIG, op0=ALU.is_lt, op1=ALU.mult),
                         r=["impm" + sfx, "m2" + sfx], w=[nspn])
                    P.op("pe", lambda e: e.transpose(tps[0:96, 0:128], nspt[:, 0:96], ident[:]),
                         r=[nspn, "ident"], w=["tps"])
                    tsl = slice(T * 128, (T + 1) * 128)
                    P.op("act", lambda e: e.activation(out=Qg[g][64:96, 0, tsl], in_=tps[64:96, 0:128],
                                                       func=AF.Copy), r=["tps"], w=["Qs%d_%d" % (g, T)])
                    for hg in range(1, 4):
                        P.op("pool", lambda e, hg=hg: e.tensor_copy(out=Qg[g][64:96, hg, tsl],
                                                                    in_=Qg[g][64:96, 0, tsl]),
                             r=["Qs%d_%d" % (g, T)], w=["Qs%d_%d_%d" % (g, T, hg)])

                pipe.push(qk, rest)
                pipe.push(None, post)
        pipe.drain()

        ci = 0
        for g in range(2):
            for T in range(NT):
                bS, bW = 2 * (ci % 2), 2 * (ci % 2) + 1
                smx = sm[ci % 2]
                sfx = "_%d" % (ci % 2)
                ci += 1
                first_in_bank["ab%d" % bS] = True
                first_in_bank["ab%d" % bW] = True
                selr = ["Qs%d_%d" % (g, T)] + ["Qs%d_%d_%d" % (g, T, hg) for hg in range(1, 4)]
                jobs = [("s", kt) for kt in range(T + 1)] + [("w", kt) for kt in range(max(0, T - 4), T + 1)]
                for kind, kt in jobs:
                    spt, spn = SPS.next()

                    def qk(spt=spt, spn=spn, kind=kind, kt=kt, g=g, T=T, selr=selr):
                        ksl = slice(kt * 128, (kt + 1) * 128)
                        tsl = slice(T * 128, (T + 1) * 128)
                        if kind == "s":
                            P.op("pe", lambda e: e.matmul(spt[:], KSg[g][0:96, ksl], Qg[g][0:96, :, tsl],
                                                          start=True, stop=True), r=selr, w=[spn])
                        else:
                            P.op("pe", lambda e: e.matmul(spt[:], KWg[g][0:64, ksl], Qg[g][0:64, :, tsl],
                                                          start=True, stop=True), w=[spn])

                    def rest(spt=spt, spn=spn, kind=kind, kt=kt, g=g, T=T, bS=bS, bW=bW):
                        masks = []
                        if kt == T:
                            masks.append(([[0, 4], [1, 128]], 0, -1))
                        if kind == "w" and kt == T - 4:
                            masks.append(([[0, 4], [-1, 128]], -1, 1))
                        p, pn = exp_tile(spt, spn, 128, masks)
                        va = vsa if kind == "s" else vwa
                        bank_i = bS if kind == "s" else bW
                        for hg in range(4):
                            acc_mm(bank_i, hg * 128, 65, p[:, hg * 128:(hg + 1) * 128], va[:, kt, g, 0:65],
                                   kt == T, [pn])

                    pipe.push(qk, rest)

                def post(g=g, T=T, bS=bS, bW=bW, smx=smx, sfx=sfx):
                    bnS, bnW = "ab%d" % bS, "ab%d" % bW
                    rs, rw = smx["rs"], smx["rw"]
                    gv = gates[:, T, :].rearrange("p (h c) -> p h c", c=3)
                    oS = ab[:, bS, :].rearrange("p (h c) -> p h c", c=128)
                    oW = ab[:, bW, :].rearrange("p (h c) -> p h c", c=128)
                    P.op("dve", lambda e: e.reciprocal(out=rs[:], in_=oS[:, :, 64]), r=[bnS], w=["rs" + sfx])
                    P.op("dve", lambda e: e.tensor_tensor(out=rs[:], in0=rs[:], in1=gv[:, g * 4:(g + 1) * 4, 1],
                                                          op=ALU.mult), r=["rs" + sfx, "gates"], w=["rs" + sfx])
                    P.op("dve", lambda e: e.reciprocal(out=rw[:], in_=oW[:, :, 64]), r=[bnW], w=["rw" + sfx])
                    P.op("dve", lambda e: e.tensor_tensor(out=rw[:], in0=rw[:], in1=gv[:, g * 4:(g + 1) * 4, 2],
                                                          op=ALU.mult), r=["rw" + sfx, "gates"], w=["rw" + sfx])
                    for hg in range(4):
                        hd = g * 4 + hg
                        on = "om%d_n%d" % (T, hd)
                        osl = om[:, T, 512 + hd * 64:512 + (hd + 1) * 64]
                        tf, tfn = tmpf[hg % 2], "tmpf%d" % (hg % 2)
                        P.op("dve", lambda e, hg=hg, osl=osl, tf=tf: e.scalar_tensor_tensor(
                            out=tf[:], in0=ab[:, bS, hg * 128:hg * 128 + 64], scalar=rs[:, hg:hg + 1], in1=osl,
                            op0=ALU.mult, op1=ALU.add), r=[bnS, "rs" + sfx, on], w=[tfn])
                        P.op("dve", lambda e, hg=hg, osl=osl, tf=tf: e.scalar_tensor_tensor(
                            out=osl, in0=ab[:, bW, hg * 128:hg * 128 + 64], scalar=rw[:, hg:hg + 1], in1=tf[:],
                            op0=ALU.mult, op1=ALU.add), r=[bnW, "rw" + sfx, tfn], w=[on])

                pipe.push(None, post)
        pipe.drain()

        for T in range(NT):
            q = T % 2
            t0 = T * 128
            omr = ["om%d_d%d" % (T, h) for h in range(4)] + ["om%d_n%d" % (T, hd) for hd in range(8)]
            for k in range(8):
                P.op("pe", lambda e, k=k, T=T: e.transpose(tps[:, k * 128:(k + 1) * 128],
                                                          om[:, T, k * 128:(k + 1) * 128], ident[:]),
                     r=omr + ["ident"], w=["tps"])
            oT, oTn = omT[q], "omT%d" % q
            P.op("act", lambda e, oT=oT: e.activation(out=oT[:].rearrange("p k t -> p (k t)"), in_=tps[:],
                                                      func=AF.Copy), r=["tps"], w=[oTn])
            b0 = 2 * q
            for n in range(2):
                for k in range(8):
                    P.op("pe", lambda e, n=n, k=k, oT=oT, b0=b0: e.matmul(
                        ab[:, b0 + n, :], oT[:, k, :], Wo[:, k, n * 512:(n + 1) * 512], start=(k == 0),
                        stop=(k == 7)), r=[oTn, "Wo"], w=["ab%d" % (b0 + n)])
            xr, yt = xres[q], ytmp[q]
            xrn, ytn = "xres%d" % q, "ytmp%d" % q
            wo2 = ab[:, b0:b0 + 2, :]
            br = ["ab%d" % b0, "ab%d" % (b0 + 1)]
            P.dma("act", lambda e, xr=xr, t0=t0: e.dma_start(out=xr[:], in_=src[b, t0:t0 + 128, :]),
                  tag + xrn + "i", w=[xrn])
            P.op("act", lambda e, yt=yt, q=q, wo2=wo2: e.activation(
                out=yt[:].rearrange("p (n f) -> p n f", n=2), in_=wo2, func=AF.Square, accum_out=ss2[:, q:q + 1]),
                 r=br, w=[ytn, "ss2%d" % q])
            P.op("act", lambda e, q=q: e.activation(out=ss2[:, q:q + 1], in_=ss2[:, q:q + 1], func=AF.Sqrt,
                                                    scale=1.0 / D, bias=epsb[:, 0:1]),
                 r=["ss2%d" % q, "epsb"], w=["ss2%d" % q])
            P.op("dve", lambda e, q=q: e.reciprocal(out=rstd2[:, q:q + 1], in_=ss2[:, q:q + 1]),
                 r=["ss2%d" % q], w=["rstd2%d" % q])
            P.op("dve", lambda e, yt=yt, q=q, wo2=wo2: e.scalar_tensor_tensor(
                out=yt[:].rearrange("p (n f) -> p n f", n=2), in0=wo2, scalar=rstd2[:, q:q + 1],
                in1=Gm[:].rearrange("p (n f) -> p n f", n=2), op0=ALU.mult, op1=ALU.mult),
                 r=br + ["rstd2%d" % q, "Gm"], w=[ytn])
            P.op("pool", lambda e, xr=xr, yt=yt: e.tensor_tensor(out=xr[:], in0=xr[:], in1=yt[:], op=ALU.add),
                 r=[ytn, xrn], w=[xrn])
            P.dma("sp", lambda e, xr=xr, t0=t0: e.dma_start(out=dst[b, t0:t0 + 128, :], in_=xr[:]),
                  tag + xrn + "o", r=[xrn], w=["dst_B_%d_%d" % (b, t0)])
        P.barrier_all()
        P.flush()


def build(stage=3):
    nc = bass.Bass("TRN2", target_bir_lowering=False)

    def dt(n, s, d=F32, k="ExternalInput"):
        return nc.dram_tensor(n, s, d, kind=k).ap()

    x = dt("x", [NB, S, D])
    out = dt("out", [NB, S, D], F32, "ExternalOutput")
    ident_d = dt("ident", [128, 128], BF16)
    f = {}
    for t in ("f1", "f2"):
        f[t] = dict(wg=dt(t + "_wg", [D, DFF]), wu=dt(t + "_wu", [D, DFF]), wd=dt(t + "_wd", [DFF, D]),
                    gpre=dt(t + "_gpre", [128, 8]), gpost=dt(t + "_gpost", [128, D]))
    dd = dict(ident=ident_d,
              w_in=dt("w_in", [D, C_END]), w_out=dt("w_out", [D, D]), m_gpre=dt("m_gpre", [128, 8]),
              Gm=dt("Gm", [128, D]), Gs=dt("Gs", [128, 128]),
              cosR=dt("cosR", [128, NT, 8, 8]), sinR=dt("sinR", [128, NT, 8, 8]),
              cosR2=dt("cosR2", [128, NT, 8, 8]), sinR2=dt("sinR2", [128, NT, 8, 8]),
              ov=dt("ov", [127, 32], BF16), ET=dt("ET", [32, S], BF16),
              Am=dt("Am", [128, NT, 32]), Bm=dt("Bm", [128, NT, 32]),
              ck_w1=dt("ck_w1", [2048, 256]), ck_w2=dt("ck_w2", [256, 64]), ck_peT=dt("ck_peT", [128, 32]),
              cv_w1=dt("cv_w1", [2048, 256]), cv_w2=dt("cv_w2", [256, 64]), cv_peT=dt("cv_peT", [128, 32]),
              lq1=dt("lq1", [128, 64]), lk1=dt("lk1", [128, 64]), lq2=dt("lq2", [128, 64]),
              lk2=dt("lk2", [128, 64]))
    x1 = nc.dram_tensor("x1s", [NB, S, D], F32).ap()
    x2 = nc.dram_tensor("x2s", [NB, S, D], F32).ap()
    with ExitStack() as stack:
        P = Prog(nc, stack)
        fa = f["f1"]
        ffn_phase(nc, P, "A", x, out if stage == 1 else x1, fa["wg"], fa["wu"], fa["wd"], fa["gpre"],
                  fa["gpost"], ident_d)
        if stage >= 2:
            mixer_phase(nc, P, x1, out if stage == 2 else x2, dd)
        if stage >= 3:
            fc = f["f2"]
            ffn_phase(nc, P, "C", x2, out, fc["wg"], fc["wu"], fc["wd"], fc["gpre"], fc["gpost"], ident_d)
    return nc


def host_inputs(inp):
    def g(k):
        return np.ascontiguousarray(np.asarray(inp[k], dtype=np.float32))

    bf = ml_dtypes.bfloat16

    def bc(v, n=128):
        return np.ascontiguousarray(np.broadcast_to(v[None, :], (n, v.shape[0])))

    common = {"ident": np.eye(128, dtype=np.float32).astype(bf)}
    for t, pfx in (("f1", "ff1"), ("f2", "ff2")):
        common[t + "_wg"] = g(pfx + "_w_gate")[0]
        common[t + "_wu"] = g(pfx + "_w_up")[0]
        common[t + "_wd"] = g(pfx + "_w_down")[0]
        common[t + "_gpre"] = np.ascontiguousarray(g(pfx + "_norm_pre")[0].reshape(8, 128).T)
        common[t + "_gpost"] = bc(g(pfx + "_norm_post")[0])
    common["w_in"] = g("w_in")[0]
    common["w_out"] = g("w_out")[0]
    common["m_gpre"] = np.ascontiguousarray(g("mix_norm_pre")[0].reshape(8, 128).T)
    common["Gm"] = bc(g("mix_norm_post")[0])
    common["Gs"] = bc(g("diff_subln")[0])
    for k, n in (("lq1", "lambda_q1"), ("lk1", "lambda_k1"), ("lq2", "lambda_q2"), ("lk2", "lambda_k2")):
        common[k] = bc(g(n)[0])
    for kv, w1n, w2n, pen in (("ck", "cmp_k_w1", "cmp_k_w2", "cmp_pe_k"), ("cv", "cmp_v_w1", "cmp_v_w2", "cmp_pe_v")):
        common[kv + "_w1"] = g(w1n)[0]
        common[kv + "_w2"] = g(w2n)[0]
        peT = g(pen)[0].T
        common[kv + "_peT"] = np.ascontiguousarray(np.concatenate([peT, peT], axis=0))
    pos = np.arange(S, dtype=np.float32)
    inv = (np.float32(500000.0) ** (-np.arange(0, 16, 2, dtype=np.float32) / np.float32(16))).astype(np.float32)
    ang = (pos[:, None] * inv[None, :]).astype(np.float32)
    cs, sn = np.cos(ang).astype(np.float32), np.sin(ang).astype(np.float32)

    def tab(a):
        a = a.reshape(NT, 128, 8).transpose(1, 0, 2)
        return np.ascontiguousarray(np.broadcast_to(a[:, :, None, :], (128, NT, 8, 8)))

    common["cosR"], common["sinR"] = tab(cs), tab(sn)
    c2, s2 = tab(cs).copy(), tab(sn).copy()
    c2[:, :, 4:8, :] = 1.0
    s2[:, :, 4:8, :] = 0.0
    common["cosR2"], common["sinR2"] = c2, s2
    c = np.arange(127)[:, None] * 16
    j = np.arange(32)[None, :] * 64
    ov = np.clip(np.minimum(c + 32, j + 64) - np.maximum(c, j), 0, None) / 32.0
    common["ov"] = ov.astype(np.float32).astype(bf)
    common["ET"] = (np.arange(S)[None, :] // 64 == np.arange(32)[:, None]).astype(np.float32).astype(bf)
    tt = np.arange(S)
    cur = (tt // 64)[:, None]
    blk = np.arange(32)[None, :]
    forced = (blk == 0) | ((blk <= cur) & (blk >= cur - 1))
    causal = blk <= cur
    A = (~forced & causal).astype(np.float32)
    Bc = np.where(forced, np.float32(1e9), np.where(causal, np.float32(0.0), np.float32(-1.0))).astype(np.float32)
    common["Am"] = np.ascontiguousarray(A.reshape(NT, 128, 32).transpose(1, 0, 2))
    common["Bm"] = np.ascontiguousarray(Bc.reshape(NT, 128, 32).transpose(1, 0, 2))
    x = g("x")
    maps = []
    for c_ in range(8):
        m = dict(common)
        m["x"] = x[c_ * NB:(c_ + 1) * NB]
        maps.append(m)
    return maps


def kernel(**inputs):
    nc = build(3)
    maps = host_inputs(inputs)
    res = run_bass_kernel_spmd(nc, maps, core_ids=list(range(8)))
    return np.concatenate([np.asarray(r["out"]) for r in res.results], axis=0).astype(np.float32)
```

```python
import numpy as np
import ml_dtypes
from contextlib import ExitStack
import concourse.bass as bass
import concourse.mybir as mybir
from concourse.bass_utils import run_bass_kernel_spmd

F32 = mybir.dt.float32
BF16 = mybir.dt.bfloat16
AF = mybir.ActivationFunctionType
ALU = mybir.AluOpType
AX = mybir.AxisListType

S = 2048
D = 1024
DFF = 2816
NF = DFF // 128
NB = 2
EPS = 1e-6
ENGS = ("pe", "act", "dve", "pool", "sp")


import re
PSUM_RE = re.compile(r"^(tr\d|gu\d|dn\d|pj\d|tq\d|pbps|hps\d|cps|sps\d|ab\d|tps)")


class Res:
    __slots__ = ("name", "w", "r")

    def __init__(self, name):
        self.name = name
        self.w = None
        self.r = {}


class Op:
    __slots__ = ("eng", "fn", "deps", "kind", "key", "val", "idx", "sig", "waits", "ordinal")


class Prog:
    def __init__(self, nc, stack):
        self.nc = nc
        self.stack = stack
        self.res = {}
        self.ops = []
        self.eng_n = {e: 0 for e in ENGS}
        self.sigbase = {e: 0 for e in ENGS}
        self.esem = {e: stack.enter_context(nc.semaphore("sem_" + e)) for e in ENGS if e != "sp"}
        self.dsem = {}
        self.dcount = {}
        self.free_sems = {"sw": [], "hw": []}
        self.dcls = {}
        self.live = []
        self.seen = {e: {} for e in ENGS}
        self.sigord = {e: {} for e in ENGS}

    def R(self, name):
        r = self.res.get(name)
        if r is None:
            r = self.res[name] = Res(name)
        return r

    def _deps(self, reads, writes):
        deps = []
        for n in reads:
            r = self.R(n)
            if r.w is not None:
                deps.append(r.w)
        for n in writes:
            r = self.R(n)
            if r.w is not None:
                deps.append(r.w)
            deps.extend(r.r.values())
        return deps

    def _mark(self, reads, writes, ev, rkey):
        for n in reads:
            self.R(n).r[rkey] = ev
        for n in writes:
            r = self.R(n)
            r.w = ev
            r.r = {}

    def op(self, eng, fn, r=(), w=()):
        o = Op()
        o.eng = eng
        o.fn = fn
        o.kind = "c"
        o.deps = self._deps(r, w)
        for n in r:
            if PSUM_RE.match(n):
                o.deps.extend(ev for k, ev in self.R(n).r.items() if k != eng)
        o.idx = self.eng_n[eng]
        self.eng_n[eng] += 1
        o.sig = False
        self._mark(r, w, ("c", eng, o.idx), eng)
        self.ops.append(o)
        return o

    def dma(self, q, fn, key, r=(), w=()):
        o = Op()
        o.eng = q
        o.fn = fn
        o.kind = "d"
        o.key = key
        if key not in self.dsem:
            cls = "sw" if q == "pool" else "hw"
            self.dcls[key] = cls
            if self.free_sems[cls]:
                self.dsem[key], self.dcount[key] = self.free_sems[cls].pop()
            else:
                self.dsem[key] = self.stack.enter_context(self.nc.semaphore("dsem%d" % len(self.dsem)))
                self.dcount[key] = 0
            self.live.append(key)
        o.deps = self._deps(r, w)
        self.dcount[key] += 16
        o.val = self.dcount[key]
        o.idx = self.eng_n[q]
        self.eng_n[q] += 1
        o.sig = False
        self._mark(r, w, ("d", key, o.val), "d_" + key)
        self.ops.append(o)
        return o

    def barrier_all(self):
        allres = list(self.res.keys())
        for e in ENGS:
            self.op(e, None, r=allres)
        self.res = {}
        for k in self.live:
            self.free_sems[self.dcls[k]].append((self.dsem[k], self.dcount[k]))
        self.live = []

    def flush(self):
        nc = self.nc
        ops = self.ops
        self.ops = []
        self.nflush = getattr(self, "nflush", 0) + 1
        if LIMIT is not None and self.nflush == LIMIT[0]:
            ops = ops[:LIMIT[1]]
        byidx = {}
        for o in ops:
            if o.kind == "c":
                byidx[(o.eng, o.idx)] = o
        for o in ops:
            o.waits = []
            seen = self.seen[o.eng]
            best = {}
            for d in o.deps:
                if d[0] == "c":
                    if d[1] == "pe" and o.eng == "pe":
                        continue
                    k = ("c", d[1])
                else:
                    k = ("d", d[1])
                if k not in best or d[2] > best[k][2]:
                    best[k] = d
            for d in best.values():
                if d[0] == "c":
                    _, pe, pi = d
                    if seen.get(pe, -1) >= pi:
                        continue
                    prod = byidx.get((pe, pi))
                    if prod is None:
                        assert pi in self.sigord[pe], (pe, pi)
                    else:
                        prod.sig = True
                    seen[pe] = pi
                    o.waits.append(d)
                else:
                    _, key, val = d
                    k = "d_" + key
                    if seen.get(k, 0) >= val:
                        continue
                    seen[k] = val
                    o.waits.append(d)
        last = {}
        for o in ops:
            if o.kind == "c":
                last[o.eng] = o
        for o in last.values():
            o.sig = True
        for o in ops:
            if o.kind == "c" and o.sig:
                self.sigbase[o.eng] += 1
                self.sigord[o.eng][o.idx] = self.sigbase[o.eng]
        per = {e: [o for o in ops if o.eng == e] for e in ENGS}

        def emit(eng_name, eng):
            for o in per[eng_name]:
                for d in o.waits:
                    if d[0] == "c":
                        eng.wait_ge(self.esem[d[1]], self.sigord[d[1]][d[2]])
                    else:
                        eng.wait_ge(self.dsem[d[1]], d[2])
                if o.fn is None:
                    if o.kind == "c" and o.sig:
                        eng.nop().then_inc(self.esem[eng_name], 1) if eng_name != "sp" else None
                    continue
                ins = o.fn(eng)
                if o.kind == "d":
                    ins.then_inc(self.dsem[o.key], 16)
                elif o.sig:
                    ins.then_inc(self.esem[eng_name], 1)

        with nc.Block() as block:
            @block.tensor
            def _(e):
                emit("pe", e)

            @block.scalar
            def _(e):
                emit("act", e)

            @block.vector
            def _(e):
                emit("dve", e)

            @block.gpsimd
            def _(e):
                emit("pool", e)

            @block.sync
            def _(e):
                emit("sp", e)


def ffn_phase(nc, P, tag, src, dst, wg_d, wu_d, wd_d, gpre_d, gpost_d, ident_d):
    with ExitStack() as st:
        def sb(name, shape, dt):
            return st.enter_context(nc.sbuf_tensor(tag + name, shape, dt))

        def ps(name, shape, dt):
            return st.enter_context(nc.psum_tensor(tag + name, shape, dt))

        Wg = sb("Wg", [128, 8, DFF], BF16)
        Wu = sb("Wu", [128, 8, DFF], BF16)
        Wd = sb("Wd", [128, NF, D], BF16)
        xin = [sb("xin%d" % i, [128, D], F32) for i in range(2)]
        hn = sb("hn", [128, 4, D], BF16)
        hT = sb("hT", [128, 8, 512], BF16)
        actT = sb("actT", [128, NF, 512], BF16)
        G = sb("G", [128, D], F32)
        gpre = sb("gpre", [128, 8], F32)
        ident = sb("ident", [128, 128], BF16)
        xres = [sb("xres%d" % i, [128, D], F32) for i in range(2)]
        ytmp = [sb("ytmp%d" % i, [128, D], F32) for i in range(2)]
        sg = [sb("sg%d" % i, [128, 512], F32) for i in range(2)]
        ss = sb("ss", [128, 8], F32)
        rstd = sb("rstd", [128, 8], F32)
        ss2 = sb("ss2", [128, 2], F32)
        rstd2 = sb("rstd2", [128, 2], F32)
        epsb = sb("epsb", [128, 1], F32)
        P.op("pool", lambda e: e.memset(epsb[:], EPS), w=["epsb"])
        tr = [ps("tr%d" % i, [128, 2, 512], BF16) for i in range(2)]
        gu = [ps("gu%d" % i, [128, 512], F32) for i in range(4)]
        dn = ps("dn", [128, D], F32)

        P.dma("sp", lambda e: e.dma_start(out=gpre[:], in_=gpre_d), tag + "c0", w=["gpre"])
        P.dma("sp", lambda e: e.dma_start(out=G[:], in_=gpost_d), tag + "c1", w=["G"])
        P.dma("sp", lambda e: e.dma_start(out=ident[:], in_=ident_d), tag + "c2", w=["ident"])
        P.op("dve", lambda e: e.tensor_scalar(out=G[:], in0=G[:], scalar1=0.5, scalar2=None, op0=ALU.mult),
             r=["G"], w=["G"])
        wg_v = wg_d.rearrange("(k p) f -> p k f", p=128)
        wu_v = wu_d.rearrange("(k p) f -> p k f", p=128)
        wd_v = wd_d.rearrange("(f p) d -> p f d", p=128)
        FG = [(0, 2), (2, 6), (6, 10), (10, 14), (14, 18), (18, 22)]
        fgrp = {}
        for gi, (a, b) in enumerate(FG):
            for f in range(a, b):
                fgrp[f] = gi
            for nm, W, v in (("Wg", Wg, wg_v), ("Wu", Wu, wu_v)):
                for kh in range(2):
                    P.dma("pool",
                          lambda e, W=W, v=v, a=a, b=b, kh=kh: e.dma_start(
                              out=W[:, kh * 4:(kh + 1) * 4, a * 128:b * 128],
                              in_=v[:, kh * 4:(kh + 1) * 4, a * 128:b * 128]),
                          "%s%s%d" % (tag, nm, gi), w=["%s%d" % (nm, gi)])
        for gi, (a, b) in enumerate(FG):
            P.dma("pool", lambda e, a=a, b=b: e.dma_start(out=Wd[:, a:b, :], in_=wd_v[:, a:b, :]),
                  "%sWd%d" % (tag, gi), w=["Wd%d" % gi])

        blocks = [(b, i) for b in range(NB) for i in range(4)]
        nxi = [0]

        def emit_N(bi, j):
            b, i = blocks[bi]
            t0 = i * 512 + j * 128
            xb = xin[nxi[0] % 2]
            xn = "xin%d" % (nxi[0] % 2)
            nxi[0] += 1
            P.dma("sp", lambda e: e.dma_start(out=xb[:], in_=src[b, t0:t0 + 128, :]), tag + xn, w=[xn])
            P.op("act", lambda e: e.activation(out=hn[:, j, :], in_=xb[:], func=AF.Square,
                                               accum_out=ss[:, j:j + 1]),
                 r=[xn], w=["hn%d" % j, "ss%d" % j])
            P.op("act", lambda e: e.activation(out=ss[:, j:j + 1], in_=ss[:, j:j + 1], func=AF.Sqrt,
                                               scale=1.0 / D, bias=epsb[:, 0:1]),
                 r=["ss%d" % j, "epsb"], w=["ss%d" % j])
            P.op("dve", lambda e: e.reciprocal(out=rstd[:, j:j + 1], in_=ss[:, j:j + 1]),
                 r=["ss%d" % j], w=["rstd%d" % j])
            P.op("act", lambda e: e.activation(out=hn[:, j, :], in_=xb[:], func=AF.Copy,
                                               scale=rstd[:, j:j + 1]),
                 r=[xn, "rstd%d" % j], w=["hn%d" % j])

        def emit_T(bi):
            for kp in range(4):
                bank = kp % 2
                for kk in range(2):
                    k = kp * 2 + kk
                    for j in range(4):
                        P.op("pe", lambda e, k=k, kk=kk, j=j, bank=bank: e.transpose(
                            tr[bank][:, kk, j * 128:(j + 1) * 128], hn[:, j, k * 128:(k + 1) * 128], ident[:]),
                             r=["hn%d" % j, "ident"], w=["tr%d" % bank])
                for kk in range(2):
                    k = kp * 2 + kk
                    P.op("dve", lambda e, k=k, kk=kk, bank=bank: e.tensor_scalar(
                        out=hT[:, k, :], in0=tr[bank][:, kk, :], scalar1=gpre[:, k:k + 1], scalar2=None,
                        op0=ALU.mult),
                         r=["tr%d" % bank, "gpre"], w=["hT%d" % k])

        gui = [0]

        def emit_GU_f(bi, f):
            pr = gui[0] % 2
            gui[0] += 1
            pg, pu = gu[2 * pr], gu[2 * pr + 1]
            gi = fgrp[f]
            for nm, W, pt in (("Wg", Wg, pg), ("Wu", Wu, pu)):
                pn = "gu%d%s" % (pr, nm)
                for k in range(8):
                    P.op("pe", lambda e, W=W, pt=pt, k=k: e.matmul(
                        pt[:], W[:, k, f * 128:(f + 1) * 128], hT[:, k, :], start=(k == 0), stop=(k == 7)),
                         r=["%s%d" % (nm, gi), "hT%d" % k], w=[pn])
            s = sg[pr]
            P.op("act", lambda e: e.activation(out=s[:], in_=pg[:], func=AF.Silu),
                 r=["gu%dWg" % pr], w=["sg%d" % pr])
            P.op("dve", lambda e: e.tensor_tensor(out=actT[:, f, :], in0=s[:], in1=pu[:], op=ALU.mult),
                 r=["sg%d" % pr, "gu%dWu" % pr], w=["actT%d" % f])

        dni = [0]

        def emit_D(bi):
            b, i = blocks[bi]
            for j in range(4):
                t0 = i * 512 + j * 128
                q = dni[0] % 2
                dni[0] += 1
                xr, yt = xres[q], ytmp[q]
                xrn, ytn = "xres%d" % q, "ytmp%d" % q
                P.dma("act", lambda e, xr=xr, t0=t0: e.dma_start(out=xr[:], in_=src[b, t0:t0 + 128, :]),
                      tag + xrn + "i", w=[xrn])
                for n in range(2):
                    for f in range(NF):
                        P.op("pe", lambda e, n=n, f=f, j=j: e.matmul(
                            dn[:, n * 512:(n + 1) * 512], actT[:, f, j * 128:(j + 1) * 128],
                            Wd[:, f, n * 512:(n + 1) * 512], start=(f == 0), stop=(f == NF - 1)),
                             r=["actT%d" % f, "Wd%d" % fgrp[f]], w=["dn%d" % n])
                P.op("act", lambda e, yt=yt, q=q: e.activation(out=yt[:], in_=dn[:], func=AF.Square,
                                                             accum_out=ss2[:, q:q + 1]),
                     r=["dn0", "dn1"], w=[ytn, "ss2%d" % q])
                P.op("act", lambda e, q=q: e.activation(out=ss2[:, q:q + 1], in_=ss2[:, q:q + 1], func=AF.Sqrt,
                                                        scale=1.0 / D, bias=epsb[:, 0:1]),
                     r=["ss2%d" % q, "epsb"], w=["ss2%d" % q])
                P.op("dve", lambda e, q=q: e.reciprocal(out=rstd2[:, q:q + 1], in_=ss2[:, q:q + 1]),
                     r=["ss2%d" % q], w=["rstd2%d" % q])
                P.op("dve", lambda e, yt=yt, q=q: e.scalar_tensor_tensor(
                    out=yt[:], in0=dn[:], scalar=rstd2[:, q:q + 1], in1=G[:], op0=ALU.mult, op1=ALU.mult),
                     r=["dn0", "dn1", "rstd2%d" % q, "G"], w=[ytn])
                P.op("pool", lambda e, xr=xr, yt=yt: e.tensor_tensor(out=xr[:], in0=xr[:], in1=yt[:],
                                                                     op=ALU.add),
                     r=[ytn, xrn], w=[xrn])
                P.dma("sp", lambda e, xr=xr, t0=t0: e.dma_start(out=dst[b, t0:t0 + 128, :], in_=xr[:]),
                      tag + xrn + "o", r=[xrn], w=["dst_%s_%d_%d" % (tag, b, t0)])

        nblk = len(blocks)
        for j in range(4):
            emit_N(0, j)
        emit_T(0)
        for bi in range(nblk):
            for f in range(NF):
                emit_GU_f(bi, f)
                if bi + 1 < nblk and f in (3, 8, 13, 18):
                    emit_N(bi + 1, (f - 3) // 5)
            if bi + 1 < nblk:
                emit_T(bi + 1)
            emit_D(bi)
        P.barrier_all()
        P.flush()


NT = S // 128
LIMIT = None
NSEQ = NB
SUB = 9
CUT = 9
BARQT = False
EVAC_ACT = False
BIG = 30000.0
C_DQ, C_DK, C_DV, C_NQ, C_KC, C_VC, C_KS, C_VS, C_KW, C_VW, C_GL, C_END = (
    0, 512, 1024, 1536, 2048, 2176, 2304, 2432, 2560, 2688, 2816, 2840)
S_DQ, S_DK, S_NQ, S_KS, S_KW, S_KC, S_VC, S_END = 0, 512, 1024, 1536, 1664, 1792, 1920, 2048


class Rot:
    def __init__(self, items, names):
        self.items, self.names, self.i = items, names, 0

    def next(self):
        k = self.i % len(self.items)
        self.i += 1
        return self.items[k], self.names[k]


def mixer_phase(nc, P, src, dst, dd):
    TB = 2
    with ExitStack() as st0:
        def sb0(name, shape, dt):
            return st0.enter_context(nc.sbuf_tensor("B" + name, shape, dt))

        dqT = sb0("dqT", [128, 4, S], BF16)
        dkT = sb0("dkT", [128, 4, S], BF16)
        Qg = [sb0("Qg%d" % g, [128, 4, S], BF16) for g in range(2)]
        KSg = [sb0("KSg%d" % g, [128, S], BF16) for g in range(2)]
        KWg = [sb0("KWg%d" % g, [128, S], BF16) for g in range(2)]
        dva = sb0("dva", [128, NT, 4, 130], BF16)
        vsa = sb0("vsa", [128, NT, 2, 66], BF16)
        vwa = sb0("vwa", [128, NT, 2, 66], BF16)
        glr = sb0("glr", [128, NT, 24], F32)
        kcc = [sb0("kcc%d" % g, [128, 128], BF16) for g in range(2)]
        vca = sb0("vca", [128, 2, 98], BF16)
        ident = sb0("ident", [128, 128], BF16)
        epsb = sb0("epsb", [128, 1], F32)

        P.dma("sp", lambda e: e.dma_start(out=ident[:], in_=dd["ident"]), "Bc_ident", w=["ident"])
        P.op("pool", lambda e: e.memset(epsb[:], EPS), w=["epsb"])
        P.op("pool", lambda e: e.memset(dva[:, :, :, 128:130], 1.0), w=["dva"])
        P.op("pool", lambda e: e.memset(vsa[:, :, :, 64:66], 1.0), w=["vsa"])
        P.op("pool", lambda e: e.memset(vwa[:, :, :, 64:66], 1.0), w=["vwa"])
        P.op("pool", lambda e: e.memset(vca[:], 0.0), w=["vca"])
        P.op("pool", lambda e: e.memset(vca[:, :, 64:65], 1.0), r=["vca"], w=["vca"])
        for g in range(2):
            P.dma("sp", lambda e, g=g: e.dma_start(out=vca[0:127, g, 65:97], in_=dd["ov"]), "Bc_ov%d" % g,
                  r=["vca"], w=["vca_ov%d" % g])
            P.dma("sp", lambda e, g=g: e.dma_start(out=KSg[g][64:96, :], in_=dd["ET"]), "Bc_ET%d" % g,
                  w=["KSgE%d" % g])

        for b in range(NSEQ):
            with ExitStack() as st1:
                kcT = st1.enter_context(nc.sbuf_tensor("BkcT%d" % b, [128, S], BF16))
                vcT = st1.enter_context(nc.sbuf_tensor("BvcT%d" % b, [128, S], BF16))
                mixer_proj(nc, P, b, TB, src, dd, dict(dqT=dqT, dkT=dkT, Qg=Qg, KSg=KSg, KWg=KWg, dva=dva,
                                                      vsa=vsa, vwa=vwa, glr=glr, kcT=kcT, vcT=vcT,
                                                      ident=ident, epsb=epsb))
                if SUB >= 2:
                    mixer_compress(nc, P, b, dd, dict(kcT=kcT, vcT=vcT, kcc=kcc, vca=vca))
            if SUB >= 3:
                mixer_attn(nc, P, b, src, dst, dd, dict(dqT=dqT, dkT=dkT, Qg=Qg, KSg=KSg, KWg=KWg, dva=dva,
                                                   vsa=vsa, vwa=vwa, glr=glr, kcc=kcc, vca=vca,
                                                   ident=ident, epsb=epsb))


def mixer_proj(nc, P, b, TB, src, dd, t):
    tag = "P%d" % b
    dqT, dkT, Qg, KSg, KWg = t["dqT"], t["dkT"], t["Qg"], t["KSg"], t["KWg"]
    dva, vsa, vwa, glr, kcT, vcT, ident, epsb = (t["dva"], t["vsa"], t["vwa"], t["glr"], t["kcT"], t["vcT"],
                                                 t["ident"], t["epsb"])
    with ExitStack() as st:
        def sb(name, shape, dt):
            return st.enter_context(nc.sbuf_tensor(tag + name, shape, dt))

        def ps(name, shape, dt):
            return st.enter_context(nc.psum_tensor(tag + name, shape, dt))

        rt = [[sb("rt%d_%d" % (i, q), [128, 8, 8], F32) for q in range(4)] for i in range(2)]
        xs = [sb("xs%d" % i, [128, 512], F32) for i in range(2)]
        xin = [sb("xin%d" % i, [128, D], F32) for i in range(2)]
        hn = sb("hn", [128, TB, D], BF16)
        hT = sb("hT", [128, 8, TB * 128], BF16)
        stg = sb("stg", [128, TB, S_END], BF16)
        cosR = sb("cosR", [128, NT, 8, 8], F32)
        sinR = sb("sinR", [128, NT, 8, 8], F32)
        gpre = sb("gpre", [128, 8], F32)
        ss = sb("ss", [128, TB], F32)
        rstd = sb("rstd", [128, TB], F32)
        Win = sb("Win", [128, 8, 2048], BF16)
        tr = [ps("tr%d" % i, [128, 1024], BF16) for i in range(2)]
        pj = [ps("pj%d" % i, [128, 512], F32) for i in range(3)]
        tq = [ps("tq%d" % i, [128, 1024], BF16) for i in range(2)]

        P.dma("sp", lambda e: e.dma_start(out=gpre[:], in_=dd["m_gpre"]), tag + "gpre", w=["gpre"])
        P.dma("sp", lambda e: e.dma_start(out=cosR[:], in_=dd["cosR"]), tag + "cos", w=["cosR"])
        P.dma("sp", lambda e: e.dma_start(out=sinR[:], in_=dd["sinR"]), tag + "sin", w=["sinR"])
        win_v = dd["w_in"].rearrange("(k p) f -> p k f", p=128)
        CB = [(0, 512), (512, 1024), (1024, 1536), (1536, 2048), (2048, 2560), (2560, C_END)]
        SEC = [(C_KS, 128), (C_KW, 128), (C_KC, 128), (C_VC, 128), (C_VS, 128), (C_VW, 128), (C_GL, 24)]
        def load_pass(ph):
            if ph == 0:
                for ci, (c0, c1) in enumerate(CB[:4]):
                    for kh in range(2):
                        P.dma("pool", lambda e, c0=c0, c1=c1, kh=kh: e.dma_start(
                            out=Win[:, kh * 4:(kh + 1) * 4, c0:c1], in_=win_v[:, kh * 4:(kh + 1) * 4, c0:c1]),
                              "%sWin%d" % (tag, ci), w=["Win%d" % ci])
            else:
                dcol = 0
                for si, (sc, sw) in enumerate(SEC):
                    ci = 4 if si < 4 else 5
                    P.dma("pool", lambda e, sc=sc, sw=sw, dcol=dcol: e.dma_start(
                        out=Win[:, :, dcol:dcol + sw], in_=win_v[:, :, sc:sc + sw]),
                          "%sWin%d" % (tag, ci), w=["Win%d" % ci] + (["Win0", "Win1"] if si == 0 else []))
                    dcol += sw
                P.dma("sp", lambda e: e.dma_start(out=cosR[:], in_=dd["cosR2"]), tag + "cos", w=["cosR"])
                P.dma("sp", lambda e: e.dma_start(out=sinR[:], in_=dd["sinR2"]), tag + "sin", w=["sinR"])

        PH = [0]
        nxi = [0]
        pji = [0]
        rti = [0]
        tqi = [0]
        evi = [0]

        def emit_N(i, j):
            t0 = (i * TB + j) * 128
            q = nxi[0] % 2
            nxi[0] += 1
            xb, xn = xin[q], "xin%d" % q
            P.dma("sp", lambda e: e.dma_start(out=xb[:], in_=src[b, t0:t0 + 128, :]), tag + xn, w=[xn])
            P.op("act", lambda e: e.activation(out=hn[:, j, :], in_=xb[:], func=AF.Square,
                                               accum_out=ss[:, j:j + 1]),
                 r=[xn], w=["hn%d" % j, "ss%d" % j])
            P.op("act", lambda e: e.activation(out=ss[:, j:j + 1], in_=ss[:, j:j + 1], func=AF.Sqrt,
                                               scale=1.0 / D, bias=epsb[:, 0:1]),
                 r=["ss%d" % j, "epsb"], w=["ss%d" % j])
            P.op("dve", lambda e: e.reciprocal(out=rstd[:, j:j + 1], in_=ss[:, j:j + 1]),
                 r=["ss%d" % j], w=["rstd%d" % j])
            P.op("act", lambda e: e.activation(out=hn[:, j, :], in_=xb[:], func=AF.Copy,
                                               scale=rstd[:, j:j + 1]),
                 r=[xn, "rstd%d" % j], w=["hn%d" % j])

        def emit_T(i):
            W = TB * 128
            for kq in range(2):
                bank = kq
                for kk in range(4):
                    k = kq * 4 + kk
                    for j in range(TB):
                        P.op("pe", lambda e, k=k, kk=kk, j=j, bank=bank: e.transpose(
                            tr[bank][:, kk * W + j * 128:kk * W + (j + 1) * 128],
                            hn[:, j, k * 128:(k + 1) * 128], ident[:]),
                             r=["hn%d" % j, "ident"], w=["tr%d" % bank])
                for kk in range(4):
                    k = kq * 4 + kk
                    P.op("dve", lambda e, k=k, kk=kk, bank=bank: e.tensor_scalar(
                        out=hT[:, k, :], in0=tr[bank][:, kk * W:(kk + 1) * W], scalar1=gpre[:, k:k + 1],
                        scalar2=None, op0=ALU.mult),
                         r=["tr%d" % bank, "gpre"], w=["hT%d" % k])

        def rope(pjt, pjn, o, nh, j, so, T, tabs=None):
            q = rti[0] % 2
            rti[0] += 1
            ta, tb_, tc, td = rt[q]
            rn = ["rt%d_%d" % (q, x) for x in range(4)]
            xst, xsn = xs[q], "xs%d" % q
            if (CUT == 4.26 and nh == 2) or (CUT == 4.28 and tabs is not None):
                sres = "stg%d_%d" % (j, so)
                P.op("act", lambda e: e.activation(out=stg[:, j, so:so + nh * 64], in_=pjt[:, o:o + nh * 64],
                                                   func=AF.Copy), r=[pjn], w=[sres + "a"])
                return [sres + "a"]
            P.op("act", lambda e: e.activation(out=xst[:, 0:nh * 64], in_=pjt[:, o:o + nh * 64], func=AF.Copy),
                 r=[pjn], w=[xsn])
            pv = xst[:, 0:nh * 64].rearrange("p (h d) -> p h d", d=64)
            sv = stg[:, j, so:so + nh * 64].rearrange("p (h d) -> p h d", d=64)
            x1, x2 = pv[:, :, 0:8], pv[:, :, 8:16]
            ct_, st_, ctn, stn = (cosR, sinR, "cosR", "sinR") if tabs is None else tabs
            cs, sn = ct_[:, T, 0:nh, :], st_[:, T, 0:nh, :]
            sres = "stg%d_%d" % (j, so)
            if CUT < 3.06:
                return [sres + "a", sres + "b", sres + "c"]
            P.op("dve", lambda e: e.tensor_tensor(out=ta[:, 0:nh, :], in0=x1, in1=cs, op=ALU.mult),
                 r=[xsn, ctn], w=[rn[0]])
            if CUT < 3.07:
                return [sres + "a", sres + "b", sres + "c"]
            P.op("dve", lambda e: e.tensor_tensor(out=tb_[:, 0:nh, :], in0=x2, in1=sn, op=ALU.mult),
                 r=[xsn, stn], w=[rn[1]])
            if CUT < 3.08:
                return [sres + "a", sres + "b", sres + "c"]
            P.op("dve", lambda e: e.tensor_tensor(out=tc[:, 0:nh, :], in0=x2, in1=cs, op=ALU.mult),
                 r=[xsn, ctn], w=[rn[2]])
            if CUT == 3.095:
                P.op("dve", lambda e: e.tensor_tensor(out=tc[:, 0:nh, :], in0=x1, in1=sn, op=ALU.mult),
                     r=[xsn, "sinR"], w=[rn[2]])
                return [sres + "a", sres + "b", sres + "c"]
            P.op("dve", lambda e: e.tensor_tensor(out=td[:, 0:nh, :], in0=x1, in1=sn, op=ALU.mult),
                 r=[xsn, stn], w=[rn[3]])
            P.op("dve", lambda e: e.tensor_tensor(out=pv[:, :, 0:8], in0=ta[:, 0:nh, :], in1=tb_[:, 0:nh, :],
                                                  op=ALU.subtract),
                 r=[rn[0], rn[1], rn[2], rn[3], xsn], w=[xsn])
            P.op("dve", lambda e: e.tensor_tensor(out=pv[:, :, 8:16], in0=tc[:, 0:nh, :], in1=td[:, 0:nh, :],
                                                  op=ALU.add),
                 r=[rn[2], rn[3], xsn], w=[xsn])
            P.op("act", lambda e: e.activation(out=stg[:, j, so:so + nh * 64], in_=xst[:, 0:nh * 64], func=AF.Copy),
                 r=[xsn], w=[sres + "a"])
            return [sres + "a"]

        def emit_proj(i, j, stres):
            T = i * TB + j
            for ci, (c0, c1) in enumerate(CB):
                if (ci < 4) != (PH[0] == 0):
                    continue
                if ci >= 4:
                    c0, c1 = c0 - 2048, c1 - 2048
                q = pji[0] % 3
                pji[0] += 1
                pjt, pjn = pj[q], "pj%d" % q
                ncol = c1 - c0
                for k in range(8):
                    P.op("pe", lambda e, k=k, pjt=pjt, c0=c0, c1=c1, ncol=ncol: e.matmul(
                        pjt[:, 0:ncol], hT[:, k, j * 128:(j + 1) * 128], Win[:, k, c0:c1],
                        start=(k == 0), stop=(k == 7)),
                         r=["hT%d" % k, "Win%d" % ci], w=[pjn])
                if ci == 0:
                    stres["dq"][j] = rope(pjt, pjn, 0, 8, j, S_DQ, T)
                elif ci == 1:
                    stres["dk"][j] = rope(pjt, pjn, 0, 8, j, S_DK, T)
                elif ci == 2:
                    P.op("act", lambda e, pjt=pjt, T=T: e.activation(
                        out=dva[:, T, :, 0:128], in_=pjt[:, 0:512].rearrange("p (h d) -> p h d", d=128),
                        func=AF.Copy), r=[pjn], w=["dva%d" % T])
                elif ci == 3:
                    stres["nq"][j] = rope(pjt, pjn, 0, 8, j, S_NQ, T)
                elif ci == 4:
                    rr = rope(pjt, pjn, 0, 8, j, S_KS, T)
                    stres["ks"][j] = rr
                    stres["kw"][j] = rr
                    stres["kcvc"][j] = rr
                else:
                    P.op("act", lambda e, pjt=pjt, T=T: e.activation(
                        out=vsa[:, T, :, 0:64], in_=pjt[:, 0:128].rearrange("p (h d) -> p h d", d=64),
                        func=AF.Copy), r=[pjn], w=["vsa%d" % T])
                    P.op("act", lambda e, pjt=pjt, T=T: e.activation(
                        out=vwa[:, T, :, 0:64], in_=pjt[:, 128:256].rearrange("p (h d) -> p h d", d=64),
                        func=AF.Copy), r=[pjn], w=["vwa%d" % T])
                    P.op("dve", lambda e, pjt=pjt, T=T: e.tensor_copy(out=glr[:, T, :], in_=pjt[:, 256:280]),
                         r=[pjn], w=["glr%d" % T])

        def emit_QT(i, stres):
            tk0 = i * TB * 128
            W = TB * 128
            units = []
            for h in range(4):
                units.append((S_DQ + h * 128, 128, dqT[:, h, tk0:tk0 + W], "dqT", "dq"))
            for h in range(4):
                units.append((S_DK + h * 128, 128, dkT[:, h, tk0:tk0 + W], "dkT", "dk"))
            for n in range(8):
                units.append((S_NQ + n * 64, 64, Qg[n // 4][0:64, n % 4, tk0:tk0 + W], "Qq%d" % (n // 4), "nq"))
            for g in range(2):
                units.append((S_KS + g * 64, 64, KSg[g][0:64, tk0:tk0 + W], "KSq%d" % g, "ks"))
            for g in range(2):
                units.append((S_KW + g * 64, 64, KWg[g][0:64, tk0:tk0 + W], "KWq%d" % g, "kw"))
            units.append((S_KC, 128, kcT[:, tk0:tk0 + W], "kcT", "kcvc"))
            units.append((S_VC, 128, vcT[:, tk0:tk0 + W], "vcT", "kcvc"))
            units = units[0:16] if PH[0] == 0 else units[16:22]
            if CUT == 9:
                pass
            elif CUT == 4.23:
                units = [(S_NQ + g * 64, 64, KSg[g][0:64, tk0:tk0 + W], "KSq%d" % g, "nq") for g in range(2)]
            elif CUT == 4.24:
                units = [(S_KS + g * 64, 64, Qg[g][0:64, 0, tk0:tk0 + W], "Qq%d" % g, "ks") for g in range(2)]
            elif CUT in (4.21, 4.26, 4.28):
                units = units[16:18]
            elif CUT == 4.22:
                units = units[18:20]
            elif CUT < 4.1:
                units = units[0:8]
            elif CUT < 4.2:
                units = units[0:16]
            elif CUT < 4.3:
                units = units[0:20]
            for u0 in range(0, len(units), 4):
                bank = tqi[0] % 2
                tqi[0] += 1
                grp = units[u0:u0 + 4]
                for ui, (so, ncol, dstap, dres, skey) in enumerate(grp):
                    for j in range(TB):
                        P.op("pe", lambda e, ui=ui, j=j, so=so, ncol=ncol, bank=bank: e.transpose(
                            tq[bank][0:ncol, ui * W + j * 128:ui * W + (j + 1) * 128],
                            stg[:, j, so:so + ncol], ident[:]),
                             r=stres[skey][j] + ["ident"], w=["tq%d" % bank])
                eng = "act" if (evi[0] % 2 == 0 or EVAC_ACT) else "dve"
                evi[0] += 1
                for ui, (so, ncol, dstap, dres, skey) in enumerate(grp):
                    dres = "%s_u%d" % (dres, u0 + ui)
                    if eng == "act":
                        P.op("act", lambda e, ui=ui, ncol=ncol, dstap=dstap, bank=bank: e.activation(
                            out=dstap, in_=tq[bank][0:ncol, ui * W:(ui + 1) * W], func=AF.Copy),
                             r=["tq%d" % bank], w=["%s_%d" % (dres, i)])
                    else:
                        P.op("dve", lambda e, ui=ui, ncol=ncol, dstap=dstap, bank=bank: e.tensor_copy(
                            out=dstap, in_=tq[bank][0:ncol, ui * W:(ui + 1) * W]),
                             r=["tq%d" % bank], w=["%s_%d" % (dres, i)])

        nblk = NT // TB
        for ph in range(2):
            PH[0] = ph
            load_pass(ph)
            for j in range(TB):
                emit_N(0, j)
            emit_T(0)
            for i in range(nblk):
                stres = {k: [None] * TB for k in ("dq", "dk", "nq", "kcvc", "ks", "kw")}
                for j in range(TB):
                    emit_proj(i, j, stres)
                    if i + 1 < nblk:
                        emit_N(i + 1, j)
                emit_QT(i, stres)
                if i + 1 < nblk:
                    emit_T(i + 1)
        P.barrier_all()
        P.flush()


def mixer_compress(nc, P, b, dd, t):
    tag = "Z%d" % b
    kcT, vcT, kcc, vca = t["kcT"], t["vcT"], t["kcc"], t["vca"]
    NCB = 127
    with ExitStack() as st:
        def sb(name, shape, dt):
            return st.enter_context(nc.sbuf_tensor(tag + name, shape, dt))

        def ps(name, shape, dt):
            return st.enter_context(nc.psum_tensor(tag + name, shape, dt))

        W1 = [sb("W1_%d" % kv, [128, 32, 256], BF16) for kv in range(2)]
        W2 = [sb("W2_%d" % kv, [128, 2, 64], BF16) for kv in range(2)]
        peT = [sb("peT%d" % kv, [128, 32], BF16) for kv in range(2)]
        pb = sb("pb", [128, 4], F32)
        xh = sb("xh", [128, 2, 128], F32)
        u = sb("u", [128, 2, 128], F32)
        sg = sb("sg", [128, 2, 128], F32)
        hact = sb("hact", [128, 2, 128], BF16)
        pbps = ps("pbps", [128, 4], F32)
        hps = [ps("hps%d" % i, [128, 2, 128], F32) for i in range(2)]
        cps = ps("cps", [128, 128], F32)

        for kv, (w1n, w2n, pen) in enumerate((("ck_w1", "ck_w2", "ck_peT"), ("cv_w1", "cv_w2", "cv_peT"))):
            w1v = dd[w1n].rearrange("(l d) h -> d l h", d=64)
            for half in range(2):
                P.dma("pool", lambda e, kv=kv, half=half, w1v=w1v: e.dma_start(
                    out=W1[kv][half * 64:(half + 1) * 64, :, :], in_=w1v),
                      "%sW1_%d" % (tag, kv), w=["W1_%d" % kv])
            P.dma("pool", lambda e, kv=kv, w2n=w2n: e.dma_start(
                out=W2[kv][:], in_=dd[w2n].rearrange("(c p) d -> p c d", p=128)),
                  "%sW2_%d" % (tag, kv), w=["W2_%d" % kv])
            P.dma("pool", lambda e, kv=kv, pen=pen: e.dma_start(out=peT[kv][:], in_=dd[pen]),
                  "%spe_%d" % (tag, kv), w=["peT%d" % kv])
        for kv in range(2):
            for ch in range(2):
                col = kv * 2 + ch
                for l in range(32):
                    P.op("pe", lambda e, kv=kv, ch=ch, l=l, col=col: e.matmul(
                        pbps[:, col:col + 1], W1[kv][0:64, l, ch * 128:(ch + 1) * 128], peT[kv][0:64, l:l + 1],
                        start=(l == 0), stop=(l == 31)),
                         r=["W1_%d" % kv, "peT%d" % kv], w=["pbps"])
        P.op("dve", lambda e: e.tensor_copy(out=pb[:], in_=pbps[:]), r=["pbps"], w=["pb"])
        hi = [0]
        for kv in range(2):
            xT = kcT if kv == 0 else vcT
            for g in range(2):
                hp = hps[hi[0] % 2]
                hpn = "hps%d" % (hi[0] % 2)
                hi[0] += 1
                for ch in range(2):
                    for l in range(32):
                        P.op("pe", lambda e, kv=kv, g=g, ch=ch, l=l, hp=hp, xT=xT: e.matmul(
                            hp[:, ch, 0:NCB], W1[kv][g * 64:(g + 1) * 64, l, ch * 128:(ch + 1) * 128],
                            xT[g * 64:(g + 1) * 64, l:l + 16 * (NCB - 1) + 1:16],
                            start=(l == 0), stop=(l == 31)),
                             r=["W1_%d" % kv], w=[hpn])
                for ch in range(2):
                    col = kv * 2 + ch
                    P.op("act", lambda e, ch=ch, col=col, hp=hp: e.activation(
                        out=xh[:, ch, 0:NCB], in_=hp[:, ch, 0:NCB], func=AF.Identity, bias=pb[:, col:col + 1]),
                         r=[hpn, "pb"], w=["xh%d" % ch])
                X, U, SG, HA = xh[:, :, 0:NCB], u[:, :, 0:NCB], sg[:, :, 0:NCB], hact[:, :, 0:NCB]
                P.op("dve", lambda e, X=X, U=U: e.tensor_tensor(out=U, in0=X, in1=X, op=ALU.mult),
                     r=["xh0", "xh1"], w=["u"])
                P.op("dve", lambda e, U=U: e.tensor_scalar(out=U, in0=U, scalar1=0.044715, scalar2=1.0,
                                                        op0=ALU.mult, op1=ALU.add), r=["u"], w=["u"])
                P.op("dve", lambda e, X=X, U=U: e.tensor_tensor(out=U, in0=U, in1=X, op=ALU.mult),
                     r=["u", "xh0", "xh1"], w=["u"])
                P.op("act", lambda e, U=U, SG=SG: e.activation(out=SG, in_=U, func=AF.Sigmoid,
                                                              scale=1.5957691216057308),
                     r=["u"], w=["sg"])
                P.op("dve", lambda e, X=X, SG=SG, HA=HA: e.tensor_tensor(out=HA, in0=X, in1=SG, op=ALU.mult),
                     r=["sg", "xh0", "xh1"], w=["hact"])
                if kv == 0:
                    for ch in range(2):
                        P.op("pe", lambda e, ch=ch: e.matmul(cps[0:64, 0:NCB], W2[0][:, ch, :], hact[:, ch, 0:NCB],
                                                            start=(ch == 0), stop=(ch == 1)),
                             r=["hact", "W2_0"], w=["cps"])
                    P.op("act", lambda e, g=g: e.activation(out=kcc[g][0:64, 0:NCB], in_=cps[0:64, 0:NCB],
                                                           func=AF.Copy), r=["cps"], w=["kcc%d" % g])
                else:
                    for ch in range(2):
                        P.op("pe", lambda e, ch=ch: e.matmul(cps[0:NCB, 0:64], hact[:, ch, 0:NCB], W2[1][:, ch, :],
                                                            start=(ch == 0), stop=(ch == 1)),
                             r=["hact", "W2_1"], w=["cps"])
                    P.op("act", lambda e, g=g: e.activation(out=vca[0:NCB, g, 0:64], in_=cps[0:NCB, 0:64],
                                                           func=AF.Copy), r=["cps"], w=["vca_v%d" % g])
        P.barrier_all()
        P.flush()


class Pipe:
    def __init__(self, lag=2):
        self.q, self.lag = [], lag

    def push(self, first, rest):
        if first is not None:
            first()
        self.q.append(rest)
        while len(self.q) > self.lag:
            self.q.pop(0)()

    def drain(self):
        while self.q:
            self.q.pop(0)()


def mixer_attn(nc, P, b, src, dst, dd, t):
    tag = "T%d" % b
    dqT, dkT, Qg, KSg, KWg = t["dqT"], t["dkT"], t["Qg"], t["KSg"], t["KWg"]
    dva, vsa, vwa, glr, kcc, vca, ident, epsb = (t["dva"], t["vsa"], t["vwa"], t["glr"], t["kcc"], t["vca"],
                                                 t["ident"], t["epsb"])
    with ExitStack() as st:
        def sb(name, shape, dt):
            return st.enter_context(nc.sbuf_tensor(tag + name, shape, dt))

        def ps(name, shape, dt):
            return st.enter_context(nc.psum_tensor(tag + name, shape, dt))

        om = sb("om", [128, NT, 1024], BF16)
        tmpf = [sb("tmpf%d" % i, [128, 64], F32) for i in range(2)]
        Wo = sb("Wo", [128, 8, 1024], BF16)
        pts = [sb("p%d" % i, [128, 512], BF16) for i in range(4)]
        gates = sb("gates", [128, NT, 24], F32)
        Am = sb("Am", [128, NT, 32], F32)
        Bm = sb("Bm", [128, NT, 32], F32)
        Gs = sb("Gs", [128, 128], F32)
        Gm = sb("Gm", [128, D], F32)
        lv = [sb("lv%d" % i, [128, 64], F32) for i in range(4)]
        lt = sb("lt", [128, 64], F32)
        le = sb("le", [128, 2], F32)
        neglam = sb("neglam", [128, 1], F32)
        o1 = [sb("o1_%d" % i, [128, 4, 128], F32) for i in range(2)]
        junk = sb("junk", [128, 128], F32)
        sm = [dict((n, sb("%s_%d" % (n, i), [128, w], F32)) for n, w in
                   (("rd", 4), ("nl", 4), ("ss4", 4), ("ln4", 4), ("r4", 4), ("den", 4), ("gr", 4), ("rs", 4),
                    ("rw", 4), ("imp", 32), ("impm", 32), ("wk", 32), ("m1", 8), ("m2", 8))) for i in range(2)]
        nsp = [sb("nsp%d" % i, [128, 96], BF16) for i in range(2)]
        omT = [sb("omT%d" % i, [128, 8, 128], BF16) for i in range(2)]
        xres = [sb("xres%d" % i, [128, D], F32) for i in range(2)]
        ytmp = [sb("ytmp%d" % i, [128, D], F32) for i in range(2)]
        ss2 = sb("ss2", [128, 2], F32)
        rstd2 = sb("rstd2", [128, 2], F32)
        sps = [ps("sps%d" % i, [128, 512], F32) for i in range(3)]
        ab = ps("ab", [128, 4, 512], F32)
        tps = ps("tps", [128, 1024], BF16)
        SPS = Rot(sps, ["sps%d" % i for i in range(3)])
        PT = Rot(pts, ["p%d" % i for i in range(4)])

        for nm, tl in (("Am", Am), ("Bm", Bm), ("Gs", Gs), ("Gm", Gm)):
            P.dma("sp", lambda e, nm=nm, tl=tl: e.dma_start(out=tl[:], in_=dd[nm]), tag + nm, w=[nm])
        for i, nm in enumerate(("lq1", "lk1", "lq2", "lk2")):
            P.dma("sp", lambda e, i=i, nm=nm: e.dma_start(out=lv[i][:], in_=dd[nm]), tag + nm, w=["lv%d" % i])
        wo_v = dd["w_out"].rearrange("(k p) f -> p k f", p=128)
        for kh in range(2):
            P.dma("pool", lambda e, kh=kh: e.dma_start(out=Wo[:, kh * 4:(kh + 1) * 4, :],
                                                       in_=wo_v[:, kh * 4:(kh + 1) * 4, :]),
                  tag + "Wo", w=["Wo"])
        for nm in ("nsp0", "nsp1"):
            pass
        P.op("pool", lambda e: e.memset(nsp[0][:], 0.0), w=["nsp0"])
        P.op("pool", lambda e: e.memset(nsp[1][:], 0.0), w=["nsp1"])
        P.op("dve", lambda e: e.tensor_scalar(out=Gs[:], in0=Gs[:], scalar1=0.8, scalar2=None, op0=ALU.mult),
             r=["Gs"], w=["Gs"])
        P.op("act", lambda e: e.activation(out=gates[:], in_=glr[:], func=AF.Sigmoid), w=["gates"])
        for i in range(2):
            P.op("dve", lambda e, i=i: e.tensor_tensor(out=lt[:], in0=lv[2 * i][:], in1=lv[2 * i + 1][:],
                                                       op=ALU.mult),
                 r=["lv%d" % (2 * i), "lv%d" % (2 * i + 1)], w=["lt"])
            P.op("dve", lambda e, i=i: e.reduce_sum(out=le[:, i:i + 1], in_=lt[:], axis=AX.X),
                 r=["lt"], w=["le%d" % i])
        P.op("act", lambda e: e.activation(out=le[:], in_=le[:], func=AF.Exp), r=["le0", "le1"], w=["le0", "le1"])
        P.op("dve", lambda e: e.tensor_tensor(out=neglam[:], in0=le[:, 1:2], in1=le[:, 0:1], op=ALU.subtract),
             r=["le0", "le1"], w=["neglam"])
        P.op("dve", lambda e: e.tensor_scalar(out=neglam[:], in0=neglam[:], scalar1=-0.2, scalar2=None,
                                              op0=ALU.add), r=["neglam"], w=["neglam"])

        pipe = Pipe(2)
        first_in_bank = {}

        def acc_mm(bank_i, col0, ncol, lhsT, rhs, last, reads):
            bn = "ab%d" % bank_i
            first = first_in_bank.get(bn, True)
            first_in_bank[bn] = False
            P.op("pe", lambda e: e.matmul(ab[:, bank_i, col0:col0 + ncol], lhsT, rhs, start=first, stop=last,
                                          skip_group_check=True),
                 r=reads, w=[bn])

        def exp_tile(spt, spn, rows, masks):
            p, pn = PT.next()
            P.op("act", lambda e: e.activation(out=p[0:rows, :], in_=spt[0:rows, :], func=AF.Exp, scale=0.125),
                 r=[spn], w=[pn])
            for (pattern, base, cm) in masks:
                P.op("pool", lambda e, pattern=pattern, base=base, cm=cm: e.affine_select(
                    out=p[0:rows, :], in_=p[0:rows, :], pattern=pattern, compare_op=ALU.is_ge, fill=0.0,
                    base=base, channel_multiplier=cm), r=[pn], w=[pn])
            return p, pn

        ci = 0
        for h in range(4):
            for qb in range(4):
                ob, obn = o1[(h * 4 + qb) % 2], "o1_%d" % ((h * 4 + qb) % 2)
                smx = sm[(h * 4 + qb) % 2]
                sfx = "_%d" % ((h * 4 + qb) % 2)
                for m in range(2):
                    bA, bB = 2 * (ci % 2), 2 * (ci % 2) + 1
                    ci += 1
                    first_in_bank["ab%d" % bA] = True
                    first_in_bank["ab%d" % bB] = True
                    nkt = 4 * qb + 4
                    for kt in range(nkt):
                        spt, spn = SPS.next()

                        def qk(spt=spt, spn=spn, kt=kt, m=m, h=h, qb=qb):
                            P.op("pe", lambda e: e.matmul(
                                spt[:], dkT[m * 64:(m + 1) * 64, h, kt * 128:(kt + 1) * 128],
                                dqT[m * 64:(m + 1) * 64, h, qb * 512:(qb + 1) * 512], start=True, stop=True),
                                 w=[spn])

                        def rest(spt=spt, spn=spn, kt=kt, h=h, qb=qb, bA=bA, bB=bB):
                            masks = []
                            if kt >= 4 * qb:
                                masks.append(([[1, 512]], qb * 512 - kt * 128, -1))
                            p, pn = exp_tile(spt, spn, 128, masks)
                            for jq in range(4):
                                if kt > 4 * qb + jq:
                                    continue
                                bank_i, col0 = (bA, jq * 130) if jq < 3 else (bB, 0)
                                acc_mm(bank_i, col0, 129, p[:, jq * 128:(jq + 1) * 128], dva[:, kt, h, 0:129],
                                       kt == 4 * qb + jq, [pn])

                        pipe.push(qk, rest)

                    def post(m=m, h=h, qb=qb, bA=bA, bB=bB, ob=ob, obn=obn, smx=smx, sfx=sfx):
                        bnA, bnB = "ab%d" % bA, "ab%d" % bB
                        rd = smx["rd"]
                        denA = ab[:, bA, 0:390].rearrange("p (j c) -> p j c", c=130)[:, :, 128]
                        P.op("dve", lambda e: e.reciprocal(out=rd[:, 0:3], in_=denA), r=[bnA], w=["rdA" + sfx])
                        P.op("dve", lambda e: e.reciprocal(out=rd[:, 3:4], in_=ab[:, bB, 128:129]), r=[bnB],
                             w=["rdB" + sfx])

                        def region(jq):
                            return (ab[:, bA, jq * 130:jq * 130 + 128], bnA) if jq < 3 else (ab[:, bB, 0:128], bnB)

                        if m == 0:
                            for jq in range(4):
                                reg, bn = region(jq)
                                P.op("act", lambda e, reg=reg, jq=jq: e.activation(
                                    out=ob[:, jq, :], in_=reg, func=AF.Copy, scale=rd[:, jq:jq + 1]),
                                     r=[bn, "rdA" + sfx, "rdB" + sfx], w=[obn + "_%d" % jq])
                        else:
                            nl, ss4, ln4, r4 = smx["nl"], smx["ss4"], smx["ln4"], smx["r4"]
                            P.op("dve", lambda e: e.tensor_scalar(out=nl[:], in0=rd[:], scalar1=neglam[:, 0:1],
                                                                  scalar2=None, op0=ALU.mult),
                                 r=["rdA" + sfx, "rdB" + sfx, "neglam"], w=["nl" + sfx])
                            for jq in range(4):
                                reg, bn = region(jq)
                                P.op("dve", lambda e, reg=reg, jq=jq: e.scalar_tensor_tensor(
                                    out=ob[:, jq, :], in0=reg, scalar=nl[:, jq:jq + 1], in1=ob[:, jq, :],
                                    op0=ALU.mult, op1=ALU.add),
                                     r=[bn, "nl" + sfx, obn + "_%d" % jq], w=[obn + "_%d" % jq])
                                P.op("act", lambda e, jq=jq: e.activation(
                                    out=junk[:], in_=ob[:, jq, :], func=AF.Square, accum_out=ss4[:, jq:jq + 1]),
                                     r=[obn + "_%d" % jq], w=["junk", "ss4%s_%d" % (sfx, jq)])
                            ssr = ["ss4%s_%d" % (sfx, jq) for jq in range(4)]
                            P.op("act", lambda e: e.activation(out=ln4[:], in_=ss4[:], func=AF.Ln, scale=1.0 / 128,
                                                               bias=epsb[:, 0:1]), r=ssr + ["epsb"], w=["ln4" + sfx])
                            P.op("act", lambda e: e.activation(out=r4[:], in_=ln4[:], func=AF.Exp, scale=-0.5),
                                 r=["ln4" + sfx], w=["r4" + sfx])
                            for jq in range(4):
                                T = 4 * qb + jq
                                P.op("dve", lambda e, jq=jq, T=T: e.scalar_tensor_tensor(
                                    out=om[:, T, h * 128:(h + 1) * 128], in0=ob[:, jq, :], scalar=r4[:, jq:jq + 1],
                                    in1=Gs[:], op0=ALU.mult, op1=ALU.mult),
                                     r=[obn + "_%d" % jq, "r4" + sfx, "Gs"], w=["om%d_d%d" % (T, h)])

                    pipe.push(None, post)
        pipe.drain()

        ci = 0
        for g in range(2):
            for T in range(NT):
                bk = ci % 4
                smx = sm[ci % 2]
                sfx = "_%d" % (ci % 2)
                nspt, nspn = nsp[ci % 2], "nsp%d" % (ci % 2)
                ci += 1
                spt, spn = SPS.next()

                def qk(spt=spt, spn=spn, g=g, T=T):
                    P.op("pe", lambda e: e.matmul(spt[0:127, :], kcc[g][0:64, 0:127],
                                                  Qg[g][0:64, :, T * 128:(T + 1) * 128], start=True, stop=True),
                         w=[spn])

                def rest(spt=spt, spn=spn, g=g, T=T, bk=bk):
                    p, pn = exp_tile(spt, spn, 127, [([[0, 4], [1, 128]], T * 128 - 31, -16)])
                    for hg in range(4):
                        P.op("pe", lambda e, hg=hg: e.matmul(ab[:, bk, hg * 128:hg * 128 + 97],
                                                             p[0:127, hg * 128:(hg + 1) * 128], vca[0:127, g, 0:97],
                                                             start=True, stop=True, skip_group_check=True),
                             r=[pn], w=["ab%d" % bk])

                def post(g=g, T=T, bk=bk, smx=smx, sfx=sfx, nspt=nspt, nspn=nspn):
                    bn = "ab%d" % bk
                    den, rd, gr, imp, impm, wk, m1, m2 = (smx["den"], smx["rd"], smx["gr"], smx["imp"],
                                                          smx["impm"], smx["wk"], smx["m1"], smx["m2"])
                    ov4 = ab[:, bk, :].rearrange("p (h c) -> p h c", c=128)
                    gv = gates[:, T, :].rearrange("p (h c) -> p h c", c=3)
                    P.op("dve", lambda e: e.tensor_scalar(out=den[:], in0=ov4[:, :, 64], scalar1=1e-30, scalar2=None,
                                                          op0=ALU.max), r=[bn], w=["den" + sfx])
                    P.op("dve", lambda e: e.reciprocal(out=rd[:], in_=den[:]), r=["den" + sfx], w=["rd" + sfx])
                    P.op("dve", lambda e: e.tensor_tensor(out=gr[:], in0=rd[:], in1=gv[:, g * 4:(g + 1) * 4, 0],
                                                          op=ALU.mult), r=["rd" + sfx, "gates"], w=["gr" + sfx])
                    for hg in range(4):
                        hd = g * 4 + hg
                        P.op("act", lambda e, hg=hg, hd=hd: e.activation(
                            out=om[:, T, 512 + hd * 64:512 + (hd + 1) * 64], in_=ab[:, bk, hg * 128:hg * 128 + 64],
                            func=AF.Copy, scale=gr[:, hg:hg + 1]), r=[bn, "gr" + sfx], w=["om%d_n%d" % (T, hd)])
                    P.op("dve", lambda e: e.tensor_scalar(out=imp[:], in0=ab[:, bk, 65:97], scalar1=rd[:, 0:1],
                                                          scalar2=None, op0=ALU.mult),
                         r=[bn, "rd" + sfx], w=["imp" + sfx])
                    for hg in range(1, 4):
                        P.op("dve", lambda e, hg=hg: e.scalar_tensor_tensor(
                            out=imp[:], in0=ab[:, bk, hg * 128 + 65:hg * 128 + 97], scalar=rd[:, hg:hg + 1],
                            in1=imp[:], op0=ALU.mult, op1=ALU.add), r=[bn, "rd" + sfx, "imp" + sfx],
                             w=["imp" + sfx])
                    P.op("dve", lambda e: e.tensor_tensor(out=impm[:], in0=imp[:], in1=Am[:, T, :], op=ALU.mult),
                         r=["imp" + sfx, "Am"], w=["impm" + sfx])
                    P.op("dve", lambda e: e.tensor_tensor(out=impm[:], in0=impm[:], in1=Bm[:, T, :], op=ALU.add),
                         r=["impm" + sfx, "Bm"], w=["impm" + sfx])
                    P.op("dve", lambda e: e.max(out=m1[:], in_=impm[:]), r=["impm" + sfx], w=["m1" + sfx])
                    P.op("dve", lambda e: e.match_replace(out=wk[:], in_to_replace=m1[:], in_values=impm[:],
                                                          imm_value=-2.0),
                         r=["impm" + sfx, "m1" + sfx], w=["wk" + sfx])
                    P.op("dve", lambda e: e.max(out=m2[:], in_=wk[:]), r=["wk" + sfx], w=["m2" + sfx])
                    P.op("dve", lambda e: e.tensor_scalar(out=nspt[:, 64:96], in0=impm[:], scalar1=m2[:, 7:8],
                                                          scalar2=-BIG, op0=ALU.is_lt, op1=ALU.mult),
                         r=["impm" + sfx, "m2" + sfx], w=[nspn])
                    P.op("pe", lambda e: e.transpose(tps[0:96, 0:128], nspt[:, 0:96], ident[:]),
                         r=[nspn, "ident"], w=["tps"])
                    tsl = slice(T * 128, (T + 1) * 128)
                    P.op("act", lambda e: e.activation(out=Qg[g][64:96, 0, tsl], in_=tps[64:96, 0:128],
                                                       func=AF.Copy), r=["tps"], w=["Qs%d_%d" % (g, T)])
                    for hg in range(1, 4):
                        P.op("pool", lambda e, hg=hg: e.tensor_copy(out=Qg[g][64:96, hg, tsl],
                                                                    in_=Qg[g][64:96, 0, tsl]),
                             r=["Qs%d_%d" % (g, T)], w=["Qs%d_%d_%d" % (g, T, hg)])

                pipe.push(qk, rest)
                pipe.push(None, post)
        pipe.drain()

        ci = 0
        for g in range(2):
            for T in range(NT):
                bS, bW = 2 * (ci % 2), 2 * (ci % 2) + 1
                smx = sm[ci % 2]
                sfx = "_%d" % (ci % 2)
                ci += 1
                first_in_bank["ab%d" % bS] = True
                first_in_bank["ab%d" % bW] = True
                selr = ["Qs%d_%d" % (g, T)] + ["Qs%d_%d_%d" % (g, T, hg) for hg in range(1, 4)]
                jobs = [("s", kt) for kt in range(T + 1)] + [("w", kt) for kt in range(max(0, T - 4), T + 1)]
                for kind, kt in jobs:
                    spt, spn = SPS.next()

                    def qk(spt=spt, spn=spn, kind=kind, kt=kt, g=g, T=T, selr=selr):
                        ksl = slice(kt * 128, (kt + 1) * 128)
                        tsl = slice(T * 128, (T + 1) * 128)
                        if kind == "s":
                            P.op("pe", lambda e: e.matmul(spt[:], KSg[g][0:96, ksl], Qg[g][0:96, :, tsl],
                                                          start=True, stop=True), r=selr, w=[spn])
                        else:
                            P.op("pe", lambda e: e.matmul(spt[:], KWg[g][0:64, ksl], Qg[g][0:64, :, tsl],
                                                          start=True, stop=True), w=[spn])

                    def rest(spt=spt, spn=spn, kind=kind, kt=kt, g=g, T=T, bS=bS, bW=bW):
                        masks = []
                        if kt == T:
                            masks.append(([[0, 4], [1, 128]], 0, -1))
                        if kind == "w" and kt == T - 4:
                            masks.append(([[0, 4], [-1, 128]], -1, 1))
                        p, pn = exp_tile(spt, spn, 128, masks)
                        va = vsa if kind == "s" else vwa
                        bank_i = bS if kind == "s" else bW
                        for hg in range(4):
                            acc_mm(bank_i, hg * 128, 65, p[:, hg * 128:(hg + 1) * 128], va[:, kt, g, 0:65],
                                   kt == T, [pn])

                    pipe.push(qk, rest)

                def post(g=g, T=T, bS=bS, bW=bW, smx=smx, sfx=sfx):
                    bnS, bnW = "ab%d" % bS, "ab%d" % bW
                    rs, rw = smx["rs"], smx["rw"]
                    gv = gates[:, T, :].rearrange("p (h c) -> p h c", c=3)
                    oS = ab[:, bS, :].rearrange("p (h c) -> p h c", c=128)
                    oW = ab[:, bW, :].rearrange("p (h c) -> p h c", c=128)
                    P.op("dve", lambda e: e.reciprocal(out=rs[:], in_=oS[:, :, 64]), r=[bnS], w=["rs" + sfx])
                    P.op("dve", lambda e: e.tensor_tensor(out=rs[:], in0=rs[:], in1=gv[:, g * 4:(g + 1) * 4, 1],
                                                          op=ALU.mult), r=["rs" + sfx, "gates"], w=["rs" + sfx])
                    P.op("dve", lambda e: e.reciprocal(out=rw[:], in_=oW[:, :, 64]), r=[bnW], w=["rw" + sfx])
                    P.op("dve", lambda e: e.tensor_tensor(out=rw[:], in0=rw[:], in1=gv[:, g * 4:(g + 1) * 4, 2],
                                                          op=ALU.mult), r=["rw" + sfx, "gates"], w=["rw" + sfx])
                    for hg in range(4):
                        hd = g * 4 + hg
                        on = "om%d_n%d" % (T, hd)
                        osl = om[:, T, 512 + hd * 64:512 + (hd + 1) * 64]
                        tf, tfn = tmpf[hg % 2], "tmpf%d" % (hg % 2)
                        P.op("dve", lambda e, hg=hg, osl=osl, tf=tf: e.scalar_tensor_tensor(
                            out=tf[:], in0=ab[:, bS, hg * 128:hg * 128 + 64], scalar=rs[:, hg:hg + 1], in1=osl,
                            op0=ALU.mult, op1=ALU.add), r=[bnS, "rs" + sfx, on], w=[tfn])
                        P.op("dve", lambda e, hg=hg, osl=osl, tf=tf: e.scalar_tensor_tensor(
                            out=osl, in0=ab[:, bW, hg * 128:hg * 128 + 64], scalar=rw[:, hg:hg + 1], in1=tf[:],
                            op0=ALU.mult, op1=ALU.add), r=[bnW, "rw" + sfx, tfn], w=[on])

                pipe.push(None, post)
        pipe.drain()

        for T in range(NT):
            q = T % 2
            t0 = T * 128
            omr = ["om%d_d%d" % (T, h) for h in range(4)] + ["om%d_n%d" % (T, hd) for hd in range(8)]
            for k in range(8):
                P.op("pe", lambda e, k=k, T=T: e.transpose(tps[:, k * 128:(k + 1) * 128],
                                                          om[:, T, k * 128:(k + 1) * 128], ident[:]),
                     r=omr + ["ident"], w=["tps"])
            oT, oTn = omT[q], "omT%d" % q
            P.op("act", lambda e, oT=oT: e.activation(out=oT[:].rearrange("p k t -> p (k t)"), in_=tps[:],
                                                      func=AF.Copy), r=["tps"], w=[oTn])
            b0 = 2 * q
            for n in range(2):
                for k in range(8):
                    P.op("pe", lambda e, n=n, k=k, oT=oT, b0=b0: e.matmul(
                        ab[:, b0 + n, :], oT[:, k, :], Wo[:, k, n * 512:(n + 1) * 512], start=(k == 0),
                        stop=(k == 7)), r=[oTn, "Wo"], w=["ab%d" % (b0 + n)])
            xr, yt = xres[q], ytmp[q]
            xrn, ytn = "xres%d" % q, "ytmp%d" % q
            wo2 = ab[:, b0:b0 + 2, :]
            br = ["ab%d" % b0, "ab%d" % (b0 + 1)]
            P.dma("act", lambda e, xr=xr, t0=t0: e.dma_start(out=xr[:], in_=src[b, t0:t0 + 128, :]),
                  tag + xrn + "i", w=[xrn])
            P.op("act", lambda e, yt=yt, q=q, wo2=wo2: e.activation(
                out=yt[:].rearrange("p (n f) -> p n f", n=2), in_=wo2, func=AF.Square, accum_out=ss2[:, q:q + 1]),
                 r=br, w=[ytn, "ss2%d" % q])
            P.op("act", lambda e, q=q: e.activation(out=ss2[:, q:q + 1], in_=ss2[:, q:q + 1], func=AF.Sqrt,
                                                    scale=1.0 / D, bias=epsb[:, 0:1]),
                 r=["ss2%d" % q, "epsb"], w=["ss2%d" % q])
            P.op("dve", lambda e, q=q: e.reciprocal(out=rstd2[:, q:q + 1], in_=ss2[:, q:q + 1]),
                 r=["ss2%d" % q], w=["rstd2%d" % q])
            P.op("dve", lambda e, yt=yt, q=q, wo2=wo2: e.scalar_tensor_tensor(
                out=yt[:].rearrange("p (n f) -> p n f", n=2), in0=wo2, scalar=rstd2[:, q:q + 1],
                in1=Gm[:].rearrange("p (n f) -> p n f", n=2), op0=ALU.mult, op1=ALU.mult),
                 r=br + ["rstd2%d" % q, "Gm"], w=[ytn])
            P.op("pool", lambda e, xr=xr, yt=yt: e.tensor_tensor(out=xr[:], in0=xr[:], in1=yt[:], op=ALU.add),
                 r=[ytn, xrn], w=[xrn])
            P.dma("sp", lambda e, xr=xr, t0=t0: e.dma_start(out=dst[b, t0:t0 + 128, :], in_=xr[:]),
                  tag + xrn + "o", r=[xrn], w=["dst_B_%d_%d" % (b, t0)])
        P.barrier_all()
        P.flush()


def build(stage=3):
    nc = bass.Bass("TRN2", target_bir_lowering=False)

    def dt(n, s, d=F32, k="ExternalInput"):
        return nc.dram_tensor(n, s, d, kind=k).ap()

    x = dt("x", [NB, S, D])
    out = dt("out", [NB, S, D], F32, "ExternalOutput")
    ident_d = dt("ident", [128, 128], BF16)
    f = {}
    for t in ("f1", "f2"):
        f[t] = dict(wg=dt(t + "_wg", [D, DFF]), wu=dt(t + "_wu", [D, DFF]), wd=dt(t + "_wd", [DFF, D]),
                    gpre=dt(t + "_gpre", [128, 8]), gpost=dt(t + "_gpost", [128, D]))
    dd = dict(ident=ident_d,
              w_in=dt("w_in", [D, C_END]), w_out=dt("w_out", [D, D]), m_gpre=dt("m_gpre", [128, 8]),
              Gm=dt("Gm", [128, D]), Gs=dt("Gs", [128, 128]),
              cosR=dt("cosR", [128, NT, 8, 8]), sinR=dt("sinR", [128, NT, 8, 8]),
              cosR2=dt("cosR2", [128, NT, 8, 8]), sinR2=dt("sinR2", [128, NT, 8, 8]),
              ov=dt("ov", [127, 32], BF16), ET=dt("ET", [32, S], BF16),
              Am=dt("Am", [128, NT, 32]), Bm=dt("Bm", [128, NT, 32]),
              ck_w1=dt("ck_w1", [2048, 256]), ck_w2=dt("ck_w2", [256, 64]), ck_peT=dt("ck_peT", [128, 32]),
              cv_w1=dt("cv_w1", [2048, 256]), cv_w2=dt("cv_w2", [256, 64]), cv_peT=dt("cv_peT", [128, 32]),
              lq1=dt("lq1", [128, 64]), lk1=dt("lk1", [128, 64]), lq2=dt("lq2", [128, 64]),
              lk2=dt("lk2", [128, 64]))
    x1 = nc.dram_tensor("x1s", [NB, S, D], F32).ap()
    x2 = nc.dram_tensor("x2s", [NB, S, D], F32).ap()
    with ExitStack() as stack:
        P = Prog(nc, stack)
        fa = f["f1"]
        ffn_phase(nc, P, "A", x, out if stage == 1 else x1, fa["wg"], fa["wu"], fa["wd"], fa["gpre"],
                  fa["gpost"], ident_d)
        if stage >= 2:
            mixer_phase(nc, P, x1, out if stage == 2 else x2, dd)
        if stage >= 3:
            fc = f["f2"]
            ffn_phase(nc, P, "C", x2, out, fc["wg"], fc["wu"], fc["wd"], fc["gpre"], fc["gpost"], ident_d)
    return nc


def host_inputs(inp):
    def g(k):
        return np.ascontiguousarray(np.asarray(inp[k], dtype=np.float32))

    bf = ml_dtypes.bfloat16

    def bc(v, n=128):
        return np.ascontiguousarray(np.broadcast_to(v[None, :], (n, v.shape[0])))

    common = {"ident": np.eye(128, dtype=np.float32).astype(bf)}
    for t, pfx in (("f1", "ff1"), ("f2", "ff2")):
        common[t + "_wg"] = g(pfx + "_w_gate")[0]
        common[t + "_wu"] = g(pfx + "_w_up")[0]
        common[t + "_wd"] = g(pfx + "_w_down")[0]
        common[t + "_gpre"] = np.ascontiguousarray(g(pfx + "_norm_pre")[0].reshape(8, 128).T)
        common[t + "_gpost"] = bc(g(pfx + "_norm_post")[0])
    common["w_in"] = g("w_in")[0]
    common["w_out"] = g("w_out")[0]
    common["m_gpre"] = np.ascontiguousarray(g("mix_norm_pre")[0].reshape(8, 128).T)
    common["Gm"] = bc(g("mix_norm_post")[0])
    common["Gs"] = bc(g("diff_subln")[0])
    for k, n in (("lq1", "lambda_q1"), ("lk1", "lambda_k1"), ("lq2", "lambda_q2"), ("lk2", "lambda_k2")):
        common[k] = bc(g(n)[0])
    for kv, w1n, w2n, pen in (("ck", "cmp_k_w1", "cmp_k_w2", "cmp_pe_k"), ("cv", "cmp_v_w1", "cmp_v_w2", "cmp_pe_v")):
        common[kv + "_w1"] = g(w1n)[0]
        common[kv + "_w2"] = g(w2n)[0]
        peT = g(pen)[0].T
        common[kv + "_peT"] = np.ascontiguousarray(np.concatenate([peT, peT], axis=0))
    pos = np.arange(S, dtype=np.float32)
    inv = (np.float32(500000.0) ** (-np.arange(0, 16, 2, dtype=np.float32) / np.float32(16))).astype(np.float32)
    ang = (pos[:, None] * inv[None, :]).astype(np.float32)
    cs, sn = np.cos(ang).astype(np.float32), np.sin(ang).astype(np.float32)

    def tab(a):
        a = a.reshape(NT, 128, 8).transpose(1, 0, 2)
        return np.ascontiguousarray(np.broadcast_to(a[:, :, None, :], (128, NT, 8, 8)))

    common["cosR"], common["sinR"] = tab(cs), tab(sn)
    c2, s2 = tab(cs).copy(), tab(sn).copy()
    c2[:, :, 4:8, :] = 1.0
    s2[:, :, 4:8, :] = 0.0
    common["cosR2"], common["sinR2"] = c2, s2
    c = np.arange(127)[:, None] * 16
    j = np.arange(32)[None, :] * 64
    ov = np.clip(np.minimum(c + 32, j + 64) - np.maximum(c, j), 0, None) / 32.0
    common["ov"] = ov.astype(np.float32).astype(bf)
    common["ET"] = (np.arange(S)[None, :] // 64 == np.arange(32)[:, None]).astype(np.float32).astype(bf)
    tt = np.arange(S)
    cur = (tt // 64)[:, None]
    blk = np.arange(32)[None, :]
    forced = (blk == 0) | ((blk <= cur) & (blk >= cur - 1))
    causal = blk <= cur
    A = (~forced & causal).astype(np.float32)
    Bc = np.where(forced, np.float32(1e9), np.where(causal, np.float32(0.0), np.float32(-1.0))).astype(np.float32)
    common["Am"] = np.ascontiguousarray(A.reshape(NT, 128, 32).transpose(1, 0, 2))
    common["Bm"] = np.ascontiguousarray(Bc.reshape(NT, 128, 32).transpose(1, 0, 2))
    x = g("x")
    maps = []
    for c_ in range(8):
        m = dict(common)
        m["x"] = x[c_ * NB:(c_ + 1) * NB]
        maps.append(m)
    return maps


def kernel(**inputs):
    nc = build(3)
    maps = host_inputs(inputs)
    res = run_bass_kernel_spmd(nc, maps, core_ids=list(range(8)))
    return np.concatenate([np.asarray(r["out"]) for r in res.results], axis=0).astype(np.float32)
```

```python
import numpy as np
import ml_dtypes
from contextlib import ExitStack
import concourse.bass as bass
import concourse.mybir as mybir
from concourse.bass_utils import run_bass_kernel_spmd

F32 = mybir.dt.float32
BF16 = mybir.dt.bfloat16
AF = mybir.ActivationFunctionType
ALU = mybir.AluOpType
AX = mybir.AxisListType

S = 2048
D = 1024
DFF = 2816
NF = DFF // 128
NB = 2
EPS = 1e-6
ENGS = ("pe", "act", "dve", "pool", "sp")


import re
PSUM_RE = re.compile(r"^(tr\d|gu\d|dn\d|pj\d|tq\d|pbps|hps\d|cps|sps\d|ab\d|tps)")


class Res:
    __slots__ = ("name", "w", "r")

    def __init__(self, name):
        self.name = name
        self.w = None
        self.r = {}


class Op:
    __slots__ = ("eng", "fn", "deps", "kind", "key", "val", "idx", "sig", "waits", "ordinal")


class Prog:
    def __init__(self, nc, stack):
        self.nc = nc
        self.stack = stack
        self.res = {}
        self.ops = []
        self.eng_n = {e: 0 for e in ENGS}
        self.sigbase = {e: 0 for e in ENGS}
        self.esem = {e: stack.enter_context(nc.semaphore("sem_" + e)) for e in ENGS if e != "sp"}
        self.dsem = {}
        self.dcount = {}
        self.free_sems = {"sw": [], "hw": []}
        self.dcls = {}
        self.live = []
        self.seen = {e: {} for e in ENGS}
        self.sigord = {e: {} for e in ENGS}

    def R(self, name):
        r = self.res.get(name)
        if r is None:
            r = self.res[name] = Res(name)
        return r

    def _deps(self, reads, writes):
        deps = []
        for n in reads:
            r = self.R(n)
            if r.w is not None:
                deps.append(r.w)
        for n in writes:
            r = self.R(n)
            if r.w is not None:
                deps.append(r.w)
            deps.extend(r.r.values())
        return deps

    def _mark(self, reads, writes, ev, rkey):
        for n in reads:
            self.R(n).r[rkey] = ev
        for n in writes:
            r = self.R(n)
            r.w = ev
            r.r = {}

    def op(self, eng, fn, r=(), w=()):
        o = Op()
        o.eng = eng
        o.fn = fn
        o.kind = "c"
        o.deps = self._deps(r, w)
        for n in r:
            if PSUM_RE.match(n):
                o.deps.extend(ev for k, ev in self.R(n).r.items() if k != eng)
        o.idx = self.eng_n[eng]
        self.eng_n[eng] += 1
        o.sig = False
        self._mark(r, w, ("c", eng, o.idx), eng)
        self.ops.append(o)
        return o

    def dma(self, q, fn, key, r=(), w=()):
        o = Op()
        o.eng = q
        o.fn = fn
        o.kind = "d"
        o.key = key
        if key not in self.dsem:
            cls = "sw" if q == "pool" else "hw"
            self.dcls[key] = cls
            if self.free_sems[cls]:
                self.dsem[key], self.dcount[key] = self.free_sems[cls].pop()
            else:
                self.dsem[key] = self.stack.enter_context(self.nc.semaphore("dsem%d" % len(self.dsem)))
                self.dcount[key] = 0
            self.live.append(key)
        o.deps = self._deps(r, w)
        self.dcount[key] += 16
        o.val = self.dcount[key]
        o.idx = self.eng_n[q]
        self.eng_n[q] += 1
        o.sig = False
        self._mark(r, w, ("d", key, o.val), "d_" + key)
        self.ops.append(o)
        return o

    def barrier_all(self):
        allres = list(self.res.keys())
        for e in ENGS:
            self.op(e, None, r=allres)
        self.res = {}
        for k in self.live:
            self.free_sems[self.dcls[k]].append((self.dsem[k], self.dcount[k]))
        self.live = []

    def flush(self):
        nc = self.nc
        ops = self.ops
        self.ops = []
        self.nflush = getattr(self, "nflush", 0) + 1
        if LIMIT is not None and self.nflush == LIMIT[0]:
            ops = ops[:LIMIT[1]]
        byidx = {}
        for o in ops:
            if o.kind == "c":
                byidx[(o.eng, o.idx)] = o
        for o in ops:
            o.waits = []
            seen = self.seen[o.eng]
            best = {}
            for d in o.deps:
                if d[0] == "c":
                    if d[1] == "pe" and o.eng == "pe":
                        continue
                    k = ("c", d[1])
                else:
                    k = ("d", d[1])
                if k not in best or d[2] > best[k][2]:
                    best[k] = d
            for d in best.values():
                if d[0] == "c":
                    _, pe, pi = d
                    if seen.get(pe, -1) >= pi:
                        continue
                    prod = byidx.get((pe, pi))
                    if prod is None:
                        assert pi in self.sigord[pe], (pe, pi)
                    else:
                        prod.sig = True
                    seen[pe] = pi
                    o.waits.append(d)
                else:
                    _, key, val = d
                    k = "d_" + key
                    if seen.get(k, 0) >= val:
                        continue
                    seen[k] = val
                    o.waits.append(d)
        last = {}
        for o in ops:
            if o.kind == "c":
                last[o.eng] = o
        for o in last.values():
            o.sig = True
        for o in ops:
            if o.kind == "c" and o.sig:
                self.sigbase[o.eng] += 1
                self.sigord[o.eng][o.idx] = self.sigbase[o.eng]
        per = {e: [o for o in ops if o.eng == e] for e in ENGS}

        def emit(eng_name, eng):
            for o in per[eng_name]:
                for d in o.waits:
                    if d[0] == "c":
                        eng.wait_ge(self.esem[d[1]], self.sigord[d[1]][d[2]])
                    else:
                        eng.wait_ge(self.dsem[d[1]], d[2])
                if o.fn is None:
                    if o.kind == "c" and o.sig:
                        eng.nop().then_inc(self.esem[eng_name], 1) if eng_name != "sp" else None
                    continue
                ins = o.fn(eng)
                if o.kind == "d":
                    ins.then_inc(self.dsem[o.key], 16)
                elif o.sig:
                    ins.then_inc(self.esem[eng_name], 1)

        with nc.Block() as block:
            @block.tensor
            def _(e):
                emit("pe", e)

            @block.scalar
            def _(e):
                emit("act", e)

            @block.vector
            def _(e):
                emit("dve", e)

            @block.gpsimd
            def _(e):
                emit("pool", e)

            @block.sync
            def _(e):
                emit("sp", e)


def ffn_phase(nc, P, tag, src, dst, wg_d, wu_d, wd_d, gpre_d, gpost_d, ident_d):
    with ExitStack() as st:
        def sb(name, shape, dt):
            return st.enter_context(nc.sbuf_tensor(tag + name, shape, dt))

        def ps(name, shape, dt):
            return st.enter_context(nc.psum_tensor(tag + name, shape, dt))

        Wg = sb("Wg", [128, 8, DFF], BF16)
        Wu = sb("Wu", [128, 8, DFF], BF16)
        Wd = sb("Wd", [128, NF, D], BF16)
        xin = [sb("xin%d" % i, [128, D], F32) for i in range(2)]
        hn = sb("hn", [128, 4, D], BF16)
        hT = sb("hT", [128, 8, 512], BF16)
        actT = sb("actT", [128, NF, 512], BF16)
        G = sb("G", [128, D], F32)
        gpre = sb("gpre", [128, 8], F32)
        ident = sb("ident", [128, 128], BF16)
        xres = [sb("xres%d" % i, [128, D], F32) for i in range(2)]
        ytmp = [sb("ytmp%d" % i, [128, D], F32) for i in range(2)]
        sg = [sb("sg%d" % i, [128, 512], F32) for i in range(2)]
        ss = sb("ss", [128, 8], F32)
        rstd = sb("rstd", [128, 8], F32)
        ss2 = sb("ss2", [128, 2], F32)
        rstd2 = sb("rstd2", [128, 2], F32)
        epsb = sb("epsb", [128, 1], F32)
        P.op("pool", lambda e: e.memset(epsb[:], EPS), w=["epsb"])
        tr = [ps("tr%d" % i, [128, 2, 512], BF16) for i in range(2)]
        gu = [ps("gu%d" % i, [128, 512], F32) for i in range(4)]
        dn = ps("dn", [128, D], F32)

        P.dma("sp", lambda e: e.dma_start(out=gpre[:], in_=gpre_d), tag + "c0", w=["gpre"])
        P.dma("sp", lambda e: e.dma_start(out=G[:], in_=gpost_d), tag + "c1", w=["G"])
        P.dma("sp", lambda e: e.dma_start(out=ident[:], in_=ident_d), tag + "c2", w=["ident"])
        P.op("dve", lambda e: e.tensor_scalar(out=G[:], in0=G[:], scalar1=0.5, scalar2=None, op0=ALU.mult),
             r=["G"], w=["G"])
        wg_v = wg_d.rearrange("(k p) f -> p k f", p=128)
        wu_v = wu_d.rearrange("(k p) f -> p k f", p=128)
        wd_v = wd_d.rearrange("(f p) d -> p f d", p=128)
        FG = [(0, 2), (2, 6), (6, 10), (10, 14), (14, 18), (18, 22)]
        fgrp = {}
        for gi, (a, b) in enumerate(FG):
            for f in range(a, b):
                fgrp[f] = gi
            for nm, W, v in (("Wg", Wg, wg_v), ("Wu", Wu, wu_v)):
                for kh in range(2):
                    P.dma("pool",
                          lambda e, W=W, v=v, a=a, b=b, kh=kh: e.dma_start(
                              out=W[:, kh * 4:(kh + 1) * 4, a * 128:b * 128],
                              in_=v[:, kh * 4:(kh + 1) * 4, a * 128:b * 128]),
                          "%s%s%d" % (tag, nm, gi), w=["%s%d" % (nm, gi)])
        for gi, (a, b) in enumerate(FG):
            P.dma("pool", lambda e, a=a, b=b: e.dma_start(out=Wd[:, a:b, :], in_=wd_v[:, a:b, :]),
                  "%sWd%d" % (tag, gi), w=["Wd%d" % gi])

        blocks = [(b, i) for b in range(NB) for i in range(4)]
        nxi = [0]

        def emit_N(bi, j):
            b, i = blocks[bi]
            t0 = i * 512 + j * 128
            xb = xin[nxi[0] % 2]
            xn = "xin%d" % (nxi[0] % 2)
            nxi[0] += 1
            P.dma("sp", lambda e: e.dma_start(out=xb[:], in_=src[b, t0:t0 + 128, :]), tag + xn, w=[xn])
            P.op("act", lambda e: e.activation(out=hn[:, j, :], in_=xb[:], func=AF.Square,
                                               accum_out=ss[:, j:j + 1]),
                 r=[xn], w=["hn%d" % j, "ss%d" % j])
            P.op("act", lambda e: e.activation(out=ss[:, j:j + 1], in_=ss[:, j:j + 1], func=AF.Sqrt,
                                               scale=1.0 / D, bias=epsb[:, 0:1]),
                 r=["ss%d" % j, "epsb"], w=["ss%d" % j])
            P.op("dve", lambda e: e.reciprocal(out=rstd[:, j:j + 1], in_=ss[:, j:j + 1]),
                 r=["ss%d" % j], w=["rstd%d" % j])
            P.op("act", lambda e: e.activation(out=hn[:, j, :], in_=xb[:], func=AF.Copy,
                                               scale=rstd[:, j:j + 1]),
                 r=[xn, "rstd%d" % j], w=["hn%d" % j])

        def emit_T(bi):
            for kp in range(4):
                bank = kp % 2
                for kk in range(2):
                    k = kp * 2 + kk
                    for j in range(4):
                        P.op("pe", lambda e, k=k, kk=kk, j=j, bank=bank: e.transpose(
                            tr[bank][:, kk, j * 128:(j + 1) * 128], hn[:, j, k * 128:(k + 1) * 128], ident[:]),
                             r=["hn%d" % j, "ident"], w=["tr%d" % bank])
                for kk in range(2):
                    k = kp * 2 + kk
                    P.op("dve", lambda e, k=k, kk=kk, bank=bank: e.tensor_scalar(
                        out=hT[:, k, :], in0=tr[bank][:, kk, :], scalar1=gpre[:, k:k + 1], scalar2=None,
                        op0=ALU.mult),
                         r=["tr%d" % bank, "gpre"], w=["hT%d" % k])

        gui = [0]

        def emit_GU_f(bi, f):
            pr = gui[0] % 2
            gui[0] += 1
            pg, pu = gu[2 * pr], gu[2 * pr + 1]
            gi = fgrp[f]
            for nm, W, pt in (("Wg", Wg, pg), ("Wu", Wu, pu)):
                pn = "gu%d%s" % (pr, nm)
                for k in range(8):
                    P.op("pe", lambda e, W=W, pt=pt, k=k: e.matmul(
                        pt[:], W[:, k, f * 128:(f + 1) * 128], hT[:, k, :], start=(k == 0), stop=(k == 7)),
                         r=["%s%d" % (nm, gi), "hT%d" % k], w=[pn])
            s = sg[pr]
            P.op("act", lambda e: e.activation(out=s[:], in_=pg[:], func=AF.Silu),
                 r=["gu%dWg" % pr], w=["sg%d" % pr])
            P.op("dve", lambda e: e.tensor_tensor(out=actT[:, f, :], in0=s[:], in1=pu[:], op=ALU.mult),
                 r=["sg%d" % pr, "gu%dWu" % pr], w=["actT%d" % f])

        dni = [0]

        def emit_D(bi):
            b, i = blocks[bi]
            for j in range(4):
                t0 = i * 512 + j * 128
                q = dni[0] % 2
                dni[0] += 1
                xr, yt = xres[q], ytmp[q]
                xrn, ytn = "xres%d" % q, "ytmp%d" % q
                P.dma("act", lambda e, xr=xr, t0=t0: e.dma_start(out=xr[:], in_=src[b, t0:t0 + 128, :]),
                      tag + xrn + "i", w=[xrn])
                for n in range(2):
                    for f in range(NF):
                        P.op("pe", lambda e, n=n, f=f, j=j: e.matmul(
                            dn[:, n * 512:(n + 1) * 512], actT[:, f, j * 128:(j + 1) * 128],
                            Wd[:, f, n * 512:(n + 1) * 512], start=(f == 0), stop=(f == NF - 1)),
                             r=["actT%d" % f, "Wd%d" % fgrp[f]], w=["dn%d" % n])
                P.op("act", lambda e, yt=yt, q=q: e.activation(out=yt[:], in_=dn[:], func=AF.Square,
                                                             accum_out=ss2[:, q:q + 1]),
                     r=["dn0", "dn1"], w=[ytn, "ss2%d" % q])
                P.op("act", lambda e, q=q: e.activation(out=ss2[:, q:q + 1], in_=ss2[:, q:q + 1], func=AF.Sqrt,
                                                        scale=1.0 / D, bias=epsb[:, 0:1]),
                     r=["ss2%d" % q, "epsb"], w=["ss2%d" % q])
                P.op("dve", lambda e, q=q: e.reciprocal(out=rstd2[:, q:q + 1], in_=ss2[:, q:q + 1]),
                     r=["ss2%d" % q], w=["rstd2%d" % q])
                P.op("dve", lambda e, yt=yt, q=q: e.scalar_tensor_tensor(
                    out=yt[:], in0=dn[:], scalar=rstd2[:, q:q + 1], in1=G[:], op0=ALU.mult, op1=ALU.mult),
                     r=["dn0", "dn1", "rstd2%d" % q, "G"], w=[ytn])
                P.op("pool", lambda e, xr=xr, yt=yt: e.tensor_tensor(out=xr[:], in0=xr[:], in1=yt[:],
                                                                     op=ALU.add),
                     r=[ytn, xrn], w=[xrn])
                P.dma("sp", lambda e, xr=xr, t0=t0: e.dma_start(out=dst[b, t0:t0 + 128, :], in_=xr[:]),
                      tag + xrn + "o", r=[xrn], w=["dst_%s_%d_%d" % (tag, b, t0)])

        nblk = len(blocks)
        for j in range(4):
            emit_N(0, j)
        emit_T(0)
        for bi in range(nblk):
            for f in range(NF):
                emit_GU_f(bi, f)
                if bi + 1 < nblk and f in (3, 8, 13, 18):
                    emit_N(bi + 1, (f - 3) // 5)
            if bi + 1 < nblk:
                emit_T(bi + 1)
            emit_D(bi)
        P.barrier_all()
        P.flush()


NT = S // 128
LIMIT = None
NSEQ = NB
SUB = 9
CUT = 9
BARQT = False
EVAC_ACT = False
BIG = 30000.0
C_DQ, C_DK, C_DV, C_NQ, C_KC, C_VC, C_KS, C_VS, C_KW, C_VW, C_GL, C_END = (
    0, 512, 1024, 1536, 2048, 2176, 2304, 2432, 2560, 2688, 2816, 2840)
S_DQ, S_DK, S_NQ, S_KS, S_KW, S_KC, S_VC, S_END = 0, 512, 1024, 1536, 1664, 1792, 1920, 2048


class Rot:
    def __init__(self, items, names):
        self.items, self.names, self.i = items, names, 0

    def next(self):
        k = self.i % len(self.items)
        self.i += 1
        return self.items[k], self.names[k]


def mixer_phase(nc, P, src, dst, dd):
    TB = 2
    with ExitStack() as st0:
        def sb0(name, shape, dt):
            return st0.enter_context(nc.sbuf_tensor("B" + name, shape, dt))

        dqT = sb0("dqT", [128, 4, S], BF16)
        dkT = sb0("dkT", [128, 4, S], BF16)
        Qg = [sb0("Qg%d" % g, [128, 4, S], BF16) for g in range(2)]
        KSg = [sb0("KSg%d" % g, [128, S], BF16) for g in range(2)]
        KWg = [sb0("KWg%d" % g, [128, S], BF16) for g in range(2)]
        dva = sb0("dva", [128, NT, 4, 130], BF16)
        vsa = sb0("vsa", [128, NT, 2, 66], BF16)
        vwa = sb0("vwa", [128, NT, 2, 66], BF16)
        glr = sb0("glr", [128, NT, 24], F32)
        kcc = [sb0("kcc%d" % g, [128, 128], BF16) for g in range(2)]
        vca = sb0("vca", [128, 2, 98], BF16)
        ident = sb0("ident", [128, 128], BF16)
        epsb = sb0("epsb", [128, 1], F32)

        P.dma("sp", lambda e: e.dma_start(out=ident[:], in_=dd["ident"]), "Bc_ident", w=["ident"])
        P.op("pool", lambda e: e.memset(epsb[:], EPS), w=["epsb"])
        P.op("pool", lambda e: e.memset(dva[:, :, :, 128:130], 1.0), w=["dva"])
        P.op("pool", lambda e: e.memset(vsa[:, :, :, 64:66], 1.0), w=["vsa"])
        P.op("pool", lambda e: e.memset(vwa[:, :, :, 64:66], 1.0), w=["vwa"])
        P.op("pool", lambda e: e.memset(vca[:], 0.0), w=["vca"])
        P.op("pool", lambda e: e.memset(vca[:, :, 64:65], 1.0), r=["vca"], w=["vca"])
        for g in range(2):
            P.dma("sp", lambda e, g=g: e.dma_start(out=vca[0:127, g, 65:97], in_=dd["ov"]), "Bc_ov%d" % g,
                  r=["vca"], w=["vca_ov%d" % g])
            P.dma("sp", lambda e, g=g: e.dma_start(out=KSg[g][64:96, :], in_=dd["ET"]), "Bc_ET%d" % g,
                  w=["KSgE%d" % g])

        for b in range(NSEQ):
            with ExitStack() as st1:
                kcT = st1.enter_context(nc.sbuf_tensor("BkcT%d" % b, [128, S], BF16))
                vcT = st1.enter_context(nc.sbuf_tensor("BvcT%d" % b, [128, S], BF16))
                mixer_proj(nc, P, b, TB, src, dd, dict(dqT=dqT, dkT=dkT, Qg=Qg, KSg=KSg, KWg=KWg, dva=dva,
                                                      vsa=vsa, vwa=vwa, glr=glr, kcT=kcT, vcT=vcT,
                                                      ident=ident, epsb=epsb))
                if SUB >= 2:
                    mixer_compress(nc, P, b, dd, dict(kcT=kcT, vcT=vcT, kcc=kcc, vca=vca))
            if SUB >= 3:
                mixer_attn(nc, P, b, src, dst, dd, dict(dqT=dqT, dkT=dkT, Qg=Qg, KSg=KSg, KWg=KWg, dva=dva,
                                                   vsa=vsa, vwa=vwa, glr=glr, kcc=kcc, vca=vca,
                                                   ident=ident, epsb=epsb))


def mixer_proj(nc, P, b, TB, src, dd, t):
    tag = "P%d" % b
    dqT, dkT, Qg, KSg, KWg = t["dqT"], t["dkT"], t["Qg"], t["KSg"], t["KWg"]
    dva, vsa, vwa, glr, kcT, vcT, ident, epsb = (t["dva"], t["vsa"], t["vwa"], t["glr"], t["kcT"], t["vcT"],
                                                 t["ident"], t["epsb"])
    with ExitStack() as st:
        def sb(name, shape, dt):
            return st.enter_context(nc.sbuf_tensor(tag + name, shape, dt))

        def ps(name, shape, dt):
            return st.enter_context(nc.psum_tensor(tag + name, shape, dt))

        rt = [[sb("rt%d_%d" % (i, q), [128, 8, 8], F32) for q in range(4)] for i in range(2)]
        xs = [sb("xs%d" % i, [128, 512], F32) for i in range(2)]
        xin = [sb("xin%d" % i, [128, D], F32) for i in range(2)]
        hn = sb("hn", [128, TB, D], BF16)
        hT = sb("hT", [128, 8, TB * 128], BF16)
        stg = sb("stg", [128, TB, S_END], BF16)
        cosR = sb("cosR", [128, NT, 8, 8], F32)
        sinR = sb("sinR", [128, NT, 8, 8], F32)
        gpre = sb("gpre", [128, 8], F32)
        ss = sb("ss", [128, TB], F32)
        rstd = sb("rstd", [128, TB], F32)
        Win = sb("Win", [128, 8, 2048], BF16)
        tr = [ps("tr%d" % i, [128, 1024], BF16) for i in range(2)]
        pj = [ps("pj%d" % i, [128, 512], F32) for i in range(3)]
        tq = [ps("tq%d" % i, [128, 1024], BF16) for i in range(2)]

        P.dma("sp", lambda e: e.dma_start(out=gpre[:], in_=dd["m_gpre"]), tag + "gpre", w=["gpre"])
        P.dma("sp", lambda e: e.dma_start(out=cosR[:], in_=dd["cosR"]), tag + "cos", w=["cosR"])
        P.dma("sp", lambda e: e.dma_start(out=sinR[:], in_=dd["sinR"]), tag + "sin", w=["sinR"])
        win_v = dd["w_in"].rearrange("(k p) f -> p k f", p=128)
        CB = [(0, 512), (512, 1024), (1024, 1536), (1536, 2048), (2048, 2560), (2560, C_END)]
        SEC = [(C_KS, 128), (C_KW, 128), (C_KC, 128), (C_VC, 128), (C_VS, 128), (C_VW, 128), (C_GL, 24)]
        def load_pass(ph):
            if ph == 0:
                for ci, (c0, c1) in enumerate(CB[:4]):
                    for kh in range(2):
                        P.dma("pool", lambda e, c0=c0, c1=c1, kh=kh: e.dma_start(
                            out=Win[:, kh * 4:(kh + 1) * 4, c0:c1], in_=win_v[:, kh * 4:(kh + 1) * 4, c0:c1]),
                              "%sWin%d" % (tag, ci), w=["Win%d" % ci])
            else:
                dcol = 0
                for si, (sc, sw) in enumerate(SEC):
                    ci = 4 if si < 4 else 5
                    P.dma("pool", lambda e, sc=sc, sw=sw, dcol=dcol: e.dma_start(
                        out=Win[:, :, dcol:dcol + sw], in_=win_v[:, :, sc:sc + sw]),
                          "%sWin%d" % (tag, ci), w=["Win%d" % ci] + (["Win0", "Win1"] if si == 0 else []))
                    dcol += sw
                P.dma("sp", lambda e: e.dma_start(out=cosR[:], in_=dd["cosR2"]), tag + "cos", w=["cosR"])
                P.dma("sp", lambda e: e.dma_start(out=sinR[:], in_=dd["sinR2"]), tag + "sin", w=["sinR"])

        PH = [0]
        nxi = [0]
        pji = [0]
        rti = [0]
        tqi = [0]
        evi = [0]

        def emit_N(i, j):
            t0 = (i * TB + j) * 128
            q = nxi[0] % 2
            nxi[0] += 1
            xb, xn = xin[q], "xin%d" % q
            P.dma("sp", lambda e: e.dma_start(out=xb[:], in_=src[b, t0:t0 + 128, :]), tag + xn, w=[xn])
            P.op("act", lambda e: e.activation(out=hn[:, j, :], in_=xb[:], func=AF.Square,
                                               accum_out=ss[:, j:j + 1]),
                 r=[xn], w=["hn%d" % j, "ss%d" % j])
            P.op("act", lambda e: e.activation(out=ss[:, j:j + 1], in_=ss[:, j:j + 1], func=AF.Sqrt,
                                               scale=1.0 / D, bias=epsb[:, 0:1]),
                 r=["ss%d" % j, "epsb"], w=["ss%d" % j])
            P.op("dve", lambda e: e.reciprocal(out=rstd[:, j:j + 1], in_=ss[:, j:j + 1]),
                 r=["ss%d" % j], w=["rstd%d" % j])
            P.op("act", lambda e: e.activation(out=hn[:, j, :], in_=xb[:], func=AF.Copy,
                                               scale=rstd[:, j:j + 1]),
                 r=[xn, "rstd%d" % j], w=["hn%d" % j])

        def emit_T(i):
            W = TB * 128
            for kq in range(2):
                bank = kq
                for kk in range(4):
                    k = kq * 4 + kk
                    for j in range(TB):
                        P.op("pe", lambda e, k=k, kk=kk, j=j, bank=bank: e.transpose(
                            tr[bank][:, kk * W + j * 128:kk * W + (j + 1) * 128],
                            hn[:, j, k * 128:(k + 1) * 128], ident[:]),
                             r=["hn%d" % j, "ident"], w=["tr%d" % bank])
                for kk in range(4):
                    k = kq * 4 + kk
                    P.op("dve", lambda e, k=k, kk=kk, bank=bank: e.tensor_scalar(
                        out=hT[:, k, :], in0=tr[bank][:, kk * W:(kk + 1) * W], scalar1=gpre[:, k:k + 1],
                        scalar2=None, op0=ALU.mult),
                         r=["tr%d" % bank, "gpre"], w=["hT%d" % k])

        def rope(pjt, pjn, o, nh, j, so, T, tabs=None):
            q = rti[0] % 2
            rti[0] += 1
            ta, tb_, tc, td = rt[q]
            rn = ["rt%d_%d" % (q, x) for x in range(4)]
            xst, xsn = xs[q], "xs%d" % q
            if (CUT == 4.26 and nh == 2) or (CUT == 4.28 and tabs is not None):
                sres = "stg%d_%d" % (j, so)
                P.op("act", lambda e: e.activation(out=stg[:, j, so:so + nh * 64], in_=pjt[:, o:o + nh * 64],
                                                   func=AF.Copy), r=[pjn], w=[sres + "a"])
                return [sres + "a"]
            P.op("act", lambda e: e.activation(out=xst[:, 0:nh * 64], in_=pjt[:, o:o + nh * 64], func=AF.Copy),
                 r=[pjn], w=[xsn])
            pv = xst[:, 0:nh * 64].rearrange("p (h d) -> p h d", d=64)
            sv = stg[:, j, so:so + nh * 64].rearrange("p (h d) -> p h d", d=64)
            x1, x2 = pv[:, :, 0:8], pv[:, :, 8:16]
            ct_, st_, ctn, stn = (cosR, sinR, "cosR", "sinR") if tabs is None else tabs
            cs, sn = ct_[:, T, 0:nh, :], st_[:, T, 0:nh, :]
            sres = "stg%d_%d" % (j, so)
            if CUT < 3.06:
                return [sres + "a", sres + "b", sres + "c"]
            P.op("dve", lambda e: e.tensor_tensor(out=ta[:, 0:nh, :], in0=x1, in1=cs, op=ALU.mult),
                 r=[xsn, ctn], w=[rn[0]])
            if CUT < 3.07:
                return [sres + "a", sres + "b", sres + "c"]
            P.op("dve", lambda e: e.tensor_tensor(out=tb_[:, 0:nh, :], in0=x2, in1=sn, op=ALU.mult),
                 r=[xsn, stn], w=[rn[1]])
            if CUT < 3.08:
                return [sres + "a", sres + "b", sres + "c"]
            P.op("dve", lambda e: e.tensor_tensor(out=tc[:, 0:nh, :], in0=x2, in1=cs, op=ALU.mult),
                 r=[xsn, ctn], w=[rn[2]])
            if CUT == 3.095:
                P.op("dve", lambda e: e.tensor_tensor(out=tc[:, 0:nh, :], in0=x1, in1=sn, op=ALU.mult),
                     r=[xsn, "sinR"], w=[rn[2]])
                return [sres + "a", sres + "b", sres + "c"]
            P.op("dve", lambda e: e.tensor_tensor(out=td[:, 0:nh, :], in0=x1, in1=sn, op=ALU.mult),
                 r=[xsn, stn], w=[rn[3]])
            P.op("dve", lambda e: e.tensor_tensor(out=pv[:, :, 0:8], in0=ta[:, 0:nh, :], in1=tb_[:, 0:nh, :],
                                                  op=ALU.subtract),
                 r=[rn[0], rn[1], rn[2], rn[3], xsn], w=[xsn])
            P.op("dve", lambda e: e.tensor_tensor(out=pv[:, :, 8:16], in0=tc[:, 0:nh, :], in1=td[:, 0:nh, :],
                                                  op=ALU.add),
                 r=[rn[2], rn[3], xsn], w=[xsn])
            P.op("act", lambda e: e.activation(out=stg[:, j, so:so + nh * 64], in_=xst[:, 0:nh * 64], func=AF.Copy),
                 r=[xsn], w=[sres + "a"])
            return [sres + "a"]

        def emit_proj(i, j, stres):
            T = i * TB + j
            for ci, (c0, c1) in enumerate(CB):
                if (ci < 4) != (PH[0] == 0):
                    continue
                if ci >= 4:
                    c0, c1 = c0 - 2048, c1 - 2048
                q = pji[0] % 3
                pji[0] += 1
                pjt, pjn = pj[q], "pj%d" % q
                ncol = c1 - c0
                for k in range(8):
                    P.op("pe", lambda e, k=k, pjt=pjt, c0=c0, c1=c1, ncol=ncol: e.matmul(
                        pjt[:, 0:ncol], hT[:, k, j * 128:(j + 1) * 128], Win[:, k, c0:c1],
                        start=(k == 0), stop=(k == 7)),
                         r=["hT%d" % k, "Win%d" % ci], w=[pjn])
                if ci == 0:
                    stres["dq"][j] = rope(pjt, pjn, 0, 8, j, S_DQ, T)
                elif ci == 1:
                    stres["dk"][j] = rope(pjt, pjn, 0, 8, j, S_DK, T)
                elif ci == 2:
                    P.op("act", lambda e, pjt=pjt, T=T: e.activation(
                        out=dva[:, T, :, 0:128], in_=pjt[:, 0:512].rearrange("p (h d) -> p h d", d=128),
                        func=AF.Copy), r=[pjn], w=["dva%d" % T])
                elif ci == 3:
                    stres["nq"][j] = rope(pjt, pjn, 0, 8, j, S_NQ, T)
                elif ci == 4:
                    rr = rope(pjt, pjn, 0, 8, j, S_KS, T)
                    stres["ks"][j] = rr
                    stres["kw"][j] = rr
                    stres["kcvc"][j] = rr
                else:
                    P.op("act", lambda e, pjt=pjt, T=T: e.activation(
                        out=vsa[:, T, :, 0:64], in_=pjt[:, 0:128].rearrange("p (h d) -> p h d", d=64),
                        func=AF.Copy), r=[pjn], w=["vsa%d" % T])
                    P.op("act", lambda e, pjt=pjt, T=T: e.activation(
                        out=vwa[:, T, :, 0:64], in_=pjt[:, 128:256].rearrange("p (h d) -> p h d", d=64),
                        func=AF.Copy), r=[pjn], w=["vwa%d" % T])
                    P.op("dve", lambda e, pjt=pjt, T=T: e.tensor_copy(out=glr[:, T, :], in_=pjt[:, 256:280]),
                         r=[pjn], w=["glr%d" % T])

        def emit_QT(i, stres):
            tk0 = i * TB * 128
            W = TB * 128
            units = []
            for h in range(4):
                units.append((S_DQ + h * 128, 128, dqT[:, h, tk0:tk0 + W], "dqT", "dq"))
            for h in range(4):
                units.append((S_DK + h * 128, 128, dkT[:, h, tk0:tk0 + W], "dkT", "dk"))
            for n in range(8):
                units.append((S_NQ + n * 64, 64, Qg[n // 4][0:64, n % 4, tk0:tk0 + W], "Qq%d" % (n // 4), "nq"))
            for g in range(2):
                units.append((S_KS + g * 64, 64, KSg[g][0:64, tk0:tk0 + W], "KSq%d" % g, "ks"))
            for g in range(2):
                units.append((S_KW + g * 64, 64, KWg[g][0:64, tk0:tk0 + W], "KWq%d" % g, "kw"))
            units.append((S_KC, 128, kcT[:, tk0:tk0 + W], "kcT", "kcvc"))
            units.append((S_VC, 128, vcT[:, tk0:tk0 + W], "vcT", "kcvc"))
            units = units[0:16] if PH[0] == 0 else units[16:22]
            if CUT == 9:
                pass
            elif CUT == 4.23:
                units = [(S_NQ + g * 64, 64, KSg[g][0:64, tk0:tk0 + W], "KSq%d" % g, "nq") for g in range(2)]
            elif CUT == 4.24:
                units = [(S_KS + g * 64, 64, Qg[g][0:64, 0, tk0:tk0 + W], "Qq%d" % g, "ks") for g in range(2)]
            elif CUT in (4.21, 4.26, 4.28):
                units = units[16:18]
            elif CUT == 4.22:
                units = units[18:20]
            elif CUT < 4.1:
                units = units[0:8]
            elif CUT < 4.2:
                units = units[0:16]
            elif CUT < 4.3:
                units = units[0:20]
            for u0 in range(0, len(units), 4):
                bank = tqi[0] % 2
                tqi[0] += 1
                grp = units[u0:u0 + 4]
                for ui, (so, ncol, dstap, dres, skey) in enumerate(grp):
                    for j in range(TB):
                        P.op("pe", lambda e, ui=ui, j=j, so=so, ncol=ncol, bank=bank: e.transpose(
                            tq[bank][0:ncol, ui * W + j * 128:ui * W + (j + 1) * 128],
                            stg[:, j, so:so + ncol], ident[:]),
                             r=stres[skey][j] + ["ident"], w=["tq%d" % bank])
                eng = "act" if (evi[0] % 2 == 0 or EVAC_ACT) else "dve"
                evi[0] += 1
                for ui, (so, ncol, dstap, dres, skey) in enumerate(grp):
                    dres = "%s_u%d" % (dres, u0 + ui)
                    if eng == "act":
                        P.op("act", lambda e, ui=ui, ncol=ncol, dstap=dstap, bank=bank: e.activation(
                            out=dstap, in_=tq[bank][0:ncol, ui * W:(ui + 1) * W], func=AF.Copy),
                             r=["tq%d" % bank], w=["%s_%d" % (dres, i)])
                    else:
                        P.op("dve", lambda e, ui=ui, ncol=ncol, dstap=dstap, bank=bank: e.tensor_copy(
                            out=dstap, in_=tq[bank][0:ncol, ui * W:(ui + 1) * W]),
                             r=["tq%d" % bank], w=["%s_%d" % (dres, i)])

        nblk = NT // TB
        for ph in range(2):
            PH[0] = ph
            load_pass(ph)
            for j in range(TB):
                emit_N(0, j)
            emit_T(0)
            for i in range(nblk):
                stres = {k: [None] * TB for k in ("dq", "dk", "nq", "kcvc", "ks", "kw")}
                for j in range(TB):
                    emit_proj(i, j, stres)
                    if i + 1 < nblk:
                        emit_N(i + 1, j)
                emit_QT(i, stres)
                if i + 1 < nblk:
                    emit_T(i + 1)
        P.barrier_all()
        P.flush()


def mixer_compress(nc, P, b, dd, t):
    tag = "Z%d" % b
    kcT, vcT, kcc, vca = t["kcT"], t["vcT"], t["kcc"], t["vca"]
    NCB = 127
    with ExitStack() as st:
        def sb(name, shape, dt):
            return st.enter_context(nc.sbuf_tensor(tag + name, shape, dt))

        def ps(name, shape, dt):
            return st.enter_context(nc.psum_tensor(tag + name, shape, dt))

        W1 = [sb("W1_%d" % kv, [128, 32, 256], BF16) for kv in range(2)]
        W2 = [sb("W2_%d" % kv, [128, 2, 64], BF16) for kv in range(2)]
        peT = [sb("peT%d" % kv, [128, 32], BF16) for kv in range(2)]
        pb = sb("pb", [128, 4], F32)
        xh = sb("xh", [128, 2, 128], F32)
        u = sb("u", [128, 2, 128], F32)
        sg = sb("sg", [128, 2, 128], F32)
        hact = sb("hact", [128, 2, 128], BF16)
        pbps = ps("pbps", [128, 4], F32)
        hps = [ps("hps%d" % i, [128, 2, 128], F32) for i in range(2)]
        cps = ps("cps", [128, 128], F32)

        for kv, (w1n, w2n, pen) in enumerate((("ck_w1", "ck_w2", "ck_peT"), ("cv_w1", "cv_w2", "cv_peT"))):
            w1v = dd[w1n].rearrange("(l d) h -> d l h", d=64)
            for half in range(2):
                P.dma("pool", lambda e, kv=kv, half=half, w1v=w1v: e.dma_start(
                    out=W1[kv][half * 64:(half + 1) * 64, :, :], in_=w1v),
                      "%sW1_%d" % (tag, kv), w=["W1_%d" % kv])
            P.dma("pool", lambda e, kv=kv, w2n=w2n: e.dma_start(
                out=W2[kv][:], in_=dd[w2n].rearrange("(c p) d -> p c d", p=128)),
                  "%sW2_%d" % (tag, kv), w=["W2_%d" % kv])
            P.dma("pool", lambda e, kv=kv, pen=pen: e.dma_start(out=peT[kv][:], in_=dd[pen]),
                  "%spe_%d" % (tag, kv), w=["peT%d" % kv])
        for kv in range(2):
            for ch in range(2):
                col = kv * 2 + ch
                for l in range(32):
                    P.op("pe", lambda e, kv=kv, ch=ch, l=l, col=col: e.matmul(
                        pbps[:, col:col + 1], W1[kv][0:64, l, ch * 128:(ch + 1) * 128], peT[kv][0:64, l:l + 1],
                        start=(l == 0), stop=(l == 31)),
                         r=["W1_%d" % kv, "peT%d" % kv], w=["pbps"])
        P.op("dve", lambda e: e.tensor_copy(out=pb[:], in_=pbps[:]), r=["pbps"], w=["pb"])
        hi = [0]
        for kv in range(2):
            xT = kcT if kv == 0 else vcT
            for g in range(2):
                hp = hps[hi[0] % 2]
                hpn = "hps%d" % (hi[0] % 2)
                hi[0] += 1
                for ch in range(2):
                    for l in range(32):
                        P.op("pe", lambda e, kv=kv, g=g, ch=ch, l=l, hp=hp, xT=xT: e.matmul(
                            hp[:, ch, 0:NCB], W1[kv][g * 64:(g + 1) * 64, l, ch * 128:(ch + 1) * 128],
                            xT[g * 64:(g + 1) * 64, l:l + 16 * (NCB - 1) + 1:16],
                            start=(l == 0), stop=(l == 31)),
                             r=["W1_%d" % kv], w=[hpn])
                for ch in range(2):
                    col = kv * 2 + ch
                    P.op("act", lambda e, ch=ch, col=col, hp=hp: e.activation(
                        out=xh[:, ch, 0:NCB], in_=hp[:, ch, 0:NCB], func=AF.Identity, bias=pb[:, col:col + 1]),
                         r=[hpn, "pb"], w=["xh%d" % ch])
                X, U, SG, HA = xh[:, :, 0:NCB], u[:, :, 0:NCB], sg[:, :, 0:NCB], hact[:, :, 0:NCB]
                P.op("dve", lambda e, X=X, U=U: e.tensor_tensor(out=U, in0=X, in1=X, op=ALU.mult),
                     r=["xh0", "xh1"], w=["u"])
                P.op("dve", lambda e, U=U: e.tensor_scalar(out=U, in0=U, scalar1=0.044715, scalar2=1.0,
                                                        op0=ALU.mult, op1=ALU.add), r=["u"], w=["u"])
                P.op("dve", lambda e, X=X, U=U: e.tensor_tensor(out=U, in0=U, in1=X, op=ALU.mult),
                     r=["u", "xh0", "xh1"], w=["u"])
                P.op("act", lambda e, U=U, SG=SG: e.activation(out=SG, in_=U, func=AF.Sigmoid,
                                                              scale=1.5957691216057308),
                     r=["u"], w=["sg"])
                P.op("dve", lambda e, X=X, SG=SG, HA=HA: e.tensor_tensor(out=HA, in0=X, in1=SG, op=ALU.mult),
                     r=["sg", "xh0", "xh1"], w=["hact"])
                if kv == 0:
                    for ch in range(2):
                        P.op("pe", lambda e, ch=ch: e.matmul(cps[0:64, 0:NCB], W2[0][:, ch, :], hact[:, ch, 0:NCB],
                                                            start=(ch == 0), stop=(ch == 1)),
                             r=["hact", "W2_0"], w=["cps"])
                    P.op("act", lambda e, g=g: e.activation(out=kcc[g][0:64, 0:NCB], in_=cps[0:64, 0:NCB],
                                                           func=AF.Copy), r=["cps"], w=["kcc%d" % g])
                else:
                    for ch in range(2):
                        P.op("pe", lambda e, ch=ch: e.matmul(cps[0:NCB, 0:64], hact[:, ch, 0:NCB], W2[1][:, ch, :],
                                                            start=(ch == 0), stop=(ch == 1)),
                             r=["hact", "W2_1"], w=["cps"])
                    P.op("act", lambda e, g=g: e.activation(out=vca[0:NCB, g, 0:64], in_=cps[0:NCB, 0:64],
                                                           func=AF.Copy), r=["cps"], w=["vca_v%d" % g])
        P.barrier_all()
        P.flush()


class Pipe:
    def __init__(self, lag=2):
        self.q, self.lag = [], lag

    def push(self, first, rest):
        if first is not None:
            first()
        self.q.append(rest)
        while len(self.q) > self.lag:
            self.q.pop(0)()

    def drain(self):
        while self.q:
            self.q.pop(0)()


def mixer_attn(nc, P, b, src, dst, dd, t):
    tag = "T%d" % b
    dqT, dkT, Qg, KSg, KWg = t["dqT"], t["dkT"], t["Qg"], t["KSg"], t["KWg"]
    dva, vsa, vwa, glr, kcc, vca, ident, epsb = (t["dva"], t["vsa"], t["vwa"], t["glr"], t["kcc"], t["vca"],
                                                 t["ident"], t["epsb"])
    with ExitStack() as st:
        def sb(name, shape, dt):
            return st.enter_context(nc.sbuf_tensor(tag + name, shape, dt))

        def ps(name, shape, dt):
            return st.enter_context(nc.psum_tensor(tag + name, shape, dt))

        om = sb("om", [128, NT, 1024], BF16)
        tmpf = [sb("tmpf%d" % i, [128, 64], F32) for i in range(2)]
        Wo = sb("Wo", [128, 8, 1024], BF16)
        pts = [sb("p%d" % i, [128, 512], BF16) for i in range(4)]
        gates = sb("gates", [128, NT, 24], F32)
        Am = sb("Am", [128, NT, 32], F32)
        Bm = sb("Bm", [128, NT, 32], F32)
        Gs = sb("Gs", [128, 128], F32)
        Gm = sb("Gm", [128, D], F32)
        lv = [sb("lv%d" % i, [128, 64], F32) for i in range(4)]
        lt = sb("lt", [128, 64], F32)
        le = sb("le", [128, 2], F32)
        neglam = sb("neglam", [128, 1], F32)
        o1 = [sb("o1_%d" % i, [128, 4, 128], F32) for i in range(2)]
        junk = sb("junk", [128, 128], F32)
        sm = [dict((n, sb("%s_%d" % (n, i), [128, w], F32)) for n, w in
                   (("rd", 4), ("nl", 4), ("ss4", 4), ("ln4", 4), ("r4", 4), ("den", 4), ("gr", 4), ("rs", 4),
                    ("rw", 4), ("imp", 32), ("impm", 32), ("wk", 32), ("m1", 8), ("m2", 8))) for i in range(2)]
        nsp = [sb("nsp%d" % i, [128, 96], BF16) for i in range(2)]
        omT = [sb("omT%d" % i, [128, 8, 128], BF16) for i in range(2)]
        xres = [sb("xres%d" % i, [128, D], F32) for i in range(2)]
        ytmp = [sb("ytmp%d" % i, [128, D], F32) for i in range(2)]
        ss2 = sb("ss2", [128, 2], F32)
        rstd2 = sb("rstd2", [128, 2], F32)
        sps = [ps("sps%d" % i, [128, 512], F32) for i in range(3)]
        ab = ps("ab", [128, 4, 512], F32)
        tps = ps("tps", [128, 1024], BF16)
        SPS = Rot(sps, ["sps%d" % i for i in range(3)])
        PT = Rot(pts, ["p%d" % i for i in range(4)])

        for nm, tl in (("Am", Am), ("Bm", Bm), ("Gs", Gs), ("Gm", Gm)):
            P.dma("sp", lambda e, nm=nm, tl=tl: e.dma_start(out=tl[:], in_=dd[nm]), tag + nm, w=[nm])
        for i, nm in enumerate(("lq1", "lk1", "lq2", "lk2")):
            P.dma("sp", lambda e, i=i, nm=nm: e.dma_start(out=lv[i][:], in_=dd[nm]), tag + nm, w=["lv%d" % i])
        wo_v = dd["w_out"].rearrange("(k p) f -> p k f", p=128)
        for kh in range(2):
            P.dma("pool", lambda e, kh=kh: e.dma_start(out=Wo[:, kh * 4:(kh + 1) * 4, :],
                                                       in_=wo_v[:, kh * 4:(kh + 1) * 4, :]),
                  tag + "Wo", w=["Wo"])
        for nm in ("nsp0", "nsp1"):
            pass
        P.op("pool", lambda e: e.memset(nsp[0][:], 0.0), w=["nsp0"])
        P.op("pool", lambda e: e.memset(nsp[1][:], 0.0), w=["nsp1"])
        P.op("dve", lambda e: e.tensor_scalar(out=Gs[:], in0=Gs[:], scalar1=0.8, scalar2=None, op0=ALU.mult),
             r=["Gs"], w=["Gs"])
        P.op("act", lambda e: e.activation(out=gates[:], in_=glr[:], func=AF.Sigmoid), w=["gates"])
        for i in range(2):
            P.op("dve", lambda e, i=i: e.tensor_tensor(out=lt[:], in0=lv[2 * i][:], in1=lv[2 * i + 1][:],
                                                       op=ALU.mult),
                 r=["lv%d" % (2 * i), "lv%d" % (2 * i + 1)], w=["lt"])
            P.op("dve", lambda e, i=i: e.reduce_sum(out=le[:, i:i + 1], in_=lt[:], axis=AX.X),
                 r=["lt"], w=["le%d" % i])
        P.op("act", lambda e: e.activation(out=le[:], in_=le[:], func=AF.Exp), r=["le0", "le1"], w=["le0", "le1"])
        P.op("dve", lambda e: e.tensor_tensor(out=neglam[:], in0=le[:, 1:2], in1=le[:, 0:1], op=ALU.subtract),
             r=["le0", "le1"], w=["neglam"])
        P.op("dve", lambda e: e.tensor_scalar(out=neglam[:], in0=neglam[:], scalar1=-0.2, scalar2=None,
                                              op0=ALU.add), r=["neglam"], w=["neglam"])

        pipe = Pipe(2)
        first_in_bank = {}

        def acc_mm(bank_i, col0, ncol, lhsT, rhs, last, reads):
            bn = "ab%d" % bank_i
            first = first_in_bank.get(bn, True)
            first_in_bank[bn] = False
            P.op("pe", lambda e: e.matmul(ab[:, bank_i, col0:col0 + ncol], lhsT, rhs, start=first, stop=last,
                                          skip_group_check=True),
                 r=reads, w=[bn])

        def exp_tile(spt, spn, rows, masks):
            p, pn = PT.next()
            P.op("act", lambda e: e.activation(out=p[0:rows, :], in_=spt[0:rows, :], func=AF.Exp, scale=0.125),
                 r=[spn], w=[pn])
            for (pattern, base, cm) in masks:
                P.op("pool", lambda e, pattern=pattern, base=base, cm=cm: e.affine_select(
                    out=p[0:rows, :], in_=p[0:rows, :], pattern=pattern, compare_op=ALU.is_ge, fill=0.0,
                    base=base, channel_multiplier=cm), r=[pn], w=[pn])
            return p, pn

        ci = 0
        for h in range(4):
            for qb in range(4):
                ob, obn = o1[(h * 4 + qb) % 2], "o1_%d" % ((h * 4 + qb) % 2)
                smx = sm[(h * 4 + qb) % 2]
                sfx = "_%d" % ((h * 4 + qb) % 2)
                for m in range(2):
                    bA, bB = 2 * (ci % 2), 2 * (ci % 2) + 1
                    ci += 1
                    first_in_bank["ab%d" % bA] = True
                    first_in_bank["ab%d" % bB] = True
                    nkt = 4 * qb + 4
                    for kt in range(nkt):
                        spt, spn = SPS.next()

                        def qk(spt=spt, spn=spn, kt=kt, m=m, h=h, qb=qb):
                            P.op("pe", lambda e: e.matmul(
                                spt[:], dkT[m * 64:(m + 1) * 64, h, kt * 128:(kt + 1) * 128],
                                dqT[m * 64:(m + 1) * 64, h, qb * 512:(qb + 1) * 512], start=True, stop=True),
                                 w=[spn])

                        def rest(spt=spt, spn=spn, kt=kt, h=h, qb=qb, bA=bA, bB=bB):
                            masks = []
                            if kt >= 4 * qb:
                                masks.append(([[1, 512]], qb * 512 - kt * 128, -1))
                            p, pn = exp_tile(spt, spn, 128, masks)
                            for jq in range(4):
                                if kt > 4 * qb + jq:
                                    continue
                                bank_i, col0 = (bA, jq * 130) if jq < 3 else (bB, 0)
                                acc_mm(bank_i, col0, 129, p[:, jq * 128:(jq + 1) * 128], dva[:, kt, h, 0:129],
                                       kt == 4 * qb + jq, [pn])

                        pipe.push(qk, rest)

                    def post(m=m, h=h, qb=qb, bA=bA, bB=bB, ob=ob, obn=obn, smx=smx, sfx=sfx):
                        bnA, bnB = "ab%d" % bA, "ab%d" % bB
                        rd = smx["rd"]
                        denA = ab[:, bA, 0:390].rearrange("p (j c) -> p j c", c=130)[:, :, 128]
                        P.op("dve", lambda e: e.reciprocal(out=rd[:, 0:3], in_=denA), r=[bnA], w=["rdA" + sfx])
                        P.op("dve", lambda e: e.reciprocal(out=rd[:, 3:4], in_=ab[:, bB, 128:129]), r=[bnB],
                             w=["rdB" + sfx])

                        def region(jq):
                            return (ab[:, bA, jq * 130:jq * 130 + 128], bnA) if jq < 3 else (ab[:, bB, 0:128], bnB)

                        if m == 0:
                            for jq in range(4):
                                reg, bn = region(jq)
                                P.op("act", lambda e, reg=reg, jq=jq: e.activation(
                                    out=ob[:, jq, :], in_=reg, func=AF.Copy, scale=rd[:, jq:jq + 1]),
                                     r=[bn, "rdA" + sfx, "rdB" + sfx], w=[obn + "_%d" % jq])
                        else:
                            nl, ss4, ln4, r4 = smx["nl"], smx["ss4"], smx["ln4"], smx["r4"]
                            P.op("dve", lambda e: e.tensor_scalar(out=nl[:], in0=rd[:], scalar1=neglam[:, 0:1],
                                                                  scalar2=None, op0=ALU.mult),
                                 r=["rdA" + sfx, "rdB" + sfx, "neglam"], w=["nl" + sfx])
                            for jq in range(4):
                                reg, bn = region(jq)
                                P.op("dve", lambda e, reg=reg, jq=jq: e.scalar_tensor_tensor(
                                    out=ob[:, jq, :], in0=reg, scalar=nl[:, jq:jq + 1], in1=ob[:, jq, :],
                                    op0=ALU.mult, op1=ALU.add),
                                     r=[bn, "nl" + sfx, obn + "_%d" % jq], w=[obn + "_%d" % jq])
                                P.op("act", lambda e, jq=jq: e.activation(
                                    out=junk[:], in_=ob[:, jq, :], func=AF.Square, accum_out=ss4[:, jq:jq + 1]),
                                     r=[obn + "_%d" % jq], w=["junk", "ss4%s_%d" % (sfx, jq)])
                            ssr = ["ss4%s_%d" % (sfx, jq) for jq in range(4)]
                            P.op("act", lambda e: e.activation(out=ln4[:], in_=ss4[:], func=AF.Ln, scale=1.0 / 128,
                                                               bias=epsb[:, 0:1]), r=ssr + ["epsb"], w=["ln4" + sfx])
                            P.op("act", lambda e: e.activation(out=r4[:], in_=ln4[:], func=AF.Exp, scale=-0.5),
                                 r=["ln4" + sfx], w=["r4" + sfx])
                            for jq in range(4):
                                T = 4 * qb + jq
                                P.op("dve", lambda e, jq=jq, T=T: e.scalar_tensor_tensor(
                                    out=om[:, T, h * 128:(h + 1) * 128], in0=ob[:, jq, :], scalar=r4[:, jq:jq + 1],
                                    in1=Gs[:], op0=ALU.mult, op1=ALU.mult),
                                     r=[obn + "_%d" % jq, "r4" + sfx, "Gs"], w=["om%d_d%d" % (T, h)])

                    pipe.push(None, post)
        pipe.drain()

        def stage1(g, T):
            if True:
                ci = g * NT + T
                bk = 2 + (ci % 2)
                smx = sm[ci % 2]
                sfx = "_%d" % (ci % 2)
                nspt, nspn = nsp[ci % 2], "nsp%d" % (ci % 2)
                spt, spn = SPS.next()

                def qk(spt=spt, spn=spn, g=g, T=T):
                    P.op("pe", lambda e: e.matmul(spt[0:127, :], kcc[g][0:64, 0:127],
                                                  Qg[g][0:64, :, T * 128:(T + 1) * 128], start=True, stop=True),
                         w=[spn])

                def rest(spt=spt, spn=spn, g=g, T=T, bk=bk):
                    p, pn = exp_tile(spt, spn, 127, [([[0, 4], [1, 128]], T * 128 - 31, -16)])
                    for hg in range(4):
                        P.op("pe", lambda e, hg=hg: e.matmul(ab[:, bk, hg * 128:hg * 128 + 97],
                                                             p[0:127, hg * 128:(hg + 1) * 128], vca[0:127, g, 0:97],
                                                             start=True, stop=True, skip_group_check=True),
                             r=[pn], w=["ab%d" % bk])

                def post(g=g, T=T, bk=bk, smx=smx, sfx=sfx, nspt=nspt, nspn=nspn):
                    bn = "ab%d" % bk
                    den, rd, gr, imp, impm, wk, m1, m2 = (smx["den"], smx["rd"], smx["gr"], smx["imp"],
                                                          smx["impm"], smx["wk"], smx["m1"], smx["m2"])
                    ov4 = ab[:, bk, :].rearrange("p (h c) -> p h c", c=128)
                    gv = gates[:, T, :].rearrange("p (h c) -> p h c", c=3)
                    P.op("dve", lambda e: e.tensor_scalar(out=den[:], in0=ov4[:, :, 64], scalar1=1e-30, scalar2=None,
                                                          op0=ALU.max), r=[bn], w=["den" + sfx])
                    P.op("dve", lambda e: e.reciprocal(out=rd[:], in_=den[:]), r=["den" + sfx], w=["rd" + sfx])
                    P.op("dve", lambda e: e.tensor_tensor(out=gr[:], in0=rd[:], in1=gv[:, g * 4:(g + 1) * 4, 0],
                                                          op=ALU.mult), r=["rd" + sfx, "gates"], w=["gr" + sfx])
                    for hg in range(4):
                        hd = g * 4 + hg
                        P.op("act", lambda e, hg=hg, hd=hd: e.activation(
                            out=om[:, T, 512 + hd * 64:512 + (hd + 1) * 64], in_=ab[:, bk, hg * 128:hg * 128 + 64],
                            func=AF.Copy, scale=gr[:, hg:hg + 1]), r=[bn, "gr" + sfx], w=["om%d_n%d" % (T, hd)])
                    P.op("dve", lambda e: e.tensor_scalar(out=imp[:], in0=ab[:, bk, 65:97], scalar1=rd[:, 0:1],
                                                          scalar2=None, op0=ALU.mult),
                         r=[bn, "rd" + sfx], w=["imp" + sfx])
                    for hg in range(1, 4):
                        P.op("dve", lambda e, hg=hg: e.scalar_tensor_tensor(
                            out=imp[:], in0=ab[:, bk, hg * 128 + 65:hg * 128 + 97], scalar=rd[:, hg:hg + 1],
                            in1=imp[:], op0=ALU.mult, op1=ALU.add), r=[bn, "rd" + sfx, "imp" + sfx],
                             w=["imp" + sfx])
                    P.op("dve", lambda e: e.tensor_tensor(out=impm[:], in0=imp[:], in1=Am[:, T, :], op=ALU.mult),
                         r=["imp" + sfx, "Am"], w=["impm" + sfx])
                    P.op("dve", lambda e: e.tensor_tensor(out=impm[:], in0=impm[:], in1=Bm[:, T, :], op=ALU.add),
                         r=["impm" + sfx, "Bm"], w=["impm" + sfx])
                    P.op("dve", lambda e: e.max(out=m1[:], in_=impm[:]), r=["impm" + sfx], w=["m1" + sfx])
                    P.op("dve", lambda e: e.match_replace(out=wk[:], in_to_replace=m1[:], in_values=impm[:],
                                                          imm_value=-2.0),
                         r=["impm" + sfx, "m1" + sfx], w=["wk" + sfx])
                    P.op("dve", lambda e: e.max(out=m2[:], in_=wk[:]), r=["wk" + sfx], w=["m2" + sfx])
                    P.op("dve", lambda e: e.tensor_scalar(out=nspt[:, 64:96], in0=impm[:], scalar1=m2[:, 7:8],
                                                          scalar2=-BIG, op0=ALU.is_lt, op1=ALU.mult),
                         r=["impm" + sfx, "m2" + sfx], w=[nspn])
                    P.op("pe", lambda e: e.transpose(tps[0:96, 0:128], nspt[:, 0:96], ident[:]),
                         r=[nspn, "ident"], w=["tps"])
                    tsl = slice(T * 128, (T + 1) * 128)
                    P.op("act", lambda e: e.activation(out=Qg[g][64:96, 0, tsl], in_=tps[64:96, 0:128],
                                                       func=AF.Copy), r=["tps"], w=["Qs%d_%d" % (g, T)])
                    for hg in range(1, 4):
                        P.op("pool", lambda e, hg=hg: e.tensor_copy(out=Qg[g][64:96, hg, tsl],
                                                                    in_=Qg[g][64:96, 0, tsl]),
                             r=["Qs%d_%d" % (g, T)], w=["Qs%d_%d_%d" % (g, T, hg)])

                pipe.push(qk, rest)
                pipe.push(None, post)

        def stage2(g, T):
            if True:
                ci = g * NT + T
                bS, bW = 0, 1
                smx = sm[ci % 2]
                sfx = "_%d" % (ci % 2)
                selr = ["Qs%d_%d" % (g, T)] + ["Qs%d_%d_%d" % (g, T, hg) for hg in range(1, 4)]
                jobs = [("s", kt) for kt in range(T + 1)] + [("w", kt) for kt in range(max(0, T - 4), T + 1)]
                for kind, kt in jobs:
                    spt, spn = SPS.next()

                    def qk(spt=spt, spn=spn, kind=kind, kt=kt, g=g, T=T, selr=selr):
                        ksl = slice(kt * 128, (kt + 1) * 128)
                        tsl = slice(T * 128, (T + 1) * 128)
                        if kind == "s":
                            P.op("pe", lambda e: e.matmul(spt[:], KSg[g][0:96, ksl], Qg[g][0:96, :, tsl],
                                                          start=True, stop=True), r=selr, w=[spn])
                        else:
                            P.op("pe", lambda e: e.matmul(spt[:], KWg[g][0:64, ksl], Qg[g][0:64, :, tsl],
                                                          start=True, stop=True), w=[spn])

                    def rest(spt=spt, spn=spn, kind=kind, kt=kt, g=g, T=T, bS=bS, bW=bW):
                        masks = []
                        if kt == T:
                            masks.append(([[0, 4], [1, 128]], 0, -1))
                        if kind == "w" and kt == T - 4:
                            masks.append(([[0, 4], [-1, 128]], -1, 1))
                        p, pn = exp_tile(spt, spn, 128, masks)
                        va = vsa if kind == "s" else vwa
                        bank_i = bS if kind == "s" else bW
                        if kt == (0 if kind == "s" else max(0, T - 4)):
                            first_in_bank["ab%d" % bank_i] = True
                        for hg in range(4):
                            acc_mm(bank_i, hg * 128, 65, p[:, hg * 128:(hg + 1) * 128], va[:, kt, g, 0:65],
                                   kt == T, [pn])

                    pipe.push(qk, rest)

                def post(g=g, T=T, bS=bS, bW=bW, smx=smx, sfx=sfx):
                    bnS, bnW = "ab%d" % bS, "ab%d" % bW
                    rs, rw = smx["rs"], smx["rw"]
                    gv = gates[:, T, :].rearrange("p (h c) -> p h c", c=3)
                    oS = ab[:, bS, :].rearrange("p (h c) -> p h c", c=128)
                    oW = ab[:, bW, :].rearrange("p (h c) -> p h c", c=128)
                    P.op("dve", lambda e: e.reciprocal(out=rs[:], in_=oS[:, :, 64]), r=[bnS], w=["rs" + sfx])
                    P.op("dve", lambda e: e.tensor_tensor(out=rs[:], in0=rs[:], in1=gv[:, g * 4:(g + 1) * 4, 1],
                                                          op=ALU.mult), r=["rs" + sfx, "gates"], w=["rs" + sfx])
                    P.op("dve", lambda e: e.reciprocal(out=rw[:], in_=oW[:, :, 64]), r=[bnW], w=["rw" + sfx])
                    P.op("dve", lambda e: e.tensor_tensor(out=rw[:], in0=rw[:], in1=gv[:, g * 4:(g + 1) * 4, 2],
                                                          op=ALU.mult), r=["rw" + sfx, "gates"], w=["rw" + sfx])
                    for hg in range(4):
                        hd = g * 4 + hg
                        on = "om%d_n%d" % (T, hd)
                        osl = om[:, T, 512 + hd * 64:512 + (hd + 1) * 64]
                        tf, tfn = tmpf[hg % 2], "tmpf%d" % (hg % 2)
                        P.op("dve", lambda e, hg=hg, osl=osl, tf=tf: e.scalar_tensor_tensor(
                            out=tf[:], in0=ab[:, bS, hg * 128:hg * 128 + 64], scalar=rs[:, hg:hg + 1], in1=osl,
                            op0=ALU.mult, op1=ALU.add), r=[bnS, "rs" + sfx, on], w=[tfn])
                        P.op("dve", lambda e, hg=hg, osl=osl, tf=tf: e.scalar_tensor_tensor(
                            out=osl, in0=ab[:, bW, hg * 128:hg * 128 + 64], scalar=rw[:, hg:hg + 1], in1=tf[:],
                            op0=ALU.mult, op1=ALU.add), r=[bnW, "rw" + sfx, tfn], w=[on])

                pipe.push(None, post)

        for g in range(2):
            stage1(g, 0)
            stage1(g, 1)
            for T in range(NT):
                stage2(g, T)
                if T + 2 < NT:
                    stage1(g, T + 2)
        pipe.drain()

        for T in range(NT):
            q = T % 2
            t0 = T * 128
            omr = ["om%d_d%d" % (T, h) for h in range(4)] + ["om%d_n%d" % (T, hd) for hd in range(8)]
            for k in range(8):
                P.op("pe", lambda e, k=k, T=T: e.transpose(tps[:, k * 128:(k + 1) * 128],
                                                          om[:, T, k * 128:(k + 1) * 128], ident[:]),
                     r=omr + ["ident"], w=["tps"])
            oT, oTn = omT[q], "omT%d" % q
            P.op("act", lambda e, oT=oT: e.activation(out=oT[:].rearrange("p k t -> p (k t)"), in_=tps[:],
                                                      func=AF.Copy), r=["tps"], w=[oTn])
            b0 = 2 * q
            for n in range(2):
                for k in range(8):
                    P.op("pe", lambda e, n=n, k=k, oT=oT, b0=b0: e.matmul(
                        ab[:, b0 + n, :], oT[:, k, :], Wo[:, k, n * 512:(n + 1) * 512], start=(k == 0),
                        stop=(k == 7)), r=[oTn, "Wo"], w=["ab%d" % (b0 + n)])
            xr, yt = xres[q], ytmp[q]
            xrn, ytn = "xres%d" % q, "ytmp%d" % q
            wo2 = ab[:, b0:b0 + 2, :]
            br = ["ab%d" % b0, "ab%d" % (b0 + 1)]
            P.dma("act", lambda e, xr=xr, t0=t0: e.dma_start(out=xr[:], in_=src[b, t0:t0 + 128, :]),
                  tag + xrn + "i", w=[xrn])
            P.op("act", lambda e, yt=yt, q=q, wo2=wo2: e.activation(
                out=yt[:].rearrange("p (n f) -> p n f", n=2), in_=wo2, func=AF.Square, accum_out=ss2[:, q:q + 1]),
                 r=br, w=[ytn, "ss2%d" % q])
            P.op("act", lambda e, q=q: e.activation(out=ss2[:, q:q + 1], in_=ss2[:, q:q + 1], func=AF.Sqrt,
                                                    scale=1.0 / D, bias=epsb[:, 0:1]),
                 r=["ss2%d" % q, "epsb"], w=["ss2%d" % q])
            P.op("dve", lambda e, q=q: e.reciprocal(out=rstd2[:, q:q + 1], in_=ss2[:, q:q + 1]),
                 r=["ss2%d" % q], w=["rstd2%d" % q])
            P.op("dve", lambda e, yt=yt, q=q, wo2=wo2: e.scalar_tensor_tensor(
                out=yt[:].rearrange("p (n f) -> p n f", n=2), in0=wo2, scalar=rstd2[:, q:q + 1],
                in1=Gm[:].rearrange("p (n f) -> p n f", n=2), op0=ALU.mult, op1=ALU.mult),
                 r=br + ["rstd2%d" % q, "Gm"], w=[ytn])
            P.op("pool", lambda e, xr=xr, yt=yt: e.tensor_tensor(out=xr[:], in0=xr[:], in1=yt[:], op=ALU.add),
                 r=[ytn, xrn], w=[xrn])
            P.dma("sp", lambda e, xr=xr, t0=t0: e.dma_start(out=dst[b, t0:t0 + 128, :], in_=xr[:]),
                  tag + xrn + "o", r=[xrn], w=["dst_B_%d_%d" % (b, t0)])
        P.barrier_all()
        P.flush()


def build(stage=3):
    nc = bass.Bass("TRN2", target_bir_lowering=False)

    def dt(n, s, d=F32, k="ExternalInput"):
        return nc.dram_tensor(n, s, d, kind=k).ap()

    x = dt("x", [NB, S, D])
    out = dt("out", [NB, S, D], F32, "ExternalOutput")
    ident_d = dt("ident", [128, 128], BF16)
    f = {}
    for t in ("f1", "f2"):
        f[t] = dict(wg=dt(t + "_wg", [D, DFF]), wu=dt(t + "_wu", [D, DFF]), wd=dt(t + "_wd", [DFF, D]),
                    gpre=dt(t + "_gpre", [128, 8]), gpost=dt(t + "_gpost", [128, D]))
    dd = dict(ident=ident_d,
              w_in=dt("w_in", [D, C_END]), w_out=dt("w_out", [D, D]), m_gpre=dt("m_gpre", [128, 8]),
              Gm=dt("Gm", [128, D]), Gs=dt("Gs", [128, 128]),
              cosR=dt("cosR", [128, NT, 8, 8]), sinR=dt("sinR", [128, NT, 8, 8]),
              cosR2=dt("cosR2", [128, NT, 8, 8]), sinR2=dt("sinR2", [128, NT, 8, 8]),
              ov=dt("ov", [127, 32], BF16), ET=dt("ET", [32, S], BF16),
              Am=dt("Am", [128, NT, 32]), Bm=dt("Bm", [128, NT, 32]),
              ck_w1=dt("ck_w1", [2048, 256]), ck_w2=dt("ck_w2", [256, 64]), ck_peT=dt("ck_peT", [128, 32]),
              cv_w1=dt("cv_w1", [2048, 256]), cv_w2=dt("cv_w2", [256, 64]), cv_peT=dt("cv_peT", [128, 32]),
              lq1=dt("lq1", [128, 64]), lk1=dt("lk1", [128, 64]), lq2=dt("lq2", [128, 64]),
              lk2=dt("lk2", [128, 64]))
    x1 = nc.dram_tensor("x1s", [NB, S, D], F32).ap()
    x2 = nc.dram_tensor("x2s", [NB, S, D], F32).ap()
    with ExitStack() as stack:
        P = Prog(nc, stack)
        fa = f["f1"]
        ffn_phase(nc, P, "A", x, out if stage == 1 else x1, fa["wg"], fa["wu"], fa["wd"], fa["gpre"],
                  fa["gpost"], ident_d)
        if stage >= 2:
            mixer_phase(nc, P, x1, out if stage == 2 else x2, dd)
        if stage >= 3:
            fc = f["f2"]
            ffn_phase(nc, P, "C", x2, out, fc["wg"], fc["wu"], fc["wd"], fc["gpre"], fc["gpost"], ident_d)
    return nc


def host_inputs(inp):
    def g(k):
        return np.ascontiguousarray(np.asarray(inp[k], dtype=np.float32))

    bf = ml_dtypes.bfloat16

    def bc(v, n=128):
        return np.ascontiguousarray(np.broadcast_to(v[None, :], (n, v.shape[0])))

    common = {"ident": np.eye(128, dtype=np.float32).astype(bf)}
    for t, pfx in (("f1", "ff1"), ("f2", "ff2")):
        common[t + "_wg"] = g(pfx + "_w_gate")[0]
        common[t + "_wu"] = g(pfx + "_w_up")[0]
        common[t + "_wd"] = g(pfx + "_w_down")[0]
        common[t + "_gpre"] = np.ascontiguousarray(g(pfx + "_norm_pre")[0].reshape(8, 128).T)
        common[t + "_gpost"] = bc(g(pfx + "_norm_post")[0])
    common["w_in"] = g("w_in")[0]
    common["w_out"] = g("w_out")[0]
    common["m_gpre"] = np.ascontiguousarray(g("mix_norm_pre")[0].reshape(8, 128).T)
    common["Gm"] = bc(g("mix_norm_post")[0])
    common["Gs"] = bc(g("diff_subln")[0])
    for k, n in (("lq1", "lambda_q1"), ("lk1", "lambda_k1"), ("lq2", "lambda_q2"), ("lk2", "lambda_k2")):
        common[k] = bc(g(n)[0])
    for kv, w1n, w2n, pen in (("ck", "cmp_k_w1", "cmp_k_w2", "cmp_pe_k"), ("cv", "cmp_v_w1", "cmp_v_w2", "cmp_pe_v")):
        common[kv + "_w1"] = g(w1n)[0]
        common[kv + "_w2"] = g(w2n)[0]
        peT = g(pen)[0].T
        common[kv + "_peT"] = np.ascontiguousarray(np.concatenate([peT, peT], axis=0))
    pos = np.arange(S, dtype=np.float32)
    inv = (np.float32(500000.0) ** (-np.arange(0, 16, 2, dtype=np.float32) / np.float32(16))).astype(np.float32)
    ang = (pos[:, None] * inv[None, :]).astype(np.float32)
    cs, sn = np.cos(ang).astype(np.float32), np.sin(ang).astype(np.float32)

    def tab(a):
        a = a.reshape(NT, 128, 8).transpose(1, 0, 2)
        return np.ascontiguousarray(np.broadcast_to(a[:, :, None, :], (128, NT, 8, 8)))

    common["cosR"], common["sinR"] = tab(cs), tab(sn)
    c2, s2 = tab(cs).copy(), tab(sn).copy()
    c2[:, :, 4:8, :] = 1.0
    s2[:, :, 4:8, :] = 0.0
    common["cosR2"], common["sinR2"] = c2, s2
    c = np.arange(127)[:, None] * 16
    j = np.arange(32)[None, :] * 64
    ov = np.clip(np.minimum(c + 32, j + 64) - np.maximum(c, j), 0, None) / 32.0
    common["ov"] = ov.astype(np.float32).astype(bf)
    common["ET"] = (np.arange(S)[None, :] // 64 == np.arange(32)[:, None]).astype(np.float32).astype(bf)
    tt = np.arange(S)
    cur = (tt // 64)[:, None]
    blk = np.arange(32)[None, :]
    forced = (blk == 0) | ((blk <= cur) & (blk >= cur - 1))
    causal = blk <= cur
    A = (~forced & causal).astype(np.float32)
    Bc = np.where(forced, np.float32(1e9), np.where(causal, np.float32(0.0), np.float32(-1.0))).astype(np.float32)
    common["Am"] = np.ascontiguousarray(A.reshape(NT, 128, 32).transpose(1, 0, 2))
    common["Bm"] = np.ascontiguousarray(Bc.reshape(NT, 128, 32).transpose(1, 0, 2))
    x = g("x")
    maps = []
    for c_ in range(8):
        m = dict(common)
        m["x"] = x[c_ * NB:(c_ + 1) * NB]
        maps.append(m)
    return maps


def kernel(**inputs):
    nc = build(3)
    maps = host_inputs(inputs)
    res = run_bass_kernel_spmd(nc, maps, core_ids=list(range(8)))
    return np.concatenate([np.asarray(r["out"]) for r in res.results], axis=0).astype(np.float32)
```

```python
import numpy as np
import ml_dtypes
from contextlib import ExitStack
import concourse.bass as bass
import concourse.mybir as mybir
from concourse.bass_utils import run_bass_kernel_spmd

F32 = mybir.dt.float32
BF16 = mybir.dt.bfloat16
AF = mybir.ActivationFunctionType
ALU = mybir.AluOpType
AX = mybir.AxisListType

S = 2048
D = 1024
DFF = 2816
NF = DFF // 128
NB = 2
EPS = 1e-6
ENGS = ("pe", "act", "dve", "pool", "sp")


import re
PSUM_RE = re.compile(r"^(tr\d|gu\d|dn\d|pj\d|tq\d|pbps|hps\d|cps|sps\d|ab\d|tps)")


class Res:
    __slots__ = ("name", "w", "r")

    def __init__(self, name):
        self.name = name
        self.w = None
        self.r = {}


class Op:
    __slots__ = ("eng", "fn", "deps", "kind", "key", "val", "idx", "sig", "waits", "ordinal")


class Prog:
    def __init__(self, nc, stack):
        self.nc = nc
        self.stack = stack
        self.res = {}
        self.ops = []
        self.eng_n = {e: 0 for e in ENGS}
        self.sigbase = {e: 0 for e in ENGS}
        self.esem = {e: stack.enter_context(nc.semaphore("sem_" + e)) for e in ENGS if e != "sp"}
        self.dsem = {}
        self.dcount = {}
        self.free_sems = {"sw": [], "hw": []}
        self.dcls = {}
        self.live = []
        self.seen = {e: {} for e in ENGS}
        self.sigord = {e: {} for e in ENGS}

    def R(self, name):
        r = self.res.get(name)
        if r is None:
            r = self.res[name] = Res(name)
        return r

    def _deps(self, reads, writes):
        deps = []
        for n in reads:
            r = self.R(n)
            if r.w is not None:
                deps.append(r.w)
        for n in writes:
            r = self.R(n)
            if r.w is not None:
                deps.append(r.w)
            deps.extend(r.r.values())
        return deps

    def _mark(self, reads, writes, ev, rkey):
        for n in reads:
            self.R(n).r[rkey] = ev
        for n in writes:
            r = self.R(n)
            r.w = ev
            r.r = {}

    def op(self, eng, fn, r=(), w=()):
        o = Op()
        o.eng = eng
        o.fn = fn
        o.kind = "c"
        o.deps = self._deps(r, w)
        for n in r:
            if PSUM_RE.match(n):
                o.deps.extend(ev for k, ev in self.R(n).r.items() if k != eng)
        o.idx = self.eng_n[eng]
        self.eng_n[eng] += 1
        o.sig = False
        self._mark(r, w, ("c", eng, o.idx), eng)
        self.ops.append(o)
        return o

    def dma(self, q, fn, key, r=(), w=()):
        o = Op()
        o.eng = q
        o.fn = fn
        o.kind = "d"
        o.key = key
        if key not in self.dsem:
            cls = "sw" if q == "pool" else "hw"
            self.dcls[key] = cls
            if self.free_sems[cls]:
                self.dsem[key], self.dcount[key] = self.free_sems[cls].pop()
            else:
                self.dsem[key] = self.stack.enter_context(self.nc.semaphore("dsem%d" % len(self.dsem)))
                self.dcount[key] = 0
            self.live.append(key)
        o.deps = self._deps(r, w)
        self.dcount[key] += 16
        o.val = self.dcount[key]
        o.idx = self.eng_n[q]
        self.eng_n[q] += 1
        o.sig = False
        self._mark(r, w, ("d", key, o.val), "d_" + key)
        self.ops.append(o)
        return o

    def barrier_all(self):
        allres = list(self.res.keys())
        for e in ENGS:
            self.op(e, None, r=allres)
        self.res = {}
        for k in self.live:
            self.free_sems[self.dcls[k]].append((self.dsem[k], self.dcount[k]))
        self.live = []

    def flush(self):
        nc = self.nc
        ops = self.ops
        self.ops = []
        self.nflush = getattr(self, "nflush", 0) + 1
        if LIMIT is not None and self.nflush == LIMIT[0]:
            ops = ops[:LIMIT[1]]
        byidx = {}
        for o in ops:
            if o.kind == "c":
                byidx[(o.eng, o.idx)] = o
        for o in ops:
            o.waits = []
            seen = self.seen[o.eng]
            best = {}
            for d in o.deps:
                if d[0] == "c":
                    if d[1] == "pe" and o.eng == "pe":
                        continue
                    k = ("c", d[1])
                else:
                    k = ("d", d[1])
                if k not in best or d[2] > best[k][2]:
                    best[k] = d
            for d in best.values():
                if d[0] == "c":
                    _, pe, pi = d
                    if seen.get(pe, -1) >= pi:
                        continue
                    prod = byidx.get((pe, pi))
                    if prod is None:
                        assert pi in self.sigord[pe], (pe, pi)
                    else:
                        prod.sig = True
                    seen[pe] = pi
                    o.waits.append(d)
                else:
                    _, key, val = d
                    k = "d_" + key
                    if seen.get(k, 0) >= val:
                        continue
                    seen[k] = val
                    o.waits.append(d)
        last = {}
        for o in ops:
            if o.kind == "c":
                last[o.eng] = o
        for o in last.values():
            o.sig = True
        for o in ops:
            if o.kind == "c" and o.sig:
                self.sigbase[o.eng] += 1
                self.sigord[o.eng][o.idx] = self.sigbase[o.eng]
        per = {e: [o for o in ops if o.eng == e] for e in ENGS}

        def emit(eng_name, eng):
            for o in per[eng_name]:
                for d in o.waits:
                    if d[0] == "c":
                        eng.wait_ge(self.esem[d[1]], self.sigord[d[1]][d[2]])
                    else:
                        eng.wait_ge(self.dsem[d[1]], d[2])
                if o.fn is None:
                    if o.kind == "c" and o.sig:
                        eng.nop().then_inc(self.esem[eng_name], 1) if eng_name != "sp" else None
                    continue
                ins = o.fn(eng)
                if o.kind == "d":
                    ins.then_inc(self.dsem[o.key], 16)
                elif o.sig:
                    ins.then_inc(self.esem[eng_name], 1)

        with nc.Block() as block:
            @block.tensor
            def _(e):
                emit("pe", e)

            @block.scalar
            def _(e):
                emit("act", e)

            @block.vector
            def _(e):
                emit("dve", e)

            @block.gpsimd
            def _(e):
                emit("pool", e)

            @block.sync
            def _(e):
                emit("sp", e)


def ffn_phase(nc, P, tag, src, dst, wg_d, wu_d, wd_d, gpre_d, gpost_d, ident_d):
    with ExitStack() as st:
        def sb(name, shape, dt):
            return st.enter_context(nc.sbuf_tensor(tag + name, shape, dt))

        def ps(name, shape, dt):
            return st.enter_context(nc.psum_tensor(tag + name, shape, dt))

        Wg = sb("Wg", [128, 8, DFF], BF16)
        Wu = sb("Wu", [128, 8, DFF], BF16)
        Wd = sb("Wd", [128, NF, D], BF16)
        xin = [sb("xin%d" % i, [128, D], F32) for i in range(2)]
        hn = sb("hn", [128, 4, D], BF16)
        hT = sb("hT", [128, 8, 512], BF16)
        actT = sb("actT", [128, NF, 512], BF16)
        G = sb("G", [128, D], F32)
        gpre = sb("gpre", [128, 8], F32)
        ident = sb("ident", [128, 128], BF16)
        xres = [sb("xres%d" % i, [128, D], F32) for i in range(2)]
        ytmp = [sb("ytmp%d" % i, [128, D], F32) for i in range(2)]
        sg = [sb("sg%d" % i, [128, 512], F32) for i in range(2)]
        ss = sb("ss", [128, 8], F32)
        rstd = sb("rstd", [128, 8], F32)
        ss2 = sb("ss2", [128, 2], F32)
        rstd2 = sb("rstd2", [128, 2], F32)
        epsb = sb("epsb", [128, 1], F32)
        P.op("pool", lambda e: e.memset(epsb[:], EPS), w=["epsb"])
        tr = [ps("tr%d" % i, [128, 2, 512], BF16) for i in range(2)]
        gu = [ps("gu%d" % i, [128, 512], F32) for i in range(4)]
        dn = ps("dn", [128, D], F32)

        P.dma("sp", lambda e: e.dma_start(out=gpre[:], in_=gpre_d), tag + "c0", w=["gpre"])
        P.dma("sp", lambda e: e.dma_start(out=G[:], in_=gpost_d), tag + "c1", w=["G"])
        P.dma("sp", lambda e: e.dma_start(out=ident[:], in_=ident_d), tag + "c2", w=["ident"])
        P.op("dve", lambda e: e.tensor_scalar(out=G[:], in0=G[:], scalar1=0.5, scalar2=None, op0=ALU.mult),
             r=["G"], w=["G"])
        wg_v = wg_d.rearrange("(k p) f -> p k f", p=128)
        wu_v = wu_d.rearrange("(k p) f -> p k f", p=128)
        wd_v = wd_d.rearrange("(f p) d -> p f d", p=128)
        FG = [(0, 2), (2, 6), (6, 10), (10, 14), (14, 18), (18, 22)]
        fgrp = {}
        for gi, (a, b) in enumerate(FG):
            for f in range(a, b):
                fgrp[f] = gi
            for nm, W, v in (("Wg", Wg, wg_v), ("Wu", Wu, wu_v)):
                for kh in range(2):
                    P.dma("pool",
                          lambda e, W=W, v=v, a=a, b=b, kh=kh: e.dma_start(
                              out=W[:, kh * 4:(kh + 1) * 4, a * 128:b * 128],
                              in_=v[:, kh * 4:(kh + 1) * 4, a * 128:b * 128]),
                          "%s%s%d" % (tag, nm, gi), w=["%s%d" % (nm, gi)])
        for gi, (a, b) in enumerate(FG):
            P.dma("pool", lambda e, a=a, b=b: e.dma_start(out=Wd[:, a:b, :], in_=wd_v[:, a:b, :]),
                  "%sWd%d" % (tag, gi), w=["Wd%d" % gi])

        blocks = [(b, i) for b in range(NB) for i in range(4)]
        nxi = [0]

        def emit_N(bi, j):
            b, i = blocks[bi]
            t0 = i * 512 + j * 128
            xb = xin[nxi[0] % 2]
            xn = "xin%d" % (nxi[0] % 2)
            nxi[0] += 1
            P.dma("sp", lambda e: e.dma_start(out=xb[:], in_=src[b, t0:t0 + 128, :]), tag + xn, w=[xn])
            P.op("act", lambda e: e.activation(out=hn[:, j, :], in_=xb[:], func=AF.Square,
                                               accum_out=ss[:, j:j + 1]),
                 r=[xn], w=["hn%d" % j, "ss%d" % j])
            P.op("act", lambda e: e.activation(out=ss[:, j:j + 1], in_=ss[:, j:j + 1], func=AF.Sqrt,
                                               scale=1.0 / D, bias=epsb[:, 0:1]),
                 r=["ss%d" % j, "epsb"], w=["ss%d" % j])
            P.op("dve", lambda e: e.reciprocal(out=rstd[:, j:j + 1], in_=ss[:, j:j + 1]),
                 r=["ss%d" % j], w=["rstd%d" % j])
            P.op("act", lambda e: e.activation(out=hn[:, j, :], in_=xb[:], func=AF.Copy,
                                               scale=rstd[:, j:j + 1]),
                 r=[xn, "rstd%d" % j], w=["hn%d" % j])

        def emit_T(bi):
            for kp in range(4):
                bank = kp % 2
                for kk in range(2):
                    k = kp * 2 + kk
                    for j in range(4):
                        P.op("pe", lambda e, k=k, kk=kk, j=j, bank=bank: e.transpose(
                            tr[bank][:, kk, j * 128:(j + 1) * 128], hn[:, j, k * 128:(k + 1) * 128], ident[:]),
                             r=["hn%d" % j, "ident"], w=["tr%d" % bank])
                for kk in range(2):
                    k = kp * 2 + kk
                    P.op("dve", lambda e, k=k, kk=kk, bank=bank: e.tensor_scalar(
                        out=hT[:, k, :], in0=tr[bank][:, kk, :], scalar1=gpre[:, k:k + 1], scalar2=None,
                        op0=ALU.mult),
                         r=["tr%d" % bank, "gpre"], w=["hT%d" % k])

        gui = [0]

        def emit_GU_f(bi, f):
            pr = gui[0] % 2
            gui[0] += 1
            pg, pu = gu[2 * pr], gu[2 * pr + 1]
            gi = fgrp[f]
            for nm, W, pt in (("Wg", Wg, pg), ("Wu", Wu, pu)):
                pn = "gu%d%s" % (pr, nm)
                for k in range(8):
                    P.op("pe", lambda e, W=W, pt=pt, k=k: e.matmul(
                        pt[:], W[:, k, f * 128:(f + 1) * 128], hT[:, k, :], start=(k == 0), stop=(k == 7)),
                         r=["%s%d" % (nm, gi), "hT%d" % k], w=[pn])
            s = sg[pr]
            P.op("act", lambda e: e.activation(out=s[:], in_=pg[:], func=AF.Silu),
                 r=["gu%dWg" % pr], w=["sg%d" % pr])
            P.op("dve", lambda e: e.tensor_tensor(out=actT[:, f, :], in0=s[:], in1=pu[:], op=ALU.mult),
                 r=["sg%d" % pr, "gu%dWu" % pr], w=["actT%d" % f])

        dni = [0]

        def emit_D(bi):
            b, i = blocks[bi]
            for j in range(4):
                t0 = i * 512 + j * 128
                q = dni[0] % 2
                dni[0] += 1
                xr, yt = xres[q], ytmp[q]
                xrn, ytn = "xres%d" % q, "ytmp%d" % q
                P.dma("act", lambda e, xr=xr, t0=t0: e.dma_start(out=xr[:], in_=src[b, t0:t0 + 128, :]),
                      tag + xrn + "i", w=[xrn])
                for n in range(2):
                    for f in range(NF):
                        P.op("pe", lambda e, n=n, f=f, j=j: e.matmul(
                            dn[:, n * 512:(n + 1) * 512], actT[:, f, j * 128:(j + 1) * 128],
                            Wd[:, f, n * 512:(n + 1) * 512], start=(f == 0), stop=(f == NF - 1)),
                             r=["actT%d" % f, "Wd%d" % fgrp[f]], w=["dn%d" % n])
                P.op("act", lambda e, yt=yt, q=q: e.activation(out=yt[:], in_=dn[:], func=AF.Square,
                                                             accum_out=ss2[:, q:q + 1]),
                     r=["dn0", "dn1"], w=[ytn, "ss2%d" % q])
                P.op("act", lambda e, q=q: e.activation(out=ss2[:, q:q + 1], in_=ss2[:, q:q + 1], func=AF.Sqrt,
                                                        scale=1.0 / D, bias=epsb[:, 0:1]),
                     r=["ss2%d" % q, "epsb"], w=["ss2%d" % q])
                P.op("dve", lambda e, q=q: e.reciprocal(out=rstd2[:, q:q + 1], in_=ss2[:, q:q + 1]),
                     r=["ss2%d" % q], w=["rstd2%d" % q])
                P.op("dve", lambda e, yt=yt, q=q: e.scalar_tensor_tensor(
                    out=yt[:], in0=dn[:], scalar=rstd2[:, q:q + 1], in1=G[:], op0=ALU.mult, op1=ALU.mult),
                     r=["dn0", "dn1", "rstd2%d" % q, "G"], w=[ytn])
                P.op("pool", lambda e, xr=xr, yt=yt: e.tensor_tensor(out=xr[:], in0=xr[:], in1=yt[:],
                                                                     op=ALU.add),
                     r=[ytn, xrn], w=[xrn])
                P.dma("sp", lambda e, xr=xr, t0=t0: e.dma_start(out=dst[b, t0:t0 + 128, :], in_=xr[:]),
                      tag + xrn + "o", r=[xrn], w=["dst_%s_%d_%d" % (tag, b, t0)])

        nblk = len(blocks)
        for j in range(4):
            emit_N(0, j)
        emit_T(0)
        for bi in range(nblk):
            for f in range(NF):
                emit_GU_f(bi, f)
                if bi + 1 < nblk and f in (3, 8, 13, 18):
                    emit_N(bi + 1, (f - 3) // 5)
            if bi + 1 < nblk:
                emit_T(bi + 1)
            emit_D(bi)
        P.barrier_all()
        P.flush()


NT = S // 128
LIMIT = None
NSEQ = NB
SUB = 9
CUT = 9
BARQT = False
EVAC_ACT = False
BIG = 30000.0
C_DQ, C_DK, C_DV, C_NQ, C_KC, C_VC, C_KS, C_VS, C_KW, C_VW, C_GL, C_END = (
    0, 512, 1024, 1536, 2048, 2176, 2304, 2432, 2560, 2688, 2816, 2840)
S_DQ, S_DK, S_NQ, S_KS, S_KW, S_KC, S_VC, S_END = 0, 512, 1024, 1536, 1664, 1792, 1920, 2048


class Rot:
    def __init__(self, items, names):
        self.items, self.names, self.i = items, names, 0

    def next(self):
        k = self.i % len(self.items)
        self.i += 1
        return self.items[k], self.names[k]


def mixer_phase(nc, P, src, dst, dd):
    TB = 2
    with ExitStack() as st0:
        def sb0(name, shape, dt):
            return st0.enter_context(nc.sbuf_tensor("B" + name, shape, dt))

        dqT = sb0("dqT", [128, 4, S], BF16)
        dkT = sb0("dkT", [128, 4, S], BF16)
        Qg = [sb0("Qg%d" % g, [128, 4, S], BF16) for g in range(2)]
        KSg = [sb0("KSg%d" % g, [128, S], BF16) for g in range(2)]
        KWg = [sb0("KWg%d" % g, [128, S], BF16) for g in range(2)]
        dva = sb0("dva", [128, NT, 4, 130], BF16)
        vsa = sb0("vsa", [128, NT, 2, 66], BF16)
        vwa = sb0("vwa", [128, NT, 2, 66], BF16)
        glr = sb0("glr", [128, NT, 24], F32)
        kcc = [sb0("kcc%d" % g, [128, 128], BF16) for g in range(2)]
        vca = sb0("vca", [128, 2, 98], BF16)
        ident = sb0("ident", [128, 128], BF16)
        epsb = sb0("epsb", [128, 1], F32)

        P.dma("sp", lambda e: e.dma_start(out=ident[:], in_=dd["ident"]), "Bc_ident", w=["ident"])
        P.op("pool", lambda e: e.memset(epsb[:], EPS), w=["epsb"])
        P.op("pool", lambda e: e.memset(dva[:, :, :, 128:130], 1.0), w=["dva"])
        P.op("pool", lambda e: e.memset(vsa[:, :, :, 64:66], 1.0), w=["vsa"])
        P.op("pool", lambda e: e.memset(vwa[:, :, :, 64:66], 1.0), w=["vwa"])
        P.op("pool", lambda e: e.memset(vca[:], 0.0), w=["vca"])
        P.op("pool", lambda e: e.memset(vca[:, :, 64:65], 1.0), r=["vca"], w=["vca"])
        for g in range(2):
            P.dma("sp", lambda e, g=g: e.dma_start(out=vca[0:127, g, 65:97], in_=dd["ov"]), "Bc_ov%d" % g,
                  r=["vca"], w=["vca_ov%d" % g])
            P.dma("sp", lambda e, g=g: e.dma_start(out=KSg[g][64:96, :], in_=dd["ET"]), "Bc_ET%d" % g,
                  w=["KSgE%d" % g])

        for b in range(NSEQ):
            with ExitStack() as st1:
                kcT = st1.enter_context(nc.sbuf_tensor("BkcT%d" % b, [128, S], BF16))
                vcT = st1.enter_context(nc.sbuf_tensor("BvcT%d" % b, [128, S], BF16))
                mixer_proj(nc, P, b, TB, src, dd, dict(dqT=dqT, dkT=dkT, Qg=Qg, KSg=KSg, KWg=KWg, dva=dva,
                                                      vsa=vsa, vwa=vwa, glr=glr, kcT=kcT, vcT=vcT,
                                                      ident=ident, epsb=epsb))
                if SUB >= 2:
                    mixer_compress(nc, P, b, dd, dict(kcT=kcT, vcT=vcT, kcc=kcc, vca=vca))
            if SUB >= 3:
                mixer_attn(nc, P, b, src, dst, dd, dict(dqT=dqT, dkT=dkT, Qg=Qg, KSg=KSg, KWg=KWg, dva=dva,
                                                   vsa=vsa, vwa=vwa, glr=glr, kcc=kcc, vca=vca,
                                                   ident=ident, epsb=epsb))


def mixer_proj(nc, P, b, TB, src, dd, t):
    tag = "P%d" % b
    dqT, dkT, Qg, KSg, KWg = t["dqT"], t["dkT"], t["Qg"], t["KSg"], t["KWg"]
    dva, vsa, vwa, glr, kcT, vcT, ident, epsb = (t["dva"], t["vsa"], t["vwa"], t["glr"], t["kcT"], t["vcT"],
                                                 t["ident"], t["epsb"])
    with ExitStack() as st:
        def sb(name, shape, dt):
            return st.enter_context(nc.sbuf_tensor(tag + name, shape, dt))

        def ps(name, shape, dt):
            return st.enter_context(nc.psum_tensor(tag + name, shape, dt))

        rt = [[sb("rt%d_%d" % (i, q), [128, 8, 8], F32) for q in range(4)] for i in range(2)]
        xs = [sb("xs%d" % i, [128, 512], F32) for i in range(2)]
        xin = [sb("xin%d" % i, [128, D], F32) for i in range(2)]
        hn = sb("hn", [128, TB, D], BF16)
        hT = sb("hT", [128, 8, TB * 128], BF16)
        stg = sb("stg", [128, TB, S_END], BF16)
        cosR = sb("cosR", [128, NT, 8, 8], F32)
        sinR = sb("sinR", [128, NT, 8, 8], F32)
        gpre = sb("gpre", [128, 8], F32)
        ss = sb("ss", [128, TB], F32)
        rstd = sb("rstd", [128, TB], F32)
        Win = sb("Win", [128, 8, 2048], BF16)
        tr = [ps("tr%d" % i, [128, 1024], BF16) for i in range(2)]
        pj = [ps("pj%d" % i, [128, 512], F32) for i in range(3)]
        tq = [ps("tq%d" % i, [128, 1024], BF16) for i in range(2)]

        P.dma("sp", lambda e: e.dma_start(out=gpre[:], in_=dd["m_gpre"]), tag + "gpre", w=["gpre"])
        P.dma("sp", lambda e: e.dma_start(out=cosR[:], in_=dd["cosR"]), tag + "cos", w=["cosR"])
        P.dma("sp", lambda e: e.dma_start(out=sinR[:], in_=dd["sinR"]), tag + "sin", w=["sinR"])
        win_v = dd["w_in"].rearrange("(k p) f -> p k f", p=128)
        CB = [(0, 512), (512, 1024), (1024, 1536), (1536, 2048), (2048, 2560), (2560, C_END)]
        SEC = [(C_KS, 128), (C_KW, 128), (C_KC, 128), (C_VC, 128), (C_VS, 128), (C_VW, 128), (C_GL, 24)]
        def load_pass(ph):
            if ph == 0:
                for ci, (c0, c1) in enumerate(CB[:4]):
                    for kh in range(2):
                        P.dma("pool", lambda e, c0=c0, c1=c1, kh=kh: e.dma_start(
                            out=Win[:, kh * 4:(kh + 1) * 4, c0:c1], in_=win_v[:, kh * 4:(kh + 1) * 4, c0:c1]),
                              "%sWin%d" % (tag, ci), w=["Win%d" % ci])
            else:
                dcol = 0
                for si, (sc, sw) in enumerate(SEC):
                    ci = 4 if si < 4 else 5
                    P.dma("pool", lambda e, sc=sc, sw=sw, dcol=dcol: e.dma_start(
                        out=Win[:, :, dcol:dcol + sw], in_=win_v[:, :, sc:sc + sw]),
                          "%sWin%d" % (tag, ci), w=["Win%d" % ci] + (["Win0", "Win1"] if si == 0 else []))
                    dcol += sw
                P.dma("sp", lambda e: e.dma_start(out=cosR[:], in_=dd["cosR2"]), tag + "cos", w=["cosR"])
                P.dma("sp", lambda e: e.dma_start(out=sinR[:], in_=dd["sinR2"]), tag + "sin", w=["sinR"])

        PH = [0]
        nxi = [0]
        pji = [0]
        rti = [0]
        tqi = [0]
        evi = [0]

        def emit_N(i, j):
            t0 = (i * TB + j) * 128
            q = nxi[0] % 2
            nxi[0] += 1
            xb, xn = xin[q], "xin%d" % q
            P.dma("sp", lambda e: e.dma_start(out=xb[:], in_=src[b, t0:t0 + 128, :]), tag + xn, w=[xn])
            P.op("act", lambda e: e.activation(out=hn[:, j, :], in_=xb[:], func=AF.Square,
                                               accum_out=ss[:, j:j + 1]),
                 r=[xn], w=["hn%d" % j, "ss%d" % j])
            P.op("act", lambda e: e.activation(out=ss[:, j:j + 1], in_=ss[:, j:j + 1], func=AF.Sqrt,
                                               scale=1.0 / D, bias=epsb[:, 0:1]),
                 r=["ss%d" % j, "epsb"], w=["ss%d" % j])
            P.op("dve", lambda e: e.reciprocal(out=rstd[:, j:j + 1], in_=ss[:, j:j + 1]),
                 r=["ss%d" % j], w=["rstd%d" % j])
            P.op("act", lambda e: e.activation(out=hn[:, j, :], in_=xb[:], func=AF.Copy,
                                               scale=rstd[:, j:j + 1]),
                 r=[xn, "rstd%d" % j], w=["hn%d" % j])

        def emit_T(i):
            W = TB * 128
            for kq in range(2):
                bank = kq
                for kk in range(4):
                    k = kq * 4 + kk
                    for j in range(TB):
                        P.op("pe", lambda e, k=k, kk=kk, j=j, bank=bank: e.transpose(
                            tr[bank][:, kk * W + j * 128:kk * W + (j + 1) * 128],
                            hn[:, j, k * 128:(k + 1) * 128], ident[:]),
                             r=["hn%d" % j, "ident"], w=["tr%d" % bank])
                for kk in range(4):
                    k = kq * 4 + kk
                    P.op("dve", lambda e, k=k, kk=kk, bank=bank: e.tensor_scalar(
                        out=hT[:, k, :], in0=tr[bank][:, kk * W:(kk + 1) * W], scalar1=gpre[:, k:k + 1],
                        scalar2=None, op0=ALU.mult),
                         r=["tr%d" % bank, "gpre"], w=["hT%d" % k])

        def rope(pjt, pjn, o, nh, j, so, T, tabs=None):
            q = rti[0] % 2
            rti[0] += 1
            ta, tb_, tc, td = rt[q]
            rn = ["rt%d_%d" % (q, x) for x in range(4)]
            xst, xsn = xs[q], "xs%d" % q
            if (CUT == 4.26 and nh == 2) or (CUT == 4.28 and tabs is not None):
                sres = "stg%d_%d" % (j, so)
                P.op("act", lambda e: e.activation(out=stg[:, j, so:so + nh * 64], in_=pjt[:, o:o + nh * 64],
                                                   func=AF.Copy), r=[pjn], w=[sres + "a"])
                return [sres + "a"]
            P.op("act", lambda e: e.activation(out=xst[:, 0:nh * 64], in_=pjt[:, o:o + nh * 64], func=AF.Copy),
                 r=[pjn], w=[xsn])
            pv = xst[:, 0:nh * 64].rearrange("p (h d) -> p h d", d=64)
            sv = stg[:, j, so:so + nh * 64].rearrange("p (h d) -> p h d", d=64)
            x1, x2 = pv[:, :, 0:8], pv[:, :, 8:16]
            ct_, st_, ctn, stn = (cosR, sinR, "cosR", "sinR") if tabs is None else tabs
            cs, sn = ct_[:, T, 0:nh, :], st_[:, T, 0:nh, :]
            sres = "stg%d_%d" % (j, so)
            if CUT < 3.06:
                return [sres + "a", sres + "b", sres + "c"]
            P.op("dve", lambda e: e.tensor_tensor(out=ta[:, 0:nh, :], in0=x1, in1=cs, op=ALU.mult),
                 r=[xsn, ctn], w=[rn[0]])
            if CUT < 3.07:
                return [sres + "a", sres + "b", sres + "c"]
            P.op("dve", lambda e: e.tensor_tensor(out=tb_[:, 0:nh, :], in0=x2, in1=sn, op=ALU.mult),
                 r=[xsn, stn], w=[rn[1]])
            if CUT < 3.08:
                return [sres + "a", sres + "b", sres + "c"]
            P.op("dve", lambda e: e.tensor_tensor(out=tc[:, 0:nh, :], in0=x2, in1=cs, op=ALU.mult),
                 r=[xsn, ctn], w=[rn[2]])
            if CUT == 3.095:
                P.op("dve", lambda e: e.tensor_tensor(out=tc[:, 0:nh, :], in0=x1, in1=sn, op=ALU.mult),
                     r=[xsn, "sinR"], w=[rn[2]])
                return [sres + "a", sres + "b", sres + "c"]
            P.op("dve", lambda e: e.tensor_tensor(out=td[:, 0:nh, :], in0=x1, in1=sn, op=ALU.mult),
                 r=[xsn, stn], w=[rn[3]])
            P.op("dve", lambda e: e.tensor_tensor(out=pv[:, :, 0:8], in0=ta[:, 0:nh, :], in1=tb_[:, 0:nh, :],
                                                  op=ALU.subtract),
                 r=[rn[0], rn[1], rn[2], rn[3], xsn], w=[xsn])
            P.op("dve", lambda e: e.tensor_tensor(out=pv[:, :, 8:16], in0=tc[:, 0:nh, :], in1=td[:, 0:nh, :],
                                                  op=ALU.add),
                 r=[rn[2], rn[3], xsn], w=[xsn])
            P.op("act", lambda e: e.activation(out=stg[:, j, so:so + nh * 64], in_=xst[:, 0:nh * 64], func=AF.Copy),
                 r=[xsn], w=[sres + "a"])
            return [sres + "a"]

        def emit_proj(i, j, stres):
            T = i * TB + j
            for ci, (c0, c1) in enumerate(CB):
                if (ci < 4) != (PH[0] == 0):
                    continue
                if ci >= 4:
                    c0, c1 = c0 - 2048, c1 - 2048
                q = pji[0] % 3
                pji[0] += 1
                pjt, pjn = pj[q], "pj%d" % q
                ncol = c1 - c0
                for k in range(8):
                    P.op("pe", lambda e, k=k, pjt=pjt, c0=c0, c1=c1, ncol=ncol: e.matmul(
                        pjt[:, 0:ncol], hT[:, k, j * 128:(j + 1) * 128], Win[:, k, c0:c1],
                        start=(k == 0), stop=(k == 7)),
                         r=["hT%d" % k, "Win%d" % ci], w=[pjn])
                if ci == 0:
                    stres["dq"][j] = rope(pjt, pjn, 0, 8, j, S_DQ, T)
                elif ci == 1:
                    stres["dk"][j] = rope(pjt, pjn, 0, 8, j, S_DK, T)
                elif ci == 2:
                    P.op("act", lambda e, pjt=pjt, T=T: e.activation(
                        out=dva[:, T, :, 0:128], in_=pjt[:, 0:512].rearrange("p (h d) -> p h d", d=128),
                        func=AF.Copy), r=[pjn], w=["dva%d" % T])
                elif ci == 3:
                    stres["nq"][j] = rope(pjt, pjn, 0, 8, j, S_NQ, T)
                elif ci == 4:
                    rr = rope(pjt, pjn, 0, 8, j, S_KS, T)
                    stres["ks"][j] = rr
                    stres["kw"][j] = rr
                    stres["kcvc"][j] = rr
                else:
                    P.op("act", lambda e, pjt=pjt, T=T: e.activation(
                        out=vsa[:, T, :, 0:64], in_=pjt[:, 0:128].rearrange("p (h d) -> p h d", d=64),
                        func=AF.Copy), r=[pjn], w=["vsa%d" % T])
                    P.op("act", lambda e, pjt=pjt, T=T: e.activation(
                        out=vwa[:, T, :, 0:64], in_=pjt[:, 128:256].rearrange("p (h d) -> p h d", d=64),
                        func=AF.Copy), r=[pjn], w=["vwa%d" % T])
                    P.op("dve", lambda e, pjt=pjt, T=T: e.tensor_copy(out=glr[:, T, :], in_=pjt[:, 256:280]),
                         r=[pjn], w=["glr%d" % T])

        def emit_QT(i, stres):
            tk0 = i * TB * 128
            W = TB * 128
            units = []
            for h in range(4):
                units.append((S_DQ + h * 128, 128, dqT[:, h, tk0:tk0 + W], "dqT", "dq"))
            for h in range(4):
                units.append((S_DK + h * 128, 128, dkT[:, h, tk0:tk0 + W], "dkT", "dk"))
            for n in range(8):
                units.append((S_NQ + n * 64, 64, Qg[n // 4][0:64, n % 4, tk0:tk0 + W], "Qq%d" % (n // 4), "nq"))
            for g in range(2):
                units.append((S_KS + g * 64, 64, KSg[g][0:64, tk0:tk0 + W], "KSq%d" % g, "ks"))
            for g in range(2):
                units.append((S_KW + g * 64, 64, KWg[g][0:64, tk0:tk0 + W], "KWq%d" % g, "kw"))
            units.append((S_KC, 128, kcT[:, tk0:tk0 + W], "kcT", "kcvc"))
            units.append((S_VC, 128, vcT[:, tk0:tk0 + W], "vcT", "kcvc"))
            units = units[0:16] if PH[0] == 0 else units[16:22]
            if CUT == 9:
                pass
            elif CUT == 4.23:
                units = [(S_NQ + g * 64, 64, KSg[g][0:64, tk0:tk0 + W], "KSq%d" % g, "nq") for g in range(2)]
            elif CUT == 4.24:
                units = [(S_KS + g * 64, 64, Qg[g][0:64, 0, tk0:tk0 + W], "Qq%d" % g, "ks") for g in range(2)]
            elif CUT in (4.21, 4.26, 4.28):
                units = units[16:18]
            elif CUT == 4.22:
                units = units[18:20]
            elif CUT < 4.1:
                units = units[0:8]
            elif CUT < 4.2:
                units = units[0:16]
            elif CUT < 4.3:
                units = units[0:20]
            for u0 in range(0, len(units), 4):
                bank = tqi[0] % 2
                tqi[0] += 1
                grp = units[u0:u0 + 4]
                for ui, (so, ncol, dstap, dres, skey) in enumerate(grp):
                    for j in range(TB):
                        P.op("pe", lambda e, ui=ui, j=j, so=so, ncol=ncol, bank=bank: e.transpose(
                            tq[bank][0:ncol, ui * W + j * 128:ui * W + (j + 1) * 128],
                            stg[:, j, so:so + ncol], ident[:]),
                             r=stres[skey][j] + ["ident"], w=["tq%d" % bank])
                eng = "act" if (evi[0] % 2 == 0 or EVAC_ACT) else "dve"
                evi[0] += 1
                for ui, (so, ncol, dstap, dres, skey) in enumerate(grp):
                    dres = "%s_u%d" % (dres, u0 + ui)
                    if eng == "act":
                        P.op("act", lambda e, ui=ui, ncol=ncol, dstap=dstap, bank=bank: e.activation(
                            out=dstap, in_=tq[bank][0:ncol, ui * W:(ui + 1) * W], func=AF.Copy),
                             r=["tq%d" % bank], w=["%s_%d" % (dres, i)])
                    else:
                        P.op("dve", lambda e, ui=ui, ncol=ncol, dstap=dstap, bank=bank: e.tensor_copy(
                            out=dstap, in_=tq[bank][0:ncol, ui * W:(ui + 1) * W]),
                             r=["tq%d" % bank], w=["%s_%d" % (dres, i)])

        nblk = NT // TB
        for ph in range(2):
            PH[0] = ph
            load_pass(ph)
            for j in range(TB):
                emit_N(0, j)
            emit_T(0)
            for i in range(nblk):
                stres = {k: [None] * TB for k in ("dq", "dk", "nq", "kcvc", "ks", "kw")}
                for j in range(TB):
                    emit_proj(i, j, stres)
                    if i + 1 < nblk:
                        emit_N(i + 1, j)
                emit_QT(i, stres)
                if i + 1 < nblk:
                    emit_T(i + 1)
        P.barrier_all()
        P.flush()


def mixer_compress(nc, P, b, dd, t):
    tag = "Z%d" % b
    kcT, vcT, kcc, vca = t["kcT"], t["vcT"], t["kcc"], t["vca"]
    NCB = 127
    with ExitStack() as st:
        def sb(name, shape, dt):
            return st.enter_context(nc.sbuf_tensor(tag + name, shape, dt))

        def ps(name, shape, dt):
            return st.enter_context(nc.psum_tensor(tag + name, shape, dt))

        W1 = [sb("W1_%d" % kv, [128, 32, 256], BF16) for kv in range(2)]
        W2 = [sb("W2_%d" % kv, [128, 2, 64], BF16) for kv in range(2)]
        peT = [sb("peT%d" % kv, [128, 32], BF16) for kv in range(2)]
        pb = sb("pb", [128, 4], F32)
        xh = sb("xh", [128, 2, 128], F32)
        u = sb("u", [128, 2, 128], F32)
        sg = sb("sg", [128, 2, 128], F32)
        hact = sb("hact", [128, 2, 128], BF16)
        pbps = ps("pbps", [128, 4], F32)
        hps = [ps("hps%d" % i, [128, 2, 128], F32) for i in range(2)]
        cps = ps("cps", [128, 128], F32)

        for kv, (w1n, w2n, pen) in enumerate((("ck_w1", "ck_w2", "ck_peT"), ("cv_w1", "cv_w2", "cv_peT"))):
            w1v = dd[w1n].rearrange("(l d) h -> d l h", d=64)
            for half in range(2):
                P.dma("pool", lambda e, kv=kv, half=half, w1v=w1v: e.dma_start(
                    out=W1[kv][half * 64:(half + 1) * 64, :, :], in_=w1v),
                      "%sW1_%d" % (tag, kv), w=["W1_%d" % kv])
            P.dma("pool", lambda e, kv=kv, w2n=w2n: e.dma_start(
                out=W2[kv][:], in_=dd[w2n].rearrange("(c p) d -> p c d", p=128)),
                  "%sW2_%d" % (tag, kv), w=["W2_%d" % kv])
            P.dma("pool", lambda e, kv=kv, pen=pen: e.dma_start(out=peT[kv][:], in_=dd[pen]),
                  "%spe_%d" % (tag, kv), w=["peT%d" % kv])
        for kv in range(2):
            for ch in range(2):
                col = kv * 2 + ch
                for l in range(32):
                    P.op("pe", lambda e, kv=kv, ch=ch, l=l, col=col: e.matmul(
                        pbps[:, col:col + 1], W1[kv][0:64, l, ch * 128:(ch + 1) * 128], peT[kv][0:64, l:l + 1],
                        start=(l == 0), stop=(l == 31)),
                         r=["W1_%d" % kv, "peT%d" % kv], w=["pbps"])
        P.op("dve", lambda e: e.tensor_copy(out=pb[:], in_=pbps[:]), r=["pbps"], w=["pb"])
        hi = [0]
        for kv in range(2):
            xT = kcT if kv == 0 else vcT
            for g in range(2):
                hp = hps[hi[0] % 2]
                hpn = "hps%d" % (hi[0] % 2)
                hi[0] += 1
                for ch in range(2):
                    for l in range(32):
                        P.op("pe", lambda e, kv=kv, g=g, ch=ch, l=l, hp=hp, xT=xT: e.matmul(
                            hp[:, ch, 0:NCB], W1[kv][g * 64:(g + 1) * 64, l, ch * 128:(ch + 1) * 128],
                            xT[g * 64:(g + 1) * 64, l:l + 16 * (NCB - 1) + 1:16],
                            start=(l == 0), stop=(l == 31)),
                             r=["W1_%d" % kv], w=[hpn])
                for ch in range(2):
                    col = kv * 2 + ch
                    P.op("act", lambda e, ch=ch, col=col, hp=hp: e.activation(
                        out=xh[:, ch, 0:NCB], in_=hp[:, ch, 0:NCB], func=AF.Identity, bias=pb[:, col:col + 1]),
                         r=[hpn, "pb"], w=["xh%d" % ch])
                X, U, SG, HA = xh[:, :, 0:NCB], u[:, :, 0:NCB], sg[:, :, 0:NCB], hact[:, :, 0:NCB]
                P.op("dve", lambda e, X=X, U=U: e.tensor_tensor(out=U, in0=X, in1=X, op=ALU.mult),
                     r=["xh0", "xh1"], w=["u"])
                P.op("dve", lambda e, U=U: e.tensor_scalar(out=U, in0=U, scalar1=0.044715, scalar2=1.0,
                                                        op0=ALU.mult, op1=ALU.add), r=["u"], w=["u"])
                P.op("dve", lambda e, X=X, U=U: e.tensor_tensor(out=U, in0=U, in1=X, op=ALU.mult),
                     r=["u", "xh0", "xh1"], w=["u"])
                P.op("act", lambda e, U=U, SG=SG: e.activation(out=SG, in_=U, func=AF.Sigmoid,
                                                              scale=1.5957691216057308),
                     r=["u"], w=["sg"])
                P.op("dve", lambda e, X=X, SG=SG, HA=HA: e.tensor_tensor(out=HA, in0=X, in1=SG, op=ALU.mult),
                     r=["sg", "xh0", "xh1"], w=["hact"])
                if kv == 0:
                    for ch in range(2):
                        P.op("pe", lambda e, ch=ch: e.matmul(cps[0:64, 0:NCB], W2[0][:, ch, :], hact[:, ch, 0:NCB],
                                                            start=(ch == 0), stop=(ch == 1)),
                             r=["hact", "W2_0"], w=["cps"])
                    P.op("act", lambda e, g=g: e.activation(out=kcc[g][0:64, 0:NCB], in_=cps[0:64, 0:NCB],
                                                           func=AF.Copy), r=["cps"], w=["kcc%d" % g])
                else:
                    for ch in range(2):
                        P.op("pe", lambda e, ch=ch: e.matmul(cps[0:NCB, 0:64], hact[:, ch, 0:NCB], W2[1][:, ch, :],
                                                            start=(ch == 0), stop=(ch == 1)),
                             r=["hact", "W2_1"], w=["cps"])
                    P.op("act", lambda e, g=g: e.activation(out=vca[0:NCB, g, 0:64], in_=cps[0:NCB, 0:64],
                                                           func=AF.Copy), r=["cps"], w=["vca_v%d" % g])
        P.barrier_all()
        P.flush()


class Pipe:
    def __init__(self, lag=2):
        self.q, self.lag = [], lag

    def push(self, first, rest):
        if first is not None:
            first()
        self.q.append(rest)
        while len(self.q) > self.lag:
            self.q.pop(0)()

    def drain(self):
        while self.q:
            self.q.pop(0)()


def mixer_attn(nc, P, b, src, dst, dd, t):
    tag = "T%d" % b
    dqT, dkT, Qg, KSg, KWg = t["dqT"], t["dkT"], t["Qg"], t["KSg"], t["KWg"]
    dva, vsa, vwa, glr, kcc, vca, ident, epsb = (t["dva"], t["vsa"], t["vwa"], t["glr"], t["kcc"], t["vca"],
                                                 t["ident"], t["epsb"])
    with ExitStack() as st:
        def sb(name, shape, dt):
            return st.enter_context(nc.sbuf_tensor(tag + name, shape, dt))

        def ps(name, shape, dt):
            return st.enter_context(nc.psum_tensor(tag + name, shape, dt))

        om = sb("om", [128, NT, 1024], BF16)
        tmpf = [sb("tmpf%d" % i, [128, 64], F32) for i in range(2)]
        Wo = sb("Wo", [128, 8, 1024], BF16)
        pts = [sb("p%d" % i, [128, 512], BF16) for i in range(6)]
        gates = sb("gates", [128, NT, 24], F32)
        Am = sb("Am", [128, NT, 32], F32)
        Bm = sb("Bm", [128, NT, 32], F32)
        Gs = sb("Gs", [128, 128], F32)
        Gm = sb("Gm", [128, D], F32)
        lv = [sb("lv%d" % i, [128, 64], F32) for i in range(4)]
        lt = sb("lt", [128, 64], F32)
        le = sb("le", [128, 2], F32)
        neglam = sb("neglam", [128, 1], F32)
        o1 = [sb("o1_%d" % i, [128, 4, 128], F32) for i in range(2)]
        junk = sb("junk", [128, 128], F32)
        sm = [dict((n, sb("%s_%d" % (n, i), [128, w], F32)) for n, w in
                   (("rd", 4), ("nl", 4), ("ss4", 4), ("ln4", 4), ("r4", 4), ("den", 4), ("gr", 4), ("rs", 4),
                    ("rw", 4), ("imp", 32), ("impm", 32), ("wk", 32), ("m1", 8), ("m2", 8))) for i in range(2)]
        nsp = [sb("nsp%d" % i, [128, 96], BF16) for i in range(2)]
        omT = [sb("omT%d" % i, [128, 8, 128], BF16) for i in range(2)]
        xres = [sb("xres%d" % i, [128, D], F32) for i in range(2)]
        ytmp = [sb("ytmp%d" % i, [128, D], F32) for i in range(2)]
        ss2 = sb("ss2", [128, 2], F32)
        rstd2 = sb("rstd2", [128, 2], F32)
        sps = [ps("sps%d" % i, [128, 512], F32) for i in range(3)]
        ab = ps("ab", [128, 4, 512], F32)
        tps = ps("tps", [128, 1024], BF16)
        SPS = Rot(sps, ["sps%d" % i for i in range(3)])
        PT = Rot(pts, ["p%d" % i for i in range(6)])

        for nm, tl in (("Am", Am), ("Bm", Bm), ("Gs", Gs), ("Gm", Gm)):
            P.dma("sp", lambda e, nm=nm, tl=tl: e.dma_start(out=tl[:], in_=dd[nm]), tag + nm, w=[nm])
        for i, nm in enumerate(("lq1", "lk1", "lq2", "lk2")):
            P.dma("sp", lambda e, i=i, nm=nm: e.dma_start(out=lv[i][:], in_=dd[nm]), tag + nm, w=["lv%d" % i])
        wo_v = dd["w_out"].rearrange("(k p) f -> p k f", p=128)
        for kh in range(2):
            P.dma("pool", lambda e, kh=kh: e.dma_start(out=Wo[:, kh * 4:(kh + 1) * 4, :],
                                                       in_=wo_v[:, kh * 4:(kh + 1) * 4, :]),
                  tag + "Wo", w=["Wo"])
        for nm in ("nsp0", "nsp1"):
            pass
        P.op("pool", lambda e: e.memset(nsp[0][:], 0.0), w=["nsp0"])
        P.op("pool", lambda e: e.memset(nsp[1][:], 0.0), w=["nsp1"])
        P.op("dve", lambda e: e.tensor_scalar(out=Gs[:], in0=Gs[:], scalar1=0.8, scalar2=None, op0=ALU.mult),
             r=["Gs"], w=["Gs"])
        P.op("act", lambda e: e.activation(out=gates[:], in_=glr[:], func=AF.Sigmoid), w=["gates"])
        for i in range(2):
            P.op("dve", lambda e, i=i: e.tensor_tensor(out=lt[:], in0=lv[2 * i][:], in1=lv[2 * i + 1][:],
                                                       op=ALU.mult),
                 r=["lv%d" % (2 * i), "lv%d" % (2 * i + 1)], w=["lt"])
            P.op("dve", lambda e, i=i: e.reduce_sum(out=le[:, i:i + 1], in_=lt[:], axis=AX.X),
                 r=["lt"], w=["le%d" % i])
        P.op("act", lambda e: e.activation(out=le[:], in_=le[:], func=AF.Exp), r=["le0", "le1"], w=["le0", "le1"])
        P.op("dve", lambda e: e.tensor_tensor(out=neglam[:], in0=le[:, 1:2], in1=le[:, 0:1], op=ALU.subtract),
             r=["le0", "le1"], w=["neglam"])
        P.op("dve", lambda e: e.tensor_scalar(out=neglam[:], in0=neglam[:], scalar1=-0.2, scalar2=None,
                                              op0=ALU.add), r=["neglam"], w=["neglam"])

        pipe = Pipe(4)
        first_in_bank = {}
        SPS3 = SPS
        SPS = Rot(sps + [ab[:, 2, :], ab[:, 3, :]], ["sps0", "sps1", "sps2", "ab2", "ab3"])

        def acc_mm(bank_i, col0, ncol, lhsT, rhs, last, reads):
            bn = "ab%d" % bank_i
            first = first_in_bank.get(bn, True)
            first_in_bank[bn] = False
            P.op("pe", lambda e: e.matmul(ab[:, bank_i, col0:col0 + ncol], lhsT, rhs, start=first, stop=last,
                                          skip_group_check=True),
                 r=reads, w=[bn])

        def exp_tile(spt, spn, rows, masks):
            p, pn = PT.next()
            P.op("act", lambda e: e.activation(out=p[0:rows, :], in_=spt[0:rows, :], func=AF.Exp, scale=0.125),
                 r=[spn], w=[pn])
            for (pattern, base, cm) in masks:
                P.op("pool", lambda e, pattern=pattern, base=base, cm=cm: e.affine_select(
                    out=p[0:rows, :], in_=p[0:rows, :], pattern=pattern, compare_op=ALU.is_ge, fill=0.0,
                    base=base, channel_multiplier=cm), r=[pn], w=[pn])
            return p, pn

        ci = 0
        for h in range(4):
            for qb in range(4):
                ob, obn = o1[(h * 4 + qb) % 2], "o1_%d" % ((h * 4 + qb) % 2)
                smx = sm[(h * 4 + qb) % 2]
                sfx = "_%d" % ((h * 4 + qb) % 2)
                for m in range(2):
                    bA, bB = 0, 1
                    ci += 1
                    nkt = 4 * qb + 4
                    for kt in range(nkt):
                        spt, spn = SPS.next()

                        def qk(spt=spt, spn=spn, kt=kt, m=m, h=h, qb=qb):
                            P.op("pe", lambda e: e.matmul(
                                spt[:], dkT[m * 64:(m + 1) * 64, h, kt * 128:(kt + 1) * 128],
                                dqT[m * 64:(m + 1) * 64, h, qb * 512:(qb + 1) * 512], start=True, stop=True),
                                 w=[spn])

                        def rest(spt=spt, spn=spn, kt=kt, h=h, qb=qb, bA=bA, bB=bB):
                            masks = []
                            if kt >= 4 * qb:
                                masks.append(([[1, 512]], qb * 512 - kt * 128, -1))
                            p, pn = exp_tile(spt, spn, 128, masks)
                            if kt == 0:
                                first_in_bank["ab%d" % bA] = True
                                first_in_bank["ab%d" % bB] = True
                            for jq in range(4):
                                if kt > 4 * qb + jq:
                                    continue
                                bank_i, col0 = (bA, jq * 130) if jq < 3 else (bB, 0)
                                acc_mm(bank_i, col0, 129, p[:, jq * 128:(jq + 1) * 128], dva[:, kt, h, 0:129],
                                       kt == 4 * qb + jq, [pn])

                        pipe.push(qk, rest)

                    def post(m=m, h=h, qb=qb, bA=bA, bB=bB, ob=ob, obn=obn, smx=smx, sfx=sfx):
                        bnA, bnB = "ab%d" % bA, "ab%d" % bB
                        rd = smx["rd"]
                        denA = ab[:, bA, 0:390].rearrange("p (j c) -> p j c", c=130)[:, :, 128]
                        P.op("dve", lambda e: e.reciprocal(out=rd[:, 0:3], in_=denA), r=[bnA], w=["rdA" + sfx])
                        P.op("dve", lambda e: e.reciprocal(out=rd[:, 3:4], in_=ab[:, bB, 128:129]), r=[bnB],
                             w=["rdB" + sfx])

                        def region(jq):
                            return (ab[:, bA, jq * 130:jq * 130 + 128], bnA) if jq < 3 else (ab[:, bB, 0:128], bnB)

                        if m == 0:
                            for jq in range(4):
                                reg, bn = region(jq)
                                P.op("act", lambda e, reg=reg, jq=jq: e.activation(
                                    out=ob[:, jq, :], in_=reg, func=AF.Copy, scale=rd[:, jq:jq + 1]),
                                     r=[bn, "rdA" + sfx, "rdB" + sfx], w=[obn + "_%d" % jq])
                        else:
                            nl, ss4, ln4, r4 = smx["nl"], smx["ss4"], smx["ln4"], smx["r4"]
                            P.op("dve", lambda e: e.tensor_scalar(out=nl[:], in0=rd[:], scalar1=neglam[:, 0:1],
                                                                  scalar2=None, op0=ALU.mult),
                                 r=["rdA" + sfx, "rdB" + sfx, "neglam"], w=["nl" + sfx])
                            for jq in range(4):
                                reg, bn = region(jq)
                                P.op("dve", lambda e, reg=reg, jq=jq: e.scalar_tensor_tensor(
                                    out=ob[:, jq, :], in0=reg, scalar=nl[:, jq:jq + 1], in1=ob[:, jq, :],
                                    op0=ALU.mult, op1=ALU.add),
                                     r=[bn, "nl" + sfx, obn + "_%d" % jq], w=[obn + "_%d" % jq])
                                P.op("act", lambda e, jq=jq: e.activation(
                                    out=junk[:], in_=ob[:, jq, :], func=AF.Square, accum_out=ss4[:, jq:jq + 1]),
                                     r=[obn + "_%d" % jq], w=["junk", "ss4%s_%d" % (sfx, jq)])
                            ssr = ["ss4%s_%d" % (sfx, jq) for jq in range(4)]
                            P.op("act", lambda e: e.activation(out=ln4[:], in_=ss4[:], func=AF.Ln, scale=1.0 / 128,
                                                               bias=epsb[:, 0:1]), r=ssr + ["epsb"], w=["ln4" + sfx])
                            P.op("act", lambda e: e.activation(out=r4[:], in_=ln4[:], func=AF.Exp, scale=-0.5),
                                 r=["ln4" + sfx], w=["r4" + sfx])
                            for jq in range(4):
                                T = 4 * qb + jq
                                P.op("dve", lambda e, jq=jq, T=T: e.scalar_tensor_tensor(
                                    out=om[:, T, h * 128:(h + 1) * 128], in0=ob[:, jq, :], scalar=r4[:, jq:jq + 1],
                                    in1=Gs[:], op0=ALU.mult, op1=ALU.mult),
                                     r=[obn + "_%d" % jq, "r4" + sfx, "Gs"], w=["om%d_d%d" % (T, h)])

                    pipe.push(None, post)
        pipe.drain()
        pipe.lag = 2
        SPS = SPS3

        def stage1(g, T):
            if True:
                ci = g * NT + T
                bk = 2 + (ci % 2)
                smx = sm[ci % 2]
                sfx = "_%d" % (ci % 2)
                nspt, nspn = nsp[ci % 2], "nsp%d" % (ci % 2)
                spt, spn = SPS.next()

                def qk(spt=spt, spn=spn, g=g, T=T):
                    P.op("pe", lambda e: e.matmul(spt[0:127, :], kcc[g][0:64, 0:127],
                                                  Qg[g][0:64, :, T * 128:(T + 1) * 128], start=True, stop=True),
                         w=[spn])

                def rest(spt=spt, spn=spn, g=g, T=T, bk=bk):
                    p, pn = exp_tile(spt, spn, 127, [([[0, 4], [1, 128]], T * 128 - 31, -16)])
                    for hg in range(4):
                        P.op("pe", lambda e, hg=hg: e.matmul(ab[:, bk, hg * 128:hg * 128 + 97],
                                                             p[0:127, hg * 128:(hg + 1) * 128], vca[0:127, g, 0:97],
                                                             start=True, stop=True, skip_group_check=True),
                             r=[pn], w=["ab%d" % bk])

                def post(g=g, T=T, bk=bk, smx=smx, sfx=sfx, nspt=nspt, nspn=nspn):
                    bn = "ab%d" % bk
                    den, rd, gr, imp, impm, wk, m1, m2 = (smx["den"], smx["rd"], smx["gr"], smx["imp"],
                                                          smx["impm"], smx["wk"], smx["m1"], smx["m2"])
                    ov4 = ab[:, bk, :].rearrange("p (h c) -> p h c", c=128)
                    gv = gates[:, T, :].rearrange("p (h c) -> p h c", c=3)
                    P.op("dve", lambda e: e.tensor_scalar(out=den[:], in0=ov4[:, :, 64], scalar1=1e-30, scalar2=None,
                                                          op0=ALU.max), r=[bn], w=["den" + sfx])
                    P.op("dve", lambda e: e.reciprocal(out=rd[:], in_=den[:]), r=["den" + sfx], w=["rd" + sfx])
                    P.op("dve", lambda e: e.tensor_tensor(out=gr[:], in0=rd[:], in1=gv[:, g * 4:(g + 1) * 4, 0],
                                                          op=ALU.mult), r=["rd" + sfx, "gates"], w=["gr" + sfx])
                    for hg in range(4):
                        hd = g * 4 + hg
                        P.op("act", lambda e, hg=hg, hd=hd: e.activation(
                            out=om[:, T, 512 + hd * 64:512 + (hd + 1) * 64], in_=ab[:, bk, hg * 128:hg * 128 + 64],
                            func=AF.Copy, scale=gr[:, hg:hg + 1]), r=[bn, "gr" + sfx], w=["om%d_n%d" % (T, hd)])
                    P.op("dve", lambda e: e.tensor_scalar(out=imp[:], in0=ab[:, bk, 65:97], scalar1=rd[:, 0:1],
                                                          scalar2=None, op0=ALU.mult),
                         r=[bn, "rd" + sfx], w=["imp" + sfx])
                    for hg in range(1, 4):
                        P.op("dve", lambda e, hg=hg: e.scalar_tensor_tensor(
                            out=imp[:], in0=ab[:, bk, hg * 128 + 65:hg * 128 + 97], scalar=rd[:, hg:hg + 1],
                            in1=imp[:], op0=ALU.mult, op1=ALU.add), r=[bn, "rd" + sfx, "imp" + sfx],
                             w=["imp" + sfx])
                    P.op("dve", lambda e: e.tensor_tensor(out=impm[:], in0=imp[:], in1=Am[:, T, :], op=ALU.mult),
                         r=["imp" + sfx, "Am"], w=["impm" + sfx])
                    P.op("dve", lambda e: e.tensor_tensor(out=impm[:], in0=impm[:], in1=Bm[:, T, :], op=ALU.add),
                         r=["impm" + sfx, "Bm"], w=["impm" + sfx])
                    P.op("dve", lambda e: e.max(out=m1[:], in_=impm[:]), r=["impm" + sfx], w=["m1" + sfx])
                    P.op("dve", lambda e: e.match_replace(out=wk[:], in_to_replace=m1[:], in_values=impm[:],
                                                          imm_value=-2.0),
                         r=["impm" + sfx, "m1" + sfx], w=["wk" + sfx])
                    P.op("dve", lambda e: e.max(out=m2[:], in_=wk[:]), r=["wk" + sfx], w=["m2" + sfx])
                    P.op("dve", lambda e: e.tensor_scalar(out=nspt[:, 64:96], in0=impm[:], scalar1=m2[:, 7:8],
                                                          scalar2=-BIG, op0=ALU.is_lt, op1=ALU.mult),
                         r=["impm" + sfx, "m2" + sfx], w=[nspn])
                    P.op("pe", lambda e: e.transpose(tps[0:96, 0:128], nspt[:, 0:96], ident[:]),
                         r=[nspn, "ident"], w=["tps"])
                    tsl = slice(T * 128, (T + 1) * 128)
                    P.op("act", lambda e: e.activation(out=Qg[g][64:96, 0, tsl], in_=tps[64:96, 0:128],
                                                       func=AF.Copy), r=["tps"], w=["Qs%d_%d" % (g, T)])
                    for hg in range(1, 4):
                        P.op("pool", lambda e, hg=hg: e.tensor_copy(out=Qg[g][64:96, hg, tsl],
                                                                    in_=Qg[g][64:96, 0, tsl]),
                             r=["Qs%d_%d" % (g, T)], w=["Qs%d_%d_%d" % (g, T, hg)])

                pipe.push(qk, rest)
                pipe.push(None, post)

        def stage2(g, T):
            if True:
                ci = g * NT + T
                bS, bW = 0, 1
                smx = sm[ci % 2]
                sfx = "_%d" % (ci % 2)
                selr = ["Qs%d_%d" % (g, T)] + ["Qs%d_%d_%d" % (g, T, hg) for hg in range(1, 4)]
                jobs = [("s", kt) for kt in range(T + 1)] + [("w", kt) for kt in range(max(0, T - 4), T + 1)]
                for kind, kt in jobs:
                    spt, spn = SPS.next()

                    def qk(spt=spt, spn=spn, kind=kind, kt=kt, g=g, T=T, selr=selr):
                        ksl = slice(kt * 128, (kt + 1) * 128)
                        tsl = slice(T * 128, (T + 1) * 128)
                        if kind == "s":
                            P.op("pe", lambda e: e.matmul(spt[:], KSg[g][0:96, ksl], Qg[g][0:96, :, tsl],
                                                          start=True, stop=True), r=selr, w=[spn])
                        else:
                            P.op("pe", lambda e: e.matmul(spt[:], KWg[g][0:64, ksl], Qg[g][0:64, :, tsl],
                                                          start=True, stop=True), w=[spn])

                    def rest(spt=spt, spn=spn, kind=kind, kt=kt, g=g, T=T, bS=bS, bW=bW):
                        masks = []
                        if kt == T:
                            masks.append(([[0, 4], [1, 128]], 0, -1))
                        if kind == "w" and kt == T - 4:
                            masks.append(([[0, 4], [-1, 128]], -1, 1))
                        p, pn = exp_tile(spt, spn, 128, masks)
                        va = vsa if kind == "s" else vwa
                        bank_i = bS if kind == "s" else bW
                        if kt == (0 if kind == "s" else max(0, T - 4)):
                            first_in_bank["ab%d" % bank_i] = True
                        for hg in range(4):
                            acc_mm(bank_i, hg * 128, 65, p[:, hg * 128:(hg + 1) * 128], va[:, kt, g, 0:65],
                                   kt == T, [pn])

                    pipe.push(qk, rest)

                def post(g=g, T=T, bS=bS, bW=bW, smx=smx, sfx=sfx):
                    bnS, bnW = "ab%d" % bS, "ab%d" % bW
                    rs, rw = smx["rs"], smx["rw"]
                    gv = gates[:, T, :].rearrange("p (h c) -> p h c", c=3)
                    oS = ab[:, bS, :].rearrange("p (h c) -> p h c", c=128)
                    oW = ab[:, bW, :].rearrange("p (h c) -> p h c", c=128)
                    P.op("dve", lambda e: e.reciprocal(out=rs[:], in_=oS[:, :, 64]), r=[bnS], w=["rs" + sfx])
                    P.op("dve", lambda e: e.tensor_tensor(out=rs[:], in0=rs[:], in1=gv[:, g * 4:(g + 1) * 4, 1],
                                                          op=ALU.mult), r=["rs" + sfx, "gates"], w=["rs" + sfx])
                    P.op("dve", lambda e: e.reciprocal(out=rw[:], in_=oW[:, :, 64]), r=[bnW], w=["rw" + sfx])
                    P.op("dve", lambda e: e.tensor_tensor(out=rw[:], in0=rw[:], in1=gv[:, g * 4:(g + 1) * 4, 2],
                                                          op=ALU.mult), r=["rw" + sfx, "gates"], w=["rw" + sfx])
                    for hg in range(4):
                        hd = g * 4 + hg
                        on = "om%d_n%d" % (T, hd)
                        osl = om[:, T, 512 + hd * 64:512 + (hd + 1) * 64]
                        tf, tfn = tmpf[hg % 2], "tmpf%d" % (hg % 2)
                        P.op("dve", lambda e, hg=hg, osl=osl, tf=tf: e.scalar_tensor_tensor(
                            out=tf[:], in0=ab[:, bS, hg * 128:hg * 128 + 64], scalar=rs[:, hg:hg + 1], in1=osl,
                            op0=ALU.mult, op1=ALU.add), r=[bnS, "rs" + sfx, on], w=[tfn])
                        P.op("dve", lambda e, hg=hg, osl=osl, tf=tf: e.scalar_tensor_tensor(
                            out=osl, in0=ab[:, bW, hg * 128:hg * 128 + 64], scalar=rw[:, hg:hg + 1], in1=tf[:],
                            op0=ALU.mult, op1=ALU.add), r=[bnW, "rw" + sfx, tfn], w=[on])

                pipe.push(None, post)

        for g in range(2):
            stage1(g, 0)
            stage1(g, 1)
            for T in range(NT):
                stage2(g, T)
                if T + 2 < NT:
                    stage1(g, T + 2)
        pipe.drain()

        for T in range(NT):
            q = T % 2
            t0 = T * 128
            omr = ["om%d_d%d" % (T, h) for h in range(4)] + ["om%d_n%d" % (T, hd) for hd in range(8)]
            for k in range(8):
                P.op("pe", lambda e, k=k, T=T: e.transpose(tps[:, k * 128:(k + 1) * 128],
                                                          om[:, T, k * 128:(k + 1) * 128], ident[:]),
                     r=omr + ["ident"], w=["tps"])
            oT, oTn = omT[q], "omT%d" % q
            P.op("act", lambda e, oT=oT: e.activation(out=oT[:].rearrange("p k t -> p (k t)"), in_=tps[:],
                                                      func=AF.Copy), r=["tps"], w=[oTn])
            b0 = 2 * q
            for n in range(2):
                for k in range(8):
                    P.op("pe", lambda e, n=n, k=k, oT=oT, b0=b0: e.matmul(
                        ab[:, b0 + n, :], oT[:, k, :], Wo[:, k, n * 512:(n + 1) * 512], start=(k == 0),
                        stop=(k == 7)), r=[oTn, "Wo"], w=["ab%d" % (b0 + n)])
            xr, yt = xres[q], ytmp[q]
            xrn, ytn = "xres%d" % q, "ytmp%d" % q
            wo2 = ab[:, b0:b0 + 2, :]
            br = ["ab%d" % b0, "ab%d" % (b0 + 1)]
            P.dma("act", lambda e, xr=xr, t0=t0: e.dma_start(out=xr[:], in_=src[b, t0:t0 + 128, :]),
                  tag + xrn + "i", w=[xrn])
            P.op("act", lambda e, yt=yt, q=q, wo2=wo2: e.activation(
                out=yt[:].rearrange("p (n f) -> p n f", n=2), in_=wo2, func=AF.Square, accum_out=ss2[:, q:q + 1]),
                 r=br, w=[ytn, "ss2%d" % q])
            P.op("act", lambda e, q=q: e.activation(out=ss2[:, q:q + 1], in_=ss2[:, q:q + 1], func=AF.Sqrt,
                                                    scale=1.0 / D, bias=epsb[:, 0:1]),
                 r=["ss2%d" % q, "epsb"], w=["ss2%d" % q])
            P.op("dve", lambda e, q=q: e.reciprocal(out=rstd2[:, q:q + 1], in_=ss2[:, q:q + 1]),
                 r=["ss2%d" % q], w=["rstd2%d" % q])
            P.op("dve", lambda e, yt=yt, q=q, wo2=wo2: e.scalar_tensor_tensor(
                out=yt[:].rearrange("p (n f) -> p n f", n=2), in0=wo2, scalar=rstd2[:, q:q + 1],
                in1=Gm[:].rearrange("p (n f) -> p n f", n=2), op0=ALU.mult, op1=ALU.mult),
                 r=br + ["rstd2%d" % q, "Gm"], w=[ytn])
            P.op("pool", lambda e, xr=xr, yt=yt: e.tensor_tensor(out=xr[:], in0=xr[:], in1=yt[:], op=ALU.add),
                 r=[ytn, xrn], w=[xrn])
            P.dma("sp", lambda e, xr=xr, t0=t0: e.dma_start(out=dst[b, t0:t0 + 128, :], in_=xr[:]),
                  tag + xrn + "o", r=[xrn], w=["dst_B_%d_%d" % (b, t0)])
        P.barrier_all()
        P.flush()


def build(stage=3):
    nc = bass.Bass("TRN2", target_bir_lowering=False)

    def dt(n, s, d=F32, k="ExternalInput"):
        return nc.dram_tensor(n, s, d, kind=k).ap()

    x = dt("x", [NB, S, D])
    out = dt("out", [NB, S, D], F32, "ExternalOutput")
    ident_d = dt("ident", [128, 128], BF16)
    f = {}
    for t in ("f1", "f2"):
        f[t] = dict(wg=dt(t + "_wg", [D, DFF]), wu=dt(t + "_wu", [D, DFF]), wd=dt(t + "_wd", [DFF, D]),
                    gpre=dt(t + "_gpre", [128, 8]), gpost=dt(t + "_gpost", [128, D]))
    dd = dict(ident=ident_d,
              w_in=dt("w_in", [D, C_END]), w_out=dt("w_out", [D, D]), m_gpre=dt("m_gpre", [128, 8]),
              Gm=dt("Gm", [128, D]), Gs=dt("Gs", [128, 128]),
              cosR=dt("cosR", [128, NT, 8, 8]), sinR=dt("sinR", [128, NT, 8, 8]),
              cosR2=dt("cosR2", [128, NT, 8, 8]), sinR2=dt("sinR2", [128, NT, 8, 8]),
              ov=dt("ov", [127, 32], BF16), ET=dt("ET", [32, S], BF16),
              Am=dt("Am", [128, NT, 32]), Bm=dt("Bm", [128, NT, 32]),
              ck_w1=dt("ck_w1", [2048, 256]), ck_w2=dt("ck_w2", [256, 64]), ck_peT=dt("ck_peT", [128, 32]),
              cv_w1=dt("cv_w1", [2048, 256]), cv_w2=dt("cv_w2", [256, 64]), cv_peT=dt("cv_peT", [128, 32]),
              lq1=dt("lq1", [128, 64]), lk1=dt("lk1", [128, 64]), lq2=dt("lq2", [128, 64]),
              lk2=dt("lk2", [128, 64]))
    x1 = nc.dram_tensor("x1s", [NB, S, D], F32).ap()
    x2 = nc.dram_tensor("x2s", [NB, S, D], F32).ap()
    with ExitStack() as stack:
        P = Prog(nc, stack)
        fa = f["f1"]
        ffn_phase(nc, P, "A", x, out if stage == 1 else x1, fa["wg"], fa["wu"], fa["wd"], fa["gpre"],
                  fa["gpost"], ident_d)
        if stage >= 2:
            mixer_phase(nc, P, x1, out if stage == 2 else x2, dd)
        if stage >= 3:
            fc = f["f2"]
            ffn_phase(nc, P, "C", x2, out, fc["wg"], fc["wu"], fc["wd"], fc["gpre"], fc["gpost"], ident_d)
    return nc


def host_inputs(inp):
    def g(k):
        return np.ascontiguousarray(np.asarray(inp[k], dtype=np.float32))

    bf = ml_dtypes.bfloat16

    def bc(v, n=128):
        return np.ascontiguousarray(np.broadcast_to(v[None, :], (n, v.shape[0])))

    common = {"ident": np.eye(128, dtype=np.float32).astype(bf)}
    for t, pfx in (("f1", "ff1"), ("f2", "ff2")):
        common[t + "_wg"] = g(pfx + "_w_gate")[0]
        common[t + "_wu"] = g(pfx + "_w_up")[0]
        common[t + "_wd"] = g(pfx + "_w_down")[0]
        common[t + "_gpre"] = np.ascontiguousarray(g(pfx + "_norm_pre")[0].reshape(8, 128).T)
        common[t + "_gpost"] = bc(g(pfx + "_norm_post")[0])
    common["w_in"] = g("w_in")[0]
    common["w_out"] = g("w_out")[0]
    common["m_gpre"] = np.ascontiguousarray(g("mix_norm_pre")[0].reshape(8, 128).T)
    common["Gm"] = bc(g("mix_norm_post")[0])
    common["Gs"] = bc(g("diff_subln")[0])
    for k, n in (("lq1", "lambda_q1"), ("lk1", "lambda_k1"), ("lq2", "lambda_q2"), ("lk2", "lambda_k2")):
        common[k] = bc(g(n)[0])
    for kv, w1n, w2n, pen in (("ck", "cmp_k_w1", "cmp_k_w2", "cmp_pe_k"), ("cv", "cmp_v_w1", "cmp_v_w2", "cmp_pe_v")):
        common[kv + "_w1"] = g(w1n)[0]
        common[kv + "_w2"] = g(w2n)[0]
        peT = g(pen)[0].T
        common[kv + "_peT"] = np.ascontiguousarray(np.concatenate([peT, peT], axis=0))
    pos = np.arange(S, dtype=np.float32)
    inv = (np.float32(500000.0) ** (-np.arange(0, 16, 2, dtype=np.float32) / np.float32(16))).astype(np.float32)
    ang = (pos[:, None] * inv[None, :]).astype(np.float32)
    cs, sn = np.cos(ang).astype(np.float32), np.sin(ang).astype(np.float32)

    def tab(a):
        a = a.reshape(NT, 128, 8).transpose(1, 0, 2)
        return np.ascontiguousarray(np.broadcast_to(a[:, :, None, :], (128, NT, 8, 8)))

    common["cosR"], common["sinR"] = tab(cs), tab(sn)
    c2, s2 = tab(cs).copy(), tab(sn).copy()
    c2[:, :, 4:8, :] = 1.0
    s2[:, :, 4:8, :] = 0.0
    common["cosR2"], common["sinR2"] = c2, s2
    c = np.arange(127)[:, None] * 16
    j = np.arange(32)[None, :] * 64
    ov = np.clip(np.minimum(c + 32, j + 64) - np.maximum(c, j), 0, None) / 32.0
    common["ov"] = ov.astype(np.float32).astype(bf)
    common["ET"] = (np.arange(S)[None, :] // 64 == np.arange(32)[:, None]).astype(np.float32).astype(bf)
    tt = np.arange(S)
    cur = (tt // 64)[:, None]
    blk = np.arange(32)[None, :]
    forced = (blk == 0) | ((blk <= cur) & (blk >= cur - 1))
    causal = blk <= cur
    A = (~forced & causal).astype(np.float32)
    Bc = np.where(forced, np.float32(1e9), np.where(causal, np.float32(0.0), np.float32(-1.0))).astype(np.float32)
    common["Am"] = np.ascontiguousarray(A.reshape(NT, 128, 32).transpose(1, 0, 2))
    common["Bm"] = np.ascontiguousarray(Bc.reshape(NT, 128, 32).transpose(1, 0, 2))
    x = g("x")
    maps = []
    for c_ in range(8):
        m = dict(common)
        m["x"] = x[c_ * NB:(c_ + 1) * NB]
        maps.append(m)
    return maps


def kernel(**inputs):
    nc = build(3)
    maps = host_inputs(inputs)
    res = run_bass_kernel_spmd(nc, maps, core_ids=list(range(8)))
    return np.concatenate([np.asarray(r["out"]) for r in res.results], axis=0).astype(np.float32)
```

```python
import numpy as np
import ml_dtypes
from contextlib import ExitStack
import concourse.bass as bass
import concourse.mybir as mybir
from concourse.bass_utils import run_bass_kernel_spmd

F32 = mybir.dt.float32
BF16 = mybir.dt.bfloat16
AF = mybir.ActivationFunctionType
ALU = mybir.AluOpType
AX = mybir.AxisListType

S = 2048
D = 1024
DFF = 2816
NF = DFF // 128
NB = 2
EPS = 1e-6
ENGS = ("pe", "act", "dve", "pool", "sp")


import re
PSUM_RE = re.compile(r"^(tr\d|gu\d|dn\d|pj\d|tq\d|pbps|hps\d|cps|sps\d|ab\d|tps)")


class Res:
    __slots__ = ("name", "w", "r")

    def __init__(self, name):
        self.name = name
        self.w = None
        self.r = {}


class Op:
    __slots__ = ("eng", "fn", "deps", "kind", "key", "val", "idx", "sig", "waits", "ordinal")


class Prog:
    def __init__(self, nc, stack):
        self.nc = nc
        self.stack = stack
        self.res = {}
        self.ops = []
        self.eng_n = {e: 0 for e in ENGS}
        self.sigbase = {e: 0 for e in ENGS}
        self.esem = {e: stack.enter_context(nc.semaphore("sem_" + e)) for e in ENGS if e != "sp"}
        self.dsem = {}
        self.dcount = {}
        self.free_sems = {"sw": [], "hw": []}
        self.dcls = {}
        self.live = []
        self.seen = {e: {} for e in ENGS}
        self.sigord = {e: {} for e in ENGS}

    def R(self, name):
        r = self.res.get(name)
        if r is None:
            r = self.res[name] = Res(name)
        return r

    def _deps(self, reads, writes):
        deps = []
        for n in reads:
            r = self.R(n)
            if r.w is not None:
                deps.append(r.w)
        for n in writes:
            r = self.R(n)
            if r.w is not None:
                deps.append(r.w)
            deps.extend(r.r.values())
        return deps

    def _mark(self, reads, writes, ev, rkey):
        for n in reads:
            self.R(n).r[rkey] = ev
        for n in writes:
            r = self.R(n)
            r.w = ev
            r.r = {}

    def op(self, eng, fn, r=(), w=()):
        o = Op()
        o.eng = eng
        o.fn = fn
        o.kind = "c"
        o.deps = self._deps(r, w)
        for n in r:
            if PSUM_RE.match(n):
                o.deps.extend(ev for k, ev in self.R(n).r.items() if k != eng)
        o.idx = self.eng_n[eng]
        self.eng_n[eng] += 1
        o.sig = False
        self._mark(r, w, ("c", eng, o.idx), eng)
        self.ops.append(o)
        return o

    def dma(self, q, fn, key, r=(), w=()):
        o = Op()
        o.eng = q
        o.fn = fn
        o.kind = "d"
        o.key = key
        if key not in self.dsem:
            cls = "sw" if q == "pool" else "hw"
            self.dcls[key] = cls
            if self.free_sems[cls]:
                self.dsem[key], self.dcount[key] = self.free_sems[cls].pop()
            else:
                self.dsem[key] = self.stack.enter_context(self.nc.semaphore("dsem%d" % len(self.dsem)))
                self.dcount[key] = 0
            self.live.append(key)
        o.deps = self._deps(r, w)
        self.dcount[key] += 16
        o.val = self.dcount[key]
        o.idx = self.eng_n[q]
        self.eng_n[q] += 1
        o.sig = False
        self._mark(r, w, ("d", key, o.val), "d_" + key)
        self.ops.append(o)
        return o

    def barrier_all(self):
        allres = list(self.res.keys())
        for e in ENGS:
            self.op(e, None, r=allres)
        self.res = {}
        for k in self.live:
            self.free_sems[self.dcls[k]].append((self.dsem[k], self.dcount[k]))
        self.live = []

    def flush(self):
        nc = self.nc
        ops = self.ops
        self.ops = []
        self.nflush = getattr(self, "nflush", 0) + 1
        if LIMIT is not None and self.nflush == LIMIT[0]:
            ops = ops[:LIMIT[1]]
        byidx = {}
        for o in ops:
            if o.kind == "c":
                byidx[(o.eng, o.idx)] = o
        for o in ops:
            o.waits = []
            seen = self.seen[o.eng]
            best = {}
            for d in o.deps:
                if d[0] == "c":
                    if d[1] == "pe" and o.eng == "pe":
                        continue
                    k = ("c", d[1])
                else:
                    k = ("d", d[1])
                if k not in best or d[2] > best[k][2]:
                    best[k] = d
            for d in best.values():
                if d[0] == "c":
                    _, pe, pi = d
                    if seen.get(pe, -1) >= pi:
                        continue
                    prod = byidx.get((pe, pi))
                    if prod is None:
                        assert pi in self.sigord[pe], (pe, pi)
                    else:
                        prod.sig = True
                    seen[pe] = pi
                    o.waits.append(d)
                else:
                    _, key, val = d
                    k = "d_" + key
                    if seen.get(k, 0) >= val:
                        continue
                    seen[k] = val
                    o.waits.append(d)
        last = {}
        for o in ops:
            if o.kind == "c":
                last[o.eng] = o
        for o in last.values():
            o.sig = True
        for o in ops:
            if o.kind == "c" and o.sig:
                self.sigbase[o.eng] += 1
                self.sigord[o.eng][o.idx] = self.sigbase[o.eng]
        per = {e: [o for o in ops if o.eng == e] for e in ENGS}

        def emit(eng_name, eng):
            for o in per[eng_name]:
                for d in o.waits:
                    if d[0] == "c":
                        eng.wait_ge(self.esem[d[1]], self.sigord[d[1]][d[2]])
                    else:
                        eng.wait_ge(self.dsem[d[1]], d[2])
                if o.fn is None:
                    if o.kind == "c" and o.sig:
                        eng.nop().then_inc(self.esem[eng_name], 1) if eng_name != "sp" else None
                    continue
                ins = o.fn(eng)
                if o.kind == "d":
                    ins.then_inc(self.dsem[o.key], 16)
                elif o.sig:
                    ins.then_inc(self.esem[eng_name], 1)

        with nc.Block() as block:
            @block.tensor
            def _(e):
                emit("pe", e)

            @block.scalar
            def _(e):
                emit("act", e)

            @block.vector
            def _(e):
                emit("dve", e)

            @block.gpsimd
            def _(e):
                emit("pool", e)

            @block.sync
            def _(e):
                emit("sp", e)


def ffn_phase(nc, P, tag, src, dst, wg_d, wu_d, wd_d, gpre_d, gpost_d, ident_d):
    with ExitStack() as st:
        def sb(name, shape, dt):
            return st.enter_context(nc.sbuf_tensor(tag + name, shape, dt))

        def ps(name, shape, dt):
            return st.enter_context(nc.psum_tensor(tag + name, shape, dt))

        Wg = sb("Wg", [128, 8, DFF], BF16)
        Wu = sb("Wu", [128, 8, DFF], BF16)
        Wd = sb("Wd", [128, NF, D], BF16)
        xin = [sb("xin%d" % i, [128, D], F32) for i in range(2)]
        hn = sb("hn", [128, 4, D], BF16)
        hT = sb("hT", [128, 8, 512], BF16)
        actT = sb("actT", [128, NF, 512], BF16)
        G = sb("G", [128, D], F32)
        gpre = sb("gpre", [128, 8], F32)
        ident = sb("ident", [128, 128], BF16)
        xres = [sb("xres%d" % i, [128, D], F32) for i in range(2)]
        ytmp = [sb("ytmp%d" % i, [128, D], F32) for i in range(2)]
        sg = [sb("sg%d" % i, [128, 512], F32) for i in range(2)]
        ss = sb("ss", [128, 8], F32)
        rstd = sb("rstd", [128, 8], F32)
        ss2 = sb("ss2", [128, 2], F32)
        rstd2 = sb("rstd2", [128, 2], F32)
        epsb = sb("epsb", [128, 1], F32)
        P.op("pool", lambda e: e.memset(epsb[:], EPS), w=["epsb"])
        tr = [ps("tr%d" % i, [128, 2, 512], BF16) for i in range(2)]
        guT = ps("gu", [128, 4, 512], F32)
        gu = [guT[:, i, :] for i in range(4)]
        dn = ps("dn", [128, D], F32)
        dnbuf = [(dn[:].rearrange("p (n f) -> p n f", n=2), ["dn0", "dn1"]), (guT[:, 2:4, :], ["gu1Wg", "gu1Wu"])]

        P.dma("sp", lambda e: e.dma_start(out=gpre[:], in_=gpre_d), tag + "c0", w=["gpre"])
        P.dma("sp", lambda e: e.dma_start(out=G[:], in_=gpost_d), tag + "c1", w=["G"])
        P.dma("sp", lambda e: e.dma_start(out=ident[:], in_=ident_d), tag + "c2", w=["ident"])
        P.op("dve", lambda e: e.tensor_scalar(out=G[:], in0=G[:], scalar1=0.5, scalar2=None, op0=ALU.mult),
             r=["G"], w=["G"])
        wg_v = wg_d.rearrange("(k p) f -> p k f", p=128)
        wu_v = wu_d.rearrange("(k p) f -> p k f", p=128)
        wd_v = wd_d.rearrange("(f p) d -> p f d", p=128)
        FG = [(0, 2), (2, 6), (6, 10), (10, 14), (14, 18), (18, 22)]
        fgrp = {}
        for gi, (a, b) in enumerate(FG):
            for f in range(a, b):
                fgrp[f] = gi
            for nm, W, v in (("Wg", Wg, wg_v), ("Wu", Wu, wu_v)):
                for kh in range(2):
                    P.dma("pool",
                          lambda e, W=W, v=v, a=a, b=b, kh=kh: e.dma_start(
                              out=W[:, kh * 4:(kh + 1) * 4, a * 128:b * 128],
                              in_=v[:, kh * 4:(kh + 1) * 4, a * 128:b * 128]),
                          "%s%s%d" % (tag, nm, gi), w=["%s%d" % (nm, gi)])
        for gi, (a, b) in enumerate(FG):
            P.dma("pool", lambda e, a=a, b=b: e.dma_start(out=Wd[:, a:b, :], in_=wd_v[:, a:b, :]),
                  "%sWd%d" % (tag, gi), w=["Wd%d" % gi])

        blocks = [(b, i) for b in range(NB) for i in range(4)]
        nxi = [0]

        def emit_N(bi, j):
            b, i = blocks[bi]
            t0 = i * 512 + j * 128
            xb = xin[nxi[0] % 2]
            xn = "xin%d" % (nxi[0] % 2)
            nxi[0] += 1
            P.dma("sp", lambda e: e.dma_start(out=xb[:], in_=src[b, t0:t0 + 128, :]), tag + xn, w=[xn])
            P.op("act", lambda e: e.activation(out=hn[:, j, :], in_=xb[:], func=AF.Square,
                                               accum_out=ss[:, j:j + 1]),
                 r=[xn], w=["hn%d" % j, "ss%d" % j])
            P.op("act", lambda e: e.activation(out=ss[:, j:j + 1], in_=ss[:, j:j + 1], func=AF.Sqrt,
                                               scale=1.0 / D, bias=epsb[:, 0:1]),
                 r=["ss%d" % j, "epsb"], w=["ss%d" % j])
            P.op("dve", lambda e: e.reciprocal(out=rstd[:, j:j + 1], in_=ss[:, j:j + 1]),
                 r=["ss%d" % j], w=["rstd%d" % j])
            P.op("act", lambda e: e.activation(out=hn[:, j, :], in_=xb[:], func=AF.Copy,
                                               scale=rstd[:, j:j + 1]),
                 r=[xn, "rstd%d" % j], w=["hn%d" % j])

        def emit_T(bi):
            for kp in range(4):
                bank = kp % 2
                for kk in range(2):
                    k = kp * 2 + kk
                    for j in range(4):
                        P.op("pe", lambda e, k=k, kk=kk, j=j, bank=bank: e.transpose(
                            tr[bank][:, kk, j * 128:(j + 1) * 128], hn[:, j, k * 128:(k + 1) * 128], ident[:]),
                             r=["hn%d" % j, "ident"], w=["tr%d" % bank])
                for kk in range(2):
                    k = kp * 2 + kk
                    P.op("dve", lambda e, k=k, kk=kk, bank=bank: e.tensor_scalar(
                        out=hT[:, k, :], in0=tr[bank][:, kk, :], scalar1=gpre[:, k:k + 1], scalar2=None,
                        op0=ALU.mult),
                         r=["tr%d" % bank, "gpre"], w=["hT%d" % k])

        gui = [0]

        def emit_GU_f(bi, f):
            pr = gui[0] % 2
            gui[0] += 1
            pg, pu = gu[2 * pr], gu[2 * pr + 1]
            gi = fgrp[f]
            for nm, W, pt in (("Wg", Wg, pg), ("Wu", Wu, pu)):
                pn = "gu%d%s" % (pr, nm)
                for k in range(8):
                    P.op("pe", lambda e, W=W, pt=pt, k=k: e.matmul(
                        pt[:], W[:, k, f * 128:(f + 1) * 128], hT[:, k, :], start=(k == 0), stop=(k == 7)),
                         r=["%s%d" % (nm, gi), "hT%d" % k], w=[pn])
            s = sg[pr]
            P.op("act", lambda e: e.activation(out=s[:], in_=pg[:], func=AF.Silu),
                 r=["gu%dWg" % pr], w=["sg%d" % pr])
            P.op("dve", lambda e: e.tensor_tensor(out=actT[:, f, :], in0=s[:], in1=pu[:], op=ALU.mult),
                 r=["sg%d" % pr, "gu%dWu" % pr], w=["actT%d" % f])

        dni = [0]

        def emit_D(bi):
            b, i = blocks[bi]
            for j in range(4):
                t0 = i * 512 + j * 128
                q = dni[0] % 2
                dni[0] += 1
                xr, yt = xres[q], ytmp[q]
                xrn, ytn = "xres%d" % q, "ytmp%d" % q
                P.dma("act", lambda e, xr=xr, t0=t0: e.dma_start(out=xr[:], in_=src[b, t0:t0 + 128, :]),
                      tag + xrn + "i", w=[xrn])
                dnv, dnr = dnbuf[j % 2]
                ytv = yt[:].rearrange("p (n f) -> p n f", n=2)
                for n in range(2):
                    for f in range(NF):
                        P.op("pe", lambda e, n=n, f=f, j=j, dnv=dnv: e.matmul(
                            dnv[:, n, :], actT[:, f, j * 128:(j + 1) * 128],
                            Wd[:, f, n * 512:(n + 1) * 512], start=(f == 0), stop=(f == NF - 1)),
                             r=["actT%d" % f, "Wd%d" % fgrp[f]], w=[dnr[n]])
                P.op("act", lambda e, ytv=ytv, q=q, dnv=dnv: e.activation(out=ytv, in_=dnv, func=AF.Square,
                                                                        accum_out=ss2[:, q:q + 1]),
                     r=dnr, w=[ytn, "ss2%d" % q])
                P.op("act", lambda e, q=q: e.activation(out=ss2[:, q:q + 1], in_=ss2[:, q:q + 1], func=AF.Sqrt,
                                                        scale=1.0 / D, bias=epsb[:, 0:1]),
                     r=["ss2%d" % q, "epsb"], w=["ss2%d" % q])
                P.op("dve", lambda e, q=q: e.reciprocal(out=rstd2[:, q:q + 1], in_=ss2[:, q:q + 1]),
                     r=["ss2%d" % q], w=["rstd2%d" % q])
                P.op("dve", lambda e, ytv=ytv, q=q, dnv=dnv: e.scalar_tensor_tensor(
                    out=ytv, in0=dnv, scalar=rstd2[:, q:q + 1], in1=G[:].rearrange("p (n f) -> p n f", n=2),
                    op0=ALU.mult, op1=ALU.mult),
                     r=dnr + ["rstd2%d" % q, "G"], w=[ytn])
                P.op("pool", lambda e, xr=xr, yt=yt: e.tensor_tensor(out=xr[:], in0=xr[:], in1=yt[:],
                                                                     op=ALU.add),
                     r=[ytn, xrn], w=[xrn])
                P.dma("sp", lambda e, xr=xr, t0=t0: e.dma_start(out=dst[b, t0:t0 + 128, :], in_=xr[:]),
                      tag + xrn + "o", r=[xrn], w=["dst_%s_%d_%d" % (tag, b, t0)])

        nblk = len(blocks)
        for j in range(4):
            emit_N(0, j)
        emit_T(0)
        for bi in range(nblk):
            for f in range(NF):
                emit_GU_f(bi, f)
                if bi + 1 < nblk and f in (3, 8, 13, 18):
                    emit_N(bi + 1, (f - 3) // 5)
            if bi + 1 < nblk:
                emit_T(bi + 1)
            emit_D(bi)
        P.barrier_all()
        P.flush()


NT = S // 128
LIMIT = None
NSEQ = NB
SUB = 9
CUT = 9
BARQT = False
EVAC_ACT = False
BIG = 30000.0
C_DQ, C_DK, C_DV, C_NQ, C_KC, C_VC, C_KS, C_VS, C_KW, C_VW, C_GL, C_END = (
    0, 512, 1024, 1536, 2048, 2176, 2304, 2432, 2560, 2688, 2816, 2840)
S_DQ, S_DK, S_NQ, S_KS, S_KW, S_KC, S_VC, S_END = 0, 512, 1024, 1536, 1664, 1792, 1920, 2048


class Rot:
    def __init__(self, items, names):
        self.items, self.names, self.i = items, names, 0

    def next(self):
        k = self.i % len(self.items)
        self.i += 1
        return self.items[k], self.names[k]


def mixer_phase(nc, P, src, dst, dd):
    TB = 2
    with ExitStack() as st0:
        def sb0(name, shape, dt):
            return st0.enter_context(nc.sbuf_tensor("B" + name, shape, dt))

        dqT = sb0("dqT", [128, 4, S], BF16)
        dkT = sb0("dkT", [128, 4, S], BF16)
        Qg = [sb0("Qg%d" % g, [128, 4, S], BF16) for g in range(2)]
        KSg = [sb0("KSg%d" % g, [128, S], BF16) for g in range(2)]
        KWg = [sb0("KWg%d" % g, [128, S], BF16) for g in range(2)]
        dva = sb0("dva", [128, NT, 4, 130], BF16)
        vsa = sb0("vsa", [128, NT, 2, 66], BF16)
        vwa = sb0("vwa", [128, NT, 2, 66], BF16)
        glr = sb0("glr", [128, NT, 24], F32)
        kcc = [sb0("kcc%d" % g, [128, 128], BF16) for g in range(2)]
        vca = sb0("vca", [128, 2, 98], BF16)
        ident = sb0("ident", [128, 128], BF16)
        epsb = sb0("epsb", [128, 1], F32)

        P.dma("sp", lambda e: e.dma_start(out=ident[:], in_=dd["ident"]), "Bc_ident", w=["ident"])
        P.op("pool", lambda e: e.memset(epsb[:], EPS), w=["epsb"])
        P.op("pool", lambda e: e.memset(dva[:, :, :, 128:130], 1.0), w=["dva"])
        P.op("pool", lambda e: e.memset(vsa[:, :, :, 64:66], 1.0), w=["vsa"])
        P.op("pool", lambda e: e.memset(vwa[:, :, :, 64:66], 1.0), w=["vwa"])
        P.op("pool", lambda e: e.memset(vca[:], 0.0), w=["vca"])
        P.op("pool", lambda e: e.memset(vca[:, :, 64:65], 1.0), r=["vca"], w=["vca"])
        for g in range(2):
            P.dma("sp", lambda e, g=g: e.dma_start(out=vca[0:127, g, 65:97], in_=dd["ov"]), "Bc_ov%d" % g,
                  r=["vca"], w=["vca_ov%d" % g])
            P.dma("sp", lambda e, g=g: e.dma_start(out=KSg[g][64:96, :], in_=dd["ET"]), "Bc_ET%d" % g,
                  w=["KSgE%d" % g])

        for b in range(NSEQ):
            with ExitStack() as st1:
                kcT = st1.enter_context(nc.sbuf_tensor("BkcT%d" % b, [128, S], BF16))
                vcT = st1.enter_context(nc.sbuf_tensor("BvcT%d" % b, [128, S], BF16))
                mixer_proj(nc, P, b, TB, src, dd, dict(dqT=dqT, dkT=dkT, Qg=Qg, KSg=KSg, KWg=KWg, dva=dva,
                                                      vsa=vsa, vwa=vwa, glr=glr, kcT=kcT, vcT=vcT,
                                                      ident=ident, epsb=epsb))
                if SUB >= 2:
                    mixer_compress(nc, P, b, dd, dict(kcT=kcT, vcT=vcT, kcc=kcc, vca=vca))
            if SUB >= 3:
                mixer_attn(nc, P, b, src, dst, dd, dict(dqT=dqT, dkT=dkT, Qg=Qg, KSg=KSg, KWg=KWg, dva=dva,
                                                   vsa=vsa, vwa=vwa, glr=glr, kcc=kcc, vca=vca,
                                                   ident=ident, epsb=epsb))


def mixer_proj(nc, P, b, TB, src, dd, t):
    tag = "P%d" % b
    dqT, dkT, Qg, KSg, KWg = t["dqT"], t["dkT"], t["Qg"], t["KSg"], t["KWg"]
    dva, vsa, vwa, glr, kcT, vcT, ident, epsb = (t["dva"], t["vsa"], t["vwa"], t["glr"], t["kcT"], t["vcT"],
                                                 t["ident"], t["epsb"])
    with ExitStack() as st:
        def sb(name, shape, dt):
            return st.enter_context(nc.sbuf_tensor(tag + name, shape, dt))

        def ps(name, shape, dt):
            return st.enter_context(nc.psum_tensor(tag + name, shape, dt))

        rt = [[sb("rt%d_%d" % (i, q), [128, 8, 8], F32) for q in range(4)] for i in range(2)]
        xs = [sb("xs%d" % i, [128, 512], F32) for i in range(2)]
        xin = [sb("xin%d" % i, [128, D], F32) for i in range(2)]
        hn = sb("hn", [128, TB, D], BF16)
        hT = sb("hT", [128, 8, TB * 128], BF16)
        stg = sb("stg", [128, TB, S_END], BF16)
        cosR = sb("cosR", [128, NT, 8, 8], F32)
        sinR = sb("sinR", [128, NT, 8, 8], F32)
        gpre = sb("gpre", [128, 8], F32)
        ss = sb("ss", [128, TB], F32)
        rstd = sb("rstd", [128, TB], F32)
        Win = sb("Win", [128, 8, 2048], BF16)
        tr = [ps("tr%d" % i, [128, 1024], BF16) for i in range(2)]
        pj = [ps("pj%d" % i, [128, 512], F32) for i in range(3)]
        tq = [ps("tq%d" % i, [128, 1024], BF16) for i in range(2)]

        P.dma("sp", lambda e: e.dma_start(out=gpre[:], in_=dd["m_gpre"]), tag + "gpre", w=["gpre"])
        P.dma("sp", lambda e: e.dma_start(out=cosR[:], in_=dd["cosR"]), tag + "cos", w=["cosR"])
        P.dma("sp", lambda e: e.dma_start(out=sinR[:], in_=dd["sinR"]), tag + "sin", w=["sinR"])
        win_v = dd["w_in"].rearrange("(k p) f -> p k f", p=128)
        CB = [(0, 512), (512, 1024), (1024, 1536), (1536, 2048), (2048, 2560), (2560, C_END)]
        SEC = [(C_KS, 128), (C_KW, 128), (C_KC, 128), (C_VC, 128), (C_VS, 128), (C_VW, 128), (C_GL, 24)]
        def load_pass(ph):
            if ph == 0:
                for ci, (c0, c1) in enumerate(CB[:4]):
                    for kh in range(2):
                        P.dma("pool", lambda e, c0=c0, c1=c1, kh=kh: e.dma_start(
                            out=Win[:, kh * 4:(kh + 1) * 4, c0:c1], in_=win_v[:, kh * 4:(kh + 1) * 4, c0:c1]),
                              "%sWin%d" % (tag, ci), w=["Win%d" % ci])
            else:
                dcol = 0
                for si, (sc, sw) in enumerate(SEC):
                    ci = 4 if si < 4 else 5
                    P.dma("pool", lambda e, sc=sc, sw=sw, dcol=dcol: e.dma_start(
                        out=Win[:, :, dcol:dcol + sw], in_=win_v[:, :, sc:sc + sw]),
                          "%sWin%d" % (tag, ci), w=["Win%d" % ci] + (["Win0", "Win1"] if si == 0 else []))
                    dcol += sw
                P.dma("sp", lambda e: e.dma_start(out=cosR[:], in_=dd["cosR2"]), tag + "cos", w=["cosR"])
                P.dma("sp", lambda e: e.dma_start(out=sinR[:], in_=dd["sinR2"]), tag + "sin", w=["sinR"])

        PH = [0]
        nxi = [0]
        pji = [0]
        rti = [0]
        tqi = [0]
        evi = [0]

        def emit_N(i, j):
            t0 = (i * TB + j) * 128
            q = nxi[0] % 2
            nxi[0] += 1
            xb, xn = xin[q], "xin%d" % q
            P.dma("sp", lambda e: e.dma_start(out=xb[:], in_=src[b, t0:t0 + 128, :]), tag + xn, w=[xn])
            P.op("act", lambda e: e.activation(out=hn[:, j, :], in_=xb[:], func=AF.Square,
                                               accum_out=ss[:, j:j + 1]),
                 r=[xn], w=["hn%d" % j, "ss%d" % j])
            P.op("act", lambda e: e.activation(out=ss[:, j:j + 1], in_=ss[:, j:j + 1], func=AF.Sqrt,
                                               scale=1.0 / D, bias=epsb[:, 0:1]),
                 r=["ss%d" % j, "epsb"], w=["ss%d" % j])
            P.op("dve", lambda e: e.reciprocal(out=rstd[:, j:j + 1], in_=ss[:, j:j + 1]),
                 r=["ss%d" % j], w=["rstd%d" % j])
            P.op("act", lambda e: e.activation(out=hn[:, j, :], in_=xb[:], func=AF.Copy,
                                               scale=rstd[:, j:j + 1]),
                 r=[xn, "rstd%d" % j], w=["hn%d" % j])

        def emit_T(i):
            W = TB * 128
            for kq in range(2):
                bank = kq
                for kk in range(4):
                    k = kq * 4 + kk
                    for j in range(TB):
                        P.op("pe", lambda e, k=k, kk=kk, j=j, bank=bank: e.transpose(
                            tr[bank][:, kk * W + j * 128:kk * W + (j + 1) * 128],
                            hn[:, j, k * 128:(k + 1) * 128], ident[:]),
                             r=["hn%d" % j, "ident"], w=["tr%d" % bank])
                for kk in range(4):
                    k = kq * 4 + kk
                    P.op("dve", lambda e, k=k, kk=kk, bank=bank: e.tensor_scalar(
                        out=hT[:, k, :], in0=tr[bank][:, kk * W:(kk + 1) * W], scalar1=gpre[:, k:k + 1],
                        scalar2=None, op0=ALU.mult),
                         r=["tr%d" % bank, "gpre"], w=["hT%d" % k])

        def rope(pjt, pjn, o, nh, j, so, T, tabs=None):
            q = rti[0] % 2
            rti[0] += 1
            ta, tb_, tc, td = rt[q]
            rn = ["rt%d_%d" % (q, x) for x in range(4)]
            xst, xsn = xs[q], "xs%d" % q
            if (CUT == 4.26 and nh == 2) or (CUT == 4.28 and tabs is not None):
                sres = "stg%d_%d" % (j, so)
                P.op("act", lambda e: e.activation(out=stg[:, j, so:so + nh * 64], in_=pjt[:, o:o + nh * 64],
                                                   func=AF.Copy), r=[pjn], w=[sres + "a"])
                return [sres + "a"]
            P.op("act", lambda e: e.activation(out=xst[:, 0:nh * 64], in_=pjt[:, o:o + nh * 64], func=AF.Copy),
                 r=[pjn], w=[xsn])
            pv = xst[:, 0:nh * 64].rearrange("p (h d) -> p h d", d=64)
            sv = stg[:, j, so:so + nh * 64].rearrange("p (h d) -> p h d", d=64)
            x1, x2 = pv[:, :, 0:8], pv[:, :, 8:16]
            ct_, st_, ctn, stn = (cosR, sinR, "cosR", "sinR") if tabs is None else tabs
            cs, sn = ct_[:, T, 0:nh, :], st_[:, T, 0:nh, :]
            sres = "stg%d_%d" % (j, so)
            if CUT < 3.06:
                return [sres + "a", sres + "b", sres + "c"]
            P.op("dve", lambda e: e.tensor_tensor(out=ta[:, 0:nh, :], in0=x1, in1=cs, op=ALU.mult),
                 r=[xsn, ctn], w=[rn[0]])
            if CUT < 3.07:
                return [sres + "a", sres + "b", sres + "c"]
            P.op("dve", lambda e: e.tensor_tensor(out=tb_[:, 0:nh, :], in0=x2, in1=sn, op=ALU.mult),
                 r=[xsn, stn], w=[rn[1]])
            if CUT < 3.08:
                return [sres + "a", sres + "b", sres + "c"]
            P.op("dve", lambda e: e.tensor_tensor(out=tc[:, 0:nh, :], in0=x2, in1=cs, op=ALU.mult),
                 r=[xsn, ctn], w=[rn[2]])
            if CUT == 3.095:
                P.op("dve", lambda e: e.tensor_tensor(out=tc[:, 0:nh, :], in0=x1, in1=sn, op=ALU.mult),
                     r=[xsn, "sinR"], w=[rn[2]])
                return [sres + "a", sres + "b", sres + "c"]
            P.op("dve", lambda e: e.tensor_tensor(out=td[:, 0:nh, :], in0=x1, in1=sn, op=ALU.mult),
                 r=[xsn, stn], w=[rn[3]])
            P.op("dve", lambda e: e.tensor_tensor(out=pv[:, :, 0:8], in0=ta[:, 0:nh, :], in1=tb_[:, 0:nh, :],
                                                  op=ALU.subtract),
                 r=[rn[0], rn[1], rn[2], rn[3], xsn], w=[xsn])
            P.op("dve", lambda e: e.tensor_tensor(out=pv[:, :, 8:16], in0=tc[:, 0:nh, :], in1=td[:, 0:nh, :],
                                                  op=ALU.add),
                 r=[rn[2], rn[3], xsn], w=[xsn])
            P.op("act", lambda e: e.activation(out=stg[:, j, so:so + nh * 64], in_=xst[:, 0:nh * 64], func=AF.Copy),
                 r=[xsn], w=[sres + "a"])
            return [sres + "a"]

        def emit_proj(i, j, stres):
            T = i * TB + j
            for ci, (c0, c1) in enumerate(CB):
                if (ci < 4) != (PH[0] == 0):
                    continue
                if ci >= 4:
                    c0, c1 = c0 - 2048, c1 - 2048
                q = pji[0] % 3
                pji[0] += 1
                pjt, pjn = pj[q], "pj%d" % q
                ncol = c1 - c0
                for k in range(8):
                    P.op("pe", lambda e, k=k, pjt=pjt, c0=c0, c1=c1, ncol=ncol: e.matmul(
                        pjt[:, 0:ncol], hT[:, k, j * 128:(j + 1) * 128], Win[:, k, c0:c1],
                        start=(k == 0), stop=(k == 7)),
                         r=["hT%d" % k, "Win%d" % ci], w=[pjn])
                if ci == 0:
                    stres["dq"][j] = rope(pjt, pjn, 0, 8, j, S_DQ, T)
                elif ci == 1:
                    stres["dk"][j] = rope(pjt, pjn, 0, 8, j, S_DK, T)
                elif ci == 2:
                    P.op("act", lambda e, pjt=pjt, T=T: e.activation(
                        out=dva[:, T, :, 0:128], in_=pjt[:, 0:512].rearrange("p (h d) -> p h d", d=128),
                        func=AF.Copy), r=[pjn], w=["dva%d" % T])
                elif ci == 3:
                    stres["nq"][j] = rope(pjt, pjn, 0, 8, j, S_NQ, T)
                elif ci == 4:
                    rr = rope(pjt, pjn, 0, 8, j, S_KS, T)
                    stres["ks"][j] = rr
                    stres["kw"][j] = rr
                    stres["kcvc"][j] = rr
                else:
                    P.op("act", lambda e, pjt=pjt, T=T: e.activation(
                        out=vsa[:, T, :, 0:64], in_=pjt[:, 0:128].rearrange("p (h d) -> p h d", d=64),
                        func=AF.Copy), r=[pjn], w=["vsa%d" % T])
                    P.op("act", lambda e, pjt=pjt, T=T: e.activation(
                        out=vwa[:, T, :, 0:64], in_=pjt[:, 128:256].rearrange("p (h d) -> p h d", d=64),
                        func=AF.Copy), r=[pjn], w=["vwa%d" % T])
                    P.op("dve", lambda e, pjt=pjt, T=T: e.tensor_copy(out=glr[:, T, :], in_=pjt[:, 256:280]),
                         r=[pjn], w=["glr%d" % T])

        def emit_QT(i, stres):
            tk0 = i * TB * 128
            W = TB * 128
            units = []
            for h in range(4):
                units.append((S_DQ + h * 128, 128, dqT[:, h, tk0:tk0 + W], "dqT", "dq"))
            for h in range(4):
                units.append((S_DK + h * 128, 128, dkT[:, h, tk0:tk0 + W], "dkT", "dk"))
            for n in range(8):
                units.append((S_NQ + n * 64, 64, Qg[n // 4][0:64, n % 4, tk0:tk0 + W], "Qq%d" % (n // 4), "nq"))
            for g in range(2):
                units.append((S_KS + g * 64, 64, KSg[g][0:64, tk0:tk0 + W], "KSq%d" % g, "ks"))
            for g in range(2):
                units.append((S_KW + g * 64, 64, KWg[g][0:64, tk0:tk0 + W], "KWq%d" % g, "kw"))
            units.append((S_KC, 128, kcT[:, tk0:tk0 + W], "kcT", "kcvc"))
            units.append((S_VC, 128, vcT[:, tk0:tk0 + W], "vcT", "kcvc"))
            units = units[0:16] if PH[0] == 0 else units[16:22]
            if CUT == 9:
                pass
            elif CUT == 4.23:
                units = [(S_NQ + g * 64, 64, KSg[g][0:64, tk0:tk0 + W], "KSq%d" % g, "nq") for g in range(2)]
            elif CUT == 4.24:
                units = [(S_KS + g * 64, 64, Qg[g][0:64, 0, tk0:tk0 + W], "Qq%d" % g, "ks") for g in range(2)]
            elif CUT in (4.21, 4.26, 4.28):
                units = units[16:18]
            elif CUT == 4.22:
                units = units[18:20]
            elif CUT < 4.1:
                units = units[0:8]
            elif CUT < 4.2:
                units = units[0:16]
            elif CUT < 4.3:
                units = units[0:20]
            for u0 in range(0, len(units), 4):
                bank = tqi[0] % 2
                tqi[0] += 1
                grp = units[u0:u0 + 4]
                for ui, (so, ncol, dstap, dres, skey) in enumerate(grp):
                    for j in range(TB):
                        P.op("pe", lambda e, ui=ui, j=j, so=so, ncol=ncol, bank=bank: e.transpose(
                            tq[bank][0:ncol, ui * W + j * 128:ui * W + (j + 1) * 128],
                            stg[:, j, so:so + ncol], ident[:]),
                             r=stres[skey][j] + ["ident"], w=["tq%d" % bank])
                eng = "act" if (evi[0] % 2 == 0 or EVAC_ACT) else "dve"
                evi[0] += 1
                for ui, (so, ncol, dstap, dres, skey) in enumerate(grp):
                    dres = "%s_u%d" % (dres, u0 + ui)
                    if eng == "act":
                        P.op("act", lambda e, ui=ui, ncol=ncol, dstap=dstap, bank=bank: e.activation(
                            out=dstap, in_=tq[bank][0:ncol, ui * W:(ui + 1) * W], func=AF.Copy),
                             r=["tq%d" % bank], w=["%s_%d" % (dres, i)])
                    else:
                        P.op("dve", lambda e, ui=ui, ncol=ncol, dstap=dstap, bank=bank: e.tensor_copy(
                            out=dstap, in_=tq[bank][0:ncol, ui * W:(ui + 1) * W]),
                             r=["tq%d" % bank], w=["%s_%d" % (dres, i)])

        nblk = NT // TB
        for ph in range(2):
            PH[0] = ph
            load_pass(ph)
            for j in range(TB):
                emit_N(0, j)
            emit_T(0)
            for i in range(nblk):
                stres = {k: [None] * TB for k in ("dq", "dk", "nq", "kcvc", "ks", "kw")}
                for j in range(TB):
                    emit_proj(i, j, stres)
                    if i + 1 < nblk:
                        emit_N(i + 1, j)
                emit_QT(i, stres)
                if i + 1 < nblk:
                    emit_T(i + 1)
        P.barrier_all()
        P.flush()


def mixer_compress(nc, P, b, dd, t):
    tag = "Z%d" % b
    kcT, vcT, kcc, vca = t["kcT"], t["vcT"], t["kcc"], t["vca"]
    NCB = 127
    with ExitStack() as st:
        def sb(name, shape, dt):
            return st.enter_context(nc.sbuf_tensor(tag + name, shape, dt))

        def ps(name, shape, dt):
            return st.enter_context(nc.psum_tensor(tag + name, shape, dt))

        W1 = [sb("W1_%d" % kv, [128, 32, 256], BF16) for kv in range(2)]
        W2 = [sb("W2_%d" % kv, [128, 2, 64], BF16) for kv in range(2)]
        peT = [sb("peT%d" % kv, [128, 32], BF16) for kv in range(2)]
        pb = sb("pb", [128, 4], F32)
        xh = sb("xh", [128, 2, 128], F32)
        u = sb("u", [128, 2, 128], F32)
        sg = sb("sg", [128, 2, 128], F32)
        hact = sb("hact", [128, 2, 128], BF16)
        pbps = ps("pbps", [128, 4], F32)
        hps = [ps("hps%d" % i, [128, 2, 128], F32) for i in range(2)]
        cps = ps("cps", [128, 128], F32)

        for kv, (w1n, w2n, pen) in enumerate((("ck_w1", "ck_w2", "ck_peT"), ("cv_w1", "cv_w2", "cv_peT"))):
            w1v = dd[w1n].rearrange("(l d) h -> d l h", d=64)
            for half in range(2):
                P.dma("pool", lambda e, kv=kv, half=half, w1v=w1v: e.dma_start(
                    out=W1[kv][half * 64:(half + 1) * 64, :, :], in_=w1v),
                      "%sW1_%d" % (tag, kv), w=["W1_%d" % kv])
            P.dma("pool", lambda e, kv=kv, w2n=w2n: e.dma_start(
                out=W2[kv][:], in_=dd[w2n].rearrange("(c p) d -> p c d", p=128)),
                  "%sW2_%d" % (tag, kv), w=["W2_%d" % kv])
            P.dma("pool", lambda e, kv=kv, pen=pen: e.dma_start(out=peT[kv][:], in_=dd[pen]),
                  "%spe_%d" % (tag, kv), w=["peT%d" % kv])
        for kv in range(2):
            for ch in range(2):
                col = kv * 2 + ch
                for l in range(32):
                    P.op("pe", lambda e, kv=kv, ch=ch, l=l, col=col: e.matmul(
                        pbps[:, col:col + 1], W1[kv][0:64, l, ch * 128:(ch + 1) * 128], peT[kv][0:64, l:l + 1],
                        start=(l == 0), stop=(l == 31)),
                         r=["W1_%d" % kv, "peT%d" % kv], w=["pbps"])
        P.op("dve", lambda e: e.tensor_copy(out=pb[:], in_=pbps[:]), r=["pbps"], w=["pb"])
        hi = [0]
        for kv in range(2):
            xT = kcT if kv == 0 else vcT
            for g in range(2):
                hp = hps[hi[0] % 2]
                hpn = "hps%d" % (hi[0] % 2)
                hi[0] += 1
                for ch in range(2):
                    for l in range(32):
                        P.op("pe", lambda e, kv=kv, g=g, ch=ch, l=l, hp=hp, xT=xT: e.matmul(
                            hp[:, ch, 0:NCB], W1[kv][g * 64:(g + 1) * 64, l, ch * 128:(ch + 1) * 128],
                            xT[g * 64:(g + 1) * 64, l:l + 16 * (NCB - 1) + 1:16],
                            start=(l == 0), stop=(l == 31)),
                             r=["W1_%d" % kv], w=[hpn])
                for ch in range(2):
                    col = kv * 2 + ch
                    P.op("act", lambda e, ch=ch, col=col, hp=hp: e.activation(
                        out=xh[:, ch, 0:NCB], in_=hp[:, ch, 0:NCB], func=AF.Identity, bias=pb[:, col:col + 1]),
                         r=[hpn, "pb"], w=["xh%d" % ch])
                X, U, SG, HA = xh[:, :, 0:NCB], u[:, :, 0:NCB], sg[:, :, 0:NCB], hact[:, :, 0:NCB]
                P.op("dve", lambda e, X=X, U=U: e.tensor_tensor(out=U, in0=X, in1=X, op=ALU.mult),
                     r=["xh0", "xh1"], w=["u"])
                P.op("dve", lambda e, U=U: e.tensor_scalar(out=U, in0=U, scalar1=0.044715, scalar2=1.0,
                                                        op0=ALU.mult, op1=ALU.add), r=["u"], w=["u"])
                P.op("dve", lambda e, X=X, U=U: e.tensor_tensor(out=U, in0=U, in1=X, op=ALU.mult),
                     r=["u", "xh0", "xh1"], w=["u"])
                P.op("act", lambda e, U=U, SG=SG: e.activation(out=SG, in_=U, func=AF.Sigmoid,
                                                              scale=1.5957691216057308),
                     r=["u"], w=["sg"])
                P.op("dve", lambda e, X=X, SG=SG, HA=HA: e.tensor_tensor(out=HA, in0=X, in1=SG, op=ALU.mult),
                     r=["sg", "xh0", "xh1"], w=["hact"])
                if kv == 0:
                    for ch in range(2):
                        P.op("pe", lambda e, ch=ch: e.matmul(cps[0:64, 0:NCB], W2[0][:, ch, :], hact[:, ch, 0:NCB],
                                                            start=(ch == 0), stop=(ch == 1)),
                             r=["hact", "W2_0"], w=["cps"])
                    P.op("act", lambda e, g=g: e.activation(out=kcc[g][0:64, 0:NCB], in_=cps[0:64, 0:NCB],
                                                           func=AF.Copy), r=["cps"], w=["kcc%d" % g])
                else:
                    for ch in range(2):
                        P.op("pe", lambda e, ch=ch: e.matmul(cps[0:NCB, 0:64], hact[:, ch, 0:NCB], W2[1][:, ch, :],
                                                            start=(ch == 0), stop=(ch == 1)),
                             r=["hact", "W2_1"], w=["cps"])
                    P.op("act", lambda e, g=g: e.activation(out=vca[0:NCB, g, 0:64], in_=cps[0:NCB, 0:64],
                                                           func=AF.Copy), r=["cps"], w=["vca_v%d" % g])
        P.barrier_all()
        P.flush()


class Pipe:
    def __init__(self, lag=2):
        self.q, self.lag = [], lag

    def push(self, first, rest):
        if first is not None:
            first()
        self.q.append(rest)
        while len(self.q) > self.lag:
            self.q.pop(0)()

    def drain(self):
        while self.q:
            self.q.pop(0)()


def mixer_attn(nc, P, b, src, dst, dd, t):
    tag = "T%d" % b
    dqT, dkT, Qg, KSg, KWg = t["dqT"], t["dkT"], t["Qg"], t["KSg"], t["KWg"]
    dva, vsa, vwa, glr, kcc, vca, ident, epsb = (t["dva"], t["vsa"], t["vwa"], t["glr"], t["kcc"], t["vca"],
                                                 t["ident"], t["epsb"])
    with ExitStack() as st:
        def sb(name, shape, dt):
            return st.enter_context(nc.sbuf_tensor(tag + name, shape, dt))

        def ps(name, shape, dt):
            return st.enter_context(nc.psum_tensor(tag + name, shape, dt))

        om = sb("om", [128, NT, 1024], BF16)
        tmpf = [sb("tmpf%d" % i, [128, 64], F32) for i in range(2)]
        Wo = sb("Wo", [128, 8, 1024], BF16)
        pts = [sb("p%d" % i, [128, 512], BF16) for i in range(4)]
        gates = sb("gates", [128, NT, 24], F32)
        Am = sb("Am", [128, NT, 32], F32)
        Bm = sb("Bm", [128, NT, 32], F32)
        Gs = sb("Gs", [128, 128], F32)
        Gm = sb("Gm", [128, D], F32)
        lv = [sb("lv%d" % i, [128, 64], F32) for i in range(4)]
        lt = sb("lt", [128, 64], F32)
        le = sb("le", [128, 2], F32)
        neglam = sb("neglam", [128, 1], F32)
        o1 = [sb("o1_%d" % i, [128, 4, 128], F32) for i in range(2)]
        junk = sb("junk", [128, 128], F32)
        sm = [dict((n, sb("%s_%d" % (n, i), [128, w], F32)) for n, w in
                   (("rd", 4), ("nl", 4), ("ss4", 4), ("ln4", 4), ("r4", 4), ("den", 4), ("gr", 4), ("rs", 4),
                    ("rw", 4), ("imp", 32), ("impm", 32), ("wk", 32), ("m1", 8), ("m2", 8))) for i in range(2)]
        nsp = [sb("nsp%d" % i, [128, 96], BF16) for i in range(2)]
        omT = [sb("omT%d" % i, [128, 8, 128], BF16) for i in range(2)]
        xres = [sb("xres%d" % i, [128, D], F32) for i in range(2)]
        ytmp = [sb("ytmp%d" % i, [128, D], F32) for i in range(2)]
        ss2 = sb("ss2", [128, 2], F32)
        rstd2 = sb("rstd2", [128, 2], F32)
        sps = [ps("sps%d" % i, [128, 512], F32) for i in range(3)]
        ab = ps("ab", [128, 4, 512], F32)
        tps = ps("tps", [128, 1024], BF16)
        SPS = Rot(sps, ["sps%d" % i for i in range(3)])
        PT = Rot(pts, ["p%d" % i for i in range(4)])

        for nm, tl in (("Am", Am), ("Bm", Bm), ("Gs", Gs), ("Gm", Gm)):
            P.dma("sp", lambda e, nm=nm, tl=tl: e.dma_start(out=tl[:], in_=dd[nm]), tag + nm, w=[nm])
        for i, nm in enumerate(("lq1", "lk1", "lq2", "lk2")):
            P.dma("sp", lambda e, i=i, nm=nm: e.dma_start(out=lv[i][:], in_=dd[nm]), tag + nm, w=["lv%d" % i])
        wo_v = dd["w_out"].rearrange("(k p) f -> p k f", p=128)
        for kh in range(2):
            P.dma("pool", lambda e, kh=kh: e.dma_start(out=Wo[:, kh * 4:(kh + 1) * 4, :],
                                                       in_=wo_v[:, kh * 4:(kh + 1) * 4, :]),
                  tag + "Wo", w=["Wo"])
        for nm in ("nsp0", "nsp1"):
            pass
        P.op("pool", lambda e: e.memset(nsp[0][:], 0.0), w=["nsp0"])
        P.op("pool", lambda e: e.memset(nsp[1][:], 0.0), w=["nsp1"])
        P.op("dve", lambda e: e.tensor_scalar(out=Gs[:], in0=Gs[:], scalar1=0.8, scalar2=None, op0=ALU.mult),
             r=["Gs"], w=["Gs"])
        P.op("act", lambda e: e.activation(out=gates[:], in_=glr[:], func=AF.Sigmoid), w=["gates"])
        for i in range(2):
            P.op("dve", lambda e, i=i: e.tensor_tensor(out=lt[:], in0=lv[2 * i][:], in1=lv[2 * i + 1][:],
                                                       op=ALU.mult),
                 r=["lv%d" % (2 * i), "lv%d" % (2 * i + 1)], w=["lt"])
            P.op("dve", lambda e, i=i: e.reduce_sum(out=le[:, i:i + 1], in_=lt[:], axis=AX.X),
                 r=["lt"], w=["le%d" % i])
        P.op("act", lambda e: e.activation(out=le[:], in_=le[:], func=AF.Exp), r=["le0", "le1"], w=["le0", "le1"])
        P.op("dve", lambda e: e.tensor_tensor(out=neglam[:], in0=le[:, 1:2], in1=le[:, 0:1], op=ALU.subtract),
             r=["le0", "le1"], w=["neglam"])
        P.op("dve", lambda e: e.tensor_scalar(out=neglam[:], in0=neglam[:], scalar1=-0.2, scalar2=None,
                                              op0=ALU.add), r=["neglam"], w=["neglam"])

        pipe = Pipe(2)
        first_in_bank = {}

        def acc_mm(bank_i, col0, ncol, lhsT, rhs, last, reads):
            bn = "ab%d" % bank_i
            first = first_in_bank.get(bn, True)
            first_in_bank[bn] = False
            P.op("pe", lambda e: e.matmul(ab[:, bank_i, col0:col0 + ncol], lhsT, rhs, start=first, stop=last,
                                          skip_group_check=True),
                 r=reads, w=[bn])

        def exp_tile(spt, spn, rows, masks):
            p, pn = PT.next()
            P.op("act", lambda e: e.activation(out=p[0:rows, :], in_=spt[0:rows, :], func=AF.Exp, scale=0.125),
                 r=[spn], w=[pn])
            for (pattern, base, cm) in masks:
                P.op("pool", lambda e, pattern=pattern, base=base, cm=cm: e.affine_select(
                    out=p[0:rows, :], in_=p[0:rows, :], pattern=pattern, compare_op=ALU.is_ge, fill=0.0,
                    base=base, channel_multiplier=cm), r=[pn], w=[pn])
            return p, pn

        ci = 0
        for h in range(4):
            for qb in range(4):
                ob, obn = o1[(h * 4 + qb) % 2], "o1_%d" % ((h * 4 + qb) % 2)
                smx = sm[(h * 4 + qb) % 2]
                sfx = "_%d" % ((h * 4 + qb) % 2)
                for m in range(2):
                    bA, bB = 2 * (ci % 2), 2 * (ci % 2) + 1
                    ci += 1
                    first_in_bank["ab%d" % bA] = True
                    first_in_bank["ab%d" % bB] = True
                    nkt = 4 * qb + 4
                    for kt in range(nkt):
                        spt, spn = SPS.next()

                        def qk(spt=spt, spn=spn, kt=kt, m=m, h=h, qb=qb):
                            P.op("pe", lambda e: e.matmul(
                                spt[:], dkT[m * 64:(m + 1) * 64, h, kt * 128:(kt + 1) * 128],
                                dqT[m * 64:(m + 1) * 64, h, qb * 512:(qb + 1) * 512], start=True, stop=True),
                                 w=[spn])

                        def rest(spt=spt, spn=spn, kt=kt, h=h, qb=qb, bA=bA, bB=bB):
                            masks = []
                            if kt >= 4 * qb:
                                masks.append(([[1, 512]], qb * 512 - kt * 128, -1))
                            p, pn = exp_tile(spt, spn, 128, masks)
                            for jq in range(4):
                                if kt > 4 * qb + jq:
                                    continue
                                bank_i, col0 = (bA, jq * 130) if jq < 3 else (bB, 0)
                                acc_mm(bank_i, col0, 129, p[:, jq * 128:(jq + 1) * 128], dva[:, kt, h, 0:129],
                                       kt == 4 * qb + jq, [pn])

                        pipe.push(qk, rest)

                    def post(m=m, h=h, qb=qb, bA=bA, bB=bB, ob=ob, obn=obn, smx=smx, sfx=sfx):
                        bnA, bnB = "ab%d" % bA, "ab%d" % bB
                        rd = smx["rd"]
                        denA = ab[:, bA, 0:390].rearrange("p (j c) -> p j c", c=130)[:, :, 128]
                        P.op("dve", lambda e: e.reciprocal(out=rd[:, 0:3], in_=denA), r=[bnA], w=["rdA" + sfx])
                        P.op("dve", lambda e: e.reciprocal(out=rd[:, 3:4], in_=ab[:, bB, 128:129]), r=[bnB],
                             w=["rdB" + sfx])

                        def region(jq):
                            return (ab[:, bA, jq * 130:jq * 130 + 128], bnA) if jq < 3 else (ab[:, bB, 0:128], bnB)

                        if m == 0:
                            for jq in range(4):
                                reg, bn = region(jq)
                                P.op("act", lambda e, reg=reg, jq=jq: e.activation(
                                    out=ob[:, jq, :], in_=reg, func=AF.Copy, scale=rd[:, jq:jq + 1]),
                                     r=[bn, "rdA" + sfx, "rdB" + sfx], w=[obn + "_%d" % jq])
                        else:
                            nl, ss4, ln4, r4 = smx["nl"], smx["ss4"], smx["ln4"], smx["r4"]
                            P.op("dve", lambda e: e.tensor_scalar(out=nl[:], in0=rd[:], scalar1=neglam[:, 0:1],
                                                                  scalar2=None, op0=ALU.mult),
                                 r=["rdA" + sfx, "rdB" + sfx, "neglam"], w=["nl" + sfx])
                            for jq in range(4):
                                reg, bn = region(jq)
                                P.op("dve", lambda e, reg=reg, jq=jq: e.scalar_tensor_tensor(
                                    out=ob[:, jq, :], in0=reg, scalar=nl[:, jq:jq + 1], in1=ob[:, jq, :],
                                    op0=ALU.mult, op1=ALU.add),
                                     r=[bn, "nl" + sfx, obn + "_%d" % jq], w=[obn + "_%d" % jq])
                                P.op("act", lambda e, jq=jq: e.activation(
                                    out=junk[:], in_=ob[:, jq, :], func=AF.Square, accum_out=ss4[:, jq:jq + 1]),
                                     r=[obn + "_%d" % jq], w=["junk", "ss4%s_%d" % (sfx, jq)])
                            ssr = ["ss4%s_%d" % (sfx, jq) for jq in range(4)]
                            P.op("act", lambda e: e.activation(out=ln4[:], in_=ss4[:], func=AF.Ln, scale=1.0 / 128,
                                                               bias=epsb[:, 0:1]), r=ssr + ["epsb"], w=["ln4" + sfx])
                            P.op("act", lambda e: e.activation(out=r4[:], in_=ln4[:], func=AF.Exp, scale=-0.5),
                                 r=["ln4" + sfx], w=["r4" + sfx])
                            for jq in range(4):
                                T = 4 * qb + jq
                                P.op("dve", lambda e, jq=jq, T=T: e.scalar_tensor_tensor(
                                    out=om[:, T, h * 128:(h + 1) * 128], in0=ob[:, jq, :], scalar=r4[:, jq:jq + 1],
                                    in1=Gs[:], op0=ALU.mult, op1=ALU.mult),
                                     r=[obn + "_%d" % jq, "r4" + sfx, "Gs"], w=["om%d_d%d" % (T, h)])

                    pipe.push(None, post)
        pipe.drain()

        def stage1(g, T):
            if True:
                ci = g * NT + T
                bk = 2 + (ci % 2)
                smx = sm[ci % 2]
                sfx = "_%d" % (ci % 2)
                nspt, nspn = nsp[ci % 2], "nsp%d" % (ci % 2)
                spt, spn = SPS.next()

                def qk(spt=spt, spn=spn, g=g, T=T):
                    P.op("pe", lambda e: e.matmul(spt[0:127, :], kcc[g][0:64, 0:127],
                                                  Qg[g][0:64, :, T * 128:(T + 1) * 128], start=True, stop=True),
                         w=[spn])

                def rest(spt=spt, spn=spn, g=g, T=T, bk=bk):
                    p, pn = exp_tile(spt, spn, 127, [([[0, 4], [1, 128]], T * 128 - 31, -16)])
                    for hg in range(4):
                        P.op("pe", lambda e, hg=hg: e.matmul(ab[:, bk, hg * 128:hg * 128 + 97],
                                                             p[0:127, hg * 128:(hg + 1) * 128], vca[0:127, g, 0:97],
                                                             start=True, stop=True, skip_group_check=True),
                             r=[pn], w=["ab%d" % bk])

                def post(g=g, T=T, bk=bk, smx=smx, sfx=sfx, nspt=nspt, nspn=nspn):
                    bn = "ab%d" % bk
                    den, rd, gr, imp, impm, wk, m1, m2 = (smx["den"], smx["rd"], smx["gr"], smx["imp"],
                                                          smx["impm"], smx["wk"], smx["m1"], smx["m2"])
                    ov4 = ab[:, bk, :].rearrange("p (h c) -> p h c", c=128)
                    gv = gates[:, T, :].rearrange("p (h c) -> p h c", c=3)
                    P.op("dve", lambda e: e.tensor_scalar(out=den[:], in0=ov4[:, :, 64], scalar1=1e-30, scalar2=None,
                                                          op0=ALU.max), r=[bn], w=["den" + sfx])
                    P.op("dve", lambda e: e.reciprocal(out=rd[:], in_=den[:]), r=["den" + sfx], w=["rd" + sfx])
                    P.op("dve", lambda e: e.tensor_tensor(out=gr[:], in0=rd[:], in1=gv[:, g * 4:(g + 1) * 4, 0],
                                                          op=ALU.mult), r=["rd" + sfx, "gates"], w=["gr" + sfx])
                    for hg in range(4):
                        hd = g * 4 + hg
                        P.op("act", lambda e, hg=hg, hd=hd: e.activation(
                            out=om[:, T, 512 + hd * 64:512 + (hd + 1) * 64], in_=ab[:, bk, hg * 128:hg * 128 + 64],
                            func=AF.Copy, scale=gr[:, hg:hg + 1]), r=[bn, "gr" + sfx], w=["om%d_n%d" % (T, hd)])
                    P.op("dve", lambda e: e.tensor_scalar(out=imp[:], in0=ab[:, bk, 65:97], scalar1=rd[:, 0:1],
                                                          scalar2=None, op0=ALU.mult),
                         r=[bn, "rd" + sfx], w=["imp" + sfx])
                    for hg in range(1, 4):
                        P.op("dve", lambda e, hg=hg: e.scalar_tensor_tensor(
                            out=imp[:], in0=ab[:, bk, hg * 128 + 65:hg * 128 + 97], scalar=rd[:, hg:hg + 1],
                            in1=imp[:], op0=ALU.mult, op1=ALU.add), r=[bn, "rd" + sfx, "imp" + sfx],
                             w=["imp" + sfx])
                    P.op("dve", lambda e: e.tensor_tensor(out=impm[:], in0=imp[:], in1=Am[:, T, :], op=ALU.mult),
                         r=["imp" + sfx, "Am"], w=["impm" + sfx])
                    P.op("dve", lambda e: e.tensor_tensor(out=impm[:], in0=impm[:], in1=Bm[:, T, :], op=ALU.add),
                         r=["impm" + sfx, "Bm"], w=["impm" + sfx])
                    P.op("dve", lambda e: e.max(out=m1[:], in_=impm[:]), r=["impm" + sfx], w=["m1" + sfx])
                    P.op("dve", lambda e: e.match_replace(out=wk[:], in_to_replace=m1[:], in_values=impm[:],
                                                          imm_value=-2.0),
                         r=["impm" + sfx, "m1" + sfx], w=["wk" + sfx])
                    P.op("dve", lambda e: e.max(out=m2[:], in_=wk[:]), r=["wk" + sfx], w=["m2" + sfx])
                    P.op("dve", lambda e: e.tensor_scalar(out=nspt[:, 64:96], in0=impm[:], scalar1=m2[:, 7:8],
                                                          scalar2=-BIG, op0=ALU.is_lt, op1=ALU.mult),
                         r=["impm" + sfx, "m2" + sfx], w=[nspn])
                    P.op("pe", lambda e: e.transpose(tps[0:96, 0:128], nspt[:, 0:96], ident[:]),
                         r=[nspn, "ident"], w=["tps"])
                    tsl = slice(T * 128, (T + 1) * 128)
                    P.op("act", lambda e: e.activation(out=Qg[g][64:96, 0, tsl], in_=tps[64:96, 0:128],
                                                       func=AF.Copy), r=["tps"], w=["Qs%d_%d" % (g, T)])
                    for hg in range(1, 4):
                        P.op("pool", lambda e, hg=hg: e.tensor_copy(out=Qg[g][64:96, hg, tsl],
                                                                    in_=Qg[g][64:96, 0, tsl]),
                             r=["Qs%d_%d" % (g, T)], w=["Qs%d_%d_%d" % (g, T, hg)])

                pipe.push(qk, rest)
                pipe.push(None, post)

        def stage2(g, T):
            if True:
                ci = g * NT + T
                bS, bW = 0, 1
                smx = sm[ci % 2]
                sfx = "_%d" % (ci % 2)
                selr = ["Qs%d_%d" % (g, T)] + ["Qs%d_%d_%d" % (g, T, hg) for hg in range(1, 4)]
                jobs = [("s", kt) for kt in range(T + 1)] + [("w", kt) for kt in range(max(0, T - 4), T + 1)]
                for kind, kt in jobs:
                    spt, spn = SPS.next()

                    def qk(spt=spt, spn=spn, kind=kind, kt=kt, g=g, T=T, selr=selr):
                        ksl = slice(kt * 128, (kt + 1) * 128)
                        tsl = slice(T * 128, (T + 1) * 128)
                        if kind == "s":
                            P.op("pe", lambda e: e.matmul(spt[:], KSg[g][0:96, ksl], Qg[g][0:96, :, tsl],
                                                          start=True, stop=True), r=selr, w=[spn])
                        else:
                            P.op("pe", lambda e: e.matmul(spt[:], KWg[g][0:64, ksl], Qg[g][0:64, :, tsl],
                                                          start=True, stop=True), w=[spn])

                    def rest(spt=spt, spn=spn, kind=kind, kt=kt, g=g, T=T, bS=bS, bW=bW):
                        masks = []
                        if kt == T:
                            masks.append(([[0, 4], [1, 128]], 0, -1))
                        if kind == "w" and kt == T - 4:
                            masks.append(([[0, 4], [-1, 128]], -1, 1))
                        p, pn = exp_tile(spt, spn, 128, masks)
                        va = vsa if kind == "s" else vwa
                        bank_i = bS if kind == "s" else bW
                        if kt == (0 if kind == "s" else max(0, T - 4)):
                            first_in_bank["ab%d" % bank_i] = True
                        for hg in range(4):
                            acc_mm(bank_i, hg * 128, 65, p[:, hg * 128:(hg + 1) * 128], va[:, kt, g, 0:65],
                                   kt == T, [pn])

                    pipe.push(qk, rest)

                def post(g=g, T=T, bS=bS, bW=bW, smx=smx, sfx=sfx):
                    bnS, bnW = "ab%d" % bS, "ab%d" % bW
                    rs, rw = smx["rs"], smx["rw"]
                    gv = gates[:, T, :].rearrange("p (h c) -> p h c", c=3)
                    oS = ab[:, bS, :].rearrange("p (h c) -> p h c", c=128)
                    oW = ab[:, bW, :].rearrange("p (h c) -> p h c", c=128)
                    P.op("dve", lambda e: e.reciprocal(out=rs[:], in_=oS[:, :, 64]), r=[bnS], w=["rs" + sfx])
                    P.op("dve", lambda e: e.tensor_tensor(out=rs[:], in0=rs[:], in1=gv[:, g * 4:(g + 1) * 4, 1],
                                                          op=ALU.mult), r=["rs" + sfx, "gates"], w=["rs" + sfx])
                    P.op("dve", lambda e: e.reciprocal(out=rw[:], in_=oW[:, :, 64]), r=[bnW], w=["rw" + sfx])
                    P.op("dve", lambda e: e.tensor_tensor(out=rw[:], in0=rw[:], in1=gv[:, g * 4:(g + 1) * 4, 2],
                                                          op=ALU.mult), r=["rw" + sfx, "gates"], w=["rw" + sfx])
                    for hg in range(4):
                        hd = g * 4 + hg
                        on = "om%d_n%d" % (T, hd)
                        osl = om[:, T, 512 + hd * 64:512 + (hd + 1) * 64]
                        tf, tfn = tmpf[hg % 2], "tmpf%d" % (hg % 2)
                        P.op("dve", lambda e, hg=hg, osl=osl, tf=tf: e.scalar_tensor_tensor(
                            out=tf[:], in0=ab[:, bS, hg * 128:hg * 128 + 64], scalar=rs[:, hg:hg + 1], in1=osl,
                            op0=ALU.mult, op1=ALU.add), r=[bnS, "rs" + sfx, on], w=[tfn])
                        P.op("dve", lambda e, hg=hg, osl=osl, tf=tf: e.scalar_tensor_tensor(
                            out=osl, in0=ab[:, bW, hg * 128:hg * 128 + 64], scalar=rw[:, hg:hg + 1], in1=tf[:],
                            op0=ALU.mult, op1=ALU.add), r=[bnW, "rw" + sfx, tfn], w=[on])

                pipe.push(None, post)

        for g in range(2):
            stage1(g, 0)
            stage1(g, 1)
            for T in range(NT):
                stage2(g, T)
                if T + 2 < NT:
                    stage1(g, T + 2)
        pipe.drain()

        for T in range(NT):
            q = T % 2
            t0 = T * 128
            omr = ["om%d_d%d" % (T, h) for h in range(4)] + ["om%d_n%d" % (T, hd) for hd in range(8)]
            for k in range(8):
                P.op("pe", lambda e, k=k, T=T: e.transpose(tps[:, k * 128:(k + 1) * 128],
                                                          om[:, T, k * 128:(k + 1) * 128], ident[:]),
                     r=omr + ["ident"], w=["tps"])
            oT, oTn = omT[q], "omT%d" % q
            P.op("act", lambda e, oT=oT: e.activation(out=oT[:].rearrange("p k t -> p (k t)"), in_=tps[:],
                                                      func=AF.Copy), r=["tps"], w=[oTn])
            b0 = 2 * q
            for n in range(2):
                for k in range(8):
                    P.op("pe", lambda e, n=n, k=k, oT=oT, b0=b0: e.matmul(
                        ab[:, b0 + n, :], oT[:, k, :], Wo[:, k, n * 512:(n + 1) * 512], start=(k == 0),
                        stop=(k == 7)), r=[oTn, "Wo"], w=["ab%d" % (b0 + n)])
            xr, yt = xres[q], ytmp[q]
            xrn, ytn = "xres%d" % q, "ytmp%d" % q
            wo2 = ab[:, b0:b0 + 2, :]
            br = ["ab%d" % b0, "ab%d" % (b0 + 1)]
            P.dma("act", lambda e, xr=xr, t0=t0: e.dma_start(out=xr[:], in_=src[b, t0:t0 + 128, :]),
                  tag + xrn + "i", w=[xrn])
            P.op("act", lambda e, yt=yt, q=q, wo2=wo2: e.activation(
                out=yt[:].rearrange("p (n f) -> p n f", n=2), in_=wo2, func=AF.Square, accum_out=ss2[:, q:q + 1]),
                 r=br, w=[ytn, "ss2%d" % q])
            P.op("act", lambda e, q=q: e.activation(out=ss2[:, q:q + 1], in_=ss2[:, q:q + 1], func=AF.Sqrt,
                                                    scale=1.0 / D, bias=epsb[:, 0:1]),
                 r=["ss2%d" % q, "epsb"], w=["ss2%d" % q])
            P.op("dve", lambda e, q=q: e.reciprocal(out=rstd2[:, q:q + 1], in_=ss2[:, q:q + 1]),
                 r=["ss2%d" % q], w=["rstd2%d" % q])
            P.op("dve", lambda e, yt=yt, q=q, wo2=wo2: e.scalar_tensor_tensor(
                out=yt[:].rearrange("p (n f) -> p n f", n=2), in0=wo2, scalar=rstd2[:, q:q + 1],
                in1=Gm[:].rearrange("p (n f) -> p n f", n=2), op0=ALU.mult, op1=ALU.mult),
                 r=br + ["rstd2%d" % q, "Gm"], w=[ytn])
            P.op("pool", lambda e, xr=xr, yt=yt: e.tensor_tensor(out=xr[:], in0=xr[:], in1=yt[:], op=ALU.add),
                 r=[ytn, xrn], w=[xrn])
            P.dma("sp", lambda e, xr=xr, t0=t0: e.dma_start(out=dst[b, t0:t0 + 128, :], in_=xr[:]),
                  tag + xrn + "o", r=[xrn], w=["dst_B_%d_%d" % (b, t0)])
        P.barrier_all()
        P.flush()


def build(stage=3):
    nc = bass.Bass("TRN2", target_bir_lowering=False)

    def dt(n, s, d=F32, k="ExternalInput"):
        return nc.dram_tensor(n, s, d, kind=k).ap()

    x = dt("x", [NB, S, D])
    out = dt("out", [NB, S, D], F32, "ExternalOutput")
    ident_d = dt("ident", [128, 128], BF16)
    f = {}
    for t in ("f1", "f2"):
        f[t] = dict(wg=dt(t + "_wg", [D, DFF]), wu=dt(t + "_wu", [D, DFF]), wd=dt(t + "_wd", [DFF, D]),
                    gpre=dt(t + "_gpre", [128, 8]), gpost=dt(t + "_gpost", [128, D]))
    dd = dict(ident=ident_d,
              w_in=dt("w_in", [D, C_END]), w_out=dt("w_out", [D, D]), m_gpre=dt("m_gpre", [128, 8]),
              Gm=dt("Gm", [128, D]), Gs=dt("Gs", [128, 128]),
              cosR=dt("cosR", [128, NT, 8, 8]), sinR=dt("sinR", [128, NT, 8, 8]),
              cosR2=dt("cosR2", [128, NT, 8, 8]), sinR2=dt("sinR2", [128, NT, 8, 8]),
              ov=dt("ov", [127, 32], BF16), ET=dt("ET", [32, S], BF16),
              Am=dt("Am", [128, NT, 32]), Bm=dt("Bm", [128, NT, 32]),
              ck_w1=dt("ck_w1", [2048, 256]), ck_w2=dt("ck_w2", [256, 64]), ck_peT=dt("ck_peT", [128, 32]),
              cv_w1=dt("cv_w1", [2048, 256]), cv_w2=dt("cv_w2", [256, 64]), cv_peT=dt("cv_peT", [128, 32]),
              lq1=dt("lq1", [128, 64]), lk1=dt("lk1", [128, 64]), lq2=dt("lq2", [128, 64]),
              lk2=dt("lk2", [128, 64]))
    x1 = nc.dram_tensor("x1s", [NB, S, D], F32).ap()
    x2 = nc.dram_tensor("x2s", [NB, S, D], F32).ap()
    with ExitStack() as stack:
        P = Prog(nc, stack)
        fa = f["f1"]
        ffn_phase(nc, P, "A", x, out if stage == 1 else x1, fa["wg"], fa["wu"], fa["wd"], fa["gpre"],
                  fa["gpost"], ident_d)
        if stage >= 2:
            mixer_phase(nc, P, x1, out if stage == 2 else x2, dd)
        if stage >= 3:
            fc = f["f2"]
            ffn_phase(nc, P, "C", x2, out, fc["wg"], fc["wu"], fc["wd"], fc["gpre"], fc["gpost"], ident_d)
    return nc


def host_inputs(inp):
    def g(k):
        return np.ascontiguousarray(np.asarray(inp[k], dtype=np.float32))

    bf = ml_dtypes.bfloat16

    def bc(v, n=128):
        return np.ascontiguousarray(np.broadcast_to(v[None, :], (n, v.shape[0])))

    common = {"ident": np.eye(128, dtype=np.float32).astype(bf)}
    for t, pfx in (("f1", "ff1"), ("f2", "ff2")):
        common[t + "_wg"] = g(pfx + "_w_gate")[0]
        common[t + "_wu"] = g(pfx + "_w_up")[0]
        common[t + "_wd"] = g(pfx + "_w_down")[0]
        common[t + "_gpre"] = np.ascontiguousarray(g(pfx + "_norm_pre")[0].reshape(8, 128).T)
        common[t + "_gpost"] = bc(g(pfx + "_norm_post")[0])
    common["w_in"] = g("w_in")[0]
    common["w_out"] = g("w_out")[0]
    common["m_gpre"] = np.ascontiguousarray(g("mix_norm_pre")[0].reshape(8, 128).T)
    common["Gm"] = bc(g("mix_norm_post")[0])
    common["Gs"] = bc(g("diff_subln")[0])
    for k, n in (("lq1", "lambda_q1"), ("lk1", "lambda_k1"), ("lq2", "lambda_q2"), ("lk2", "lambda_k2")):
        common[k] = bc(g(n)[0])
    for kv, w1n, w2n, pen in (("ck", "cmp_k_w1", "cmp_k_w2", "cmp_pe_k"), ("cv", "cmp_v_w1", "cmp_v_w2", "cmp_pe_v")):
        common[kv + "_w1"] = g(w1n)[0]
        common[kv + "_w2"] = g(w2n)[0]
        peT = g(pen)[0].T
        common[kv + "_peT"] = np.ascontiguousarray(np.concatenate([peT, peT], axis=0))
    pos = np.arange(S, dtype=np.float32)
    inv = (np.float32(500000.0) ** (-np.arange(0, 16, 2, dtype=np.float32) / np.float32(16))).astype(np.float32)
    ang = (pos[:, None] * inv[None, :]).astype(np.float32)
    cs, sn = np.cos(ang).astype(np.float32), np.sin(ang).astype(np.float32)

    def tab(a):
        a = a.reshape(NT, 128, 8).transpose(1, 0, 2)
        return np.ascontiguousarray(np.broadcast_to(a[:, :, None, :], (128, NT, 8, 8)))

    common["cosR"], common["sinR"] = tab(cs), tab(sn)
    c2, s2 = tab(cs).copy(), tab(sn).copy()
    c2[:, :, 4:8, :] = 1.0
    s2[:, :, 4:8, :] = 0.0
    common["cosR2"], common["sinR2"] = c2, s2
    c = np.arange(127)[:, None] * 16
    j = np.arange(32)[None, :] * 64
    ov = np.clip(np.minimum(c + 32, j + 64) - np.maximum(c, j), 0, None) / 32.0
    common["ov"] = ov.astype(np.float32).astype(bf)
    common["ET"] = (np.arange(S)[None, :] // 64 == np.arange(32)[:, None]).astype(np.float32).astype(bf)
    tt = np.arange(S)
    cur = (tt // 64)[:, None]
    blk = np.arange(32)[None, :]
    forced = (blk == 0) | ((blk <= cur) & (blk >= cur - 1))
    causal = blk <= cur
    A = (~forced & causal).astype(np.float32)
    Bc = np.where(forced, np.float32(1e9), np.where(causal, np.float32(0.0), np.float32(-1.0))).astype(np.float32)
    common["Am"] = np.ascontiguousarray(A.reshape(NT, 128, 32).transpose(1, 0, 2))
    common["Bm"] = np.ascontiguousarray(Bc.reshape(NT, 128, 32).transpose(1, 0, 2))
    x = g("x")
    maps = []
    for c_ in range(8):
        m = dict(common)
        m["x"] = x[c_ * NB:(c_ + 1) * NB]
        maps.append(m)
    return maps


def kernel(**inputs):
    nc = build(3)
    maps = host_inputs(inputs)
    res = run_bass_kernel_spmd(nc, maps, core_ids=list(range(8)))
    return np.concatenate([np.asarray(r["out"]) for r in res.results], axis=0).astype(np.float32)
```

```python
import numpy as np
import ml_dtypes
from contextlib import ExitStack
import concourse.bass as bass
import concourse.mybir as mybir
from concourse.bass_utils import run_bass_kernel_spmd

F32 = mybir.dt.float32
BF16 = mybir.dt.bfloat16
AF = mybir.ActivationFunctionType
ALU = mybir.AluOpType
AX = mybir.AxisListType

S = 2048
D = 1024
DFF = 2816
NF = DFF // 128
NB = 2
EPS = 1e-6
ENGS = ("pe", "act", "dve", "pool", "sp")


import re
PSUM_RE = re.compile(r"^(tr\d|gu\d|dn\d|pj\d|tq\d|pbps|hps\d|cps|sps\d|ab\d|tps)")


class Res:
    __slots__ = ("name", "w", "r")

    def __init__(self, name):
        self.name = name
        self.w = None
        self.r = {}


class Op:
    __slots__ = ("eng", "fn", "deps", "kind", "key", "val", "idx", "sig", "waits", "ordinal")


class Prog:
    def __init__(self, nc, stack):
        self.nc = nc
        self.stack = stack
        self.res = {}
        self.ops = []
        self.eng_n = {e: 0 for e in ENGS}
        self.sigbase = {e: 0 for e in ENGS}
        self.esem = {e: stack.enter_context(nc.semaphore("sem_" + e)) for e in ENGS if e != "sp"}
        self.dsem = {}
        self.dcount = {}
        self.free_sems = {"sw": [], "hw": []}
        self.dcls = {}
        self.live = []
        self.seen = {e: {} for e in ENGS}
        self.sigord = {e: {} for e in ENGS}

    def R(self, name):
        r = self.res.get(name)
        if r is None:
            r = self.res[name] = Res(name)
        return r

    def _deps(self, reads, writes):
        deps = []
        for n in reads:
            r = self.R(n)
            if r.w is not None:
                deps.append(r.w)
        for n in writes:
            r = self.R(n)
            if r.w is not None:
                deps.append(r.w)
            deps.extend(r.r.values())
        return deps

    def _mark(self, reads, writes, ev, rkey):
        for n in reads:
            self.R(n).r[rkey] = ev
        for n in writes:
            r = self.R(n)
            r.w = ev
            r.r = {}

    def op(self, eng, fn, r=(), w=()):
        o = Op()
        o.eng = eng
        o.fn = fn
        o.kind = "c"
        o.deps = self._deps(r, w)
        for n in r:
            if PSUM_RE.match(n):
                o.deps.extend(ev for k, ev in self.R(n).r.items() if k != eng)
        o.idx = self.eng_n[eng]
        self.eng_n[eng] += 1
        o.sig = False
        self._mark(r, w, ("c", eng, o.idx), eng)
        self.ops.append(o)
        return o

    def dma(self, q, fn, key, r=(), w=()):
        o = Op()
        o.eng = q
        o.fn = fn
        o.kind = "d"
        o.key = key
        if key not in self.dsem:
            cls = "sw" if q == "pool" else "hw"
            self.dcls[key] = cls
            if self.free_sems[cls]:
                self.dsem[key], self.dcount[key] = self.free_sems[cls].pop()
            else:
                self.dsem[key] = self.stack.enter_context(self.nc.semaphore("dsem%d" % len(self.dsem)))
                self.dcount[key] = 0
            self.live.append(key)
        o.deps = self._deps(r, w)
        self.dcount[key] += 16
        o.val = self.dcount[key]
        o.idx = self.eng_n[q]
        self.eng_n[q] += 1
        o.sig = False
        self._mark(r, w, ("d", key, o.val), "d_" + key)
        self.ops.append(o)
        return o

    def barrier_all(self):
        allres = list(self.res.keys())
        for e in ENGS:
            self.op(e, None, r=allres)
        self.res = {}
        for k in self.live:
            self.free_sems[self.dcls[k]].append((self.dsem[k], self.dcount[k]))
        self.live = []

    def flush(self):
        nc = self.nc
        ops = self.ops
        self.ops = []
        self.nflush = getattr(self, "nflush", 0) + 1
        if LIMIT is not None and self.nflush == LIMIT[0]:
            ops = ops[:LIMIT[1]]
        byidx = {}
        for o in ops:
            if o.kind == "c":
                byidx[(o.eng, o.idx)] = o
        for o in ops:
            o.waits = []
            seen = self.seen[o.eng]
            best = {}
            for d in o.deps:
                if d[0] == "c":
                    if d[1] == "pe" and o.eng == "pe":
                        continue
                    k = ("c", d[1])
                else:
                    k = ("d", d[1])
                if k not in best or d[2] > best[k][2]:
                    best[k] = d
            for d in best.values():
                if d[0] == "c":
                    _, pe, pi = d
                    if seen.get(pe, -1) >= pi:
                        continue
                    prod = byidx.get((pe, pi))
                    if prod is None:
                        assert pi in self.sigord[pe], (pe, pi)
                    else:
                        prod.sig = True
                    seen[pe] = pi
                    o.waits.append(d)
                else:
                    _, key, val = d
                    k = "d_" + key
                    if seen.get(k, 0) >= val:
                        continue
                    seen[k] = val
                    o.waits.append(d)
        last = {}
        for o in ops:
            if o.kind == "c":
                last[o.eng] = o
        for o in last.values():
            o.sig = True
        for o in ops:
            if o.kind == "c" and o.sig:
                self.sigbase[o.eng] += 1
                self.sigord[o.eng][o.idx] = self.sigbase[o.eng]
        per = {e: [o for o in ops if o.eng == e] for e in ENGS}

        def emit(eng_name, eng):
            for o in per[eng_name]:
                for d in o.waits:
                    if d[0] == "c":
                        eng.wait_ge(self.esem[d[1]], self.sigord[d[1]][d[2]])
                    else:
                        eng.wait_ge(self.dsem[d[1]], d[2])
                if o.fn is None:
                    if o.kind == "c" and o.sig:
                        eng.nop().then_inc(self.esem[eng_name], 1) if eng_name != "sp" else None
                    continue
                ins = o.fn(eng)
                if o.kind == "d":
                    ins.then_inc(self.dsem[o.key], 16)
                elif o.sig:
                    ins.then_inc(self.esem[eng_name], 1)

        with nc.Block() as block:
            @block.tensor
            def _(e):
                emit("pe", e)

            @block.scalar
            def _(e):
                emit("act", e)

            @block.vector
            def _(e):
                emit("dve", e)

            @block.gpsimd
            def _(e):
                emit("pool", e)

            @block.sync
            def _(e):
                emit("sp", e)


def ffn_phase(nc, P, tag, src, dst, wg_d, wu_d, wd_d, gpre_d, gpost_d, ident_d):
    with ExitStack() as st:
        def sb(name, shape, dt):
            return st.enter_context(nc.sbuf_tensor(tag + name, shape, dt))

        def ps(name, shape, dt):
            return st.enter_context(nc.psum_tensor(tag + name, shape, dt))

        Wg = sb("Wg", [128, 8, DFF], BF16)
        Wu = sb("Wu", [128, 8, DFF], BF16)
        Wd = sb("Wd", [128, NF, D], BF16)
        xin = [sb("xin%d" % i, [128, D], F32) for i in range(2)]
        hn = sb("hn", [128, 4, D], BF16)
        hT = sb("hT", [128, 8, 512], BF16)
        actT = sb("actT", [128, NF, 512], BF16)
        G = sb("G", [128, D], F32)
        gpre = sb("gpre", [128, 8], F32)
        ident = sb("ident", [128, 128], BF16)
        xres = [sb("xres%d" % i, [128, D], F32) for i in range(2)]
        ytmp = [sb("ytmp%d" % i, [128, D], F32) for i in range(2)]
        sg = [sb("sg%d" % i, [128, 512], F32) for i in range(2)]
        ss = sb("ss", [128, 8], F32)
        rstd = sb("rstd", [128, 8], F32)
        ss2 = sb("ss2", [128, 2], F32)
        rstd2 = sb("rstd2", [128, 2], F32)
        epsb = sb("epsb", [128, 1], F32)
        P.op("pool", lambda e: e.memset(epsb[:], EPS), w=["epsb"])
        tr = [ps("tr%d" % i, [128, 2, 512], BF16) for i in range(2)]
        guT = ps("gu", [128, 4, 512], F32)
        gu = [guT[:, i, :] for i in range(4)]
        dn = ps("dn", [128, D], F32)
        dnbuf = [(dn[:].rearrange("p (n f) -> p n f", n=2), ["dn0", "dn1"]), (guT[:, 2:4, :], ["gu1Wg", "gu1Wu"])]

        P.dma("sp", lambda e: e.dma_start(out=gpre[:], in_=gpre_d), tag + "c0", w=["gpre"])
        P.dma("sp", lambda e: e.dma_start(out=G[:], in_=gpost_d), tag + "c1", w=["G"])
        P.dma("sp", lambda e: e.dma_start(out=ident[:], in_=ident_d), tag + "c2", w=["ident"])
        P.op("dve", lambda e: e.tensor_scalar(out=G[:], in0=G[:], scalar1=0.5, scalar2=None, op0=ALU.mult),
             r=["G"], w=["G"])
        wg_v = wg_d.rearrange("(k p) f -> p k f", p=128)
        wu_v = wu_d.rearrange("(k p) f -> p k f", p=128)
        wd_v = wd_d.rearrange("(f p) d -> p f d", p=128)
        FG = [(0, 2), (2, 6), (6, 10), (10, 14), (14, 18), (18, 22)]
        fgrp = {}
        for gi, (a, b) in enumerate(FG):
            for f in range(a, b):
                fgrp[f] = gi
            for nm, W, v in (("Wg", Wg, wg_v), ("Wu", Wu, wu_v)):
                for kh in range(2):
                    P.dma("pool",
                          lambda e, W=W, v=v, a=a, b=b, kh=kh: e.dma_start(
                              out=W[:, kh * 4:(kh + 1) * 4, a * 128:b * 128],
                              in_=v[:, kh * 4:(kh + 1) * 4, a * 128:b * 128]),
                          "%s%s%d" % (tag, nm, gi), w=["%s%d" % (nm, gi)])
        for gi, (a, b) in enumerate(FG):
            P.dma("pool", lambda e, a=a, b=b: e.dma_start(out=Wd[:, a:b, :], in_=wd_v[:, a:b, :]),
                  "%sWd%d" % (tag, gi), w=["Wd%d" % gi])

        blocks = [(b, i) for b in range(NB) for i in range(4)]
        nxi = [0]

        def emit_N(bi, j):
            b, i = blocks[bi]
            t0 = i * 512 + j * 128
            xb = xin[nxi[0] % 2]
            xn = "xin%d" % (nxi[0] % 2)
            nxi[0] += 1
            P.dma("sp", lambda e: e.dma_start(out=xb[:], in_=src[b, t0:t0 + 128, :]), tag + xn, w=[xn])
            P.op("act", lambda e: e.activation(out=hn[:, j, :], in_=xb[:], func=AF.Square,
                                               accum_out=ss[:, j:j + 1]),
                 r=[xn], w=["hn%d" % j, "ss%d" % j])
            P.op("act", lambda e: e.activation(out=ss[:, j:j + 1], in_=ss[:, j:j + 1], func=AF.Sqrt,
                                               scale=1.0 / D, bias=epsb[:, 0:1]),
                 r=["ss%d" % j, "epsb"], w=["ss%d" % j])
            P.op("dve", lambda e: e.reciprocal(out=rstd[:, j:j + 1], in_=ss[:, j:j + 1]),
                 r=["ss%d" % j], w=["rstd%d" % j])
            P.op("act", lambda e: e.activation(out=hn[:, j, :], in_=xb[:], func=AF.Copy,
                                               scale=rstd[:, j:j + 1]),
                 r=[xn, "rstd%d" % j], w=["hn%d" % j])

        def emit_T(bi):
            for kp in range(4):
                bank = kp % 2
                for kk in range(2):
                    k = kp * 2 + kk
                    for j in range(4):
                        P.op("pe", lambda e, k=k, kk=kk, j=j, bank=bank: e.transpose(
                            tr[bank][:, kk, j * 128:(j + 1) * 128], hn[:, j, k * 128:(k + 1) * 128], ident[:]),
                             r=["hn%d" % j, "ident"], w=["tr%d" % bank])
                for kk in range(2):
                    k = kp * 2 + kk
                    P.op("dve", lambda e, k=k, kk=kk, bank=bank: e.tensor_scalar(
                        out=hT[:, k, :], in0=tr[bank][:, kk, :], scalar1=gpre[:, k:k + 1], scalar2=None,
                        op0=ALU.mult),
                         r=["tr%d" % bank, "gpre"], w=["hT%d" % k])

        gui = [0]

        def emit_GU_f(bi, f):
            pr = gui[0] % 2
            gui[0] += 1
            pg, pu = gu[2 * pr], gu[2 * pr + 1]
            gi = fgrp[f]
            for nm, W, pt in (("Wg", Wg, pg), ("Wu", Wu, pu)):
                pn = "gu%d%s" % (pr, nm)
                for k in range(8):
                    P.op("pe", lambda e, W=W, pt=pt, k=k: e.matmul(
                        pt[:], W[:, k, f * 128:(f + 1) * 128], hT[:, k, :], start=(k == 0), stop=(k == 7)),
                         r=["%s%d" % (nm, gi), "hT%d" % k], w=[pn])
            s = sg[pr]
            P.op("act", lambda e: e.activation(out=s[:], in_=pg[:], func=AF.Silu),
                 r=["gu%dWg" % pr], w=["sg%d" % pr])
            P.op("dve", lambda e: e.tensor_tensor(out=actT[:, f, :], in0=s[:], in1=pu[:], op=ALU.mult),
                 r=["sg%d" % pr, "gu%dWu" % pr], w=["actT%d" % f])

        dni = [0]

        def emit_D(bi):
            b, i = blocks[bi]
            for j in range(4):
                t0 = i * 512 + j * 128
                q = dni[0] % 2
                dni[0] += 1
                xr, yt = xres[q], ytmp[q]
                xrn, ytn = "xres%d" % q, "ytmp%d" % q
                P.dma("act", lambda e, xr=xr, t0=t0: e.dma_start(out=xr[:], in_=src[b, t0:t0 + 128, :]),
                      tag + xrn + "i", w=[xrn])
                dnv, dnr = dnbuf[j % 2]
                ytv = yt[:].rearrange("p (n f) -> p n f", n=2)
                for n in range(2):
                    for f in range(NF):
                        P.op("pe", lambda e, n=n, f=f, j=j, dnv=dnv: e.matmul(
                            dnv[:, n, :], actT[:, f, j * 128:(j + 1) * 128],
                            Wd[:, f, n * 512:(n + 1) * 512], start=(f == 0), stop=(f == NF - 1)),
                             r=["actT%d" % f, "Wd%d" % fgrp[f]], w=[dnr[n]])
                P.op("act", lambda e, ytv=ytv, q=q, dnv=dnv: e.activation(out=ytv, in_=dnv, func=AF.Square,
                                                                        accum_out=ss2[:, q:q + 1]),
                     r=dnr, w=[ytn, "ss2%d" % q])
                P.op("act", lambda e, q=q: e.activation(out=ss2[:, q:q + 1], in_=ss2[:, q:q + 1], func=AF.Sqrt,
                                                        scale=1.0 / D, bias=epsb[:, 0:1]),
                     r=["ss2%d" % q, "epsb"], w=["ss2%d" % q])
                P.op("dve", lambda e, q=q: e.reciprocal(out=rstd2[:, q:q + 1], in_=ss2[:, q:q + 1]),
                     r=["ss2%d" % q], w=["rstd2%d" % q])
                P.op("dve", lambda e, ytv=ytv, q=q, dnv=dnv: e.scalar_tensor_tensor(
                    out=ytv, in0=dnv, scalar=rstd2[:, q:q + 1], in1=G[:].rearrange("p (n f) -> p n f", n=2),
                    op0=ALU.mult, op1=ALU.mult),
                     r=dnr + ["rstd2%d" % q, "G"], w=[ytn])
                P.op("pool", lambda e, xr=xr, yt=yt: e.tensor_tensor(out=xr[:], in0=xr[:], in1=yt[:],
                                                                     op=ALU.add),
                     r=[ytn, xrn], w=[xrn])
                P.dma("sp", lambda e, xr=xr, t0=t0: e.dma_start(out=dst[b, t0:t0 + 128, :], in_=xr[:]),
                      tag + xrn + "o", r=[xrn], w=["dst_%s_%d_%d" % (tag, b, t0)])

        nblk = len(blocks)
        for j in range(4):
            emit_N(0, j)
        emit_T(0)
        for bi in range(nblk):
            for f in range(NF):
                emit_GU_f(bi, f)
                if bi + 1 < nblk and f in (3, 8, 13, 18):
                    emit_N(bi + 1, (f - 3) // 5)
            if bi + 1 < nblk:
                emit_T(bi + 1)
            emit_D(bi)
        P.barrier_all()
        P.flush()


NT = S // 128
LIMIT = None
NSEQ = NB
SUB = 9
CUT = 9
BARQT = False
EVAC_ACT = False
BIG = 30000.0
C_DQ, C_DK, C_DV, C_NQ, C_KC, C_VC, C_KS, C_VS, C_KW, C_VW, C_GL, C_END = (
    0, 512, 1024, 1536, 2048, 2176, 2304, 2432, 2560, 2688, 2816, 2840)
S_DQ, S_DK, S_NQ, S_KS, S_KW, S_KC, S_VC, S_END = 0, 512, 1024, 1536, 1664, 1792, 1920, 2048


class Rot:
    def __init__(self, items, names):
        self.items, self.names, self.i = items, names, 0

    def next(self):
        k = self.i % len(self.items)
        self.i += 1
        return self.items[k], self.names[k]


def mixer_phase(nc, P, src, dst, dd):
    TB = 2
    with ExitStack() as st0:
        def sb0(name, shape, dt):
            return st0.enter_context(nc.sbuf_tensor("B" + name, shape, dt))

        dqT = sb0("dqT", [128, 4, S], BF16)
        dkT = sb0("dkT", [128, 4, S], BF16)
        Qg = [sb0("Qg%d" % g, [128, 4, S], BF16) for g in range(2)]
        KSg = [sb0("KSg%d" % g, [128, S], BF16) for g in range(2)]
        KWg = [sb0("KWg%d" % g, [128, S], BF16) for g in range(2)]
        dva = sb0("dva", [128, NT, 4, 130], BF16)
        vsa = sb0("vsa", [128, NT, 2, 66], BF16)
        vwa = sb0("vwa", [128, NT, 2, 66], BF16)
        glr = sb0("glr", [128, NT, 24], F32)
        kcc = [sb0("kcc%d" % g, [128, 128], BF16) for g in range(2)]
        vca = sb0("vca", [128, 2, 98], BF16)
        ident = sb0("ident", [128, 128], BF16)
        epsb = sb0("epsb", [128, 1], F32)

        P.dma("sp", lambda e: e.dma_start(out=ident[:], in_=dd["ident"]), "Bc_ident", w=["ident"])
        P.op("pool", lambda e: e.memset(epsb[:], EPS), w=["epsb"])
        P.op("pool", lambda e: e.memset(dva[:, :, :, 128:130], 1.0), w=["dva"])
        P.op("pool", lambda e: e.memset(vsa[:, :, :, 64:66], 1.0), w=["vsa"])
        P.op("pool", lambda e: e.memset(vwa[:, :, :, 64:66], 1.0), w=["vwa"])
        P.op("pool", lambda e: e.memset(vca[:], 0.0), w=["vca"])
        P.op("pool", lambda e: e.memset(vca[:, :, 64:65], 1.0), r=["vca"], w=["vca"])
        for g in range(2):
            P.dma("sp", lambda e, g=g: e.dma_start(out=vca[0:127, g, 65:97], in_=dd["ov"]), "Bc_ov%d" % g,
                  r=["vca"], w=["vca_ov%d" % g])
            P.dma("sp", lambda e, g=g: e.dma_start(out=KSg[g][64:96, :], in_=dd["ET"]), "Bc_ET%d" % g,
                  w=["KSgE%d" % g])

        for b in range(NSEQ):
            with ExitStack() as st1:
                kcT = st1.enter_context(nc.sbuf_tensor("BkcT%d" % b, [128, S], BF16))
                vcT = st1.enter_context(nc.sbuf_tensor("BvcT%d" % b, [128, S], BF16))
                mixer_proj(nc, P, b, TB, src, dd, dict(dqT=dqT, dkT=dkT, Qg=Qg, KSg=KSg, KWg=KWg, dva=dva,
                                                      vsa=vsa, vwa=vwa, glr=glr, kcT=kcT, vcT=vcT,
                                                      ident=ident, epsb=epsb))
                if SUB >= 2:
                    mixer_compress(nc, P, b, dd, dict(kcT=kcT, vcT=vcT, kcc=kcc, vca=vca))
            if SUB >= 3:
                mixer_attn(nc, P, b, src, dst, dd, dict(dqT=dqT, dkT=dkT, Qg=Qg, KSg=KSg, KWg=KWg, dva=dva,
                                                   vsa=vsa, vwa=vwa, glr=glr, kcc=kcc, vca=vca,
                                                   ident=ident, epsb=epsb))


def mixer_proj(nc, P, b, TB, src, dd, t):
    tag = "P%d" % b
    dqT, dkT, Qg, KSg, KWg = t["dqT"], t["dkT"], t["Qg"], t["KSg"], t["KWg"]
    dva, vsa, vwa, glr, kcT, vcT, ident, epsb = (t["dva"], t["vsa"], t["vwa"], t["glr"], t["kcT"], t["vcT"],
                                                 t["ident"], t["epsb"])
    with ExitStack() as st:
        def sb(name, shape, dt):
            return st.enter_context(nc.sbuf_tensor(tag + name, shape, dt))

        def ps(name, shape, dt):
            return st.enter_context(nc.psum_tensor(tag + name, shape, dt))

        rt = [[sb("rt%d_%d" % (i, q), [128, 8, 8], F32) for q in range(4)] for i in range(2)]
        xs = [sb("xs%d" % i, [128, 512], F32) for i in range(2)]
        xin = [sb("xin%d" % i, [128, D], F32) for i in range(2)]
        hn = sb("hn", [128, TB, D], BF16)
        hT = sb("hT", [128, 8, TB * 128], BF16)
        stg = sb("stg", [128, TB, S_END], BF16)
        cosR = sb("cosR", [128, NT, 8, 8], F32)
        sinR = sb("sinR", [128, NT, 8, 8], F32)
        gpre = sb("gpre", [128, 8], F32)
        ss = sb("ss", [128, TB], F32)
        rstd = sb("rstd", [128, TB], F32)
        Win = sb("Win", [128, 8, 2048], BF16)
        tr = [ps("tr%d" % i, [128, 1024], BF16) for i in range(2)]
        pj = [ps("pj%d" % i, [128, 512], F32) for i in range(3)]
        tq = [ps("tq%d" % i, [128, 1024], BF16) for i in range(2)]

        P.dma("sp", lambda e: e.dma_start(out=gpre[:], in_=dd["m_gpre"]), tag + "gpre", w=["gpre"])
        P.dma("sp", lambda e: e.dma_start(out=cosR[:], in_=dd["cosR"]), tag + "cos", w=["cosR"])
        P.dma("sp", lambda e: e.dma_start(out=sinR[:], in_=dd["sinR"]), tag + "sin", w=["sinR"])
        win_v = dd["w_in"].rearrange("(k p) f -> p k f", p=128)
        CB = [(0, 512), (512, 1024), (1024, 1536), (1536, 2048), (2048, 2560), (2560, C_END)]
        SEC = [(C_KS, 128), (C_KW, 128), (C_KC, 128), (C_VC, 128), (C_VS, 128), (C_VW, 128), (C_GL, 24)]
        def load_pass(ph):
            if ph == 0:
                for ci, (c0, c1) in enumerate(CB[:4]):
                    for kh in range(2):
                        P.dma("pool", lambda e, c0=c0, c1=c1, kh=kh: e.dma_start(
                            out=Win[:, kh * 4:(kh + 1) * 4, c0:c1], in_=win_v[:, kh * 4:(kh + 1) * 4, c0:c1]),
                              "%sWin%d" % (tag, ci), w=["Win%d" % ci])
            else:
                dcol = 0
                for si, (sc, sw) in enumerate(SEC):
                    ci = 4 if si < 4 else 5
                    P.dma("pool", lambda e, sc=sc, sw=sw, dcol=dcol: e.dma_start(
                        out=Win[:, :, dcol:dcol + sw], in_=win_v[:, :, sc:sc + sw]),
                          "%sWin%d" % (tag, ci), w=["Win%d" % ci] + (["Win0", "Win1"] if si == 0 else []))
                    dcol += sw
                P.dma("sp", lambda e: e.dma_start(out=cosR[:], in_=dd["cosR2"]), tag + "cos", w=["cosR"])
                P.dma("sp", lambda e: e.dma_start(out=sinR[:], in_=dd["sinR2"]), tag + "sin", w=["sinR"])

        PH = [0]
        nxi = [0]
        pji = [0]
        rti = [0]
        tqi = [0]
        evi = [0]

        def emit_N(i, j):
            t0 = (i * TB + j) * 128
            q = nxi[0] % 2
            nxi[0] += 1
            xb, xn = xin[q], "xin%d" % q
            P.dma("sp", lambda e: e.dma_start(out=xb[:], in_=src[b, t0:t0 + 128, :]), tag + xn, w=[xn])
            P.op("act", lambda e: e.activation(out=hn[:, j, :], in_=xb[:], func=AF.Square,
                                               accum_out=ss[:, j:j + 1]),
                 r=[xn], w=["hn%d" % j, "ss%d" % j])
            P.op("act", lambda e: e.activation(out=ss[:, j:j + 1], in_=ss[:, j:j + 1], func=AF.Sqrt,
                                               scale=1.0 / D, bias=epsb[:, 0:1]),
                 r=["ss%d" % j, "epsb"], w=["ss%d" % j])
            P.op("dve", lambda e: e.reciprocal(out=rstd[:, j:j + 1], in_=ss[:, j:j + 1]),
                 r=["ss%d" % j], w=["rstd%d" % j])
            P.op("act", lambda e: e.activation(out=hn[:, j, :], in_=xb[:], func=AF.Copy,
                                               scale=rstd[:, j:j + 1]),
                 r=[xn, "rstd%d" % j], w=["hn%d" % j])

        def emit_T(i):
            W = TB * 128
            for kq in range(2):
                bank = kq
                for kk in range(4):
                    k = kq * 4 + kk
                    for j in range(TB):
                        P.op("pe", lambda e, k=k, kk=kk, j=j, bank=bank: e.transpose(
                            tr[bank][:, kk * W + j * 128:kk * W + (j + 1) * 128],
                            hn[:, j, k * 128:(k + 1) * 128], ident[:]),
                             r=["hn%d" % j, "ident"], w=["tr%d" % bank])
                for kk in range(4):
                    k = kq * 4 + kk
                    P.op("dve", lambda e, k=k, kk=kk, bank=bank: e.tensor_scalar(
                        out=hT[:, k, :], in0=tr[bank][:, kk * W:(kk + 1) * W], scalar1=gpre[:, k:k + 1],
                        scalar2=None, op0=ALU.mult),
                         r=["tr%d" % bank, "gpre"], w=["hT%d" % k])

        def rope(pjt, pjn, o, nh, j, so, T, tabs=None):
            q = rti[0] % 2
            rti[0] += 1
            ta, tb_, tc, td = rt[q]
            rn = ["rt%d_%d" % (q, x) for x in range(4)]
            xst, xsn = xs[q], "xs%d" % q
            if (CUT == 4.26 and nh == 2) or (CUT == 4.28 and tabs is not None):
                sres = "stg%d_%d" % (j, so)
                P.op("act", lambda e: e.activation(out=stg[:, j, so:so + nh * 64], in_=pjt[:, o:o + nh * 64],
                                                   func=AF.Copy), r=[pjn], w=[sres + "a"])
                return [sres + "a"]
            P.op("act", lambda e: e.activation(out=xst[:, 0:nh * 64], in_=pjt[:, o:o + nh * 64], func=AF.Copy),
                 r=[pjn], w=[xsn])
            pv = xst[:, 0:nh * 64].rearrange("p (h d) -> p h d", d=64)
            sv = stg[:, j, so:so + nh * 64].rearrange("p (h d) -> p h d", d=64)
            x1, x2 = pv[:, :, 0:8], pv[:, :, 8:16]
            ct_, st_, ctn, stn = (cosR, sinR, "cosR", "sinR") if tabs is None else tabs
            cs, sn = ct_[:, T, 0:nh, :], st_[:, T, 0:nh, :]
            sres = "stg%d_%d" % (j, so)
            if CUT < 3.06:
                return [sres + "a", sres + "b", sres + "c"]
            P.op("dve", lambda e: e.tensor_tensor(out=ta[:, 0:nh, :], in0=x1, in1=cs, op=ALU.mult),
                 r=[xsn, ctn], w=[rn[0]])
            if CUT < 3.07:
                return [sres + "a", sres + "b", sres + "c"]
            P.op("dve", lambda e: e.tensor_tensor(out=tb_[:, 0:nh, :], in0=x2, in1=sn, op=ALU.mult),
                 r=[xsn, stn], w=[rn[1]])
            if CUT < 3.08:
                return [sres + "a", sres + "b", sres + "c"]
            P.op("dve", lambda e: e.tensor_tensor(out=tc[:, 0:nh, :], in0=x2, in1=cs, op=ALU.mult),
                 r=[xsn, ctn], w=[rn[2]])
            if CUT == 3.095:
                P.op("dve", lambda e: e.tensor_tensor(out=tc[:, 0:nh, :], in0=x1, in1=sn, op=ALU.mult),
                     r=[xsn, "sinR"], w=[rn[2]])
                return [sres + "a", sres + "b", sres + "c"]
            P.op("dve", lambda e: e.tensor_tensor(out=td[:, 0:nh, :], in0=x1, in1=sn, op=ALU.mult),
                 r=[xsn, stn], w=[rn[3]])
            P.op("dve", lambda e: e.tensor_tensor(out=pv[:, :, 0:8], in0=ta[:, 0:nh, :], in1=tb_[:, 0:nh, :],
                                                  op=ALU.subtract),
                 r=[rn[0], rn[1], rn[2], rn[3], xsn], w=[xsn])
            P.op("dve", lambda e: e.tensor_tensor(out=pv[:, :, 8:16], in0=tc[:, 0:nh, :], in1=td[:, 0:nh, :],
                                                  op=ALU.add),
                 r=[rn[2], rn[3], xsn], w=[xsn])
            P.op("act", lambda e: e.activation(out=stg[:, j, so:so + nh * 64], in_=xst[:, 0:nh * 64], func=AF.Copy),
                 r=[xsn], w=[sres + "a"])
            return [sres + "a"]

        def emit_proj(i, j, stres):
            T = i * TB + j
            for ci, (c0, c1) in enumerate(CB):
                if (ci < 4) != (PH[0] == 0):
                    continue
                if ci >= 4:
                    c0, c1 = c0 - 2048, c1 - 2048
                q = pji[0] % 3
                pji[0] += 1
                pjt, pjn = pj[q], "pj%d" % q
                ncol = c1 - c0
                for k in range(8):
                    P.op("pe", lambda e, k=k, pjt=pjt, c0=c0, c1=c1, ncol=ncol: e.matmul(
                        pjt[:, 0:ncol], hT[:, k, j * 128:(j + 1) * 128], Win[:, k, c0:c1],
                        start=(k == 0), stop=(k == 7)),
                         r=["hT%d" % k, "Win%d" % ci], w=[pjn])
                if ci == 0:
                    stres["dq"][j] = rope(pjt, pjn, 0, 8, j, S_DQ, T)
                elif ci == 1:
                    stres["dk"][j] = rope(pjt, pjn, 0, 8, j, S_DK, T)
                elif ci == 2:
                    P.op("act", lambda e, pjt=pjt, T=T: e.activation(
                        out=dva[:, T, :, 0:128], in_=pjt[:, 0:512].rearrange("p (h d) -> p h d", d=128),
                        func=AF.Copy), r=[pjn], w=["dva%d" % T])
                elif ci == 3:
                    stres["nq"][j] = rope(pjt, pjn, 0, 8, j, S_NQ, T)
                elif ci == 4:
                    rr = rope(pjt, pjn, 0, 8, j, S_KS, T)
                    stres["ks"][j] = rr
                    stres["kw"][j] = rr
                    stres["kcvc"][j] = rr
                else:
                    P.op("act", lambda e, pjt=pjt, T=T: e.activation(
                        out=vsa[:, T, :, 0:64], in_=pjt[:, 0:128].rearrange("p (h d) -> p h d", d=64),
                        func=AF.Copy), r=[pjn], w=["vsa%d" % T])
                    P.op("act", lambda e, pjt=pjt, T=T: e.activation(
                        out=vwa[:, T, :, 0:64], in_=pjt[:, 128:256].rearrange("p (h d) -> p h d", d=64),
                        func=AF.Copy), r=[pjn], w=["vwa%d" % T])
                    P.op("dve", lambda e, pjt=pjt, T=T: e.tensor_copy(out=glr[:, T, :], in_=pjt[:, 256:280]),
                         r=[pjn], w=["glr%d" % T])

        def emit_QT(i, stres):
            tk0 = i * TB * 128
            W = TB * 128
            units = []
            for h in range(4):
                units.append((S_DQ + h * 128, 128, dqT[:, h, tk0:tk0 + W], "dqT", "dq"))
            for h in range(4):
                units.append((S_DK + h * 128, 128, dkT[:, h, tk0:tk0 + W], "dkT", "dk"))
            for n in range(8):
                units.append((S_NQ + n * 64, 64, Qg[n // 4][0:64, n % 4, tk0:tk0 + W], "Qq%d" % (n // 4), "nq"))
            for g in range(2):
                units.append((S_KS + g * 64, 64, KSg[g][0:64, tk0:tk0 + W], "KSq%d" % g, "ks"))
            for g in range(2):
                units.append((S_KW + g * 64, 64, KWg[g][0:64, tk0:tk0 + W], "KWq%d" % g, "kw"))
            units.append((S_KC, 128, kcT[:, tk0:tk0 + W], "kcT", "kcvc"))
            units.append((S_VC, 128, vcT[:, tk0:tk0 + W], "vcT", "kcvc"))
            units = units[0:16] if PH[0] == 0 else units[16:22]
            if CUT == 9:
                pass
            elif CUT == 4.23:
                units = [(S_NQ + g * 64, 64, KSg[g][0:64, tk0:tk0 + W], "KSq%d" % g, "nq") for g in range(2)]
            elif CUT == 4.24:
                units = [(S_KS + g * 64, 64, Qg[g][0:64, 0, tk0:tk0 + W], "Qq%d" % g, "ks") for g in range(2)]
            elif CUT in (4.21, 4.26, 4.28):
                units = units[16:18]
            elif CUT == 4.22:
                units = units[18:20]
            elif CUT < 4.1:
                units = units[0:8]
            elif CUT < 4.2:
                units = units[0:16]
            elif CUT < 4.3:
                units = units[0:20]
            for u0 in range(0, len(units), 4):
                bank = tqi[0] % 2
                tqi[0] += 1
                grp = units[u0:u0 + 4]
                for ui, (so, ncol, dstap, dres, skey) in enumerate(grp):
                    for j in range(TB):
                        P.op("pe", lambda e, ui=ui, j=j, so=so, ncol=ncol, bank=bank: e.transpose(
                            tq[bank][0:ncol, ui * W + j * 128:ui * W + (j + 1) * 128],
                            stg[:, j, so:so + ncol], ident[:]),
                             r=stres[skey][j] + ["ident"], w=["tq%d" % bank])
                eng = "act" if (evi[0] % 2 == 0 or EVAC_ACT) else "dve"
                evi[0] += 1
                for ui, (so, ncol, dstap, dres, skey) in enumerate(grp):
                    dres = "%s_u%d" % (dres, u0 + ui)
                    if eng == "act":
                        P.op("act", lambda e, ui=ui, ncol=ncol, dstap=dstap, bank=bank: e.activation(
                            out=dstap, in_=tq[bank][0:ncol, ui * W:(ui + 1) * W], func=AF.Copy),
                             r=["tq%d" % bank], w=["%s_%d" % (dres, i)])
                    else:
                        P.op("dve", lambda e, ui=ui, ncol=ncol, dstap=dstap, bank=bank: e.tensor_copy(
                            out=dstap, in_=tq[bank][0:ncol, ui * W:(ui + 1) * W]),
                             r=["tq%d" % bank], w=["%s_%d" % (dres, i)])

        nblk = NT // TB
        for ph in range(2):
            PH[0] = ph
            load_pass(ph)
            for j in range(TB):
                emit_N(0, j)
            emit_T(0)
            for i in range(nblk):
                stres = {k: [None] * TB for k in ("dq", "dk", "nq", "kcvc", "ks", "kw")}
                for j in range(TB):
                    emit_proj(i, j, stres)
                    if i + 1 < nblk:
                        emit_N(i + 1, j)
                emit_QT(i, stres)
                if i + 1 < nblk:
                    emit_T(i + 1)
        P.barrier_all()
        P.flush()


def mixer_compress(nc, P, b, dd, t):
    tag = "Z%d" % b
    kcT, vcT, kcc, vca = t["kcT"], t["vcT"], t["kcc"], t["vca"]
    NCB = 127
    with ExitStack() as st:
        def sb(name, shape, dt):
            return st.enter_context(nc.sbuf_tensor(tag + name, shape, dt))

        def ps(name, shape, dt):
            return st.enter_context(nc.psum_tensor(tag + name, shape, dt))

        W1 = [sb("W1_%d" % kv, [128, 32, 256], BF16) for kv in range(2)]
        W2 = [sb("W2_%d" % kv, [128, 2, 64], BF16) for kv in range(2)]
        peT = [sb("peT%d" % kv, [128, 32], BF16) for kv in range(2)]
        pb = sb("pb", [128, 4], F32)
        xh = sb("xh", [128, 2, 128], F32)
        u = sb("u", [128, 2, 128], F32)
        sg = sb("sg", [128, 2, 128], F32)
        hact = sb("hact", [128, 2, 128], BF16)
        pbps = ps("pbps", [128, 4], F32)
        hps = [ps("hps%d" % i, [128, 2, 128], F32) for i in range(2)]
        cps = ps("cps", [128, 128], F32)

        for kv, (w1n, w2n, pen) in enumerate((("ck_w1", "ck_w2", "ck_peT"), ("cv_w1", "cv_w2", "cv_peT"))):
            w1v = dd[w1n].rearrange("(l d) h -> d l h", d=64)
            for half in range(2):
                P.dma("pool", lambda e, kv=kv, half=half, w1v=w1v: e.dma_start(
                    out=W1[kv][half * 64:(half + 1) * 64, :, :], in_=w1v),
                      "%sW1_%d" % (tag, kv), w=["W1_%d" % kv])
            P.dma("pool", lambda e, kv=kv, w2n=w2n: e.dma_start(
                out=W2[kv][:], in_=dd[w2n].rearrange("(c p) d -> p c d", p=128)),
                  "%sW2_%d" % (tag, kv), w=["W2_%d" % kv])
            P.dma("pool", lambda e, kv=kv, pen=pen: e.dma_start(out=peT[kv][:], in_=dd[pen]),
                  "%spe_%d" % (tag, kv), w=["peT%d" % kv])
        for kv in range(2):
            for ch in range(2):
                col = kv * 2 + ch
                for l in range(32):
                    P.op("pe", lambda e, kv=kv, ch=ch, l=l, col=col: e.matmul(
                        pbps[:, col:col + 1], W1[kv][0:64, l, ch * 128:(ch + 1) * 128], peT[kv][0:64, l:l + 1],
                        start=(l == 0), stop=(l == 31)),
                         r=["W1_%d" % kv, "peT%d" % kv], w=["pbps"])
        P.op("dve", lambda e: e.tensor_copy(out=pb[:], in_=pbps[:]), r=["pbps"], w=["pb"])
        hi = [0]
        for kv in range(2):
            xT = kcT if kv == 0 else vcT
            for g in range(2):
                hp = hps[hi[0] % 2]
                hpn = "hps%d" % (hi[0] % 2)
                hi[0] += 1
                for ch in range(2):
                    for l in range(32):
                        P.op("pe", lambda e, kv=kv, g=g, ch=ch, l=l, hp=hp, xT=xT: e.matmul(
                            hp[:, ch, 0:NCB], W1[kv][g * 64:(g + 1) * 64, l, ch * 128:(ch + 1) * 128],
                            xT[g * 64:(g + 1) * 64, l:l + 16 * (NCB - 1) + 1:16],
                            start=(l == 0), stop=(l == 31)),
                             r=["W1_%d" % kv], w=[hpn])
                for ch in range(2):
                    col = kv * 2 + ch
                    P.op("act", lambda e, ch=ch, col=col, hp=hp: e.activation(
                        out=xh[:, ch, 0:NCB], in_=hp[:, ch, 0:NCB], func=AF.Identity, bias=pb[:, col:col + 1]),
                         r=[hpn, "pb"], w=["xh%d" % ch])
                X, U, SG, HA = xh[:, :, 0:NCB], u[:, :, 0:NCB], sg[:, :, 0:NCB], hact[:, :, 0:NCB]
                P.op("dve", lambda e, X=X, U=U: e.tensor_tensor(out=U, in0=X, in1=X, op=ALU.mult),
                     r=["xh0", "xh1"], w=["u"])
                P.op("dve", lambda e, U=U: e.tensor_scalar(out=U, in0=U, scalar1=0.044715, scalar2=1.0,
                                                        op0=ALU.mult, op1=ALU.add), r=["u"], w=["u"])
                P.op("dve", lambda e, X=X, U=U: e.tensor_tensor(out=U, in0=U, in1=X, op=ALU.mult),
                     r=["u", "xh0", "xh1"], w=["u"])
                P.op("act", lambda e, U=U, SG=SG: e.activation(out=SG, in_=U, func=AF.Sigmoid,
                                                              scale=1.5957691216057308),
                     r=["u"], w=["sg"])
                P.op("dve", lambda e, X=X, SG=SG, HA=HA: e.tensor_tensor(out=HA, in0=X, in1=SG, op=ALU.mult),
                     r=["sg", "xh0", "xh1"], w=["hact"])
                if kv == 0:
                    for ch in range(2):
                        P.op("pe", lambda e, ch=ch: e.matmul(cps[0:64, 0:NCB], W2[0][:, ch, :], hact[:, ch, 0:NCB],
                                                            start=(ch == 0), stop=(ch == 1)),
                             r=["hact", "W2_0"], w=["cps"])
                    P.op("act", lambda e, g=g: e.activation(out=kcc[g][0:64, 0:NCB], in_=cps[0:64, 0:NCB],
                                                           func=AF.Copy), r=["cps"], w=["kcc%d" % g])
                else:
                    for ch in range(2):
                        P.op("pe", lambda e, ch=ch: e.matmul(cps[0:NCB, 0:64], hact[:, ch, 0:NCB], W2[1][:, ch, :],
                                                            start=(ch == 0), stop=(ch == 1)),
                             r=["hact", "W2_1"], w=["cps"])
                    P.op("act", lambda e, g=g: e.activation(out=vca[0:NCB, g, 0:64], in_=cps[0:NCB, 0:64],
                                                           func=AF.Copy), r=["cps"], w=["vca_v%d" % g])
        P.barrier_all()
        P.flush()


class Pipe:
    def __init__(self, lag=2):
        self.q, self.lag = [], lag

    def push(self, first, rest):
        if first is not None:
            first()
        self.q.append(rest)
        while len(self.q) > self.lag:
            self.q.pop(0)()

    def drain(self):
        while self.q:
            self.q.pop(0)()


def mixer_attn(nc, P, b, src, dst, dd, t):
    tag = "T%d" % b
    dqT, dkT, Qg, KSg, KWg = t["dqT"], t["dkT"], t["Qg"], t["KSg"], t["KWg"]
    dva, vsa, vwa, glr, kcc, vca, ident, epsb = (t["dva"], t["vsa"], t["vwa"], t["glr"], t["kcc"], t["vca"],
                                                 t["ident"], t["epsb"])
    with ExitStack() as st:
        def sb(name, shape, dt):
            return st.enter_context(nc.sbuf_tensor(tag + name, shape, dt))

        def ps(name, shape, dt):
            return st.enter_context(nc.psum_tensor(tag + name, shape, dt))

        om = sb("om", [128, NT, 1024], BF16)
        tmpf = [sb("tmpf%d" % i, [128, 64], F32) for i in range(4)]
        Wo = sb("Wo", [128, 8, 1024], BF16)
        pts = [sb("p%d" % i, [128, 512], BF16) for i in range(4)]
        gates = sb("gates", [128, NT, 24], F32)
        Am = sb("Am", [128, NT, 32], F32)
        Bm = sb("Bm", [128, NT, 32], F32)
        Gs = sb("Gs", [128, 128], F32)
        Gm = sb("Gm", [128, D], F32)
        lv = [sb("lv%d" % i, [128, 64], F32) for i in range(4)]
        lt = sb("lt", [128, 64], F32)
        le = sb("le", [128, 2], F32)
        neglam = sb("neglam", [128, 1], F32)
        o1 = [sb("o1_%d" % i, [128, 4, 128], F32) for i in range(2)]
        junk = sb("junk", [128, 128], F32)
        sm = [dict((n, sb("%s_%d" % (n, i), [128, w], F32)) for n, w in
                   (("rd", 4), ("nl", 4), ("ss4", 4), ("ln4", 4), ("r4", 4), ("den", 4), ("gr", 4), ("rs", 4),
                    ("rw", 4), ("imp", 32), ("impm", 32), ("wk", 32), ("m1", 8), ("m2", 8))) for i in range(2)]
        nsp = [sb("nsp%d" % i, [128, 96], BF16) for i in range(2)]
        omT = [sb("omT%d" % i, [128, 8, 128], BF16) for i in range(2)]
        xres = [sb("xres%d" % i, [128, D], F32) for i in range(2)]
        ytmp = [sb("ytmp%d" % i, [128, D], F32) for i in range(2)]
        ss2 = sb("ss2", [128, 2], F32)
        rstd2 = sb("rstd2", [128, 2], F32)
        sps = [ps("sps%d" % i, [128, 512], F32) for i in range(3)]
        ab = ps("ab", [128, 4, 512], F32)
        tps = ps("tps", [128, 1024], BF16)
        SPS = Rot(sps, ["sps%d" % i for i in range(3)])
        PT = Rot(pts, ["p%d" % i for i in range(4)])

        for nm, tl in (("Am", Am), ("Bm", Bm), ("Gs", Gs), ("Gm", Gm)):
            P.dma("sp", lambda e, nm=nm, tl=tl: e.dma_start(out=tl[:], in_=dd[nm]), tag + nm, w=[nm])
        for i, nm in enumerate(("lq1", "lk1", "lq2", "lk2")):
            P.dma("sp", lambda e, i=i, nm=nm: e.dma_start(out=lv[i][:], in_=dd[nm]), tag + nm, w=["lv%d" % i])
        wo_v = dd["w_out"].rearrange("(k p) f -> p k f", p=128)
        for kh in range(2):
            P.dma("pool", lambda e, kh=kh: e.dma_start(out=Wo[:, kh * 4:(kh + 1) * 4, :],
                                                       in_=wo_v[:, kh * 4:(kh + 1) * 4, :]),
                  tag + "Wo", w=["Wo"])
        for nm in ("nsp0", "nsp1"):
            pass
        P.op("pool", lambda e: e.memset(nsp[0][:], 0.0), w=["nsp0"])
        P.op("pool", lambda e: e.memset(nsp[1][:], 0.0), w=["nsp1"])
        P.op("dve", lambda e: e.tensor_scalar(out=Gs[:], in0=Gs[:], scalar1=0.8, scalar2=None, op0=ALU.mult),
             r=["Gs"], w=["Gs"])
        P.op("act", lambda e: e.activation(out=gates[:], in_=glr[:], func=AF.Sigmoid), w=["gates"])
        for i in range(2):
            P.op("dve", lambda e, i=i: e.tensor_tensor(out=lt[:], in0=lv[2 * i][:], in1=lv[2 * i + 1][:],
                                                       op=ALU.mult),
                 r=["lv%d" % (2 * i), "lv%d" % (2 * i + 1)], w=["lt"])
            P.op("dve", lambda e, i=i: e.reduce_sum(out=le[:, i:i + 1], in_=lt[:], axis=AX.X),
                 r=["lt"], w=["le%d" % i])
        P.op("act", lambda e: e.activation(out=le[:], in_=le[:], func=AF.Exp), r=["le0", "le1"], w=["le0", "le1"])
        P.op("dve", lambda e: e.tensor_tensor(out=neglam[:], in0=le[:, 1:2], in1=le[:, 0:1], op=ALU.subtract),
             r=["le0", "le1"], w=["neglam"])
        P.op("dve", lambda e: e.tensor_scalar(out=neglam[:], in0=neglam[:], scalar1=-0.2, scalar2=None,
                                              op0=ALU.add), r=["neglam"], w=["neglam"])

        pipe = Pipe(2)
        first_in_bank = {}

        def acc_mm(bank_i, col0, ncol, lhsT, rhs, last, reads):
            bn = "ab%d" % bank_i
            first = first_in_bank.get(bn, True)
            first_in_bank[bn] = False
            P.op("pe", lambda e: e.matmul(ab[:, bank_i, col0:col0 + ncol], lhsT, rhs, start=first, stop=last,
                                          skip_group_check=True),
                 r=reads, w=[bn])

        def exp_tile(spt, spn, rows, masks):
            p, pn = PT.next()
            P.op("act", lambda e: e.activation(out=p[0:rows, :], in_=spt[0:rows, :], func=AF.Exp, scale=0.125),
                 r=[spn], w=[pn])
            for (pattern, base, cm) in masks:
                P.op("pool", lambda e, pattern=pattern, base=base, cm=cm: e.affine_select(
                    out=p[0:rows, :], in_=p[0:rows, :], pattern=pattern, compare_op=ALU.is_ge, fill=0.0,
                    base=base, channel_multiplier=cm), r=[pn], w=[pn])
            return p, pn

        ci = 0
        for h in range(4):
            for qb in range(4):
                ob, obn = o1[(h * 4 + qb) % 2], "o1_%d" % ((h * 4 + qb) % 2)
                smx = sm[(h * 4 + qb) % 2]
                sfx = "_%d" % ((h * 4 + qb) % 2)
                for m in range(2):
                    bA, bB = 2 * (ci % 2), 2 * (ci % 2) + 1
                    ci += 1
                    first_in_bank["ab%d" % bA] = True
                    first_in_bank["ab%d" % bB] = True
                    nkt = 4 * qb + 4
                    for kt in range(nkt):
                        spt, spn = SPS.next()

                        def qk(spt=spt, spn=spn, kt=kt, m=m, h=h, qb=qb):
                            P.op("pe", lambda e: e.matmul(
                                spt[:], dkT[m * 64:(m + 1) * 64, h, kt * 128:(kt + 1) * 128],
                                dqT[m * 64:(m + 1) * 64, h, qb * 512:(qb + 1) * 512], start=True, stop=True),
                                 w=[spn])

                        def rest(spt=spt, spn=spn, kt=kt, h=h, qb=qb, bA=bA, bB=bB):
                            masks = []
                            if kt >= 4 * qb:
                                masks.append(([[1, 512]], qb * 512 - kt * 128, -1))
                            p, pn = exp_tile(spt, spn, 128, masks)
                            for jq in range(4):
                                if kt > 4 * qb + jq:
                                    continue
                                bank_i, col0 = (bA, jq * 130) if jq < 3 else (bB, 0)
                                acc_mm(bank_i, col0, 129, p[:, jq * 128:(jq + 1) * 128], dva[:, kt, h, 0:129],
                                       kt == 4 * qb + jq, [pn])

                        pipe.push(qk, rest)

                    def post(m=m, h=h, qb=qb, bA=bA, bB=bB, ob=ob, obn=obn, smx=smx, sfx=sfx):
                        bnA, bnB = "ab%d" % bA, "ab%d" % bB
                        rd = smx["rd"]
                        denA = ab[:, bA, 0:390].rearrange("p (j c) -> p j c", c=130)[:, :, 128]
                        P.op("dve", lambda e: e.reciprocal(out=rd[:, 0:3], in_=denA), r=[bnA], w=["rdA" + sfx])
                        P.op("dve", lambda e: e.reciprocal(out=rd[:, 3:4], in_=ab[:, bB, 128:129]), r=[bnB],
                             w=["rdB" + sfx])

                        def region(jq):
                            return (ab[:, bA, jq * 130:jq * 130 + 128], bnA) if jq < 3 else (ab[:, bB, 0:128], bnB)

                        if m == 0:
                            for jq in range(4):
                                reg, bn = region(jq)
                                P.op("act", lambda e, reg=reg, jq=jq: e.activation(
                                    out=ob[:, jq, :], in_=reg, func=AF.Copy, scale=rd[:, jq:jq + 1]),
                                     r=[bn, "rdA" + sfx, "rdB" + sfx], w=[obn + "_%d" % jq])
                        else:
                            nl, ss4, ln4, r4 = smx["nl"], smx["ss4"], smx["ln4"], smx["r4"]
                            P.op("dve", lambda e: e.tensor_scalar(out=nl[:], in0=rd[:], scalar1=neglam[:, 0:1],
                                                                  scalar2=None, op0=ALU.mult),
                                 r=["rdA" + sfx, "rdB" + sfx, "neglam"], w=["nl" + sfx])
                            for jq in range(4):
                                reg, bn = region(jq)
                                P.op("dve", lambda e, reg=reg, jq=jq: e.scalar_tensor_tensor(
                                    out=ob[:, jq, :], in0=reg, scalar=nl[:, jq:jq + 1], in1=ob[:, jq, :],
                                    op0=ALU.mult, op1=ALU.add),
                                     r=[bn, "nl" + sfx, obn + "_%d" % jq], w=[obn + "_%d" % jq])
                                P.op("act", lambda e, jq=jq: e.activation(
                                    out=junk[:], in_=ob[:, jq, :], func=AF.Square, accum_out=ss4[:, jq:jq + 1]),
                                     r=[obn + "_%d" % jq], w=["junk", "ss4%s_%d" % (sfx, jq)])
                            ssr = ["ss4%s_%d" % (sfx, jq) for jq in range(4)]
                            P.op("act", lambda e: e.activation(out=ln4[:], in_=ss4[:], func=AF.Ln, scale=1.0 / 128,
                                                               bias=epsb[:, 0:1]), r=ssr + ["epsb"], w=["ln4" + sfx])
                            P.op("act", lambda e: e.activation(out=r4[:], in_=ln4[:], func=AF.Exp, scale=-0.5),
                                 r=["ln4" + sfx], w=["r4" + sfx])
                            for jq in range(4):
                                T = 4 * qb + jq
                                P.op("dve", lambda e, jq=jq, T=T: e.scalar_tensor_tensor(
                                    out=om[:, T, h * 128:(h + 1) * 128], in0=ob[:, jq, :], scalar=r4[:, jq:jq + 1],
                                    in1=Gs[:], op0=ALU.mult, op1=ALU.mult),
                                     r=[obn + "_%d" % jq, "r4" + sfx, "Gs"], w=["om%d_d%d" % (T, h)])

                    pipe.push(None, post)
        pipe.drain()

        def stage1(g, T):
            if True:
                ci = g * NT + T
                bk = 2 + (ci % 2)
                smx = sm[ci % 2]
                sfx = "_%d" % (ci % 2)
                nspt, nspn = nsp[ci % 2], "nsp%d" % (ci % 2)
                spt, spn = SPS.next()

                def qk(spt=spt, spn=spn, g=g, T=T):
                    P.op("pe", lambda e: e.matmul(spt[0:127, :], kcc[g][0:64, 0:127],
                                                  Qg[g][0:64, :, T * 128:(T + 1) * 128], start=True, stop=True),
                         w=[spn])

                def rest(spt=spt, spn=spn, g=g, T=T, bk=bk):
                    p, pn = exp_tile(spt, spn, 127, [([[0, 4], [1, 128]], T * 128 - 31, -16)])
                    for hg in range(4):
                        P.op("pe", lambda e, hg=hg: e.matmul(ab[:, bk, hg * 128:hg * 128 + 97],
                                                             p[0:127, hg * 128:(hg + 1) * 128], vca[0:127, g, 0:97],
                                                             start=True, stop=True, skip_group_check=True),
                             r=[pn], w=["ab%d" % bk])

                def post(g=g, T=T, bk=bk, smx=smx, sfx=sfx, nspt=nspt, nspn=nspn):
                    bn = "ab%d" % bk
                    den, rd, gr, imp, impm, wk, m1, m2 = (smx["den"], smx["rd"], smx["gr"], smx["imp"],
                                                          smx["impm"], smx["wk"], smx["m1"], smx["m2"])
                    ov4 = ab[:, bk, :].rearrange("p (h c) -> p h c", c=128)
                    gv = gates[:, T, :].rearrange("p (h c) -> p h c", c=3)
                    P.op("dve", lambda e: e.tensor_scalar(out=den[:], in0=ov4[:, :, 64], scalar1=1e-30, scalar2=None,
                                                          op0=ALU.max), r=[bn], w=["den" + sfx])
                    P.op("dve", lambda e: e.reciprocal(out=rd[:], in_=den[:]), r=["den" + sfx], w=["rd" + sfx])
                    P.op("dve", lambda e: e.tensor_tensor(out=gr[:], in0=rd[:], in1=gv[:, g * 4:(g + 1) * 4, 0],
                                                          op=ALU.mult), r=["rd" + sfx, "gates"], w=["gr" + sfx])
                    for hg in range(4):
                        hd = g * 4 + hg
                        P.op("act", lambda e, hg=hg, hd=hd: e.activation(
                            out=om[:, T, 512 + hd * 64:512 + (hd + 1) * 64], in_=ab[:, bk, hg * 128:hg * 128 + 64],
                            func=AF.Copy, scale=gr[:, hg:hg + 1]), r=[bn, "gr" + sfx], w=["om%d_n%d" % (T, hd)])
                    P.op("dve", lambda e: e.tensor_scalar(out=imp[:], in0=ab[:, bk, 65:97], scalar1=rd[:, 0:1],
                                                          scalar2=None, op0=ALU.mult),
                         r=[bn, "rd" + sfx], w=["imp" + sfx])
                    for hg in range(1, 4):
                        P.op("dve", lambda e, hg=hg: e.scalar_tensor_tensor(
                            out=imp[:], in0=ab[:, bk, hg * 128 + 65:hg * 128 + 97], scalar=rd[:, hg:hg + 1],
                            in1=imp[:], op0=ALU.mult, op1=ALU.add), r=[bn, "rd" + sfx, "imp" + sfx],
                             w=["imp" + sfx])
                    P.op("dve", lambda e: e.tensor_tensor(out=impm[:], in0=imp[:], in1=Am[:, T, :], op=ALU.mult),
                         r=["imp" + sfx, "Am"], w=["impm" + sfx])
                    P.op("dve", lambda e: e.tensor_tensor(out=impm[:], in0=impm[:], in1=Bm[:, T, :], op=ALU.add),
                         r=["impm" + sfx, "Bm"], w=["impm" + sfx])
                    P.op("dve", lambda e: e.max(out=m1[:], in_=impm[:]), r=["impm" + sfx], w=["m1" + sfx])
                    P.op("dve", lambda e: e.match_replace(out=wk[:], in_to_replace=m1[:], in_values=impm[:],
                                                          imm_value=-2.0),
                         r=["impm" + sfx, "m1" + sfx], w=["wk" + sfx])
                    P.op("dve", lambda e: e.max(out=m2[:], in_=wk[:]), r=["wk" + sfx], w=["m2" + sfx])
                    P.op("dve", lambda e: e.tensor_scalar(out=nspt[:, 64:96], in0=impm[:], scalar1=m2[:, 7:8],
                                                          scalar2=-BIG, op0=ALU.is_lt, op1=ALU.mult),
                         r=["impm" + sfx, "m2" + sfx], w=[nspn])
                    P.op("pe", lambda e: e.transpose(tps[0:96, 0:128], nspt[:, 0:96], ident[:]),
                         r=[nspn, "ident"], w=["tps"])
                    tsl = slice(T * 128, (T + 1) * 128)
                    P.op("act", lambda e: e.activation(out=Qg[g][64:96, 0, tsl], in_=tps[64:96, 0:128],
                                                       func=AF.Copy), r=["tps"], w=["Qs%d_%d" % (g, T)])
                    for hg in range(1, 4):
                        P.op("pool", lambda e, hg=hg: e.tensor_copy(out=Qg[g][64:96, hg, tsl],
                                                                    in_=Qg[g][64:96, 0, tsl]),
                             r=["Qs%d_%d" % (g, T)], w=["Qs%d_%d_%d" % (g, T, hg)])

                pipe.push(qk, rest)
                pipe.push(None, post)

        def stage2(g, T):
            if True:
                ci = g * NT + T
                bS, bW = 0, 1
                smx = sm[ci % 2]
                sfx = "_%d" % (ci % 2)
                selr = ["Qs%d_%d" % (g, T)] + ["Qs%d_%d_%d" % (g, T, hg) for hg in range(1, 4)]
                jobs = [("s", kt) for kt in range(T + 1)] + [("w", kt) for kt in range(max(0, T - 4), T + 1)]
                for kind, kt in jobs:
                    spt, spn = SPS.next()

                    def qk(spt=spt, spn=spn, kind=kind, kt=kt, g=g, T=T, selr=selr):
                        ksl = slice(kt * 128, (kt + 1) * 128)
                        tsl = slice(T * 128, (T + 1) * 128)
                        if kind == "s":
                            P.op("pe", lambda e: e.matmul(spt[:], KSg[g][0:96, ksl], Qg[g][0:96, :, tsl],
                                                          start=True, stop=True), r=selr, w=[spn])
                        else:
                            P.op("pe", lambda e: e.matmul(spt[:], KWg[g][0:64, ksl], Qg[g][0:64, :, tsl],
                                                          start=True, stop=True), w=[spn])

                    def rest(spt=spt, spn=spn, kind=kind, kt=kt, g=g, T=T, bS=bS, bW=bW):
                        masks = []
                        if kt == T:
                            masks.append(([[0, 4], [1, 128]], 0, -1))
                        if kind == "w" and kt == T - 4:
                            masks.append(([[0, 4], [-1, 128]], -1, 1))
                        p, pn = exp_tile(spt, spn, 128, masks)
                        va = vsa if kind == "s" else vwa
                        bank_i = bS if kind == "s" else bW
                        if kt == (0 if kind == "s" else max(0, T - 4)):
                            first_in_bank["ab%d" % bank_i] = True
                        for hg in range(4):
                            acc_mm(bank_i, hg * 128, 65, p[:, hg * 128:(hg + 1) * 128], va[:, kt, g, 0:65],
                                   kt == T, [pn])

                    pipe.push(qk, rest)

                def post(g=g, T=T, bS=bS, bW=bW, smx=smx, sfx=sfx):
                    bnS, bnW = "ab%d" % bS, "ab%d" % bW
                    rs, rw = smx["rs"], smx["rw"]
                    gv = gates[:, T, :].rearrange("p (h c) -> p h c", c=3)
                    oS = ab[:, bS, :].rearrange("p (h c) -> p h c", c=128)
                    oW = ab[:, bW, :].rearrange("p (h c) -> p h c", c=128)
                    P.op("dve", lambda e: e.reciprocal(out=rs[:], in_=oS[:, :, 64]), r=[bnS], w=["rs" + sfx])
                    P.op("dve", lambda e: e.tensor_tensor(out=rs[:], in0=rs[:], in1=gv[:, g * 4:(g + 1) * 4, 1],
                                                          op=ALU.mult), r=["rs" + sfx, "gates"], w=["rs" + sfx])
                    P.op("dve", lambda e: e.reciprocal(out=rw[:], in_=oW[:, :, 64]), r=[bnW], w=["rw" + sfx])
                    P.op("dve", lambda e: e.tensor_tensor(out=rw[:], in0=rw[:], in1=gv[:, g * 4:(g + 1) * 4, 2],
                                                          op=ALU.mult), r=["rw" + sfx, "gates"], w=["rw" + sfx])
                    for hg in range(4):
                        hd = g * 4 + hg
                        on = "om%d_n%d" % (T, hd)
                        osl = om[:, T, 512 + hd * 64:512 + (hd + 1) * 64]
                        tf, tfn = tmpf[hg], "tmpf%d" % hg
                        P.op("dve", lambda e, hg=hg, osl=osl, tf=tf: e.scalar_tensor_tensor(
                            out=tf[:], in0=ab[:, bS, hg * 128:hg * 128 + 64], scalar=rs[:, hg:hg + 1], in1=osl,
                            op0=ALU.mult, op1=ALU.add), r=[bnS, "rs" + sfx, on], w=[tfn])
                    for hg in range(4):
                        hd = g * 4 + hg
                        on = "om%d_n%d" % (T, hd)
                        osl = om[:, T, 512 + hd * 64:512 + (hd + 1) * 64]
                        tf, tfn = tmpf[hg], "tmpf%d" % hg
                        P.op("dve", lambda e, hg=hg, osl=osl, tf=tf: e.scalar_tensor_tensor(
                            out=osl, in0=ab[:, bW, hg * 128:hg * 128 + 64], scalar=rw[:, hg:hg + 1], in1=tf[:],
                            op0=ALU.mult, op1=ALU.add), r=[bnW, "rw" + sfx, tfn], w=[on])

                pipe.push(None, post)

        for g in range(2):
            stage1(g, 0)
            stage1(g, 1)
            for T in range(NT):
                stage2(g, T)
                if T + 2 < NT:
                    stage1(g, T + 2)
        pipe.drain()

        for T in range(NT):
            q = T % 2
            t0 = T * 128
            omr = ["om%d_d%d" % (T, h) for h in range(4)] + ["om%d_n%d" % (T, hd) for hd in range(8)]
            for k in range(8):
                P.op("pe", lambda e, k=k, T=T: e.transpose(tps[:, k * 128:(k + 1) * 128],
                                                          om[:, T, k * 128:(k + 1) * 128], ident[:]),
                     r=omr + ["ident"], w=["tps"])
            oT, oTn = omT[q], "omT%d" % q
            P.op("act", lambda e, oT=oT: e.activation(out=oT[:].rearrange("p k t -> p (k t)"), in_=tps[:],
                                                      func=AF.Copy), r=["tps"], w=[oTn])
            b0 = 2 * q
            for n in range(2):
                for k in range(8):
                    P.op("pe", lambda e, n=n, k=k, oT=oT, b0=b0: e.matmul(
                        ab[:, b0 + n, :], oT[:, k, :], Wo[:, k, n * 512:(n + 1) * 512], start=(k == 0),
                        stop=(k == 7)), r=[oTn, "Wo"], w=["ab%d" % (b0 + n)])
            xr, yt = xres[q], ytmp[q]
            xrn, ytn = "xres%d" % q, "ytmp%d" % q
            wo2 = ab[:, b0:b0 + 2, :]
            br = ["ab%d" % b0, "ab%d" % (b0 + 1)]
            P.dma("act", lambda e, xr=xr, t0=t0: e.dma_start(out=xr[:], in_=src[b, t0:t0 + 128, :]),
                  tag + xrn + "i", w=[xrn])
            P.op("act", lambda e, yt=yt, q=q, wo2=wo2: e.activation(
                out=yt[:].rearrange("p (n f) -> p n f", n=2), in_=wo2, func=AF.Square, accum_out=ss2[:, q:q + 1]),
                 r=br, w=[ytn, "ss2%d" % q])
            P.op("act", lambda e, q=q: e.activation(out=ss2[:, q:q + 1], in_=ss2[:, q:q + 1], func=AF.Sqrt,
                                                    scale=1.0 / D, bias=epsb[:, 0:1]),
                 r=["ss2%d" % q, "epsb"], w=["ss2%d" % q])
            P.op("dve", lambda e, q=q: e.reciprocal(out=rstd2[:, q:q + 1], in_=ss2[:, q:q + 1]),
                 r=["ss2%d" % q], w=["rstd2%d" % q])
            P.op("dve", lambda e, yt=yt, q=q, wo2=wo2: e.scalar_tensor_tensor(
                out=yt[:].rearrange("p (n f) -> p n f", n=2), in0=wo2, scalar=rstd2[:, q:q + 1],
                in1=Gm[:].rearrange("p (n f) -> p n f", n=2), op0=ALU.mult, op1=ALU.mult),
                 r=br + ["rstd2%d" % q, "Gm"], w=[ytn])
            P.op("pool", lambda e, xr=xr, yt=yt: e.tensor_tensor(out=xr[:], in0=xr[:], in1=yt[:], op=ALU.add),
                 r=[ytn, xrn], w=[xrn])
            P.dma("sp", lambda e, xr=xr, t0=t0: e.dma_start(out=dst[b, t0:t0 + 128, :], in_=xr[:]),
                  tag + xrn + "o", r=[xrn], w=["dst_B_%d_%d" % (b, t0)])
        P.barrier_all()
        P.flush()


def build(stage=3):
    nc = bass.Bass("TRN2", target_bir_lowering=False)

    def dt(n, s, d=F32, k="ExternalInput"):
        return nc.dram_tensor(n, s, d, kind=k).ap()

    x = dt("x", [NB, S, D])
    out = dt("out", [NB, S, D], F32, "ExternalOutput")
    ident_d = dt("ident", [128, 128], BF16)
    f = {}
    for t in ("f1", "f2"):
        f[t] = dict(wg=dt(t + "_wg", [D, DFF]), wu=dt(t + "_wu", [D, DFF]), wd=dt(t + "_wd", [DFF, D]),
                    gpre=dt(t + "_gpre", [128, 8]), gpost=dt(t + "_gpost", [128, D]))
    dd = dict(ident=ident_d,
              w_in=dt("w_in", [D, C_END]), w_out=dt("w_out", [D, D]), m_gpre=dt("m_gpre", [128, 8]),
              Gm=dt("Gm", [128, D]), Gs=dt("Gs", [128, 128]),
              cosR=dt("cosR", [128, NT, 8, 8]), sinR=dt("sinR", [128, NT, 8, 8]),
              cosR2=dt("cosR2", [128, NT, 8, 8]), sinR2=dt("sinR2", [128, NT, 8, 8]),
              ov=dt("ov", [127, 32], BF16), ET=dt("ET", [32, S], BF16),
              Am=dt("Am", [128, NT, 32]), Bm=dt("Bm", [128, NT, 32]),
              ck_w1=dt("ck_w1", [2048, 256]), ck_w2=dt("ck_w2", [256, 64]), ck_peT=dt("ck_peT", [128, 32]),
              cv_w1=dt("cv_w1", [2048, 256]), cv_w2=dt("cv_w2", [256, 64]), cv_peT=dt("cv_peT", [128, 32]),
              lq1=dt("lq1", [128, 64]), lk1=dt("lk1", [128, 64]), lq2=dt("lq2", [128, 64]),
              lk2=dt("lk2", [128, 64]))
    x1 = nc.dram_tensor("x1s", [NB, S, D], F32).ap()
    x2 = nc.dram_tensor("x2s", [NB, S, D], F32).ap()
    with ExitStack() as stack:
        P = Prog(nc, stack)
        fa = f["f1"]
        ffn_phase(nc, P, "A", x, out if stage == 1 else x1, fa["wg"], fa["wu"], fa["wd"], fa["gpre"],
                  fa["gpost"], ident_d)
        if stage >= 2:
            mixer_phase(nc, P, x1, out if stage == 2 else x2, dd)
        if stage >= 3:
            fc = f["f2"]
            ffn_phase(nc, P, "C", x2, out, fc["wg"], fc["wu"], fc["wd"], fc["gpre"], fc["gpost"], ident_d)
    return nc


def host_inputs(inp):
    def g(k):
        return np.ascontiguousarray(np.asarray(inp[k], dtype=np.float32))

    bf = ml_dtypes.bfloat16

    def bc(v, n=128):
        return np.ascontiguousarray(np.broadcast_to(v[None, :], (n, v.shape[0])))

    common = {"ident": np.eye(128, dtype=np.float32).astype(bf)}
    for t, pfx in (("f1", "ff1"), ("f2", "ff2")):
        common[t + "_wg"] = g(pfx + "_w_gate")[0]
        common[t + "_wu"] = g(pfx + "_w_up")[0]
        common[t + "_wd"] = g(pfx + "_w_down")[0]
        common[t + "_gpre"] = np.ascontiguousarray(g(pfx + "_norm_pre")[0].reshape(8, 128).T)
        common[t + "_gpost"] = bc(g(pfx + "_norm_post")[0])
    common["w_in"] = g("w_in")[0]
    common["w_out"] = g("w_out")[0]
    common["m_gpre"] = np.ascontiguousarray(g("mix_norm_pre")[0].reshape(8, 128).T)
    common["Gm"] = bc(g("mix_norm_post")[0])
    common["Gs"] = bc(g("diff_subln")[0])
    for k, n in (("lq1", "lambda_q1"), ("lk1", "lambda_k1"), ("lq2", "lambda_q2"), ("lk2", "lambda_k2")):
        common[k] = bc(g(n)[0])
    for kv, w1n, w2n, pen in (("ck", "cmp_k_w1", "cmp_k_w2", "cmp_pe_k"), ("cv", "cmp_v_w1", "cmp_v_w2", "cmp_pe_v")):
        common[kv + "_w1"] = g(w1n)[0]
        common[kv + "_w2"] = g(w2n)[0]
        peT = g(pen)[0].T
        common[kv + "_peT"] = np.ascontiguousarray(np.concatenate([peT, peT], axis=0))
    pos = np.arange(S, dtype=np.float32)
    inv = (np.float32(500000.0) ** (-np.arange(0, 16, 2, dtype=np.float32) / np.float32(16))).astype(np.float32)
    ang = (pos[:, None] * inv[None, :]).astype(np.float32)
    cs, sn = np.cos(ang).astype(np.float32), np.sin(ang).astype(np.float32)

    def tab(a):
        a = a.reshape(NT, 128, 8).transpose(1, 0, 2)
        return np.ascontiguousarray(np.broadcast_to(a[:, :, None, :], (128, NT, 8, 8)))

    common["cosR"], common["sinR"] = tab(cs), tab(sn)
    c2, s2 = tab(cs).copy(), tab(sn).copy()
    c2[:, :, 4:8, :] = 1.0
    s2[:, :, 4:8, :] = 0.0
    common["cosR2"], common["sinR2"] = c2, s2
    c = np.arange(127)[:, None] * 16
    j = np.arange(32)[None, :] * 64
    ov = np.clip(np.minimum(c + 32, j + 64) - np.maximum(c, j), 0, None) / 32.0
    common["ov"] = ov.astype(np.float32).astype(bf)
    common["ET"] = (np.arange(S)[None, :] // 64 == np.arange(32)[:, None]).astype(np.float32).astype(bf)
    tt = np.arange(S)
    cur = (tt // 64)[:, None]
    blk = np.arange(32)[None, :]
    forced = (blk == 0) | ((blk <= cur) & (blk >= cur - 1))
    causal = blk <= cur
    A = (~forced & causal).astype(np.float32)
    Bc = np.where(forced, np.float32(1e9), np.where(causal, np.float32(0.0), np.float32(-1.0))).astype(np.float32)
    common["Am"] = np.ascontiguousarray(A.reshape(NT, 128, 32).transpose(1, 0, 2))
    common["Bm"] = np.ascontiguousarray(Bc.reshape(NT, 128, 32).transpose(1, 0, 2))
    x = g("x")
    maps = []
    for c_ in range(8):
        m = dict(common)
        m["x"] = x[c_ * NB:(c_ + 1) * NB]
        maps.append(m)
    return maps


def kernel(**inputs):
    nc = build(3)
    maps = host_inputs(inputs)
    res = run_bass_kernel_spmd(nc, maps, core_ids=list(range(8)))
    return np.concatenate([np.asarray(r["out"]) for r in res.results], axis=0).astype(np.float32)
```
